# Optimizing a Trainium2 kernel written in Bass

```python
import math
import jax, jax.numpy as jnp
from jax import lax
import numpy as np

D_MODEL = 1024
BATCH = 4
SEQ = 8192
DEPTH = 1

PLE_DIM = 256
MLA_HEADS = 8
QK_NOPE = 64
QK_ROPE = 32
V_HEAD = 64
Q_LORA = 384
KV_LORA = 256
ROPE_THETA = 10000.0
Q_BLOCK = 128
SSM_WIDTH = 512
SSM_GROUP = 16
SSM_GROUPS = SSM_WIDTH // SSM_GROUP
SSM_STATE = 64
DT_MIN = 1e-3
DT_MAX = 1e-1
D_FF = 2816
LN_EPS = 1e-5
RMS_EPS = 1e-6
ALPHA = (2.0 * DEPTH) ** 0.25
BETA = (8.0 * DEPTH) ** -0.25
MLA_WIDTH = MLA_HEADS * V_HEAD
SPLIT_POINTS = (Q_LORA,
                Q_LORA + KV_LORA,
                Q_LORA + KV_LORA + QK_ROPE,
                Q_LORA + KV_LORA + QK_ROPE + SSM_WIDTH,
                Q_LORA + KV_LORA + QK_ROPE + SSM_WIDTH + D_MODEL)
IN_COLS = Q_LORA + KV_LORA + QK_ROPE + SSM_WIDTH + 2 * D_MODEL

kernel_name = "hybrid_mla_s5_macaron_deepnorm_encoder"


def layer_norm(x, g, b):
    xf = x.astype(jnp.float32)
    mu = jnp.mean(xf, axis=-1, keepdims=True)
    var = jnp.mean(jnp.square(xf - mu), axis=-1, keepdims=True)
    y = (xf - mu) * lax.rsqrt(var + LN_EPS)
    return (y * g.astype(jnp.float32) + b.astype(jnp.float32)).astype(x.dtype)


def rms_norm(x, g):
    xf = x.astype(jnp.float32)
    y = xf * lax.rsqrt(jnp.mean(jnp.square(xf), axis=-1, keepdims=True) + RMS_EPS)
    return (y * g.astype(jnp.float32)).astype(x.dtype)


def swiglu(x, w1, w3, w2):
    return (jax.nn.silu(x @ w1) * (x @ w3)) @ w2


def rope_tables(seq_len, dtype):
    pos = jnp.arange(seq_len, dtype=jnp.float32)
    inv_freq = ROPE_THETA ** (-jnp.arange(0, QK_ROPE, 2, dtype=jnp.float32) / QK_ROPE)
    ang = pos[:, None] * inv_freq[None, :]
    return jnp.cos(ang).astype(dtype), jnp.sin(ang).astype(dtype)


def apply_rope(x, cos, sin):
    half = x.shape[-1] // 2
    x1, x2 = x[..., :half], x[..., half:]
    return jnp.concatenate([x1 * cos - x2 * sin, x2 * cos + x1 * sin], axis=-1)


def mla_attention(q_lat, kv_lat, k_rope, q_norm_g, kv_norm_g, w_uq, w_uk, w_uv):
    B, S, _ = q_lat.shape
    cos, sin = rope_tables(S, q_lat.dtype)
    c_q = rms_norm(q_lat, q_norm_g)
    q = jnp.einsum('bsr,rhd->bshd', c_q, w_uq)
    q_nope = q[..., :QK_NOPE]
    q_rope = apply_rope(q[..., QK_NOPE:], cos[:, None, :], sin[:, None, :])
    c_kv = rms_norm(kv_lat, kv_norm_g)
    k_nope = jnp.einsum('bsr,rhd->bshd', c_kv, w_uk)
    v = jnp.einsum('bsr,rhd->bshd', c_kv, w_uv)
    k_rope = apply_rope(k_rope, cos, sin)
    scale = (QK_NOPE + QK_ROPE) ** -0.5
    n_blk = S // Q_BLOCK
    qn_blk = q_nope.reshape(B, n_blk, Q_BLOCK, MLA_HEADS, QK_NOPE).transpose(1, 0, 2, 3, 4)
    qr_blk = q_rope.reshape(B, n_blk, Q_BLOCK, MLA_HEADS, QK_ROPE).transpose(1, 0, 2, 3, 4)

    def attend(blk):
        qn, qr = blk
        s = (jnp.einsum('bqhd,bkhd->bhqk', qn, k_nope)
             + jnp.einsum('bqhd,bkd->bhqk', qr, k_rope))
        pr = jax.nn.softmax(s.astype(jnp.float32) * scale, axis=-1).astype(v.dtype)
        return jnp.einsum('bhqk,bkhd->bqhd', pr, v)

    o = lax.map(attend, (qn_blk, qr_blk))
    return o.transpose(1, 0, 2, 3, 4).reshape(B, S, MLA_WIDTH)


def _ssm_combine(left, right):
    a_l, b_l = left
    a_r, b_r = right
    return a_r * a_l, a_r * b_l + b_r


def s5_direction(u, lam_re, lam_im, log_dt, b_re, b_im, c_re, c_im, reverse):
    S = u.shape[1]
    f32 = jnp.float32
    lam = lax.complex(lam_re.astype(f32), lam_im.astype(f32))
    dt = jnp.exp(log_dt.astype(f32))[:, None]
    lam_bar = jnp.exp(lam * dt)
    b = lax.complex(b_re.astype(f32), b_im.astype(f32))
    b_bar = ((lam_bar - 1.0) / lam)[..., None] * b
    bu = jnp.einsum('bsgc,gpc->sbgp', u.astype(jnp.complex64), b_bar)
    a = jnp.broadcast_to(lam_bar[None, None], (S, 1) + lam_bar.shape)
    _, xs = lax.associative_scan(_ssm_combine, (a, bu), reverse=reverse, axis=0)
    c = lax.complex(c_re.astype(f32), c_im.astype(f32))
    return jnp.einsum('sbgp,gcp->bsgc', xs, c).real


def s5_branch(u, f_params, b_params, d_skip, w_glu):
    B, S, _ = u.shape
    ug = u.astype(jnp.float32).reshape(B, S, SSM_GROUPS, SSM_GROUP)
    y = (s5_direction(ug, *f_params, reverse=False)
         + s5_direction(ug, *b_params, reverse=True)
         + d_skip.astype(jnp.float32) * ug)
    y = y.reshape(B, S, SSM_WIDTH).astype(u.dtype)
    z = jax.nn.gelu(y)
    return z * jax.nn.sigmoid(z @ w_glu)


def setup_inputs(seed: int = 0) -> dict:
    key = jax.random.key(seed)
    ks = iter(jax.random.split(key, 64))
    f32 = jnp.float32
    L, D, G, P, C = DEPTH, D_MODEL, SSM_GROUPS, SSM_STATE, SSM_GROUP

    def nrm(shape, scale):
        return jax.random.normal(next(ks), shape, f32) * scale

    def gain(n):
        return 1.0 + nrm((L, n), 0.02)

    def bias(n):
        return nrm((L, n), 0.02)

    def ssm_dir():
        lam_re = -0.5 + nrm((L, G, P), 0.01)
        lam_im = math.pi * jnp.arange(P, dtype=f32) + nrm((L, G, P), 0.01)
        log_dt = jax.random.uniform(next(ks), (L, G), f32, math.log(DT_MIN), math.log(DT_MAX))
        b_re = nrm((L, G, P, C), (2.0 * C) ** -0.5)
        b_im = nrm((L, G, P, C), (2.0 * C) ** -0.5)
        c_re = nrm((L, G, C, P), (2.0 * P) ** -0.5)
        c_im = nrm((L, G, C, P), (2.0 * P) ** -0.5)
        return lam_re, lam_im, log_dt, b_re, b_im, c_re, c_im

    x = nrm((BATCH, SEQ, D), 1.0)
    p = nrm((DEPTH, BATCH, SEQ, PLE_DIM), 1.0)
    ffn1_w1 = nrm((L, D, D_FF), D ** -0.5)
    ffn1_w3 = nrm((L, D, D_FF), D ** -0.5)
    ffn1_w2 = nrm((L, D_FF, D), BETA * D_FF ** -0.5)
    ln1_g, ln1_b = gain(D), bias(D)
    w_in = nrm((L, D, IN_COLS), D ** -0.5)
    q_norm_g = gain(Q_LORA)
    kv_norm_g = gain(KV_LORA)
    w_uq = nrm((L, Q_LORA, MLA_HEADS, QK_NOPE + QK_ROPE), Q_LORA ** -0.5)
    w_uk = nrm((L, KV_LORA, MLA_HEADS, QK_NOPE), KV_LORA ** -0.5)
    w_uv = nrm((L, KV_LORA, MLA_HEADS, V_HEAD), BETA * KV_LORA ** -0.5)
    w_o_attn = nrm((L, MLA_WIDTH, D), BETA * MLA_WIDTH ** -0.5)
    (ssm_lam_re_f, ssm_lam_im_f, ssm_log_dt_f, ssm_b_re_f, ssm_b_im_f,
     ssm_c_re_f, ssm_c_im_f) = ssm_dir()
    (ssm_lam_re_b, ssm_lam_im_b, ssm_log_dt_b, ssm_b_re_b, ssm_b_im_b,
     ssm_c_re_b, ssm_c_im_b) = ssm_dir()
    ssm_d = nrm((L, G, C), 1.0)
    w_glu = nrm((L, SSM_WIDTH, SSM_WIDTH), SSM_WIDTH ** -0.5)
    w_o_ssm = nrm((L, SSM_WIDTH, D), BETA * SSM_WIDTH ** -0.5)
    w_out = nrm((L, D, D), BETA * D ** -0.5)
    ln2_g, ln2_b = gain(D), bias(D)
    ffn2_w1 = nrm((L, D, D_FF), D ** -0.5)
    ffn2_w3 = nrm((L, D, D_FF), D ** -0.5)
    ffn2_w2 = nrm((L, D_FF, D), BETA * D_FF ** -0.5)
    ln3_g, ln3_b = gain(D), bias(D)
    ple_w_proj = nrm((L, PLE_DIM, D), BETA * PLE_DIM ** -0.5)
    ple_w_gate = nrm((L, D, D), D ** -0.5)
    ln4_g, ln4_b = gain(D), bias(D)
    return {
        "x": x, "p": p,
        "ffn1_w1": ffn1_w1, "ffn1_w3": ffn1_w3, "ffn1_w2": ffn1_w2,
        "ln1_g": ln1_g, "ln1_b": ln1_b,
        "w_in": w_in, "q_norm_g": q_norm_g, "kv_norm_g": kv_norm_g,
        "w_uq": w_uq, "w_uk": w_uk, "w_uv": w_uv, "w_o_attn": w_o_attn,
        "ssm_lam_re_f": ssm_lam_re_f, "ssm_lam_im_f": ssm_lam_im_f, "ssm_log_dt_f": ssm_log_dt_f,
        "ssm_b_re_f": ssm_b_re_f, "ssm_b_im_f": ssm_b_im_f,
        "ssm_c_re_f": ssm_c_re_f, "ssm_c_im_f": ssm_c_im_f,
        "ssm_lam_re_b": ssm_lam_re_b, "ssm_lam_im_b": ssm_lam_im_b, "ssm_log_dt_b": ssm_log_dt_b,
        "ssm_b_re_b": ssm_b_re_b, "ssm_b_im_b": ssm_b_im_b,
        "ssm_c_re_b": ssm_c_re_b, "ssm_c_im_b": ssm_c_im_b,
        "ssm_d": ssm_d, "w_glu": w_glu, "w_o_ssm": w_o_ssm, "w_out": w_out,
        "ln2_g": ln2_g, "ln2_b": ln2_b,
        "ffn2_w1": ffn2_w1, "ffn2_w3": ffn2_w3, "ffn2_w2": ffn2_w2,
        "ln3_g": ln3_g, "ln3_b": ln3_b,
        "ple_w_proj": ple_w_proj, "ple_w_gate": ple_w_gate,
        "ln4_g": ln4_g, "ln4_b": ln4_b,
    }


def reference(x, p, ffn1_w1, ffn1_w3, ffn1_w2, ln1_g, ln1_b,
              w_in, q_norm_g, kv_norm_g, w_uq, w_uk, w_uv, w_o_attn,
              ssm_lam_re_f, ssm_lam_im_f, ssm_log_dt_f, ssm_b_re_f, ssm_b_im_f,
              ssm_c_re_f, ssm_c_im_f,
              ssm_lam_re_b, ssm_lam_im_b, ssm_log_dt_b, ssm_b_re_b, ssm_b_im_b,
              ssm_c_re_b, ssm_c_im_b,
              ssm_d, w_glu, w_o_ssm, w_out, ln2_g, ln2_b,
              ffn2_w1, ffn2_w3, ffn2_w2, ln3_g, ln3_b,
              ple_w_proj, ple_w_gate, ln4_g, ln4_b):
    for i in range(DEPTH):
        x = layer_norm(ALPHA * x + 0.5 * swiglu(x, ffn1_w1[i], ffn1_w3[i], ffn1_w2[i]),
                       ln1_g[i], ln1_b[i])
        proj = x @ w_in[i]
        q_lat, kv_lat, k_rope, u, g_a, g_b = jnp.split(proj, SPLIT_POINTS, axis=-1)
        y_a = mla_attention(q_lat, kv_lat, k_rope, q_norm_g[i], kv_norm_g[i],
                            w_uq[i], w_uk[i], w_uv[i]) @ w_o_attn[i]
        f_params = (ssm_lam_re_f[i], ssm_lam_im_f[i], ssm_log_dt_f[i],
                    ssm_b_re_f[i], ssm_b_im_f[i], ssm_c_re_f[i], ssm_c_im_f[i])
        b_params = (ssm_lam_re_b[i], ssm_lam_im_b[i], ssm_log_dt_b[i],
                    ssm_b_re_b[i], ssm_b_im_b[i], ssm_c_re_b[i], ssm_c_im_b[i])
        y_b = s5_branch(u, f_params, b_params, ssm_d[i].reshape(SSM_GROUPS, SSM_GROUP),
                        w_glu[i]) @ w_o_ssm[i]
        merged = jax.nn.sigmoid(g_a) * y_a + jax.nn.sigmoid(g_b) * y_b
        x = layer_norm(ALPHA * x + merged @ w_out[i], ln2_g[i], ln2_b[i])
        x = layer_norm(ALPHA * x + 0.5 * swiglu(x, ffn2_w1[i], ffn2_w3[i], ffn2_w2[i]),
                       ln3_g[i], ln3_b[i])
        ple = jax.nn.sigmoid(x @ ple_w_gate[i]) * (p[i] @ ple_w_proj[i])
        x = layer_norm(ALPHA * x + ple, ln4_g[i], ln4_b[i])
    return x
```

```python
import math
from contextlib import ExitStack
import numpy as np
import concourse.bass as bass
import concourse.mybir as mybir
from concourse.bass_utils import run_bass_kernel_spmd

F32 = mybir.dt.float32
BF16 = mybir.dt.bfloat16
ALU = mybir.AluOpType
AF = mybir.ActivationFunctionType

D = 1024
DFF = 2816
NCH = 8
HCH = 22
NH = 8
ALPHA = 2.0 ** 0.25
LN_EPS = 1e-5
RMS_EPS = 1e-6
TT = 512
PI = math.pi


class Prog:
    CH = 1000
    NDS = 24

    def __init__(self, nc):
        self.nc = nc
        self.ops = []
        self.groups = {}
        self.eng = {"pe": nc.tensor, "act": nc.scalar, "dve": nc.vector, "pool": nc.gpsimd, "sp": nc.sync}

    def add(self, eng, fn, reads=(), writes=(), dma=False, inc=None):
        self.ops.append(dict(eng=eng, fn=fn, reads=tuple(reads), writes=tuple(writes), dma=dma, inc=inc, bar=False))

    def barrier(self):
        self.ops.append(dict(eng=None, fn=None, reads=(), writes=(), dma=False, inc=None, bar=True))

    def wr(self, grp):
        g = self.groups.setdefault(grp, [])
        nm = f"{grp}#{len(g)}"
        g.append(nm)
        return nm

    def rd(self, grp):
        return list(self.groups.get(grp, []))

    def emit(self, final_res, stack):
        ops = self.ops
        n = len(ops)
        last_w = {}
        readers = {}
        deps = [None] * n
        dma_ids = []
        last_eng = {}
        pend_bar = {}
        for i, o in enumerate(ops):
            if o["bar"]:
                bd = set(last_eng.values()) | set(dma_ids[-self.NDS:])
                pend_bar = {e: set(bd) for e in self.eng}
                deps[i] = []
                continue
            d = set()
            if o["eng"] in pend_bar:
                d |= pend_bar.pop(o["eng"])
            if not o["dma"]:
                last_eng[o["eng"]] = i
            for r in o["reads"]:
                if r in last_w:
                    d.add(last_w[r])
            for w in o["writes"]:
                if w in last_w:
                    d.add(last_w[w])
                for x in readers.get(w, ()):
                    d.add(x)
            if o["dma"]:
                if len(dma_ids) >= self.NDS:
                    d.add(dma_ids[len(dma_ids) - self.NDS])
                dma_ids.append(i)
            d.discard(i)
            keep = []
            for x in d:
                ox = ops[x]
                if (not ox["dma"]) and ox["eng"] == o["eng"] and o["eng"] == "pe" and not o["dma"]:
                    continue
                keep.append(x)
            deps[i] = sorted(keep)
            for r in o["reads"]:
                readers.setdefault(r, []).append(i)
            for w in o["writes"]:
                last_w[w] = i
                readers[w] = []
        fin = sorted({last_w[r] for r in final_res if r in last_w})
        needed = [False] * n
        for i in range(n):
            for x in deps[i]:
                needed[x] = True
        for x in fin:
            needed[x] = True
        cnt = {e: 0 for e in self.eng}
        dcnt = [0] * self.NDS
        sig = [None] * n
        nd = 0
        for i, o in enumerate(ops):
            if o["bar"]:
                continue
            if o["dma"]:
                k = nd % self.NDS
                nd += 1
                dcnt[k] += 1
                sig[i] = ("d", k, dcnt[k] * (o["inc"] or 16))
                if o["inc"]:
                    raise RuntimeError("custom inc unsupported")
            elif needed[i]:
                cnt[o["eng"]] += 1
                sig[i] = ("e", o["eng"], cnt[o["eng"]])
        sems = {}
        for e in self.eng:
            for c in range((cnt[e] + self.CH - 1) // self.CH + 1):
                sems[("e", e, c)] = stack.enter_context(self.nc.semaphore(f"s_{e}_{c}"))
        for k in range(self.NDS):
            sems[("d", k)] = stack.enter_context(self.nc.semaphore(f"s_d{k}"))
        known = {e: {} for e in self.eng}
        snap = [None] * n

        def key_of(s):
            return ("e", s[1]) if s[0] == "e" else ("d", s[1])

        def wait(e, x):
            s = sig[x]
            kk = key_of(s)
            kn = known[e]
            if kn.get(kk, 0) >= s[2]:
                return
            h = self.eng[e]
            if s[0] == "e":
                c = (s[2] - 1) // self.CH
                h.wait_ge(sems[("e", s[1], c)], (s[2] - 1) % self.CH + 1)
            else:
                h.wait_ge(sems[("d", s[1])], s[2])
            kn[kk] = s[2]
            sn = snap[x]
            if sn:
                for k2, v2 in sn.items():
                    if kn.get(k2, 0) < v2:
                        kn[k2] = v2

        for i, o in enumerate(ops):
            if o["bar"]:
                continue
            e = o["eng"]
            for x in deps[i]:
                wait(e, x)
            ins = o["fn"]()
            s = sig[i]
            if s is not None:
                if s[0] == "e":
                    c = (s[2] - 1) // self.CH
                    ins.then_inc(sems[("e", e, c)], 1)
                else:
                    ins.then_inc(sems[("d", s[1])], 16)
                snap[i] = dict(known[e])
        for x in fin:
            wait("sp", x)


def build(L):
    NT = L // TT
    NK = 2 * L
    K1 = L // 16
    T3 = K1 // 16
    assert L % 512 == 0 and K1 % 16 == 0 and T3 >= 1
    nc = bass.Bass("TRN2", target_bir_lowering=False)
    P = Prog(nc)
    es = ExitStack()
    cur = [es]

    def din(name, shape, dt=F32):
        return nc.dram_tensor(name, list(shape), dt, kind="ExternalInput")

    def dscr(name, shape, dt):
        return nc.dram_tensor(name, list(shape), dt)

    sbn = [0]

    def sb(name, shape, dt):
        sbn[0] += 1
        return cur[0].enter_context(nc.sbuf_tensor(f"{name}_{sbn[0]}", list(shape), dt))

    xT = din("xT", [NCH, 128, L])
    pT = din("pT", [2, 128, L])
    cosT = din("cosT", [128, L])
    sinT = din("sinT", [128, L])
    wnames = {
        "w1a": (D, DFF), "w3a": (D, DFF), "w2a": (DFF, D), "win": (D, 3232), "wuq": (384, 768),
        "wuk": (256, 512), "wuv": (256, 512), "woa": (512, D), "wglu": (512, 512), "wos": (512, D),
        "wout": (D, D), "w1b": (D, DFF), "w3b": (D, DFF), "w2b": (DFF, D), "wpp": (256, D), "wpg": (D, D),
    }
    wf = {k: din(k, v) for k, v in wnames.items()}
    wb = {k: dscr(k + "_bf", v, BF16) for k, v in wnames.items()}
    lnp = din("lnp", [128, 8 * NCH])
    qkg = din("qkg", [128, 5])
    ssd = din("ssd", [128, 4])
    sst = din("sst", [2, 3, 128, 16])
    ssr = din("ssr", [2, 3, 128, 2048])
    ssb = din("ssb", [2, 2, 128, 2048])
    ssc = din("ssc", [2, 2, 128, 2048])
    selp = din("selp", [128, 2])
    out = nc.dram_tensor("outT", [NCH, 128, L], F32, kind="ExternalOutput")

    X1 = dscr("X1", [NCH, 128, L], F32)
    QT = dscr("QT", [NH, 96, L], BF16)
    NSPL = max(1, L // 1024)
    Ls = L // NSPL
    CKVs = [dscr(f"CKV{i}", [288, Ls], BF16) for i in range(NSPL)]
    CKVALLs = [dscr(f"CKVALL{i}", [2 * 288, Ls], BF16) for i in range(NSPL)]
    U32 = dscr("U32", [4, 128, L], F32)
    UBF = dscr("UBF", [4, 128, L], BF16)
    GA = dscr("GA", [NCH, 128, L], F32)
    GB = dscr("GB", [NCH, 128, L], F32)
    OT = dscr("OT", [4, 128, L], BF16)
    YS = dscr("YS", [4, 128, L], F32)
    SX = dscr("SX", [128, 32], F32)
    SXALL = dscr("SXALL", [256, 32], F32)
    import os as _os
    if _os.environ.get("DUMMY_MB"):
        DUM = dscr("DUM", [int(_os.environ["DUMMY_MB"]) * 2, 128, 1024], F32)

    ps = [es.enter_context(nc.psum_tensor(f"ps{i}", [128, 512], F32)) for i in range(8)]
    pctr = [0]

    def bank():
        b = pctr[0] % 8
        pctr[0] += 1
        return b

    V, S, T, G, SP_ = "dve", "act", "pe", "pool", "sp"

    def mm(o, lhsT, rhs, start, stop, reads, writes):
        P.add(T, lambda: nc.tensor.matmul(o, lhsT, rhs, start=start, stop=stop), reads, writes)

    def act(o, i, func, reads, writes, scale=None, bias=None):
        kw = {}
        if scale is not None:
            kw["scale"] = scale
        if bias is not None:
            kw["bias"] = bias
        P.add(S, lambda: nc.scalar.activation(out=o, in_=i, func=func, **kw), reads, writes)

    def tt(o, a, b, op, reads, writes, eng=V):
        h = nc.vector if eng == V else nc.gpsimd
        P.add(eng, lambda: h.tensor_tensor(out=o, in0=a, in1=b, op=op), reads, writes)

    def ts(o, a, s1, s2, op0, op1, reads, writes, eng=V):
        h = nc.vector if eng == V else nc.gpsimd
        if op1 is None:
            P.add(eng, lambda: h.tensor_scalar(out=o, in0=a, scalar1=s1, scalar2=None, op0=op0), reads, writes)
        else:
            P.add(eng, lambda: h.tensor_scalar(out=o, in0=a, scalar1=s1, scalar2=s2, op0=op0, op1=op1), reads, writes)

    def stt(o, a, s, b, op0, op1, reads, writes):
        P.add(V, lambda: nc.vector.scalar_tensor_tensor(out=o, in0=a, scalar=s, in1=b, op0=op0, op1=op1), reads, writes)

    def cp(o, i, reads, writes, eng=V):
        h = nc.vector if eng == V else nc.gpsimd
        P.add(eng, lambda: h.tensor_copy(out=o, in_=i), reads, writes)

    def dma(o, i, reads, writes, q=SP_):
        h = {"sp": nc.sync, "pool": nc.gpsimd, "act": nc.scalar}[q]
        P.add(q, lambda: h.dma_start(out=o, in_=i), reads, writes, dma=True)

    for k in ["w1a", "w3a", "w2a", "win", "wuq", "wuk", "wuv", "wout", "woa", "wglu", "wos", "w1b", "w3b", "w2b", "wpp", "wpg"]:
        r, c = wnames[k]
        if r * c > 1500000:
            hh = r // 2
            dma(wb[k][0:hh, :], wf[k][0:hh, :], [], ["W_" + k + "0"], q=G)
            dma(wb[k][hh:r, :], wf[k][hh:r, :], [], ["W_" + k + "1"], q=G)
        else:
            dma(wb[k][:, :], wf[k][:, :], [], ["W_" + k + "0"], q=G)

    def wres(k):
        r, c = wnames[k]
        return ["W_" + k + "0", "W_" + k + "1"] if r * c > 1500000 else ["W_" + k + "0"]

    def wview(k):
        return wb[k].ap().rearrange("(kc p) n -> p kc n", p=128)

    if _os.environ.get("DUMMY_MB"):
        dma(DUM[0, :, :], wf["wpg"][0:128, :], [], ["DUM"], q=G)
    ones_d = sb("ones_d", [128, 128], BF16)
    ones_q = sb("ones_q", [128, 128], BF16)
    lnp_s = sb("lnp_s", [128, 8 * NCH], F32)
    qkg_s = sb("qkg_s", [128, 5], F32)
    ssd_s = sb("ssd_s", [128, 4], F32)
    sel_s = sb("sel_s", [128, 2], F32)
    P.add(V, lambda: nc.vector.memset(ones_d[:, :], 1.0 / 1024.0), [], ["ones_d"])
    P.add(V, lambda: nc.vector.memset(ones_q[:, :], 1.0), [], ["ones_q"])
    dma(lnp_s[:, :], lnp[:, :], [], ["lnp"])
    dma(qkg_s[:, :], qkg[:, :], [], ["qkg"])
    dma(ssd_s[:, :], ssd[:, :], [], ["ssd"])
    dma(sel_s[:, :], selp[:, :], [], ["sel"])

    CM = {}

    def alloc_common():
        CM["xin"] = sb("xin", [128, NCH, TT], F32)
        CM["xbf"] = sb("xbf", [128, NCH, TT], BF16)
        CM["hbf"] = sb("hbf", [128, HCH, TT], BF16)
        CM["zb"] = sb("zb", [128, NCH, TT], BF16)
        CM["tmpf"] = [sb(f"tmpf{i}", [128, TT], F32) for i in range(3)]
        CM["mean"] = sb("mean_s", [128, TT], F32)
        CM["rstd"] = sb("rstd_s", [128, TT], F32)
        CM["wA"] = [sb(f"wA{i}", [128, 8, 512], BF16) for i in range(3)]
        CM["wB"] = [sb(f"wB{i}", [128, HCH, 256], BF16) for i in range(2)]

    wactr = [0]
    wbctr = [0]

    def load_wA(k, c0, c1):
        i = wactr[0] % 3
        wactr[0] += 1
        dma(CM["wA"][i][:, :, 0:c1 - c0], wview(k)[:, :, c0:c1], wres(k), [f"wA{i}"])
        return i

    def load_wB(k, c0, c1):
        i = wbctr[0] % 2
        wbctr[0] += 1
        dma(CM["wB"][i][:, :, 0:c1 - c0], wview(k)[:, :, c0:c1], wres(k), [f"wB{i}"])
        return i

    tctr = [0]

    def tmp():
        i = tctr[0] % 3
        tctr[0] += 1
        return i

    def ffn(k1, k3, k2):
        xt, xbf, hbf, tmpf, wA, wB = CM["xin"], CM["xbf"], CM["hbf"], CM["tmpf"], CM["wA"], CM["wB"]
        ngrp = (DFF + 511) // 512
        for g in range(ngrp):
            c0, c1 = g * 512, min(DFF, (g + 1) * 512)
            i1 = load_wA(k1, c0, c1)
            i3 = load_wA(k3, c0, c1)
            for cc in range((c1 - c0) // 128):
                c = g * 4 + cc
                b1, b3 = bank(), bank()
                for k in range(NCH):
                    mm(ps[b1][:, :], wA[i1][:, k, cc * 128:(cc + 1) * 128], xbf[:, k, :], k == 0, k == NCH - 1,
                       [f"wA{i1}", "xbf"], [f"ps{b1}"])
                for k in range(NCH):
                    mm(ps[b3][:, :], wA[i3][:, k, cc * 128:(cc + 1) * 128], xbf[:, k, :], k == 0, k == NCH - 1,
                       [f"wA{i3}", "xbf"], [f"ps{b3}"])
                ti = tmp()
                act(tmpf[ti][:, :], ps[b1][:, :], AF.Silu, [f"ps{b1}"], [f"tmpf{ti}"])
                tt(hbf[:, c, :], tmpf[ti][:, :], ps[b3][:, :], ALU.mult, [f"tmpf{ti}", f"ps{b3}"], [f"hbf{c}"])
        for g in range(4):
            i2 = load_wB(k2, g * 256, (g + 1) * 256)
            for oc in range(2):
                o = g * 2 + oc
                b = bank()
                for k in range(HCH):
                    mm(ps[b][:, :], wB[i2][:, k, oc * 128:(oc + 1) * 128], hbf[:, k, :], k == 0, k == HCH - 1,
                       [f"wB{i2}", f"hbf{k}"], [f"ps{b}"])
                stt(xt[:, o, :], ps[b][:, :], 0.5 / ALPHA, xt[:, o, :], ALU.mult, ALU.add, [f"ps{b}", "xin"], ["xin"])

    def rsqrt_inplace(r, res):
        act(r, r, AF.Ln, [res], [res])
        act(r, r, AF.Exp, [res], [res], scale=-0.5)

    def layer_norm(li, eps, want_bf):
        xt, xbf, zb, tmpf, mean_s, rstd_s = CM["xin"], CM["xbf"], CM["zb"], CM["tmpf"], CM["mean"], CM["rstd"]
        act(zb[:, :, :], xt[:, :, :], AF.Copy, ["xin"], ["zb"])
        bm, bq = bank(), bank()
        for k in range(NCH):
            mm(ps[bm][:, :], ones_d[:, :], zb[:, k, :], k == 0, k == NCH - 1, ["ones_d", "zb"], [f"ps{bm}"])
        act(zb[:, :, :], xt[:, :, :], AF.Square, ["xin"], ["zb"])
        for k in range(NCH):
            mm(ps[bq][:, :], ones_d[:, :], zb[:, k, :], k == 0, k == NCH - 1, ["ones_d", "zb"], [f"ps{bq}"])
        act(mean_s[:, :], ps[bm][:, :], AF.Copy, [f"ps{bm}"], ["mean"])
        ti = tmp()
        tt(tmpf[ti][:, :], mean_s[:, :], mean_s[:, :], ALU.mult, ["mean"], [f"tmpf{ti}"])
        tt(tmpf[ti][:, :], ps[bq][:, :], tmpf[ti][:, :], ALU.subtract, [f"ps{bq}", f"tmpf{ti}"], [f"tmpf{ti}"])
        ts(rstd_s[:, :], tmpf[ti][:, :], eps, None, ALU.add, None, [f"tmpf{ti}"], ["rstd"])
        rsqrt_inplace(rstd_s[:, :], "rstd")
        gcol = li * 16
        for o in range(NCH):
            tt(xt[:, o, :], xt[:, o, :], mean_s[:, :], ALU.subtract, ["xin", "mean"], ["xin"])
            tt(xt[:, o, :], xt[:, o, :], rstd_s[:, :], ALU.mult, ["xin", "rstd"], ["xin"], eng=G)
            act(xt[:, o, :], xt[:, o, :], AF.Identity, ["xin", "lnp"], ["xin"],
                scale=lnp_s[:, gcol + o:gcol + o + 1], bias=lnp_s[:, gcol + 8 + o:gcol + 8 + o + 1])
        if want_bf:
            cp(xbf[:, :, :], xt[:, :, :], ["xin"], ["xbf"], eng=G)

    pst = ExitStack()
    cur[0] = pst
    alloc_common()
    xt, xbf, tmpf, rstd_s, wA = CM["xin"], CM["xbf"], CM["tmpf"], CM["rstd"], CM["wA"]
    xv = xT.ap().rearrange("c p n -> p c n")
    x1v = X1.ap().rearrange("c p n -> p c n")
    ql = sb("ql", [128, 3, TT], F32)
    qsq = sb("qsq", [128, 3, TT], BF16)
    cqb = sb("cqb", [128, 3, TT], BF16)
    wuq_s = sb("wuq_s", [128, 3, 768], BF16)
    qn = sb("qn", [128, 4, TT], BF16)
    cos_s = sb("cos_s", [128, TT], F32)
    sin_s = sb("sin_s", [128, TT], F32)
    r1s = sb("r1s", [128, TT], F32)
    r2s = sb("r2s", [128, TT], F32)
    ro1 = sb("ro1", [128, TT], BF16)
    ro2 = sb("ro2", [128, TT], BF16)
    u32s = sb("u32s", [128, 4, TT], F32)
    ubfs = sb("ubfs", [128, 4, TT], BF16)
    gsm = [sb(f"gsm{i}", [128, TT], F32) for i in range(2)]
    dma(wuq_s[:, :, :], wview("wuq"), wres("wuq"), ["wuq_s"])

    def rope(pa, pb, np_, outa, outb, ra, rb_):
        act(r1s[0:np_, :], ps[pa][0:np_, :], AF.Copy, [f"ps{pa}"], ["r1s"])
        act(r2s[0:np_, :], ps[pb][0:np_, :], AF.Copy, [f"ps{pb}"], ["r2s"])
        t0, t1 = tmp(), tmp()
        tt(tmpf[t0][0:np_, :], r1s[0:np_, :], cos_s[0:np_, :], ALU.mult, ["r1s", "cos"], [f"tmpf{t0}"])
        tt(tmpf[t1][0:np_, :], r2s[0:np_, :], sin_s[0:np_, :], ALU.mult, ["r2s", "sin"], [f"tmpf{t1}"])
        tt(outa, tmpf[t0][0:np_, :], tmpf[t1][0:np_, :], ALU.subtract, [f"tmpf{t0}", f"tmpf{t1}"], [ra])
        tt(tmpf[t0][0:np_, :], r2s[0:np_, :], cos_s[0:np_, :], ALU.mult, ["r2s", "cos"], [f"tmpf{t0}"])
        tt(tmpf[t1][0:np_, :], r1s[0:np_, :], sin_s[0:np_, :], ALU.mult, ["r1s", "sin"], [f"tmpf{t1}"])
        tt(outb, tmpf[t0][0:np_, :], tmpf[t1][0:np_, :], ALU.add, [f"tmpf{t0}", f"tmpf{t1}"], [rb_])

    def rmsnorm(src, sres, nch, dim, gcol0, dst, dres):
        act(qsq[:, 0:nch, :], src[:, 0:nch, :], AF.Square, [sres], ["qsq"])
        b = bank()
        for k in range(nch):
            mm(ps[b][:, :], ones_q[:, :], qsq[:, k, :], k == 0, k == nch - 1, ["ones_q", "qsq"], [f"ps{b}"])
        ts(rstd_s[:, :], ps[b][:, :], 1.0 / dim, RMS_EPS, ALU.mult, ALU.add, [f"ps{b}"], ["rstd"])
        rsqrt_inplace(rstd_s[:, :], "rstd")
        for k in range(nch):
            tt(src[:, k, :], src[:, k, :], rstd_s[:, :], ALU.mult, [sres, "rstd"], [sres])
            act(dst[:, k, :], src[:, k, :], AF.Copy, [sres, "qkg"], [dres], scale=qkg_s[:, gcol0 + k:gcol0 + k + 1])

    def proj(iw, cc, M=128, col0=None):
        b = bank()
        c_lo = cc * 128 if col0 is None else col0
        for k in range(NCH):
            mm(ps[b][0:M, :], wA[iw][:, k, c_lo:c_lo + M], xbf[:, k, :], k == 0, k == NCH - 1, [f"wA{iw}", "xbf"], [f"ps{b}"])
        return b

    for t in range(NT):
        c0, c1 = t * TT, (t + 1) * TT
        dma(xt[:, :, :], xv[:, :, c0:c1], [], ["xin"])
        cp(xbf[:, :, :], xt[:, :, :], ["xin"], ["xbf"], eng=G)
        ffn("w1a", "w3a", "w2a")
        layer_norm(0, LN_EPS / (ALPHA * ALPHA), True)
        dma(x1v[:, :, c0:c1], xt[:, :, :], ["xin"], [P.wr("X1")], q=G)
        dma(cos_s[:, :], cosT[:, c0:c1], [], ["cos"])
        dma(sin_s[:, :], sinT[:, c0:c1], [], ["sin"])
        iw = load_wA("win", 0, 384)
        for c in range(3):
            b = proj(iw, c)
            act(ql[:, c, :], ps[b][:, :], AF.Copy, [f"ps{b}"], ["ql"])
        rmsnorm(ql, "ql", 3, 384.0, 0, cqb, "cqb")
        for c in range(4):
            b = bank()
            for k in range(3):
                mm(ps[b][:, :], wuq_s[:, k, c * 128:(c + 1) * 128], cqb[:, k, :], k == 0, k == 2, ["wuq_s", "cqb"], [f"ps{b}"])
            act(qn[:, c, :], ps[b][:, :], AF.Copy, [f"ps{b}"], [f"qn{c}"])
            dma(QT[2 * c, 0:64, c0:c1], qn[0:64, c, :], [f"qn{c}"], [P.wr("QT")], q=G)
            dma(QT[2 * c + 1, 0:64, c0:c1], qn[64:128, c, :], [f"qn{c}"], [P.wr("QT")], q=G)
        b1, b2 = bank(), bank()
        for k in range(3):
            mm(ps[b1][:, :], wuq_s[:, k, 512:640], cqb[:, k, :], k == 0, k == 2, ["wuq_s", "cqb"], [f"ps{b1}"])
        for k in range(3):
            mm(ps[b2][:, :], wuq_s[:, k, 640:768], cqb[:, k, :], k == 0, k == 2, ["wuq_s", "cqb"], [f"ps{b2}"])
        rope(b1, b2, 128, ro1[:, :], ro2[:, :], "ro1", "ro2")
        for h in range(NH):
            dma(QT[h, 64:80, c0:c1], ro1[h * 16:(h + 1) * 16, :], ["ro1"], [P.wr("QT")], q=G)
            dma(QT[h, 80:96, c0:c1], ro2[h * 16:(h + 1) * 16, :], ["ro2"], [P.wr("QT")], q=G)
        iw = load_wA("win", 384, 672)
        for c in range(2):
            b = proj(iw, c)
            act(ql[:, c, :], ps[b][:, :], AF.Copy, [f"ps{b}"], ["ql"])
        b1 = proj(iw, 0, M=16, col0=256)
        b2 = proj(iw, 0, M=16, col0=272)
        rmsnorm(ql, "ql", 2, 256.0, 3, cqb, "cqb")
        CKV = CKVs[c0 // Ls]
        s0, s1 = c0 % Ls, c0 % Ls + TT
        for k in range(2):
            dma(CKV[k * 128:(k + 1) * 128, s0:s1], cqb[:, k, :], ["cqb"], [P.wr(f"CKV{c0 // Ls}")], q=G)
        rope(b1, b2, 16, ro1[0:16, :], ro2[0:16, :], "ro1", "ro2")
        dma(CKV[256:272, s0:s1], ro1[0:16, :], ["ro1"], [P.wr(f"CKV{c0 // Ls}")], q=G)
        dma(CKV[272:288, s0:s1], ro2[0:16, :], ["ro2"], [P.wr(f"CKV{c0 // Ls}")], q=G)
        iw = load_wA("win", 672, 1184)
        for c in range(4):
            b = proj(iw, c)
            act(u32s[:, c, :], ps[b][:, :], AF.Copy, [f"ps{b}"], ["u32s"])
        cp(ubfs[:, :, :], u32s[:, :, :], ["u32s"], ["ubfs"], eng=G)
        dma(U32.ap().rearrange("c p n -> p c n")[:, :, c0:c1], u32s[:, :, :], ["u32s"], [P.wr("U32")], q=G)
        dma(UBF.ap().rearrange("c p n -> p c n")[:, :, c0:c1], ubfs[:, :, :], ["ubfs"], [P.wr("UBF")], q=G)
        gi_ = 0
        for gname, GD, base in (("G0", GA, 1184), ("G1", GB, 2208)):
            for half in range(2):
                iw = load_wA("win", base + half * 512, base + (half + 1) * 512)
                for cc in range(4):
                    c = half * 4 + cc
                    b = proj(iw, cc)
                    gb_ = gi_ % 2
                    gi_ += 1
                    act(gsm[gb_][:, :], ps[b][:, :], AF.Sigmoid, [f"ps{b}"], [f"gsm{gb_}"])
                    dma(GD[c, :, c0:c1], gsm[gb_][:, :], [f"gsm{gb_}"], [P.wr(gname)], q=G)
    P.barrier()
    pst.close()
    if _os.environ.get("STOP") == "A":
        P.emit(P.rd("X1"), es); es.close(); return nc

    for i in range(NSPL):
        P.add(G, lambda i=i: nc.gpsimd.collective_compute("AllGather", ALU.bypass, replica_groups=[[0, 1], [2, 3], [4, 5], [6, 7]],
                                                           ins=[CKVs[i].ap().opt()], outs=[CKVALLs[i].ap().opt()]),
              P.rd(f"CKV{i}"), [f"CKVALL{i}"])

    if _os.environ.get("STOP") == "X":
        P.emit([f"CKVALL{i}" for i in range(NSPL)], es); es.close(); return nc
    pst = ExitStack()
    cur[0] = pst
    bbR = [sb(f"bbR{d}", [128, 2048], BF16) for d in range(2)]
    bbI = [sb(f"bbI{d}", [128, 2048], BF16) for d in range(2)]
    ccR = [sb(f"ccR{d}", [128, 2048], BF16) for d in range(2)]
    ccI = [sb(f"ccI{d}", [128, 2048], BF16) for d in range(2)]
    pwA = [[sb(f"pwA{d}{l}", [128, 16, 17], F32) for l in range(3)] for d in range(2)]
    pwB = [[sb(f"pwB{d}{l}", [128, 16, 17], F32) for l in range(3)] for d in range(2)]
    pwN = [[sb(f"pwN{d}{l}", [128, 16, 17], F32) for l in range(3)] for d in range(2)]
    small = [sb(f"sm{i}", [128, 16], F32) for i in range(8)]
    zero_s = sb("zero_s", [128, 32], F32)
    finR = sb("finR", [128, 32], F32)
    gat = sb("gat", [128, 2, 32], F32)
    iniS = sb("iniS", [128, 32], F32)
    P.add(V, lambda: nc.vector.memset(zero_s[:, :], 0.0), [], ["zero_s"])
    pst2 = ExitStack()
    cur[0] = pst2
    pt = [sb(f"pt{i}", [128, 2048], F32) for i in range(7)]

    I32 = mybir.dt.int32
    isml = sb("isml", [128, 16], I32)
    ibig = sb("ibig", [128, 2048], I32)

    def sincos(zi, zres, so, co, res_s, res_c, scratch, sres, itile, ires):
        for shift, dst, dres in ((0.0, so, res_s), (0.5 * PI, co, res_c)):
            ts(scratch, zi, shift, 1.0 / (2 * PI), ALU.add, ALU.mult, [zres], [sres])
            cp(itile, scratch, [sres], [ires])
            cp(scratch, itile, [ires], [sres])
            ts(scratch, scratch, -2 * PI, None, ALU.mult, None, [sres], [sres])
            ts(dst, zi, shift, None, ALU.add, None, [zres], [dres])
            tt(dst, dst, scratch, ALU.add, [dres, sres], [dres])
            ts(scratch, dst, PI, 2 * PI, ALU.is_gt, ALU.mult, [dres], [sres])
            tt(dst, dst, scratch, ALU.subtract, [dres, sres], [dres])
            ts(scratch, dst, -PI, 2 * PI, ALU.is_lt, ALU.mult, [dres], [sres])
            tt(dst, dst, scratch, ALU.add, [dres, sres], [dres])
            act(dst, dst, AF.Sin, [dres], [dres])

    for d in range(2):
        lre, lim, ldt, zr, zi, mg, sn, cs = [small[i][:, :] for i in range(8)]
        dma(lre, sst[d, 0, :, :], [], ["sm0"])
        dma(lim, sst[d, 1, :, :], [], ["sm1"])
        dma(ldt, sst[d, 2, :, :], [], ["sm2"])
        act(ldt, ldt, AF.Exp, ["sm2"], ["sm2"])
        tt(zr, lre, ldt, ALU.mult, ["sm0", "sm2"], ["sm3"])
        tt(zi, lim, ldt, ALU.mult, ["sm1", "sm2"], ["sm4"])
        act(mg, zr, AF.Exp, ["sm3"], ["sm5"])
        sincos(zi, "sm4", sn, cs, "sm6", "sm7", zr, "sm3", isml[:, :], "isml")
        for l in range(3):
            A, B, N = pwA[d][l], pwB[d][l], pwN[d][l]
            rA, rB, rN = f"pwA{d}{l}", f"pwB{d}{l}", f"pwN{d}{l}"
            P.add(V, lambda A=A: nc.vector.memset(A[:, :, 0:1], 1.0), [], [rA])
            P.add(V, lambda B=B: nc.vector.memset(B[:, :, 0:1], 0.0), [], [rB])
            if l == 0:
                tt(A[:, :, 1], mg, cs, ALU.mult, ["sm5", "sm7"], [rA])
                tt(B[:, :, 1], mg, sn, ALU.mult, ["sm5", "sm6"], [rB])
            else:
                cp(A[:, :, 1], pwA[d][l - 1][:, :, 16], [f"pwA{d}{l-1}"], [rA])
                cp(B[:, :, 1], pwB[d][l - 1][:, :, 16], [f"pwB{d}{l-1}"], [rB])
            for j in range(2, 17):
                tt(zr, A[:, :, j - 1], A[:, :, 1], ALU.mult, [rA], ["sm3"])
                tt(zi, B[:, :, j - 1], B[:, :, 1], ALU.mult, [rB], ["sm4"])
                tt(A[:, :, j], zr, zi, ALU.subtract, ["sm3", "sm4"], [rA])
                tt(zr, A[:, :, j - 1], B[:, :, 1], ALU.mult, [rA, rB], ["sm3"])
                tt(zi, B[:, :, j - 1], A[:, :, 1], ALU.mult, [rA, rB], ["sm4"])
                tt(B[:, :, j], zr, zi, ALU.add, ["sm3", "sm4"], [rB])
            ts(N[:, :, :], B[:, :, :], -1.0, None, ALU.mult, None, [rB], [rN])
        LR, LI, DT, t0, t1, t2, t3 = [pt[i][:, :] for i in range(7)]
        dma(LR, ssr[d, 0, :, :], [], ["pt0"])
        dma(LI, ssr[d, 1, :, :], [], ["pt1"])
        dma(DT, ssr[d, 2, :, :], [], ["pt2"])
        act(DT, DT, AF.Exp, ["pt2"], ["pt2"])
        tt(t0, LR, DT, ALU.mult, ["pt0", "pt2"], ["pt3"])
        tt(t1, LI, DT, ALU.mult, ["pt1", "pt2"], ["pt4"])
        act(t0, t0, AF.Exp, ["pt3"], ["pt3"])
        sincos(t1, "pt4", t2, t3, "pt5", "pt6", DT, "pt2", ibig[:, :], "ibig")
        tt(t2, t2, t0, ALU.mult, ["pt5", "pt3"], ["pt5"])
        tt(t3, t3, t0, ALU.mult, ["pt6", "pt3"], ["pt6"])
        ts(t3, t3, -1.0, None, ALU.add, None, ["pt6"], ["pt6"])
        tt(t0, LR, LR, ALU.mult, ["pt0"], ["pt3"])
        tt(t1, LI, LI, ALU.mult, ["pt1"], ["pt4"])
        tt(t0, t0, t1, ALU.add, ["pt3", "pt4"], ["pt3"])
        P.add(V, lambda t0=t0: nc.vector.reciprocal(out=t0, in_=t0), ["pt3"], ["pt3"])
        tt(t1, t3, LR, ALU.mult, ["pt6", "pt0"], ["pt4"])
        tt(DT, t2, LI, ALU.mult, ["pt5", "pt1"], ["pt2"])
        tt(t1, t1, DT, ALU.add, ["pt4", "pt2"], ["pt4"])
        tt(t1, t1, t0, ALU.mult, ["pt4", "pt3"], ["pt4"])
        tt(DT, t2, LR, ALU.mult, ["pt5", "pt0"], ["pt2"])
        tt(LR, t3, LI, ALU.mult, ["pt6", "pt1"], ["pt0"])
        tt(DT, DT, LR, ALU.subtract, ["pt2", "pt0"], ["pt2"])
        tt(DT, DT, t0, ALU.mult, ["pt2", "pt3"], ["pt2"])
        dma(LR, ssb[d, 0, :, :], [], ["pt0"])
        dma(LI, ssb[d, 1, :, :], [], ["pt1"])
        tt(t0, t1, LR, ALU.mult, ["pt4", "pt0"], ["pt3"])
        tt(t2, DT, LI, ALU.mult, ["pt2", "pt1"], ["pt5"])
        tt(bbR[d][:, :], t0, t2, ALU.subtract, ["pt3", "pt5"], [f"bbR{d}"])
        tt(t0, t1, LI, ALU.mult, ["pt4", "pt1"], ["pt3"])
        tt(t2, DT, LR, ALU.mult, ["pt2", "pt0"], ["pt5"])
        tt(bbI[d][:, :], t0, t2, ALU.add, ["pt3", "pt5"], [f"bbI{d}"])
        dma(t3, ssc[d, 0, :, :], [], ["pt6"])
        cp(ccR[d][:, :], t3, ["pt6"], [f"ccR{d}"])
        dma(t3, ssc[d, 1, :, :], [], ["pt6"])
        ts(ccI[d][:, :], t3, -1.0, None, ALU.mult, None, ["pt6"], [f"ccI{d}"])
    P.barrier()
    pst2.close()
    cur[0] = pst
    Rt = sb("Rst", [128, L], F32)
    It = sb("Ist", [128, L], F32)
    Rbf = sb("Rbf", [128, L], BF16)
    Ibf = sb("Ibf", [128, L], BF16)
    ubig = sb("ubig", [128, 4, L], BF16)
    yacc = sb("yacc", [128, L], F32)
    e2R = sb("e2R", [128, 16, T3], F32)
    e2I = sb("e2I", [128, 16, T3], F32)
    e3R = sb("e3R", [128, T3], F32)
    e3I = sb("e3I", [128, T3], F32)
    x2pR = sb("x2pR", [128, T3], F32)
    x2pI = sb("x2pI", [128, T3], F32)
    x1pR = sb("x1pR", [128, 16, T3], F32)
    x1pI = sb("x1pI", [128, 16, T3], F32)
    dma(ubig[:, :, :], UBF.ap().rearrange("c p n -> p c n"), P.rd("UBF"), ["ubig"])

    def cmul_acc(oR, oI, pR, pI, a, b, nb, res_o, res_p, extra_reads=()):
        rd = [res_o, res_p] + list(extra_reads)
        stt(oR, pR, a, oR, ALU.mult, ALU.add, rd, [res_o])
        stt(oR, pI, nb, oR, ALU.mult, ALU.add, rd, [res_o])
        stt(oI, pR, b, oI, ALU.mult, ALU.add, rd, [res_o])
        stt(oI, pI, a, oI, ALU.mult, ALU.add, rd, [res_o])

    def blk(tn, j):
        return tn[:, j * K1:(j + 1) * K1]

    def blk3(tn, j):
        return tn[:, j * K1:(j + 1) * K1].rearrange("p (a b) -> p a b", b=T3)

    for pas in range(2):
        d = pas
        rev = (pas == 1)

        def ix(i, n):
            return (n - 1 - i) if rev else i

        ini = zero_s if pas == 0 else iniS
        ini_res = "zero_s" if pas == 0 else "iniS"
        if pas == 1:
            dma(SX[:, :], finR[:, :], ["finR"], ["SX"], q=G)
            P.add(G, lambda: nc.gpsimd.collective_compute("AllGather", ALU.bypass, replica_groups=[[0, 1], [2, 3], [4, 5], [6, 7]],
                                                           ins=[SX.ap().opt()], outs=[SXALL.ap().opt()]),
                  ["SX"], ["SXALL"])
            dma(gat[:, :, :], SXALL.ap().rearrange("(r p) n -> p r n", p=128), ["SXALL"], ["gat"])
            ts(iniS[:, :], gat[:, 0, :], sel_s[:, 0:1], None, ALU.mult, None, ["gat", "sel"], ["iniS"])
            stt(iniS[:, :], gat[:, 1, :], sel_s[:, 1:2], iniS[:, :], ALU.mult, ALU.add, ["gat", "sel", "iniS"], ["iniS"])
        for c in range(4):
            if pas == 0:
                dma(yacc[:, :], U32[c, :, :], P.rd("U32"), ["yacc"])
                ts(yacc[:, :], yacc[:, :], ssd_s[:, c:c + 1], None, ALU.mult, None, ["yacc", "ssd"], ["yacc"])
            else:
                dma(yacc[:, :], YS[c, :, :], [f"YS{c}"], ["yacc"])
            for qq in range(4):
                q = c * 4 + qq
                A0, B0, N0 = pwA[d][0], pwB[d][0], pwN[d][0]
                A1, B1, N1 = pwA[d][1], pwB[d][1], pwN[d][1]
                A2, B2, N2 = pwA[d][2], pwB[d][2], pwN[d][2]
                pres = [f"pw{x}{d}{l}" for x in "ABN" for l in range(3)]
                for t in range(NT):
                    bR, bI = bank(), bank()
                    mm(ps[bR][:, :], bbR[d][:, q * 128:(q + 1) * 128], ubig[:, c, t * TT:(t + 1) * TT], True, True, [f"bbR{d}", "ubig"], [f"ps{bR}"])
                    mm(ps[bI][:, :], bbI[d][:, q * 128:(q + 1) * 128], ubig[:, c, t * TT:(t + 1) * TT], True, True, [f"bbI{d}", "ubig"], [f"ps{bI}"])
                    act(Rt[:, t * TT:(t + 1) * TT], ps[bR][:, :], AF.Copy, [f"ps{bR}"], ["RI"])
                    act(It[:, t * TT:(t + 1) * TT], ps[bI][:, :], AF.Copy, [f"ps{bI}"], ["RI"])
                for j in range(1, 16):
                    jc, jp = ix(j, 16), ix(j - 1, 16)
                    cmul_acc(blk(Rt, jc), blk(It, jc), blk(Rt, jp), blk(It, jp),
                             A0[:, q, 1:2], B0[:, q, 1:2], N0[:, q, 1:2], "RI", "RI", pres)
                jl = ix(15, 16)
                cp(e2R[:, :, :], blk3(Rt, jl), ["RI"], ["e2"])
                cp(e2I[:, :, :], blk3(It, jl), ["RI"], ["e2"])
                for j in range(1, 16):
                    jc, jp = ix(j, 16), ix(j - 1, 16)
                    cmul_acc(e2R[:, jc, :], e2I[:, jc, :], e2R[:, jp, :], e2I[:, jp, :],
                             A1[:, q, 1:2], B1[:, q, 1:2], N1[:, q, 1:2], "e2", "e2", pres)
                cp(e3R[:, :], e2R[:, jl, :], ["e2"], ["e3"])
                cp(e3I[:, :], e2I[:, jl, :], ["e2"], ["e3"])
                for k in range(T3):
                    kc = ix(k, T3)
                    if k == 0:
                        pR, pI = ini[:, q:q + 1], ini[:, 16 + q:16 + q + 1]
                    else:
                        kp = ix(k - 1, T3)
                        pR, pI = e3R[:, kp:kp + 1], e3I[:, kp:kp + 1]
                    cmul_acc(e3R[:, kc:kc + 1], e3I[:, kc:kc + 1], pR, pI, A2[:, q, 1:2], B2[:, q, 1:2], N2[:, q, 1:2],
                             "e3", "e3", pres + [ini_res])
                kl = ix(T3 - 1, T3)
                if pas == 0:
                    cp(finR[:, q:q + 1], e3R[:, kl:kl + 1], ["e3"], ["finR"])
                    cp(finR[:, 16 + q:16 + q + 1], e3I[:, kl:kl + 1], ["e3"], ["finR"])
                k0 = ix(0, T3)
                cp(x2pR[:, k0:k0 + 1], ini[:, q:q + 1], [ini_res], ["x2p"])
                cp(x2pI[:, k0:k0 + 1], ini[:, 16 + q:16 + q + 1], [ini_res], ["x2p"])
                if T3 > 1:
                    if not rev:
                        cp(x2pR[:, 1:T3], e3R[:, 0:T3 - 1], ["e3"], ["x2p"])
                        cp(x2pI[:, 1:T3], e3I[:, 0:T3 - 1], ["e3"], ["x2p"])
                    else:
                        cp(x2pR[:, 0:T3 - 1], e3R[:, 1:T3], ["e3"], ["x2p"])
                        cp(x2pI[:, 0:T3 - 1], e3I[:, 1:T3], ["e3"], ["x2p"])
                for j in range(16):
                    jc = ix(j, 16)
                    cmul_acc(e2R[:, jc, :], e2I[:, jc, :], x2pR[:, :], x2pI[:, :],
                             A1[:, q, j + 1:j + 2], B1[:, q, j + 1:j + 2], N1[:, q, j + 1:j + 2], "e2", "x2p", pres)
                j0 = ix(0, 16)
                cp(x1pR[:, j0, :], x2pR[:, :], ["x2p"], ["x1p"])
                cp(x1pI[:, j0, :], x2pI[:, :], ["x2p"], ["x1p"])
                if not rev:
                    cp(x1pR[:, 1:16, :], e2R[:, 0:15, :], ["e2"], ["x1p"])
                    cp(x1pI[:, 1:16, :], e2I[:, 0:15, :], ["e2"], ["x1p"])
                else:
                    cp(x1pR[:, 0:15, :], e2R[:, 1:16, :], ["e2"], ["x1p"])
                    cp(x1pI[:, 0:15, :], e2I[:, 1:16, :], ["e2"], ["x1p"])
                for j in range(16):
                    jc = ix(j, 16)
                    cmul_acc(blk3(Rt, jc), blk3(It, jc), x1pR[:, :, :], x1pI[:, :, :],
                             A0[:, q, j + 1:j + 2], B0[:, q, j + 1:j + 2], N0[:, q, j + 1:j + 2], "RI", "x1p", pres)
                act(Rbf[:, :], Rt[:, :], AF.Copy, ["RI"], ["Rbf"])
                cp(Ibf[:, :], It[:, :], ["RI"], ["Ibf"], eng=G)
                for t in range(NT):
                    b = bank()
                    mm(ps[b][:, :], ccR[d][:, q * 128:(q + 1) * 128], Rbf[:, t * TT:(t + 1) * TT], True, False, [f"ccR{d}", "Rbf"], [f"ps{b}"])
                    mm(ps[b][:, :], ccI[d][:, q * 128:(q + 1) * 128], Ibf[:, t * TT:(t + 1) * TT], False, True, [f"ccI{d}", "Ibf"], [f"ps{b}"])
                    tt(yacc[:, t * TT:(t + 1) * TT], yacc[:, t * TT:(t + 1) * TT], ps[b][:, :], ALU.add, ["yacc", f"ps{b}"], ["yacc"])
            dma(YS[c, :, :], yacc[:, :], ["yacc"], [f"YS{c}"], q=G)
    P.barrier()
    pst.close()

    if _os.environ.get("STOP") == "S":
        P.emit(["YS0", "YS1", "YS2", "YS3"], es); es.close(); return nc
    pst = ExitStack()
    cur[0] = pst
    ckv = sb("ckv", [128, 2, NK], BF16)
    Kt = sb("Kt", [96, NK], BF16)
    Va = sb("Va", [128, NK // 128, 128], BF16)
    qts = sb("qts", [96, L], BF16)
    wuk_s = sb("wuk_s", [128, 2, 512], BF16)
    wuv_s = sb("wuv_s", [128, 2, 512], BF16)
    Pb = [sb(f"Pb{i}", [128, 1024], BF16) for i in range(3)]
    rcs = sb("rcs", [64, TT], F32)
    ob = sb("ob", [64, TT], BF16)
    for i in range(NSPL):
        ckall = CKVALLs[i].ap().rearrange("(r f) n -> r f n", r=2)
        for r in range(2):
            o0 = (i * 2 + r) * Ls
            for k in range(2):
                dma(ckv[:, k, o0:o0 + Ls], ckall[r, k * 128:(k + 1) * 128, :], [f"CKVALL{i}"], [P.wr("ckv")])
            dma(Kt[64:96, o0:o0 + Ls], ckall[r, 256:288, :], [f"CKVALL{i}"], [P.wr("Ktr")])
    dma(wuk_s[:, :, :], wview("wuk"), wres("wuk"), ["wuk_s"])
    dma(wuv_s[:, :, :], wview("wuv"), wres("wuv"), ["wuv_s"])
    P.add(G, lambda: nc.gpsimd.memset(Va[:, :, :], 1.0), [], ["Va"])
    scale = 96.0 ** -0.5
    NKC = NK // 128
    for h in range(NH):
        for kt in range(NK // 512):
            b = bank()
            for k in range(2):
                mm(ps[b][0:64, :], wuk_s[:, k, h * 64:(h + 1) * 64], ckv[:, k, kt * 512:(kt + 1) * 512], k == 0, k == 1, ["wuk_s"] + P.rd("ckv"), [f"ps{b}"])
            cp(Kt[0:64, kt * 512:(kt + 1) * 512], ps[b][0:64, :], [f"ps{b}"], ["Kt"])
        for kg in range(NKC // 8):
            b = bank()
            for j in range(8):
                kc = kg * 8 + j
                for k in range(2):
                    mm(ps[b][:, j * 64:(j + 1) * 64], ckv[:, k, kc * 128:(kc + 1) * 128], wuv_s[:, k, h * 64:(h + 1) * 64], k == 0, k == 1, ["wuv_s"] + P.rd("ckv"), [f"ps{b}"])
            cp(Va[:, kg * 8:(kg + 1) * 8, 0:64], ps[b][:, :].rearrange("p (a b) -> p a b", b=64), [f"ps{b}"], ["Va"])
        dma(qts[:, :], QT[h, :, :], P.rd("QT"), ["qts"])
        for qt_ in range(L // 512):
            bo = 6 + (qt_ % 2)
            q0 = qt_ * 512
            npair = NKC // 2

            def s_mm(i):
                p = i % 3
                for j in range(2):
                    kc = 2 * i + j
                    mm(ps[2 * p + j][:, :], Kt[0:96, kc * 128:(kc + 1) * 128], qts[0:96, q0:q0 + 512], True, True, ["Kt", "qts"] + P.rd("Ktr"), [f"ps{2*p+j}"])

            def s_exp(i):
                p = i % 3
                for j in range(2):
                    act(Pb[p][:, j * 512:(j + 1) * 512], ps[2 * p + j][:, :], AF.Exp, [f"ps{2*p+j}"], [f"Pb{p}"], scale=scale)

            def pv(i):
                p = i % 3
                for j in range(2):
                    kc = 2 * i + j
                    mm(ps[bo][:, :], Va[:, kc, :], Pb[p][:, j * 512:(j + 1) * 512], kc == 0, kc == NKC - 1, ["Va", f"Pb{p}"], [f"ps{bo}"])

            s_mm(0)
            s_exp(0)
            for i in range(npair):
                if i + 1 < npair:
                    s_mm(i + 1)
                    s_exp(i + 1)
                pv(i)
            P.add(V, lambda bo=bo: nc.vector.reciprocal(out=rcs[0:64, :], in_=ps[bo][64:128, :]), [f"ps{bo}"], ["rcs"])
            tt(ob[:, :], ps[bo][0:64, :], rcs[:, :], ALU.mult, [f"ps{bo}", "rcs"], ["ob"])
            dma(OT[h // 2, (h % 2) * 64:(h % 2) * 64 + 64, q0:q0 + 512], ob[:, :], ["ob"], [P.wr("OT")], q=G)
    P.barrier()
    pst.close()

    if _os.environ.get("STOP") == "B":
        P.emit(P.rd("OT"), es); es.close(); return nc
    pst = ExitStack()
    cur[0] = pst
    alloc_common()
    xt, xbf, tmpf, wA = CM["xin"], CM["xbf"], CM["tmpf"], CM["wA"]
    woa_s = sb("woa_s", [128, 4, D], BF16)
    wos_s = sb("wos_s", [128, 4, D], BF16)
    wglu_s = sb("wglu_s", [128, 4, 512], BF16)
    wpp_s = sb("wpp_s", [128, 2, D], BF16)
    dma(woa_s[:, :, :], wview("woa"), wres("woa"), ["woa_s"])
    dma(wos_s[:, :, :], wview("wos"), wres("wos"), ["wos_s"])
    dma(wglu_s[:, :, :], wview("wglu"), wres("wglu"), ["wglu_s"])
    dma(wpp_s[:, :, :], wview("wpp"), wres("wpp"), ["wpp_s"])
    ots = sb("ots", [128, 4, TT], BF16)
    ysf = sb("ysf", [128, 4, TT], F32)
    zf = sb("zf", [128, 4, TT], F32)
    zbf4 = sb("zbf4", [128, 4, TT], BF16)
    zzb = sb("zzb", [128, 4, TT], BF16)
    gaf = [sb(f"gaf{i}", [128, TT], F32) for i in range(2)]
    gbf = [sb(f"gbf{i}", [128, TT], F32) for i in range(2)]
    mrb = sb("mrb", [128, NCH, TT], BF16)
    pf = sb("pf", [128, 2, TT], F32)
    pbf = sb("pbf", [128, 2, TT], BF16)
    outv = out.ap().rearrange("c p n -> p c n")
    pv_ = pT.ap().rearrange("c p n -> p c n")
    GC = 1.5957691216057308
    ysall = [f"YS{c}" for c in range(4)]
    for t in range(NT):
        c0, c1 = t * TT, (t + 1) * TT
        dma(xt[:, :, :], x1v[:, :, c0:c1], P.rd("X1"), ["xin"])
        dma(ots[:, :, :], OT.ap().rearrange("c p n -> p c n")[:, :, c0:c1], P.rd("OT"), ["ots"])
        dma(ysf[:, :, :], YS.ap().rearrange("c p n -> p c n")[:, :, c0:c1], ysall, ["ysf"])
        dma(pf[:, :, :], pv_[:, :, c0:c1], [], ["pf"])
        cp(pbf[:, :, :], pf[:, :, :], ["pf"], ["pbf"], eng=G)
        tt(zf[:, :, :], ysf[:, :, :], ysf[:, :, :], ALU.mult, ["ysf"], ["zf"])
        ts(zf[:, :, :], zf[:, :, :], 0.044715, 1.0, ALU.mult, ALU.add, ["zf"], ["zf"])
        tt(zf[:, :, :], zf[:, :, :], ysf[:, :, :], ALU.mult, ["zf", "ysf"], ["zf"])
        act(zf[:, :, :], zf[:, :, :], AF.Sigmoid, ["zf"], ["zf"], scale=GC)
        tt(zf[:, :, :], zf[:, :, :], ysf[:, :, :], ALU.mult, ["zf", "ysf"], ["zf"])
        cp(zbf4[:, :, :], zf[:, :, :], ["zf"], ["zbf4"], eng=G)
        for c in range(4):
            b = bank()
            for k in range(4):
                mm(ps[b][:, :], wglu_s[:, k, c * 128:(c + 1) * 128], zbf4[:, k, :], k == 0, k == 3, ["wglu_s", "zbf4"], [f"ps{b}"])
            ti = tmp()
            act(tmpf[ti][:, :], ps[b][:, :], AF.Sigmoid, [f"ps{b}"], [f"tmpf{ti}"])
            tt(zzb[:, c, :], zf[:, c, :], tmpf[ti][:, :], ALU.mult, ["zf", f"tmpf{ti}"], ["zzb"])
        for o in range(NCH):
            gi_ = o % 2
            dma(gaf[gi_][:, :], GA[o, :, c0:c1], P.rd("G0"), [f"gaf{gi_}"])
            dma(gbf[gi_][:, :], GB[o, :, c0:c1], P.rd("G1"), [f"gbf{gi_}"])
            ba, bb = bank(), bank()
            for k in range(4):
                mm(ps[ba][:, :], woa_s[:, k, o * 128:(o + 1) * 128], ots[:, k, :], k == 0, k == 3, ["woa_s", "ots"], [f"ps{ba}"])
            for k in range(4):
                mm(ps[bb][:, :], wos_s[:, k, o * 128:(o + 1) * 128], zzb[:, k, :], k == 0, k == 3, ["wos_s", "zzb"], [f"ps{bb}"])
            tt(gaf[gi_][:, :], gaf[gi_][:, :], ps[ba][:, :], ALU.mult, [f"gaf{gi_}", f"ps{ba}"], [f"gaf{gi_}"])
            tt(gbf[gi_][:, :], gbf[gi_][:, :], ps[bb][:, :], ALU.mult, [f"gbf{gi_}", f"ps{bb}"], [f"gbf{gi_}"])
            tt(mrb[:, o, :], gaf[gi_][:, :], gbf[gi_][:, :], ALU.add, [f"gaf{gi_}", f"gbf{gi_}"], ["mrb"], eng=G)
        for half in range(2):
            iw = load_wA("wout", half * 512, (half + 1) * 512)
            for cc in range(4):
                o = half * 4 + cc
                b = bank()
                for k in range(NCH):
                    mm(ps[b][:, :], wA[iw][:, k, cc * 128:(cc + 1) * 128], mrb[:, k, :], k == 0, k == NCH - 1, [f"wA{iw}", "mrb"], [f"ps{b}"])
                stt(xt[:, o, :], ps[b][:, :], 1.0 / ALPHA, xt[:, o, :], ALU.mult, ALU.add, [f"ps{b}", "xin"], ["xin"])
        layer_norm(1, LN_EPS / (ALPHA * ALPHA), True)
        ffn("w1b", "w3b", "w2b")
        layer_norm(2, LN_EPS / (ALPHA * ALPHA), True)
        for half in range(2):
            iw = load_wA("wpg", half * 512, (half + 1) * 512)
            for cc in range(4):
                o = half * 4 + cc
                bg, bp = bank(), bank()
                for k in range(NCH):
                    mm(ps[bg][:, :], wA[iw][:, k, cc * 128:(cc + 1) * 128], xbf[:, k, :], k == 0, k == NCH - 1, [f"wA{iw}", "xbf"], [f"ps{bg}"])
                for k in range(2):
                    mm(ps[bp][:, :], wpp_s[:, k, o * 128:(o + 1) * 128], pbf[:, k, :], k == 0, k == 1, ["wpp_s", "pbf"], [f"ps{bp}"])
                ti = tmp()
                act(tmpf[ti][:, :], ps[bg][:, :], AF.Sigmoid, [f"ps{bg}"], [f"tmpf{ti}"])
                tt(tmpf[ti][:, :], tmpf[ti][:, :], ps[bp][:, :], ALU.mult, [f"tmpf{ti}", f"ps{bp}"], [f"tmpf{ti}"])
                stt(xt[:, o, :], tmpf[ti][:, :], 1.0 / ALPHA, xt[:, o, :], ALU.mult, ALU.add, [f"tmpf{ti}", "xin"], ["xin"])
        layer_norm(3, LN_EPS / (ALPHA * ALPHA), False)
        dma(outv[:, :, c0:c1], xt[:, :, :], ["xin"], [P.wr("OUT")], q=G)

    P.emit(P.rd("OUT"), es)
    pst.close()
    es.close()
    return nc


def _perm(L):
    K1 = L // 16
    T3 = K1 // 16
    n = np.arange(L)
    j = n // K1
    j2 = (n % K1) // T3
    k1 = n % T3
    return (k1 * 16 + j2) * 16 + j


_NC_CACHE = {}


def kernel(**inp):
    B, S, _ = inp["x"].shape
    L = S // 2
    f32 = np.float32
    perm = _perm(L)
    if L not in _NC_CACHE:
        _NC_CACHE[L] = build(L)
    nc = _NC_CACHE[L]

    def chunks(a):
        return np.ascontiguousarray(a.reshape(a.shape[0] // 128, 128, a.shape[1]))

    def col(v):
        return np.ascontiguousarray(v.reshape(-1, 128).T)

    inv_freq = (10000.0 ** (-np.arange(0, 32, 2, dtype=np.float32) / 32)).astype(f32)
    W = {
        "w1a": inp["ffn1_w1"][0], "w3a": inp["ffn1_w3"][0], "w2a": inp["ffn1_w2"][0], "win": inp["w_in"][0],
        "wuk": inp["w_uk"][0].reshape(256, 512), "wuv": inp["w_uv"][0].reshape(256, 512),
        "woa": inp["w_o_attn"][0], "wglu": inp["w_glu"][0], "wos": inp["w_o_ssm"][0], "wout": inp["w_out"][0],
        "w1b": inp["ffn2_w1"][0], "w3b": inp["ffn2_w3"][0], "w2b": inp["ffn2_w2"][0],
        "wpp": inp["ple_w_proj"][0], "wpg": inp["ple_w_gate"][0],
    }
    wq = inp["w_uq"][0]
    W["wuq"] = np.concatenate([wq[:, :, 0:64].reshape(384, 512), wq[:, :, 64:80].reshape(384, 128), wq[:, :, 80:96].reshape(384, 128)], axis=1)
    W = {k: np.ascontiguousarray(v, dtype=f32) for k, v in W.items()}
    lnp = np.concatenate([col(inp[f"ln{i}_{gb}"][0]) for i in (1, 2, 3, 4) for gb in ("g", "b")], axis=1).astype(f32)
    qkg = np.concatenate([col(inp["q_norm_g"][0]), col(inp["kv_norm_g"][0])], axis=1).astype(f32)
    ssd = col(inp["ssm_d"][0].reshape(512)).astype(f32)

    def ssm_pack(sfx):
        lre, lim, ldt = inp["ssm_lam_re_" + sfx][0], inp["ssm_lam_im_" + sfx][0], inp["ssm_log_dt_" + sfx][0]
        bre, bim = inp["ssm_b_re_" + sfx][0], inp["ssm_b_im_" + sfx][0]
        cre, cim = inp["ssm_c_re_" + sfx][0], inp["ssm_c_im_" + sfx][0]
        ldtb = np.broadcast_to(ldt[:, None], (32, 64))
        st = np.zeros((3, 128, 16), f32)
        rp = np.zeros((3, 128, 2048), f32)
        bp = np.zeros((2, 128, 2048), f32)
        cpad = np.zeros((2, 128, 2048), f32)
        for q in range(16):
            for gs_ in range(2):
                g = 2 * q + gs_
                for i, a in enumerate((lre, lim, ldtb)):
                    st[i, gs_ * 64:(gs_ + 1) * 64, q] = a[g]
                    rp[i, :, q * 128 + gs_ * 64:q * 128 + (gs_ + 1) * 64] = a[g][None, :]
                r0 = (g % 8) * 16
                bp[0, r0:r0 + 16, q * 128 + gs_ * 64:q * 128 + (gs_ + 1) * 64] = bre[g].T
                bp[1, r0:r0 + 16, q * 128 + gs_ * 64:q * 128 + (gs_ + 1) * 64] = bim[g].T
                cpad[0, gs_ * 64:(gs_ + 1) * 64, q * 128 + r0:q * 128 + r0 + 16] = cre[g].T
                cpad[1, gs_ * 64:(gs_ + 1) * 64, q * 128 + r0:q * 128 + r0 + 16] = cim[g].T
        return st, rp, bp, cpad

    packs = {"f": ssm_pack("f"), "b": ssm_pack("b")}
    in_maps = []
    for c in range(8):
        b, half = c // 2, c % 2
        tl = perm if half == 0 else (L - 1 - perm)
        tg = half * L + tl
        m = dict(W)
        m["xT"] = chunks(np.ascontiguousarray(inp["x"][b][tg].T))
        m["pT"] = chunks(np.ascontiguousarray(inp["p"][0, b][tg].T))
        ang = tg.astype(f32)[None, :] * inv_freq[:, None]
        m["cosT"] = np.ascontiguousarray(np.tile(np.cos(ang).astype(f32), (8, 1)))
        m["sinT"] = np.ascontiguousarray(np.tile(np.sin(ang).astype(f32), (8, 1)))
        order = ("f", "b") if half == 0 else ("b", "f")
        m["sst"] = np.stack([packs[o][0] for o in order])
        m["ssr"] = np.stack([packs[o][1] for o in order])
        m["ssb"] = np.stack([packs[o][2] for o in order])
        m["ssc"] = np.stack([packs[o][3] for o in order])
        m["lnp"], m["qkg"], m["ssd"] = lnp, qkg, ssd
        sel = np.zeros((128, 2), f32)
        sel[:, 1 - half] = 1.0
        m["selp"] = sel
        in_maps.append(m)
    res = run_bass_kernel_spmd(nc, in_maps, core_ids=list(range(8)))
    outp = np.empty((B, S, D), f32)
    for c in range(8):
        b, half = c // 2, c % 2
        tl = perm if half == 0 else (L - 1 - perm)
        tg = half * L + tl
        o = res.results[c]["outT"].reshape(D, L)
        outp[b, tg, :] = o.T
    return outp
```

```python
import math
from contextlib import ExitStack
import numpy as np
import concourse.bass as bass
import concourse.mybir as mybir
from concourse.bass_utils import run_bass_kernel_spmd

F32 = mybir.dt.float32
BF16 = mybir.dt.bfloat16
ALU = mybir.AluOpType
AF = mybir.ActivationFunctionType

D = 1024
DFF = 2816
NCH = 8
HCH = 22
NH = 8
ALPHA = 2.0 ** 0.25
LN_EPS = 1e-5
RMS_EPS = 1e-6
TT = 512
PI = math.pi


import os as _os0
SAME_ENGINE_NOSYNC = set(_os0.environ.get("NOSYNC", "pe").split(","))


class Prog:
    CH = 1000
    NDS = 24

    def __init__(self, nc):
        self.nc = nc
        self.ops = []
        self.groups = {}
        self.eng = {"pe": nc.tensor, "act": nc.scalar, "dve": nc.vector, "pool": nc.gpsimd, "sp": nc.sync}

    def add(self, eng, fn, reads=(), writes=(), dma=False, inc=None):
        self.ops.append(dict(eng=eng, fn=fn, reads=tuple(reads), writes=tuple(writes), dma=dma, inc=inc, bar=False))

    def barrier(self):
        self.ops.append(dict(eng=None, fn=None, reads=(), writes=(), dma=False, inc=None, bar=True))

    def wr(self, grp):
        g = self.groups.setdefault(grp, [])
        nm = f"{grp}#{len(g)}"
        g.append(nm)
        return nm

    def rd(self, grp):
        return list(self.groups.get(grp, []))

    def emit(self, final_res, stack):
        ops = self.ops
        n = len(ops)
        last_w = {}
        readers = {}
        deps = [None] * n
        dma_ids = []
        last_eng = {}
        pend_bar = {}
        for i, o in enumerate(ops):
            if o["bar"]:
                bd = set(last_eng.values()) | set(dma_ids[-self.NDS:])
                pend_bar = {e: set(bd) for e in self.eng}
                deps[i] = []
                continue
            d = set()
            if o["eng"] in pend_bar:
                d |= pend_bar.pop(o["eng"])
            if not o["dma"]:
                last_eng[o["eng"]] = i
            for r in o["reads"]:
                if r in last_w:
                    d.add(last_w[r])
            for w in o["writes"]:
                if w in last_w:
                    d.add(last_w[w])
                for x in readers.get(w, ()):
                    d.add(x)
            if o["dma"]:
                if len(dma_ids) >= self.NDS:
                    d.add(dma_ids[len(dma_ids) - self.NDS])
                dma_ids.append(i)
            d.discard(i)
            keep = []
            for x in d:
                ox = ops[x]
                if (not ox["dma"]) and ox["eng"] == o["eng"] and o["eng"] in SAME_ENGINE_NOSYNC and not o["dma"]:
                    continue
                keep.append(x)
            deps[i] = sorted(keep)
            for r in o["reads"]:
                readers.setdefault(r, []).append(i)
            for w in o["writes"]:
                last_w[w] = i
                readers[w] = []
        fin = sorted({last_w[r] for r in final_res if r in last_w})
        needed = [False] * n
        for i in range(n):
            for x in deps[i]:
                needed[x] = True
        for x in fin:
            needed[x] = True
        cnt = {e: 0 for e in self.eng}
        dcnt = [0] * self.NDS
        sig = [None] * n
        nd = 0
        for i, o in enumerate(ops):
            if o["bar"]:
                continue
            if o["dma"]:
                k = nd % self.NDS
                nd += 1
                dcnt[k] += 1
                sig[i] = ("d", k, dcnt[k] * (o["inc"] or 16))
                if o["inc"]:
                    raise RuntimeError("custom inc unsupported")
            elif needed[i]:
                cnt[o["eng"]] += 1
                sig[i] = ("e", o["eng"], cnt[o["eng"]])
        sems = {}
        for e in self.eng:
            for c in range((cnt[e] + self.CH - 1) // self.CH + 1):
                sems[("e", e, c)] = stack.enter_context(self.nc.semaphore(f"s_{e}_{c}"))
        for k in range(self.NDS):
            sems[("d", k)] = stack.enter_context(self.nc.semaphore(f"s_d{k}"))
        known = {e: {} for e in self.eng}
        snap = [None] * n

        def key_of(s):
            return ("e", s[1]) if s[0] == "e" else ("d", s[1])

        def wait(e, x):
            s = sig[x]
            kk = key_of(s)
            kn = known[e]
            if kn.get(kk, 0) >= s[2]:
                return
            h = self.eng[e]
            if s[0] == "e":
                c = (s[2] - 1) // self.CH
                h.wait_ge(sems[("e", s[1], c)], (s[2] - 1) % self.CH + 1)
            else:
                h.wait_ge(sems[("d", s[1])], s[2])
            kn[kk] = s[2]
            sn = snap[x]
            if sn:
                for k2, v2 in sn.items():
                    if kn.get(k2, 0) < v2:
                        kn[k2] = v2

        for i, o in enumerate(ops):
            if o["bar"]:
                continue
            e = o["eng"]
            for x in deps[i]:
                wait(e, x)
            ins = o["fn"]()
            s = sig[i]
            if s is not None:
                if s[0] == "e":
                    c = (s[2] - 1) // self.CH
                    ins.then_inc(sems[("e", e, c)], 1)
                else:
                    ins.then_inc(sems[("d", s[1])], 16)
                snap[i] = dict(known[e])
        for x in fin:
            wait("sp", x)


def build(L):
    NT = L // TT
    NK = 2 * L
    K1 = L // 16
    T3 = K1 // 16
    assert L % 512 == 0 and K1 % 16 == 0 and T3 >= 1
    nc = bass.Bass("TRN2", target_bir_lowering=False)
    P = Prog(nc)
    es = ExitStack()
    cur = [es]

    def din(name, shape, dt=F32):
        return nc.dram_tensor(name, list(shape), dt, kind="ExternalInput")

    def dscr(name, shape, dt):
        return nc.dram_tensor(name, list(shape), dt)

    sbn = [0]

    def sb(name, shape, dt):
        sbn[0] += 1
        return cur[0].enter_context(nc.sbuf_tensor(f"{name}_{sbn[0]}", list(shape), dt))

    xT = din("xT", [NCH, 128, L])
    pT = din("pT", [2, 128, L])
    cosT = din("cosT", [128, L])
    sinT = din("sinT", [128, L])
    wnames = {
        "w1a": (D, DFF), "w3a": (D, DFF), "w2a": (DFF, D), "win": (D, 3232), "wuq": (384, 768),
        "wuk": (256, 512), "wuv": (256, 512), "woa": (512, D), "wglu": (512, 512), "wos": (512, D),
        "wout": (D, D), "w1b": (D, DFF), "w3b": (D, DFF), "w2b": (DFF, D), "wpp": (256, D), "wpg": (D, D),
    }
    wf = {k: din(k, v) for k, v in wnames.items()}
    wb = {k: dscr(k + "_bf", v, BF16) for k, v in wnames.items()}
    lnp = din("lnp", [128, 8 * NCH])
    qkg = din("qkg", [128, 5])
    ssd = din("ssd", [128, 4])
    sst = din("sst", [2, 3, 128, 16])
    ssr = din("ssr", [2, 3, 128, 2048])
    ssb = din("ssb", [2, 2, 128, 2048])
    ssc = din("ssc", [2, 2, 128, 2048])
    selp = din("selp", [128, 2])
    out = nc.dram_tensor("outT", [NCH, 128, L], F32, kind="ExternalOutput")

    X1 = dscr("X1", [NCH, 128, L], F32)
    QT = dscr("QT", [NH, 96, L], BF16)
    NSPL = max(1, L // 1024)
    Ls = L // NSPL
    CKVs = [dscr(f"CKV{i}", [288, Ls], BF16) for i in range(NSPL)]
    CKVALLs = [dscr(f"CKVALL{i}", [2 * 288, Ls], BF16) for i in range(NSPL)]
    U32 = dscr("U32", [4, 128, L], F32)
    UBF = dscr("UBF", [4, 128, L], BF16)
    GA = dscr("GA", [NCH, 128, L], F32)
    GB = dscr("GB", [NCH, 128, L], F32)
    OT = dscr("OT", [4, 128, L], BF16)
    YS = dscr("YS", [4, 128, L], F32)
    SX = dscr("SX", [128, 32], F32)
    SXALL = dscr("SXALL", [256, 32], F32)
    import os as _os
    if _os.environ.get("DUMMY_MB"):
        DUM = dscr("DUM", [int(_os.environ["DUMMY_MB"]) * 2, 128, 1024], F32)

    ps = [es.enter_context(nc.psum_tensor(f"ps{i}", [128, 512], F32)) for i in range(8)]
    pctr = [0]

    def bank():
        b = pctr[0] % 8
        pctr[0] += 1
        return b

    V, S, T, G, SP_ = "dve", "act", "pe", "pool", "sp"

    def mm(o, lhsT, rhs, start, stop, reads, writes):
        P.add(T, lambda: nc.tensor.matmul(o, lhsT, rhs, start=start, stop=stop), reads, writes)

    def act(o, i, func, reads, writes, scale=None, bias=None):
        kw = {}
        if scale is not None:
            kw["scale"] = scale
        if bias is not None:
            kw["bias"] = bias
        P.add(S, lambda: nc.scalar.activation(out=o, in_=i, func=func, **kw), reads, writes)

    def tt(o, a, b, op, reads, writes, eng=V):
        h = nc.vector if eng == V else nc.gpsimd
        P.add(eng, lambda: h.tensor_tensor(out=o, in0=a, in1=b, op=op), reads, writes)

    def ts(o, a, s1, s2, op0, op1, reads, writes, eng=V):
        h = nc.vector if eng == V else nc.gpsimd
        if op1 is None:
            P.add(eng, lambda: h.tensor_scalar(out=o, in0=a, scalar1=s1, scalar2=None, op0=op0), reads, writes)
        else:
            P.add(eng, lambda: h.tensor_scalar(out=o, in0=a, scalar1=s1, scalar2=s2, op0=op0, op1=op1), reads, writes)

    def stt(o, a, s, b, op0, op1, reads, writes):
        P.add(V, lambda: nc.vector.scalar_tensor_tensor(out=o, in0=a, scalar=s, in1=b, op0=op0, op1=op1), reads, writes)

    def cp(o, i, reads, writes, eng=V):
        h = nc.vector if eng == V else nc.gpsimd
        P.add(eng, lambda: h.tensor_copy(out=o, in_=i), reads, writes)

    def dma(o, i, reads, writes, q=SP_):
        h = {"sp": nc.sync, "pool": nc.gpsimd, "act": nc.scalar}[q]
        P.add(q, lambda: h.dma_start(out=o, in_=i), reads, writes, dma=True)

    for k in ["w1a", "w3a", "w2a", "win", "wuq", "wuk", "wuv", "wout", "woa", "wglu", "wos", "w1b", "w3b", "w2b", "wpp", "wpg"]:
        r, c = wnames[k]
        if r * c > 1500000:
            hh = r // 2
            dma(wb[k][0:hh, :], wf[k][0:hh, :], [], ["W_" + k + "0"], q=G)
            dma(wb[k][hh:r, :], wf[k][hh:r, :], [], ["W_" + k + "1"], q=G)
        else:
            dma(wb[k][:, :], wf[k][:, :], [], ["W_" + k + "0"], q=G)

    def wres(k):
        r, c = wnames[k]
        return ["W_" + k + "0", "W_" + k + "1"] if r * c > 1500000 else ["W_" + k + "0"]

    def wview(k):
        return wb[k].ap().rearrange("(kc p) n -> p kc n", p=128)

    if _os.environ.get("DUMMY_MB"):
        dma(DUM[0, :, :], wf["wpg"][0:128, :], [], ["DUM"], q=G)
    ones_d = sb("ones_d", [128, 128], BF16)
    ones_q = sb("ones_q", [128, 128], BF16)
    lnp_s = sb("lnp_s", [128, 8 * NCH], F32)
    qkg_s = sb("qkg_s", [128, 5], F32)
    ssd_s = sb("ssd_s", [128, 4], F32)
    sel_s = sb("sel_s", [128, 2], F32)
    P.add(V, lambda: nc.vector.memset(ones_d[:, :], 1.0 / 1024.0), [], ["ones_d"])
    P.add(V, lambda: nc.vector.memset(ones_q[:, :], 1.0), [], ["ones_q"])
    dma(lnp_s[:, :], lnp[:, :], [], ["lnp"])
    dma(qkg_s[:, :], qkg[:, :], [], ["qkg"])
    dma(ssd_s[:, :], ssd[:, :], [], ["ssd"])
    dma(sel_s[:, :], selp[:, :], [], ["sel"])

    CM = {}

    def alloc_common():
        CM["xin"] = sb("xin", [128, NCH, TT], F32)
        CM["xbf"] = sb("xbf", [128, NCH, TT], BF16)
        CM["hbf"] = sb("hbf", [128, HCH, TT], BF16)
        CM["zb"] = sb("zb", [128, NCH, TT], BF16)
        CM["tmpf"] = [sb(f"tmpf{i}", [128, TT], F32) for i in range(3)]
        CM["mean"] = sb("mean_s", [128, TT], F32)
        CM["rstd"] = sb("rstd_s", [128, TT], F32)
        CM["wA"] = [sb(f"wA{i}", [128, 8, 512], BF16) for i in range(3)]
        CM["wB"] = [sb(f"wB{i}", [128, HCH, 256], BF16) for i in range(2)]

    wactr = [0]
    wbctr = [0]

    def load_wA(k, c0, c1):
        i = wactr[0] % 3
        wactr[0] += 1
        dma(CM["wA"][i][:, :, 0:c1 - c0], wview(k)[:, :, c0:c1], wres(k), [f"wA{i}"])
        return i

    def load_wB(k, c0, c1):
        i = wbctr[0] % 2
        wbctr[0] += 1
        dma(CM["wB"][i][:, :, 0:c1 - c0], wview(k)[:, :, c0:c1], wres(k), [f"wB{i}"])
        return i

    tctr = [0]

    def tmp():
        i = tctr[0] % 3
        tctr[0] += 1
        return i

    def ffn(k1, k3, k2):
        xt, xbf, hbf, tmpf, wA, wB = CM["xin"], CM["xbf"], CM["hbf"], CM["tmpf"], CM["wA"], CM["wB"]
        ngrp = (DFF + 511) // 512
        for g in range(ngrp):
            c0, c1 = g * 512, min(DFF, (g + 1) * 512)
            i1 = load_wA(k1, c0, c1)
            i3 = load_wA(k3, c0, c1)
            for cc in range((c1 - c0) // 128):
                c = g * 4 + cc
                b1, b3 = bank(), bank()
                for k in range(NCH):
                    mm(ps[b1][:, :], wA[i1][:, k, cc * 128:(cc + 1) * 128], xbf[:, k, :], k == 0, k == NCH - 1,
                       [f"wA{i1}", "xbf"], [f"ps{b1}"])
                for k in range(NCH):
                    mm(ps[b3][:, :], wA[i3][:, k, cc * 128:(cc + 1) * 128], xbf[:, k, :], k == 0, k == NCH - 1,
                       [f"wA{i3}", "xbf"], [f"ps{b3}"])
                ti = tmp()
                act(tmpf[ti][:, :], ps[b1][:, :], AF.Silu, [f"ps{b1}"], [f"tmpf{ti}"])
                tt(hbf[:, c, :], tmpf[ti][:, :], ps[b3][:, :], ALU.mult, [f"tmpf{ti}", f"ps{b3}"], [f"hbf{c}"])
        for g in range(4):
            i2 = load_wB(k2, g * 256, (g + 1) * 256)
            for oc in range(2):
                o = g * 2 + oc
                b = bank()
                for k in range(HCH):
                    mm(ps[b][:, :], wB[i2][:, k, oc * 128:(oc + 1) * 128], hbf[:, k, :], k == 0, k == HCH - 1,
                       [f"wB{i2}", f"hbf{k}"], [f"ps{b}"])
                stt(xt[:, o, :], ps[b][:, :], 0.5 / ALPHA, xt[:, o, :], ALU.mult, ALU.add, [f"ps{b}", f"xin{o}"], [f"xin{o}"])

    XALL = [f"xin{o}" for o in range(NCH)]

    def rsqrt_inplace(r, res):
        act(r, r, AF.Ln, [res], [res])
        act(r, r, AF.Exp, [res], [res], scale=-0.5)

    def layer_norm(li, eps, want_bf):
        xt, xbf, zb, tmpf, mean_s, rstd_s = CM["xin"], CM["xbf"], CM["zb"], CM["tmpf"], CM["mean"], CM["rstd"]
        act(zb[:, :, :], xt[:, :, :], AF.Copy, XALL, ["zb"])
        bm, bq = bank(), bank()
        for k in range(NCH):
            mm(ps[bm][:, :], ones_d[:, :], zb[:, k, :], k == 0, k == NCH - 1, ["ones_d", "zb"], [f"ps{bm}"])
        act(zb[:, :, :], xt[:, :, :], AF.Square, XALL, ["zb"])
        for k in range(NCH):
            mm(ps[bq][:, :], ones_d[:, :], zb[:, k, :], k == 0, k == NCH - 1, ["ones_d", "zb"], [f"ps{bq}"])
        act(mean_s[:, :], ps[bm][:, :], AF.Copy, [f"ps{bm}"], ["mean"])
        ti = tmp()
        tt(tmpf[ti][:, :], mean_s[:, :], mean_s[:, :], ALU.mult, ["mean"], [f"tmpf{ti}"])
        tt(tmpf[ti][:, :], ps[bq][:, :], tmpf[ti][:, :], ALU.subtract, [f"ps{bq}", f"tmpf{ti}"], [f"tmpf{ti}"])
        ts(rstd_s[:, :], tmpf[ti][:, :], eps, None, ALU.add, None, [f"tmpf{ti}"], ["rstd"])
        rsqrt_inplace(rstd_s[:, :], "rstd")
        gcol = li * 16
        for o in range(NCH):
            tt(xt[:, o, :], xt[:, o, :], mean_s[:, :], ALU.subtract, [f"xin{o}", "mean"], [f"xin{o}"])
            tt(xt[:, o, :], xt[:, o, :], rstd_s[:, :], ALU.mult, [f"xin{o}", "rstd"], [f"xin{o}"], eng=G)
            act(xt[:, o, :], xt[:, o, :], AF.Identity, [f"xin{o}", "lnp"], [f"xin{o}"],
                scale=lnp_s[:, gcol + o:gcol + o + 1], bias=lnp_s[:, gcol + 8 + o:gcol + 8 + o + 1])
        if want_bf:
            cp(xbf[:, :, :], xt[:, :, :], XALL, ["xbf"], eng=G)

    pst = ExitStack()
    cur[0] = pst
    alloc_common()
    xt, xbf, tmpf, rstd_s, wA = CM["xin"], CM["xbf"], CM["tmpf"], CM["rstd"], CM["wA"]
    xv = xT.ap().rearrange("c p n -> p c n")
    x1v = X1.ap().rearrange("c p n -> p c n")
    ql = sb("ql", [128, 3, TT], F32)
    qsq = sb("qsq", [128, 3, TT], BF16)
    cqb = sb("cqb", [128, 3, TT], BF16)
    wuq_s = sb("wuq_s", [128, 3, 768], BF16)
    qn = sb("qn", [128, 4, TT], BF16)
    cos_s = sb("cos_s", [128, TT], F32)
    sin_s = sb("sin_s", [128, TT], F32)
    r1s = sb("r1s", [128, TT], F32)
    r2s = sb("r2s", [128, TT], F32)
    ro1 = sb("ro1", [128, TT], BF16)
    ro2 = sb("ro2", [128, TT], BF16)
    u32s = sb("u32s", [128, 4, TT], F32)
    ubfs = sb("ubfs", [128, 4, TT], BF16)
    gsm = [sb(f"gsm{i}", [128, TT], F32) for i in range(2)]
    dma(wuq_s[:, :, :], wview("wuq"), wres("wuq"), ["wuq_s"])

    def rope(pa, pb, np_, outa, outb, ra, rb_):
        act(r1s[0:np_, :], ps[pa][0:np_, :], AF.Copy, [f"ps{pa}"], ["r1s"])
        act(r2s[0:np_, :], ps[pb][0:np_, :], AF.Copy, [f"ps{pb}"], ["r2s"])
        t0, t1 = tmp(), tmp()
        tt(tmpf[t0][0:np_, :], r1s[0:np_, :], cos_s[0:np_, :], ALU.mult, ["r1s", "cos"], [f"tmpf{t0}"])
        tt(tmpf[t1][0:np_, :], r2s[0:np_, :], sin_s[0:np_, :], ALU.mult, ["r2s", "sin"], [f"tmpf{t1}"])
        tt(outa, tmpf[t0][0:np_, :], tmpf[t1][0:np_, :], ALU.subtract, [f"tmpf{t0}", f"tmpf{t1}"], [ra])
        tt(tmpf[t0][0:np_, :], r2s[0:np_, :], cos_s[0:np_, :], ALU.mult, ["r2s", "cos"], [f"tmpf{t0}"])
        tt(tmpf[t1][0:np_, :], r1s[0:np_, :], sin_s[0:np_, :], ALU.mult, ["r1s", "sin"], [f"tmpf{t1}"])
        tt(outb, tmpf[t0][0:np_, :], tmpf[t1][0:np_, :], ALU.add, [f"tmpf{t0}", f"tmpf{t1}"], [rb_])

    def rmsnorm(src, sres, nch, dim, gcol0, dst, dres):
        act(qsq[:, 0:nch, :], src[:, 0:nch, :], AF.Square, [sres], ["qsq"])
        b = bank()
        for k in range(nch):
            mm(ps[b][:, :], ones_q[:, :], qsq[:, k, :], k == 0, k == nch - 1, ["ones_q", "qsq"], [f"ps{b}"])
        ts(rstd_s[:, :], ps[b][:, :], 1.0 / dim, RMS_EPS, ALU.mult, ALU.add, [f"ps{b}"], ["rstd"])
        rsqrt_inplace(rstd_s[:, :], "rstd")
        for k in range(nch):
            tt(src[:, k, :], src[:, k, :], rstd_s[:, :], ALU.mult, [sres, "rstd"], [sres])
            act(dst[:, k, :], src[:, k, :], AF.Copy, [sres, "qkg"], [dres], scale=qkg_s[:, gcol0 + k:gcol0 + k + 1])

    def proj(iw, cc, M=128, col0=None):
        b = bank()
        c_lo = cc * 128 if col0 is None else col0
        for k in range(NCH):
            mm(ps[b][0:M, :], wA[iw][:, k, c_lo:c_lo + M], xbf[:, k, :], k == 0, k == NCH - 1, [f"wA{iw}", "xbf"], [f"ps{b}"])
        return b

    for t in range(NT):
        c0, c1 = t * TT, (t + 1) * TT
        dma(xt[:, :, :], xv[:, :, c0:c1], [], XALL)
        cp(xbf[:, :, :], xt[:, :, :], XALL, ["xbf"], eng=G)
        ffn("w1a", "w3a", "w2a")
        layer_norm(0, LN_EPS / (ALPHA * ALPHA), True)
        dma(x1v[:, :, c0:c1], xt[:, :, :], XALL, [P.wr("X1")], q=G)
        dma(cos_s[:, :], cosT[:, c0:c1], [], ["cos"])
        dma(sin_s[:, :], sinT[:, c0:c1], [], ["sin"])
        iw = load_wA("win", 0, 384)
        for c in range(3):
            b = proj(iw, c)
            act(ql[:, c, :], ps[b][:, :], AF.Copy, [f"ps{b}"], ["ql"])
        rmsnorm(ql, "ql", 3, 384.0, 0, cqb, "cqb")
        for c in range(4):
            b = bank()
            for k in range(3):
                mm(ps[b][:, :], wuq_s[:, k, c * 128:(c + 1) * 128], cqb[:, k, :], k == 0, k == 2, ["wuq_s", "cqb"], [f"ps{b}"])
            act(qn[:, c, :], ps[b][:, :], AF.Copy, [f"ps{b}"], [f"qn{c}"])
            dma(QT[2 * c, 0:64, c0:c1], qn[0:64, c, :], [f"qn{c}"], [P.wr("QT")], q=G)
            dma(QT[2 * c + 1, 0:64, c0:c1], qn[64:128, c, :], [f"qn{c}"], [P.wr("QT")], q=G)
        b1, b2 = bank(), bank()
        for k in range(3):
            mm(ps[b1][:, :], wuq_s[:, k, 512:640], cqb[:, k, :], k == 0, k == 2, ["wuq_s", "cqb"], [f"ps{b1}"])
        for k in range(3):
            mm(ps[b2][:, :], wuq_s[:, k, 640:768], cqb[:, k, :], k == 0, k == 2, ["wuq_s", "cqb"], [f"ps{b2}"])
        rope(b1, b2, 128, ro1[:, :], ro2[:, :], "ro1", "ro2")
        for h in range(NH):
            dma(QT[h, 64:80, c0:c1], ro1[h * 16:(h + 1) * 16, :], ["ro1"], [P.wr("QT")], q=G)
            dma(QT[h, 80:96, c0:c1], ro2[h * 16:(h + 1) * 16, :], ["ro2"], [P.wr("QT")], q=G)
        iw = load_wA("win", 384, 672)
        for c in range(2):
            b = proj(iw, c)
            act(ql[:, c, :], ps[b][:, :], AF.Copy, [f"ps{b}"], ["ql"])
        b1 = proj(iw, 0, M=16, col0=256)
        b2 = proj(iw, 0, M=16, col0=272)
        rmsnorm(ql, "ql", 2, 256.0, 3, cqb, "cqb")
        CKV = CKVs[c0 // Ls]
        s0, s1 = c0 % Ls, c0 % Ls + TT
        for k in range(2):
            dma(CKV[k * 128:(k + 1) * 128, s0:s1], cqb[:, k, :], ["cqb"], [P.wr(f"CKV{c0 // Ls}")], q=G)
        rope(b1, b2, 16, ro1[0:16, :], ro2[0:16, :], "ro1", "ro2")
        dma(CKV[256:272, s0:s1], ro1[0:16, :], ["ro1"], [P.wr(f"CKV{c0 // Ls}")], q=G)
        dma(CKV[272:288, s0:s1], ro2[0:16, :], ["ro2"], [P.wr(f"CKV{c0 // Ls}")], q=G)
        iw = load_wA("win", 672, 1184)
        for c in range(4):
            b = proj(iw, c)
            act(u32s[:, c, :], ps[b][:, :], AF.Copy, [f"ps{b}"], ["u32s"])
        cp(ubfs[:, :, :], u32s[:, :, :], ["u32s"], ["ubfs"], eng=G)
        dma(U32.ap().rearrange("c p n -> p c n")[:, :, c0:c1], u32s[:, :, :], ["u32s"], [P.wr("U32")], q=G)
        dma(UBF.ap().rearrange("c p n -> p c n")[:, :, c0:c1], ubfs[:, :, :], ["ubfs"], [P.wr("UBF")], q=G)
        gi_ = 0
        for gname, GD, base in (("G0", GA, 1184), ("G1", GB, 2208)):
            for half in range(2):
                iw = load_wA("win", base + half * 512, base + (half + 1) * 512)
                for cc in range(4):
                    c = half * 4 + cc
                    b = proj(iw, cc)
                    gb_ = gi_ % 2
                    gi_ += 1
                    act(gsm[gb_][:, :], ps[b][:, :], AF.Sigmoid, [f"ps{b}"], [f"gsm{gb_}"])
                    dma(GD[c, :, c0:c1], gsm[gb_][:, :], [f"gsm{gb_}"], [P.wr(gname)], q=G)
    P.barrier()
    pst.close()
    if _os.environ.get("STOP") == "A":
        P.emit(P.rd("X1"), es); es.close(); return nc

    for i in range(NSPL):
        P.add(G, lambda i=i: nc.gpsimd.collective_compute("AllGather", ALU.bypass, replica_groups=[[0, 1], [2, 3], [4, 5], [6, 7]],
                                                           ins=[CKVs[i].ap().opt()], outs=[CKVALLs[i].ap().opt()]),
              P.rd(f"CKV{i}"), [f"CKVALL{i}"])

    if _os.environ.get("STOP") == "X":
        P.emit([f"CKVALL{i}" for i in range(NSPL)], es); es.close(); return nc
    pst = ExitStack()
    cur[0] = pst
    bbR = [sb(f"bbR{d}", [128, 2048], BF16) for d in range(2)]
    bbI = [sb(f"bbI{d}", [128, 2048], BF16) for d in range(2)]
    ccR = [sb(f"ccR{d}", [128, 2048], BF16) for d in range(2)]
    ccI = [sb(f"ccI{d}", [128, 2048], BF16) for d in range(2)]
    pwA = [[sb(f"pwA{d}{l}", [128, 16, 17], F32) for l in range(3)] for d in range(2)]
    pwB = [[sb(f"pwB{d}{l}", [128, 16, 17], F32) for l in range(3)] for d in range(2)]
    pwN = [[sb(f"pwN{d}{l}", [128, 16, 17], F32) for l in range(3)] for d in range(2)]
    small = [sb(f"sm{i}", [128, 16], F32) for i in range(8)]
    zero_s = sb("zero_s", [128, 32], F32)
    finR = sb("finR", [128, 32], F32)
    gat = sb("gat", [128, 2, 32], F32)
    iniS = sb("iniS", [128, 32], F32)
    P.add(V, lambda: nc.vector.memset(zero_s[:, :], 0.0), [], ["zero_s"])
    pst2 = ExitStack()
    cur[0] = pst2
    pt = [sb(f"pt{i}", [128, 2048], F32) for i in range(7)]

    I32 = mybir.dt.int32
    isml = sb("isml", [128, 16], I32)
    ibig = sb("ibig", [128, 2048], I32)

    def sincos(zi, zres, so, co, res_s, res_c, scratch, sres, itile, ires):
        for shift, dst, dres in ((0.0, so, res_s), (0.5 * PI, co, res_c)):
            ts(scratch, zi, shift, 1.0 / (2 * PI), ALU.add, ALU.mult, [zres], [sres])
            cp(itile, scratch, [sres], [ires])
            cp(scratch, itile, [ires], [sres])
            ts(scratch, scratch, -2 * PI, None, ALU.mult, None, [sres], [sres])
            ts(dst, zi, shift, None, ALU.add, None, [zres], [dres])
            tt(dst, dst, scratch, ALU.add, [dres, sres], [dres])
            ts(scratch, dst, PI, 2 * PI, ALU.is_gt, ALU.mult, [dres], [sres])
            tt(dst, dst, scratch, ALU.subtract, [dres, sres], [dres])
            ts(scratch, dst, -PI, 2 * PI, ALU.is_lt, ALU.mult, [dres], [sres])
            tt(dst, dst, scratch, ALU.add, [dres, sres], [dres])
            act(dst, dst, AF.Sin, [dres], [dres])

    for d in range(2):
        lre, lim, ldt, zr, zi, mg, sn, cs = [small[i][:, :] for i in range(8)]
        dma(lre, sst[d, 0, :, :], [], ["sm0"])
        dma(lim, sst[d, 1, :, :], [], ["sm1"])
        dma(ldt, sst[d, 2, :, :], [], ["sm2"])
        act(ldt, ldt, AF.Exp, ["sm2"], ["sm2"])
        tt(zr, lre, ldt, ALU.mult, ["sm0", "sm2"], ["sm3"])
        tt(zi, lim, ldt, ALU.mult, ["sm1", "sm2"], ["sm4"])
        act(mg, zr, AF.Exp, ["sm3"], ["sm5"])
        sincos(zi, "sm4", sn, cs, "sm6", "sm7", zr, "sm3", isml[:, :], "isml")
        for l in range(3):
            A, B, N = pwA[d][l], pwB[d][l], pwN[d][l]
            rA, rB, rN = f"pwA{d}{l}", f"pwB{d}{l}", f"pwN{d}{l}"
            P.add(V, lambda A=A: nc.vector.memset(A[:, :, 0:1], 1.0), [], [rA])
            P.add(V, lambda B=B: nc.vector.memset(B[:, :, 0:1], 0.0), [], [rB])
            if l == 0:
                tt(A[:, :, 1], mg, cs, ALU.mult, ["sm5", "sm7"], [rA])
                tt(B[:, :, 1], mg, sn, ALU.mult, ["sm5", "sm6"], [rB])
            else:
                cp(A[:, :, 1], pwA[d][l - 1][:, :, 16], [f"pwA{d}{l-1}"], [rA])
                cp(B[:, :, 1], pwB[d][l - 1][:, :, 16], [f"pwB{d}{l-1}"], [rB])
            for j in range(2, 17):
                tt(zr, A[:, :, j - 1], A[:, :, 1], ALU.mult, [rA], ["sm3"])
                tt(zi, B[:, :, j - 1], B[:, :, 1], ALU.mult, [rB], ["sm4"])
                tt(A[:, :, j], zr, zi, ALU.subtract, ["sm3", "sm4"], [rA])
                tt(zr, A[:, :, j - 1], B[:, :, 1], ALU.mult, [rA, rB], ["sm3"])
                tt(zi, B[:, :, j - 1], A[:, :, 1], ALU.mult, [rA, rB], ["sm4"])
                tt(B[:, :, j], zr, zi, ALU.add, ["sm3", "sm4"], [rB])
            ts(N[:, :, :], B[:, :, :], -1.0, None, ALU.mult, None, [rB], [rN])
        LR, LI, DT, t0, t1, t2, t3 = [pt[i][:, :] for i in range(7)]
        dma(LR, ssr[d, 0, :, :], [], ["pt0"])
        dma(LI, ssr[d, 1, :, :], [], ["pt1"])
        dma(DT, ssr[d, 2, :, :], [], ["pt2"])
        act(DT, DT, AF.Exp, ["pt2"], ["pt2"])
        tt(t0, LR, DT, ALU.mult, ["pt0", "pt2"], ["pt3"])
        tt(t1, LI, DT, ALU.mult, ["pt1", "pt2"], ["pt4"])
        act(t0, t0, AF.Exp, ["pt3"], ["pt3"])
        sincos(t1, "pt4", t2, t3, "pt5", "pt6", DT, "pt2", ibig[:, :], "ibig")
        tt(t2, t2, t0, ALU.mult, ["pt5", "pt3"], ["pt5"])
        tt(t3, t3, t0, ALU.mult, ["pt6", "pt3"], ["pt6"])
        ts(t3, t3, -1.0, None, ALU.add, None, ["pt6"], ["pt6"])
        tt(t0, LR, LR, ALU.mult, ["pt0"], ["pt3"])
        tt(t1, LI, LI, ALU.mult, ["pt1"], ["pt4"])
        tt(t0, t0, t1, ALU.add, ["pt3", "pt4"], ["pt3"])
        P.add(V, lambda t0=t0: nc.vector.reciprocal(out=t0, in_=t0), ["pt3"], ["pt3"])
        tt(t1, t3, LR, ALU.mult, ["pt6", "pt0"], ["pt4"])
        tt(DT, t2, LI, ALU.mult, ["pt5", "pt1"], ["pt2"])
        tt(t1, t1, DT, ALU.add, ["pt4", "pt2"], ["pt4"])
        tt(t1, t1, t0, ALU.mult, ["pt4", "pt3"], ["pt4"])
        tt(DT, t2, LR, ALU.mult, ["pt5", "pt0"], ["pt2"])
        tt(LR, t3, LI, ALU.mult, ["pt6", "pt1"], ["pt0"])
        tt(DT, DT, LR, ALU.subtract, ["pt2", "pt0"], ["pt2"])
        tt(DT, DT, t0, ALU.mult, ["pt2", "pt3"], ["pt2"])
        dma(LR, ssb[d, 0, :, :], [], ["pt0"])
        dma(LI, ssb[d, 1, :, :], [], ["pt1"])
        tt(t0, t1, LR, ALU.mult, ["pt4", "pt0"], ["pt3"])
        tt(t2, DT, LI, ALU.mult, ["pt2", "pt1"], ["pt5"])
        tt(bbR[d][:, :], t0, t2, ALU.subtract, ["pt3", "pt5"], [f"bbR{d}"])
        tt(t0, t1, LI, ALU.mult, ["pt4", "pt1"], ["pt3"])
        tt(t2, DT, LR, ALU.mult, ["pt2", "pt0"], ["pt5"])
        tt(bbI[d][:, :], t0, t2, ALU.add, ["pt3", "pt5"], [f"bbI{d}"])
        dma(t3, ssc[d, 0, :, :], [], ["pt6"])
        cp(ccR[d][:, :], t3, ["pt6"], [f"ccR{d}"])
        dma(t3, ssc[d, 1, :, :], [], ["pt6"])
        ts(ccI[d][:, :], t3, -1.0, None, ALU.mult, None, ["pt6"], [f"ccI{d}"])
    P.barrier()
    pst2.close()
    cur[0] = pst
    Rt = sb("Rst", [128, L], F32)
    It = sb("Ist", [128, L], F32)
    Rbf = sb("Rbf", [128, L], BF16)
    Ibf = sb("Ibf", [128, L], BF16)
    ubig = sb("ubig", [128, 4, L], BF16)
    yacc = sb("yacc", [128, L], F32)
    e2R = sb("e2R", [128, 16, T3], F32)
    e2I = sb("e2I", [128, 16, T3], F32)
    e3R = sb("e3R", [128, T3], F32)
    e3I = sb("e3I", [128, T3], F32)
    x2pR = sb("x2pR", [128, T3], F32)
    x2pI = sb("x2pI", [128, T3], F32)
    x1pR = sb("x1pR", [128, 16, T3], F32)
    x1pI = sb("x1pI", [128, 16, T3], F32)
    dma(ubig[:, :, :], UBF.ap().rearrange("c p n -> p c n"), P.rd("UBF"), ["ubig"])

    def cmul_acc(oR, oI, pR, pI, a, b, nb, res_o, res_p, extra_reads=()):
        ex = list(extra_reads)
        roR, roI, rpR, rpI = res_o + "R", res_o + "I", res_p + "R", res_p + "I"
        stt(oR, pR, a, oR, ALU.mult, ALU.add, [roR, rpR] + ex, [roR])
        stt(oI, pR, b, oI, ALU.mult, ALU.add, [roI, rpR] + ex, [roI])
        stt(oR, pI, nb, oR, ALU.mult, ALU.add, [roR, rpI] + ex, [roR])
        stt(oI, pI, a, oI, ALU.mult, ALU.add, [roI, rpI] + ex, [roI])

    def blk(tn, j):
        return tn[:, j * K1:(j + 1) * K1]

    def blkres(t, ri):
        j0_, j1_ = (t * TT) // K1, ((t + 1) * TT - 1) // K1
        return [f"b{j}{ri}" for j in range(j0_, j1_ + 1)]

    def blk3(tn, j):
        return tn[:, j * K1:(j + 1) * K1].rearrange("p (a b) -> p a b", b=T3)

    for pas in range(2):
        d = pas
        rev = (pas == 1)

        def ix(i, n):
            return (n - 1 - i) if rev else i

        ini = zero_s if pas == 0 else iniS
        ini_res = "zero_s" if pas == 0 else "iniS"
        if pas == 1:
            dma(SX[:, :], finR[:, :], ["finR"], ["SX"], q=G)
            P.add(G, lambda: nc.gpsimd.collective_compute("AllGather", ALU.bypass, replica_groups=[[0, 1], [2, 3], [4, 5], [6, 7]],
                                                           ins=[SX.ap().opt()], outs=[SXALL.ap().opt()]),
                  ["SX"], ["SXALL"])
            dma(gat[:, :, :], SXALL.ap().rearrange("(r p) n -> p r n", p=128), ["SXALL"], ["gat"])
            ts(iniS[:, :], gat[:, 0, :], sel_s[:, 0:1], None, ALU.mult, None, ["gat", "sel"], ["iniS"])
            stt(iniS[:, :], gat[:, 1, :], sel_s[:, 1:2], iniS[:, :], ALU.mult, ALU.add, ["gat", "sel", "iniS"], ["iniS"])
        for c in range(4):
            if pas == 0:
                dma(yacc[:, :], U32[c, :, :], P.rd("U32"), ["yacc"])
                ts(yacc[:, :], yacc[:, :], ssd_s[:, c:c + 1], None, ALU.mult, None, ["yacc", "ssd"], ["yacc"])
            else:
                dma(yacc[:, :], YS[c, :, :], [f"YS{c}"], ["yacc"])
            for qq in range(4):
                q = c * 4 + qq
                A0, B0, N0 = pwA[d][0], pwB[d][0], pwN[d][0]
                A1, B1, N1 = pwA[d][1], pwB[d][1], pwN[d][1]
                A2, B2, N2 = pwA[d][2], pwB[d][2], pwN[d][2]
                pres = [f"pw{x}{d}{l}" for x in "ABN" for l in range(3)]
                for t in range(NT):
                    bR, bI = bank(), bank()
                    mm(ps[bR][:, :], bbR[d][:, q * 128:(q + 1) * 128], ubig[:, c, t * TT:(t + 1) * TT], True, True, [f"bbR{d}", "ubig"], [f"ps{bR}"])
                    mm(ps[bI][:, :], bbI[d][:, q * 128:(q + 1) * 128], ubig[:, c, t * TT:(t + 1) * TT], True, True, [f"bbI{d}", "ubig"], [f"ps{bI}"])
                    act(Rt[:, t * TT:(t + 1) * TT], ps[bR][:, :], AF.Copy, [f"ps{bR}"], blkres(t, "R"))
                    act(It[:, t * TT:(t + 1) * TT], ps[bI][:, :], AF.Copy, [f"ps{bI}"], blkres(t, "I"))
                for j in range(1, 16):
                    jc, jp = ix(j, 16), ix(j - 1, 16)
                    cmul_acc(blk(Rt, jc), blk(It, jc), blk(Rt, jp), blk(It, jp),
                             A0[:, q, 1:2], B0[:, q, 1:2], N0[:, q, 1:2], f"b{jc}", f"b{jp}", pres)
                jl = ix(15, 16)
                cp(e2R[:, :, :], blk3(Rt, jl), [f"b{jl}R"], [f"e2_{j}R" for j in range(16)])
                cp(e2I[:, :, :], blk3(It, jl), [f"b{jl}I"], [f"e2_{j}I" for j in range(16)])
                for j in range(1, 16):
                    jc, jp = ix(j, 16), ix(j - 1, 16)
                    cmul_acc(e2R[:, jc, :], e2I[:, jc, :], e2R[:, jp, :], e2I[:, jp, :],
                             A1[:, q, 1:2], B1[:, q, 1:2], N1[:, q, 1:2], f"e2_{jc}", f"e2_{jp}", pres)
                cp(e3R[:, :], e2R[:, jl, :], [f"e2_{jl}R"], [f"e3_{k}R" for k in range(T3)])
                cp(e3I[:, :], e2I[:, jl, :], [f"e2_{jl}I"], [f"e3_{k}I" for k in range(T3)])
                for k in range(T3):
                    kc = ix(k, T3)
                    if k == 0:
                        pR, pI = ini[:, q:q + 1], ini[:, 16 + q:16 + q + 1]
                        rp_ = "ini"
                    else:
                        kp = ix(k - 1, T3)
                        pR, pI = e3R[:, kp:kp + 1], e3I[:, kp:kp + 1]
                        rp_ = f"e3_{kp}"
                    cmul_acc(e3R[:, kc:kc + 1], e3I[:, kc:kc + 1], pR, pI, A2[:, q, 1:2], B2[:, q, 1:2], N2[:, q, 1:2],
                             f"e3_{kc}", rp_, pres + [ini_res])
                kl = ix(T3 - 1, T3)
                if pas == 0:
                    cp(finR[:, q:q + 1], e3R[:, kl:kl + 1], [f"e3_{kl}R"], ["finR"])
                    cp(finR[:, 16 + q:16 + q + 1], e3I[:, kl:kl + 1], [f"e3_{kl}I"], ["finR"])
                k0 = ix(0, T3)
                e3allR = [f"e3_{k}R" for k in range(T3)]
                e3allI = [f"e3_{k}I" for k in range(T3)]
                cp(x2pR[:, k0:k0 + 1], ini[:, q:q + 1], [ini_res], ["x2pR"])
                cp(x2pI[:, k0:k0 + 1], ini[:, 16 + q:16 + q + 1], [ini_res], ["x2pI"])
                if T3 > 1:
                    if not rev:
                        cp(x2pR[:, 1:T3], e3R[:, 0:T3 - 1], e3allR, ["x2pR"])
                        cp(x2pI[:, 1:T3], e3I[:, 0:T3 - 1], e3allI, ["x2pI"])
                    else:
                        cp(x2pR[:, 0:T3 - 1], e3R[:, 1:T3], e3allR, ["x2pR"])
                        cp(x2pI[:, 0:T3 - 1], e3I[:, 1:T3], e3allI, ["x2pI"])
                for j in range(16):
                    jc = ix(j, 16)
                    cmul_acc(e2R[:, jc, :], e2I[:, jc, :], x2pR[:, :], x2pI[:, :],
                             A1[:, q, j + 1:j + 2], B1[:, q, j + 1:j + 2], N1[:, q, j + 1:j + 2], f"e2_{jc}", "x2p", pres)
                j0 = ix(0, 16)
                e2allR = [f"e2_{j}R" for j in range(16)]
                e2allI = [f"e2_{j}I" for j in range(16)]
                cp(x1pR[:, j0, :], x2pR[:, :], ["x2pR"], ["x1pR"])
                cp(x1pI[:, j0, :], x2pI[:, :], ["x2pI"], ["x1pI"])
                if not rev:
                    cp(x1pR[:, 1:16, :], e2R[:, 0:15, :], e2allR, ["x1pR"])
                    cp(x1pI[:, 1:16, :], e2I[:, 0:15, :], e2allI, ["x1pI"])
                else:
                    cp(x1pR[:, 0:15, :], e2R[:, 1:16, :], e2allR, ["x1pR"])
                    cp(x1pI[:, 0:15, :], e2I[:, 1:16, :], e2allI, ["x1pI"])
                for j in range(16):
                    jc = ix(j, 16)
                    cmul_acc(blk3(Rt, jc), blk3(It, jc), x1pR[:, :, :], x1pI[:, :, :],
                             A0[:, q, j + 1:j + 2], B0[:, q, j + 1:j + 2], N0[:, q, j + 1:j + 2], f"b{jc}", "x1p", pres)
                act(Rbf[:, :], Rt[:, :], AF.Copy, [f"b{j}R" for j in range(16)], ["Rbf"])
                cp(Ibf[:, :], It[:, :], [f"b{j}I" for j in range(16)], ["Ibf"], eng=G)
                for t in range(NT):
                    b = bank()
                    mm(ps[b][:, :], ccR[d][:, q * 128:(q + 1) * 128], Rbf[:, t * TT:(t + 1) * TT], True, False, [f"ccR{d}", "Rbf"], [f"ps{b}"])
                    mm(ps[b][:, :], ccI[d][:, q * 128:(q + 1) * 128], Ibf[:, t * TT:(t + 1) * TT], False, True, [f"ccI{d}", "Ibf"], [f"ps{b}"])
                    tt(yacc[:, t * TT:(t + 1) * TT], yacc[:, t * TT:(t + 1) * TT], ps[b][:, :], ALU.add, ["yacc", f"ps{b}"], ["yacc"])
            dma(YS[c, :, :], yacc[:, :], ["yacc"], [f"YS{c}"], q=G)
    P.barrier()
    pst.close()

    if _os.environ.get("STOP") == "S":
        P.emit(["YS0", "YS1", "YS2", "YS3"], es); es.close(); return nc
    pst = ExitStack()
    cur[0] = pst
    ckv = sb("ckv", [128, 2, NK], BF16)
    Kt = sb("Kt", [96, NK], BF16)
    Va = sb("Va", [128, NK // 128, 128], BF16)
    qts = sb("qts", [96, L], BF16)
    wuk_s = sb("wuk_s", [128, 2, 512], BF16)
    wuv_s = sb("wuv_s", [128, 2, 512], BF16)
    Pb = [sb(f"Pb{i}", [128, 1024], BF16) for i in range(3)]
    rcs = sb("rcs", [64, TT], F32)
    ob = sb("ob", [64, TT], BF16)
    for i in range(NSPL):
        ckall = CKVALLs[i].ap().rearrange("(r f) n -> r f n", r=2)
        for r in range(2):
            o0 = (i * 2 + r) * Ls
            for k in range(2):
                dma(ckv[:, k, o0:o0 + Ls], ckall[r, k * 128:(k + 1) * 128, :], [f"CKVALL{i}"], [P.wr("ckv")])
            dma(Kt[64:96, o0:o0 + Ls], ckall[r, 256:288, :], [f"CKVALL{i}"], [P.wr("Ktr")])
    dma(wuk_s[:, :, :], wview("wuk"), wres("wuk"), ["wuk_s"])
    dma(wuv_s[:, :, :], wview("wuv"), wres("wuv"), ["wuv_s"])
    P.add(G, lambda: nc.gpsimd.memset(Va[:, :, :], 1.0), [], ["Va"])
    scale = 96.0 ** -0.5
    NKC = NK // 128
    for h in range(NH):
        for kt in range(NK // 512):
            b = bank()
            for k in range(2):
                mm(ps[b][0:64, :], wuk_s[:, k, h * 64:(h + 1) * 64], ckv[:, k, kt * 512:(kt + 1) * 512], k == 0, k == 1, ["wuk_s"] + P.rd("ckv"), [f"ps{b}"])
            cp(Kt[0:64, kt * 512:(kt + 1) * 512], ps[b][0:64, :], [f"ps{b}"], ["Kt"])
        for kg in range(NKC // 8):
            b = bank()
            for j in range(8):
                kc = kg * 8 + j
                for k in range(2):
                    mm(ps[b][:, j * 64:(j + 1) * 64], ckv[:, k, kc * 128:(kc + 1) * 128], wuv_s[:, k, h * 64:(h + 1) * 64], k == 0, k == 1, ["wuv_s"] + P.rd("ckv"), [f"ps{b}"])
            cp(Va[:, kg * 8:(kg + 1) * 8, 0:64], ps[b][:, :].rearrange("p (a b) -> p a b", b=64), [f"ps{b}"], ["Va"])
        dma(qts[:, :], QT[h, :, :], P.rd("QT"), ["qts"])
        for qt_ in range(L // 512):
            bo = 6 + (qt_ % 2)
            q0 = qt_ * 512
            npair = NKC // 2

            def s_mm(i):
                p = i % 3
                for j in range(2):
                    kc = 2 * i + j
                    mm(ps[2 * p + j][:, :], Kt[0:96, kc * 128:(kc + 1) * 128], qts[0:96, q0:q0 + 512], True, True, ["Kt", "qts"] + P.rd("Ktr"), [f"ps{2*p+j}"])

            def s_exp(i):
                p = i % 3
                for j in range(2):
                    act(Pb[p][:, j * 512:(j + 1) * 512], ps[2 * p + j][:, :], AF.Exp, [f"ps{2*p+j}"], [f"Pb{p}"], scale=scale)

            def pv(i):
                p = i % 3
                for j in range(2):
                    kc = 2 * i + j
                    mm(ps[bo][:, :], Va[:, kc, :], Pb[p][:, j * 512:(j + 1) * 512], kc == 0, kc == NKC - 1, ["Va", f"Pb{p}"], [f"ps{bo}"])

            s_mm(0)
            s_exp(0)
            for i in range(npair):
                if i + 1 < npair:
                    s_mm(i + 1)
                    s_exp(i + 1)
                pv(i)
            P.add(V, lambda bo=bo: nc.vector.reciprocal(out=rcs[0:64, :], in_=ps[bo][64:128, :]), [f"ps{bo}"], ["rcs"])
            tt(ob[:, :], ps[bo][0:64, :], rcs[:, :], ALU.mult, [f"ps{bo}", "rcs"], ["ob"])
            dma(OT[h // 2, (h % 2) * 64:(h % 2) * 64 + 64, q0:q0 + 512], ob[:, :], ["ob"], [P.wr("OT")], q=G)
    P.barrier()
    pst.close()

    if _os.environ.get("STOP") == "B":
        P.emit(P.rd("OT"), es); es.close(); return nc
    pst = ExitStack()
    cur[0] = pst
    alloc_common()
    xt, xbf, tmpf, wA = CM["xin"], CM["xbf"], CM["tmpf"], CM["wA"]
    woa_s = sb("woa_s", [128, 4, D], BF16)
    wos_s = sb("wos_s", [128, 4, D], BF16)
    wglu_s = sb("wglu_s", [128, 4, 512], BF16)
    wpp_s = sb("wpp_s", [128, 2, D], BF16)
    dma(woa_s[:, :, :], wview("woa"), wres("woa"), ["woa_s"])
    dma(wos_s[:, :, :], wview("wos"), wres("wos"), ["wos_s"])
    dma(wglu_s[:, :, :], wview("wglu"), wres("wglu"), ["wglu_s"])
    dma(wpp_s[:, :, :], wview("wpp"), wres("wpp"), ["wpp_s"])
    ots = sb("ots", [128, 4, TT], BF16)
    ysf = sb("ysf", [128, 4, TT], F32)
    zf = sb("zf", [128, 4, TT], F32)
    zbf4 = sb("zbf4", [128, 4, TT], BF16)
    zzb = sb("zzb", [128, 4, TT], BF16)
    gaf = [sb(f"gaf{i}", [128, TT], F32) for i in range(2)]
    gbf = [sb(f"gbf{i}", [128, TT], F32) for i in range(2)]
    mrb = sb("mrb", [128, NCH, TT], BF16)
    pf = sb("pf", [128, 2, TT], F32)
    pbf = sb("pbf", [128, 2, TT], BF16)
    outv = out.ap().rearrange("c p n -> p c n")
    pv_ = pT.ap().rearrange("c p n -> p c n")
    GC = 1.5957691216057308
    ysall = [f"YS{c}" for c in range(4)]
    for t in range(NT):
        c0, c1 = t * TT, (t + 1) * TT
        dma(xt[:, :, :], x1v[:, :, c0:c1], P.rd("X1"), XALL)
        dma(ots[:, :, :], OT.ap().rearrange("c p n -> p c n")[:, :, c0:c1], P.rd("OT"), ["ots"])
        dma(ysf[:, :, :], YS.ap().rearrange("c p n -> p c n")[:, :, c0:c1], ysall, ["ysf"])
        dma(pf[:, :, :], pv_[:, :, c0:c1], [], ["pf"])
        cp(pbf[:, :, :], pf[:, :, :], ["pf"], ["pbf"], eng=G)
        tt(zf[:, :, :], ysf[:, :, :], ysf[:, :, :], ALU.mult, ["ysf"], ["zf"])
        ts(zf[:, :, :], zf[:, :, :], 0.044715, 1.0, ALU.mult, ALU.add, ["zf"], ["zf"])
        tt(zf[:, :, :], zf[:, :, :], ysf[:, :, :], ALU.mult, ["zf", "ysf"], ["zf"])
        act(zf[:, :, :], zf[:, :, :], AF.Sigmoid, ["zf"], ["zf"], scale=GC)
        tt(zf[:, :, :], zf[:, :, :], ysf[:, :, :], ALU.mult, ["zf", "ysf"], ["zf"])
        cp(zbf4[:, :, :], zf[:, :, :], ["zf"], ["zbf4"], eng=G)
        for c in range(4):
            b = bank()
            for k in range(4):
                mm(ps[b][:, :], wglu_s[:, k, c * 128:(c + 1) * 128], zbf4[:, k, :], k == 0, k == 3, ["wglu_s", "zbf4"], [f"ps{b}"])
            ti = tmp()
            act(tmpf[ti][:, :], ps[b][:, :], AF.Sigmoid, [f"ps{b}"], [f"tmpf{ti}"])
            tt(zzb[:, c, :], zf[:, c, :], tmpf[ti][:, :], ALU.mult, ["zf", f"tmpf{ti}"], ["zzb"])
        for o in range(NCH):
            gi_ = o % 2
            dma(gaf[gi_][:, :], GA[o, :, c0:c1], P.rd("G0"), [f"gaf{gi_}"])
            dma(gbf[gi_][:, :], GB[o, :, c0:c1], P.rd("G1"), [f"gbf{gi_}"])
            ba, bb = bank(), bank()
            for k in range(4):
                mm(ps[ba][:, :], woa_s[:, k, o * 128:(o + 1) * 128], ots[:, k, :], k == 0, k == 3, ["woa_s", "ots"], [f"ps{ba}"])
            for k in range(4):
                mm(ps[bb][:, :], wos_s[:, k, o * 128:(o + 1) * 128], zzb[:, k, :], k == 0, k == 3, ["wos_s", "zzb"], [f"ps{bb}"])
            tt(gaf[gi_][:, :], gaf[gi_][:, :], ps[ba][:, :], ALU.mult, [f"gaf{gi_}", f"ps{ba}"], [f"gaf{gi_}"])
            tt(gbf[gi_][:, :], gbf[gi_][:, :], ps[bb][:, :], ALU.mult, [f"gbf{gi_}", f"ps{bb}"], [f"gbf{gi_}"])
            tt(mrb[:, o, :], gaf[gi_][:, :], gbf[gi_][:, :], ALU.add, [f"gaf{gi_}", f"gbf{gi_}"], ["mrb"], eng=G)
        for half in range(2):
            iw = load_wA("wout", half * 512, (half + 1) * 512)
            for cc in range(4):
                o = half * 4 + cc
                b = bank()
                for k in range(NCH):
                    mm(ps[b][:, :], wA[iw][:, k, cc * 128:(cc + 1) * 128], mrb[:, k, :], k == 0, k == NCH - 1, [f"wA{iw}", "mrb"], [f"ps{b}"])
                stt(xt[:, o, :], ps[b][:, :], 1.0 / ALPHA, xt[:, o, :], ALU.mult, ALU.add, [f"ps{b}", f"xin{o}"], [f"xin{o}"])
        layer_norm(1, LN_EPS / (ALPHA * ALPHA), True)
        ffn("w1b", "w3b", "w2b")
        layer_norm(2, LN_EPS / (ALPHA * ALPHA), True)
        for half in range(2):
            iw = load_wA("wpg", half * 512, (half + 1) * 512)
            for cc in range(4):
                o = half * 4 + cc
                bg, bp = bank(), bank()
                for k in range(NCH):
                    mm(ps[bg][:, :], wA[iw][:, k, cc * 128:(cc + 1) * 128], xbf[:, k, :], k == 0, k == NCH - 1, [f"wA{iw}", "xbf"], [f"ps{bg}"])
                for k in range(2):
                    mm(ps[bp][:, :], wpp_s[:, k, o * 128:(o + 1) * 128], pbf[:, k, :], k == 0, k == 1, ["wpp_s", "pbf"], [f"ps{bp}"])
                ti = tmp()
                act(tmpf[ti][:, :], ps[bg][:, :], AF.Sigmoid, [f"ps{bg}"], [f"tmpf{ti}"])
                tt(tmpf[ti][:, :], tmpf[ti][:, :], ps[bp][:, :], ALU.mult, [f"tmpf{ti}", f"ps{bp}"], [f"tmpf{ti}"])
                stt(xt[:, o, :], tmpf[ti][:, :], 1.0 / ALPHA, xt[:, o, :], ALU.mult, ALU.add, [f"tmpf{ti}", f"xin{o}"], [f"xin{o}"])
        layer_norm(3, LN_EPS / (ALPHA * ALPHA), False)
        dma(outv[:, :, c0:c1], xt[:, :, :], XALL, [P.wr("OUT")], q=G)

    P.emit(P.rd("OUT"), es)
    pst.close()
    es.close()
    return nc


def _perm(L):
    K1 = L // 16
    T3 = K1 // 16
    n = np.arange(L)
    j = n // K1
    j2 = (n % K1) // T3
    k1 = n % T3
    return (k1 * 16 + j2) * 16 + j


_NC_CACHE = {}


def kernel(**inp):
    B, S, _ = inp["x"].shape
    L = S // 2
    f32 = np.float32
    perm = _perm(L)
    if L not in _NC_CACHE:
        _NC_CACHE[L] = build(L)
    nc = _NC_CACHE[L]

    def chunks(a):
        return np.ascontiguousarray(a.reshape(a.shape[0] // 128, 128, a.shape[1]))

    def col(v):
        return np.ascontiguousarray(v.reshape(-1, 128).T)

    inv_freq = (10000.0 ** (-np.arange(0, 32, 2, dtype=np.float32) / 32)).astype(f32)
    W = {
        "w1a": inp["ffn1_w1"][0], "w3a": inp["ffn1_w3"][0], "w2a": inp["ffn1_w2"][0], "win": inp["w_in"][0],
        "wuk": inp["w_uk"][0].reshape(256, 512), "wuv": inp["w_uv"][0].reshape(256, 512),
        "woa": inp["w_o_attn"][0], "wglu": inp["w_glu"][0], "wos": inp["w_o_ssm"][0], "wout": inp["w_out"][0],
        "w1b": inp["ffn2_w1"][0], "w3b": inp["ffn2_w3"][0], "w2b": inp["ffn2_w2"][0],
        "wpp": inp["ple_w_proj"][0], "wpg": inp["ple_w_gate"][0],
    }
    wq = inp["w_uq"][0]
    W["wuq"] = np.concatenate([wq[:, :, 0:64].reshape(384, 512), wq[:, :, 64:80].reshape(384, 128), wq[:, :, 80:96].reshape(384, 128)], axis=1)
    W = {k: np.ascontiguousarray(v, dtype=f32) for k, v in W.items()}
    lnp = np.concatenate([col(inp[f"ln{i}_{gb}"][0]) for i in (1, 2, 3, 4) for gb in ("g", "b")], axis=1).astype(f32)
    qkg = np.concatenate([col(inp["q_norm_g"][0]), col(inp["kv_norm_g"][0])], axis=1).astype(f32)
    ssd = col(inp["ssm_d"][0].reshape(512)).astype(f32)

    def ssm_pack(sfx):
        lre, lim, ldt = inp["ssm_lam_re_" + sfx][0], inp["ssm_lam_im_" + sfx][0], inp["ssm_log_dt_" + sfx][0]
        bre, bim = inp["ssm_b_re_" + sfx][0], inp["ssm_b_im_" + sfx][0]
        cre, cim = inp["ssm_c_re_" + sfx][0], inp["ssm_c_im_" + sfx][0]
        ldtb = np.broadcast_to(ldt[:, None], (32, 64))
        st = np.zeros((3, 128, 16), f32)
        rp = np.zeros((3, 128, 2048), f32)
        bp = np.zeros((2, 128, 2048), f32)
        cpad = np.zeros((2, 128, 2048), f32)
        for q in range(16):
            for gs_ in range(2):
                g = 2 * q + gs_
                for i, a in enumerate((lre, lim, ldtb)):
                    st[i, gs_ * 64:(gs_ + 1) * 64, q] = a[g]
                    rp[i, :, q * 128 + gs_ * 64:q * 128 + (gs_ + 1) * 64] = a[g][None, :]
                r0 = (g % 8) * 16
                bp[0, r0:r0 + 16, q * 128 + gs_ * 64:q * 128 + (gs_ + 1) * 64] = bre[g].T
                bp[1, r0:r0 + 16, q * 128 + gs_ * 64:q * 128 + (gs_ + 1) * 64] = bim[g].T
                cpad[0, gs_ * 64:(gs_ + 1) * 64, q * 128 + r0:q * 128 + r0 + 16] = cre[g].T
                cpad[1, gs_ * 64:(gs_ + 1) * 64, q * 128 + r0:q * 128 + r0 + 16] = cim[g].T
        return st, rp, bp, cpad

    packs = {"f": ssm_pack("f"), "b": ssm_pack("b")}
    in_maps = []
    for c in range(8):
        b, half = c // 2, c % 2
        tl = perm if half == 0 else (L - 1 - perm)
        tg = half * L + tl
        m = dict(W)
        m["xT"] = chunks(np.ascontiguousarray(inp["x"][b][tg].T))
        m["pT"] = chunks(np.ascontiguousarray(inp["p"][0, b][tg].T))
        ang = tg.astype(f32)[None, :] * inv_freq[:, None]
        m["cosT"] = np.ascontiguousarray(np.tile(np.cos(ang).astype(f32), (8, 1)))
        m["sinT"] = np.ascontiguousarray(np.tile(np.sin(ang).astype(f32), (8, 1)))
        order = ("f", "b") if half == 0 else ("b", "f")
        m["sst"] = np.stack([packs[o][0] for o in order])
        m["ssr"] = np.stack([packs[o][1] for o in order])
        m["ssb"] = np.stack([packs[o][2] for o in order])
        m["ssc"] = np.stack([packs[o][3] for o in order])
        m["lnp"], m["qkg"], m["ssd"] = lnp, qkg, ssd
        sel = np.zeros((128, 2), f32)
        sel[:, 1 - half] = 1.0
        m["selp"] = sel
        in_maps.append(m)
    res = run_bass_kernel_spmd(nc, in_maps, core_ids=list(range(8)))
    outp = np.empty((B, S, D), f32)
    for c in range(8):
        b, half = c // 2, c % 2
        tl = perm if half == 0 else (L - 1 - perm)
        tg = half * L + tl
        o = res.results[c]["outT"].reshape(D, L)
        outp[b, tg, :] = o.T
    return outp
```

```python
import math
from contextlib import ExitStack
import numpy as np
import concourse.bass as bass
import concourse.mybir as mybir
from concourse.bass_utils import run_bass_kernel_spmd

F32 = mybir.dt.float32
BF16 = mybir.dt.bfloat16
ALU = mybir.AluOpType
AF = mybir.ActivationFunctionType

D = 1024
DFF = 2816
NCH = 8
HCH = 22
NH = 8
ALPHA = 2.0 ** 0.25
LN_EPS = 1e-5
RMS_EPS = 1e-6
TT = 512
PI = math.pi


import os as _os0
SAME_ENGINE_NOSYNC = set(_os0.environ.get("NOSYNC", "pe").split(","))


class Prog:
    CH = 1000
    NDS = 24

    def __init__(self, nc):
        self.nc = nc
        self.ops = []
        self.groups = {}
        self.eng = {"pe": nc.tensor, "act": nc.scalar, "dve": nc.vector, "pool": nc.gpsimd, "sp": nc.sync}

    def add(self, eng, fn, reads=(), writes=(), dma=False, inc=None):
        self.ops.append(dict(eng=eng, fn=fn, reads=tuple(reads), writes=tuple(writes), dma=dma, inc=inc, bar=False))

    def barrier(self):
        self.ops.append(dict(eng=None, fn=None, reads=(), writes=(), dma=False, inc=None, bar=True))

    def wr(self, grp):
        g = self.groups.setdefault(grp, [])
        nm = f"{grp}#{len(g)}"
        g.append(nm)
        return nm

    def rd(self, grp):
        return list(self.groups.get(grp, []))

    def emit(self, final_res, stack):
        ops = self.ops
        n = len(ops)
        last_w = {}
        readers = {}
        deps = [None] * n
        dma_ids = []
        last_eng = {}
        pend_bar = {}
        for i, o in enumerate(ops):
            if o["bar"]:
                bd = set(last_eng.values()) | set(dma_ids[-self.NDS:])
                pend_bar = {e: set(bd) for e in self.eng}
                deps[i] = []
                continue
            d = set()
            if o["eng"] in pend_bar:
                d |= pend_bar.pop(o["eng"])
            if not o["dma"]:
                last_eng[o["eng"]] = i
            for r in o["reads"]:
                if r in last_w:
                    d.add(last_w[r])
            for w in o["writes"]:
                if w in last_w:
                    d.add(last_w[w])
                for x in readers.get(w, ()):
                    d.add(x)
            if o["dma"]:
                if len(dma_ids) >= self.NDS:
                    d.add(dma_ids[len(dma_ids) - self.NDS])
                dma_ids.append(i)
            d.discard(i)
            keep = []
            for x in d:
                ox = ops[x]
                if (not ox["dma"]) and ox["eng"] == o["eng"] and o["eng"] in SAME_ENGINE_NOSYNC and not o["dma"]:
                    continue
                keep.append(x)
            deps[i] = sorted(keep)
            for r in o["reads"]:
                readers.setdefault(r, []).append(i)
            for w in o["writes"]:
                last_w[w] = i
                readers[w] = []
        fin = sorted({last_w[r] for r in final_res if r in last_w})
        needed = [False] * n
        for i in range(n):
            for x in deps[i]:
                needed[x] = True
        for x in fin:
            needed[x] = True
        cnt = {e: 0 for e in self.eng}
        dcnt = [0] * self.NDS
        sig = [None] * n
        nd = 0
        for i, o in enumerate(ops):
            if o["bar"]:
                continue
            if o["dma"]:
                k = nd % self.NDS
                nd += 1
                dcnt[k] += 1
                sig[i] = ("d", k, dcnt[k] * (o["inc"] or 16))
                if o["inc"]:
                    raise RuntimeError("custom inc unsupported")
            elif needed[i]:
                cnt[o["eng"]] += 1
                sig[i] = ("e", o["eng"], cnt[o["eng"]])
        sems = {}
        for e in self.eng:
            for c in range((cnt[e] + self.CH - 1) // self.CH + 1):
                sems[("e", e, c)] = stack.enter_context(self.nc.semaphore(f"s_{e}_{c}"))
        for k in range(self.NDS):
            sems[("d", k)] = stack.enter_context(self.nc.semaphore(f"s_d{k}"))
        known = {e: {} for e in self.eng}
        snap = [None] * n

        def key_of(s):
            return ("e", s[1]) if s[0] == "e" else ("d", s[1])

        def wait(e, x):
            s = sig[x]
            kk = key_of(s)
            kn = known[e]
            if kn.get(kk, 0) >= s[2]:
                return
            h = self.eng[e]
            if s[0] == "e":
                c = (s[2] - 1) // self.CH
                h.wait_ge(sems[("e", s[1], c)], (s[2] - 1) % self.CH + 1)
            else:
                h.wait_ge(sems[("d", s[1])], s[2])
            kn[kk] = s[2]
            sn = snap[x]
            if sn:
                for k2, v2 in sn.items():
                    if kn.get(k2, 0) < v2:
                        kn[k2] = v2

        for i, o in enumerate(ops):
            if o["bar"]:
                continue
            e = o["eng"]
            for x in deps[i]:
                wait(e, x)
            ins = o["fn"]()
            s = sig[i]
            if s is not None:
                if s[0] == "e":
                    c = (s[2] - 1) // self.CH
                    ins.then_inc(sems[("e", e, c)], 1)
                else:
                    ins.then_inc(sems[("d", s[1])], 16)
                snap[i] = dict(known[e])
        for x in fin:
            wait("sp", x)


def build(L):
    NT = L // TT
    NK = 2 * L
    K1 = L // 16
    T3 = K1 // 16
    assert L % 512 == 0 and K1 % 16 == 0 and T3 >= 1
    nc = bass.Bass("TRN2", target_bir_lowering=False)
    P = Prog(nc)
    es = ExitStack()
    cur = [es]

    def din(name, shape, dt=F32):
        return nc.dram_tensor(name, list(shape), dt, kind="ExternalInput")

    def dscr(name, shape, dt):
        return nc.dram_tensor(name, list(shape), dt)

    sbn = [0]

    def sb(name, shape, dt):
        sbn[0] += 1
        return cur[0].enter_context(nc.sbuf_tensor(f"{name}_{sbn[0]}", list(shape), dt))

    xT = din("xT", [NCH, 128, L])
    pT = din("pT", [2, 128, L])
    cosT = din("cosT", [128, L])
    sinT = din("sinT", [128, L])
    wnames = {
        "w1a": (D, DFF), "w3a": (D, DFF), "w2a": (DFF, D), "win": (D, 3232), "wuq": (384, 768),
        "wuk": (256, 512), "wuv": (256, 512), "woa": (512, D), "wglu": (512, 512), "wos": (512, D),
        "wout": (D, D), "w1b": (D, DFF), "w3b": (D, DFF), "w2b": (DFF, D), "wpp": (256, D), "wpg": (D, D),
    }
    wf = {k: din(k, v) for k, v in wnames.items()}
    wb = {k: dscr(k + "_bf", v, BF16) for k, v in wnames.items()}
    lnp = din("lnp", [128, 8 * NCH])
    qkg = din("qkg", [128, 5])
    ssd = din("ssd", [128, 4])
    sst = din("sst", [2, 3, 128, 16])
    ssr = din("ssr", [2, 3, 128, 2048])
    ssb = din("ssb", [2, 2, 128, 2048])
    ssc = din("ssc", [2, 2, 128, 2048])
    selp = din("selp", [128, 2])
    out = nc.dram_tensor("outT", [NCH, 128, L], F32, kind="ExternalOutput")

    X1 = dscr("X1", [NCH, 128, L], F32)
    QT = dscr("QT", [NH, 96, L], BF16)
    NSPL = max(1, L // 1024)
    Ls = L // NSPL
    CKVs = [dscr(f"CKV{i}", [288, Ls], BF16) for i in range(NSPL)]
    CKVALLs = [dscr(f"CKVALL{i}", [2 * 288, Ls], BF16) for i in range(NSPL)]
    U32 = dscr("U32", [4, 128, L], F32)
    UBF = dscr("UBF", [4, 128, L], BF16)
    GA = dscr("GA", [NCH, 128, L], F32)
    GB = dscr("GB", [NCH, 128, L], F32)
    OT = dscr("OT", [4, 128, L], BF16)
    YS = dscr("YS", [4, 128, L], F32)
    SX = dscr("SX", [128, 32], F32)
    SXALL = dscr("SXALL", [256, 32], F32)
    import os as _os
    if _os.environ.get("DUMMY_MB"):
        DUM = dscr("DUM", [int(_os.environ["DUMMY_MB"]) * 2, 128, 1024], F32)

    ps = [es.enter_context(nc.psum_tensor(f"ps{i}", [128, 512], F32)) for i in range(8)]
    pctr = [0]

    def bank():
        b = pctr[0] % 8
        pctr[0] += 1
        return b

    V, S, T, G, SP_ = "dve", "act", "pe", "pool", "sp"

    def mm(o, lhsT, rhs, start, stop, reads, writes):
        P.add(T, lambda: nc.tensor.matmul(o, lhsT, rhs, start=start, stop=stop), reads, writes)

    def act(o, i, func, reads, writes, scale=None, bias=None):
        kw = {}
        if scale is not None:
            kw["scale"] = scale
        if bias is not None:
            kw["bias"] = bias
        P.add(S, lambda: nc.scalar.activation(out=o, in_=i, func=func, **kw), reads, writes)

    def tt(o, a, b, op, reads, writes, eng=V):
        h = nc.vector if eng == V else nc.gpsimd
        P.add(eng, lambda: h.tensor_tensor(out=o, in0=a, in1=b, op=op), reads, writes)

    def ts(o, a, s1, s2, op0, op1, reads, writes, eng=V):
        h = nc.vector if eng == V else nc.gpsimd
        if op1 is None:
            P.add(eng, lambda: h.tensor_scalar(out=o, in0=a, scalar1=s1, scalar2=None, op0=op0), reads, writes)
        else:
            P.add(eng, lambda: h.tensor_scalar(out=o, in0=a, scalar1=s1, scalar2=s2, op0=op0, op1=op1), reads, writes)

    def stt(o, a, s, b, op0, op1, reads, writes):
        P.add(V, lambda: nc.vector.scalar_tensor_tensor(out=o, in0=a, scalar=s, in1=b, op0=op0, op1=op1), reads, writes)

    def cp(o, i, reads, writes, eng=V):
        h = nc.vector if eng == V else nc.gpsimd
        P.add(eng, lambda: h.tensor_copy(out=o, in_=i), reads, writes)

    def dma(o, i, reads, writes, q=SP_):
        h = {"sp": nc.sync, "pool": nc.gpsimd, "act": nc.scalar}[q]
        P.add(q, lambda: h.dma_start(out=o, in_=i), reads, writes, dma=True)

    for k in ["w1a", "w3a", "w2a", "win", "wuq", "wuk", "wuv", "wout", "woa", "wglu", "wos", "w1b", "w3b", "w2b", "wpp", "wpg"]:
        r, c = wnames[k]
        if r * c > 1500000:
            hh = r // 2
            dma(wb[k][0:hh, :], wf[k][0:hh, :], [], ["W_" + k + "0"], q=G)
            dma(wb[k][hh:r, :], wf[k][hh:r, :], [], ["W_" + k + "1"], q=G)
        else:
            dma(wb[k][:, :], wf[k][:, :], [], ["W_" + k + "0"], q=G)

    def wres(k):
        r, c = wnames[k]
        return ["W_" + k + "0", "W_" + k + "1"] if r * c > 1500000 else ["W_" + k + "0"]

    def wview(k):
        return wb[k].ap().rearrange("(kc p) n -> p kc n", p=128)

    if _os.environ.get("DUMMY_MB"):
        dma(DUM[0, :, :], wf["wpg"][0:128, :], [], ["DUM"], q=G)
    ones_d = sb("ones_d", [128, 128], BF16)
    ones_q = sb("ones_q", [128, 128], BF16)
    lnp_s = sb("lnp_s", [128, 8 * NCH], F32)
    qkg_s = sb("qkg_s", [128, 5], F32)
    ssd_s = sb("ssd_s", [128, 4], F32)
    sel_s = sb("sel_s", [128, 2], F32)
    P.add(V, lambda: nc.vector.memset(ones_d[:, :], 1.0 / 1024.0), [], ["ones_d"])
    P.add(V, lambda: nc.vector.memset(ones_q[:, :], 1.0), [], ["ones_q"])
    dma(lnp_s[:, :], lnp[:, :], [], ["lnp"])
    dma(qkg_s[:, :], qkg[:, :], [], ["qkg"])
    dma(ssd_s[:, :], ssd[:, :], [], ["ssd"])
    dma(sel_s[:, :], selp[:, :], [], ["sel"])

    CM = {}

    def alloc_common():
        CM["xin"] = sb("xin", [128, NCH, TT], F32)
        CM["xbf"] = sb("xbf", [128, NCH, TT], BF16)
        CM["hbf"] = sb("hbf", [128, HCH, TT], BF16)
        CM["zb"] = sb("zb", [128, NCH, TT], BF16)
        CM["tmpf"] = [sb(f"tmpf{i}", [128, TT], F32) for i in range(3)]
        CM["mean"] = sb("mean_s", [128, TT], F32)
        CM["rstd"] = sb("rstd_s", [128, TT], F32)
        CM["wA"] = [sb(f"wA{i}", [128, 8, 512], BF16) for i in range(3)]
        CM["wB"] = [sb(f"wB{i}", [128, HCH, 256], BF16) for i in range(2)]

    wactr = [0]
    wbctr = [0]

    def load_wA(k, c0, c1):
        i = wactr[0] % 3
        wactr[0] += 1
        dma(CM["wA"][i][:, :, 0:c1 - c0], wview(k)[:, :, c0:c1], wres(k), [f"wA{i}"])
        return i

    def load_wB(k, c0, c1):
        i = wbctr[0] % 2
        wbctr[0] += 1
        dma(CM["wB"][i][:, :, 0:c1 - c0], wview(k)[:, :, c0:c1], wres(k), [f"wB{i}"])
        return i

    tctr = [0]

    def tmp():
        i = tctr[0] % 3
        tctr[0] += 1
        return i

    def ffn(k1, k3, k2):
        xt, xbf, hbf, tmpf, wA, wB = CM["xin"], CM["xbf"], CM["hbf"], CM["tmpf"], CM["wA"], CM["wB"]
        ngrp = (DFF + 511) // 512
        for g in range(ngrp):
            c0, c1 = g * 512, min(DFF, (g + 1) * 512)
            i1 = load_wA(k1, c0, c1)
            i3 = load_wA(k3, c0, c1)
            for cc in range((c1 - c0) // 128):
                c = g * 4 + cc
                b1, b3 = bank(), bank()
                for k in range(NCH):
                    mm(ps[b1][:, :], wA[i1][:, k, cc * 128:(cc + 1) * 128], xbf[:, k, :], k == 0, k == NCH - 1,
                       [f"wA{i1}", "xbf"], [f"ps{b1}"])
                for k in range(NCH):
                    mm(ps[b3][:, :], wA[i3][:, k, cc * 128:(cc + 1) * 128], xbf[:, k, :], k == 0, k == NCH - 1,
                       [f"wA{i3}", "xbf"], [f"ps{b3}"])
                ti = tmp()
                act(tmpf[ti][:, :], ps[b1][:, :], AF.Silu, [f"ps{b1}"], [f"tmpf{ti}"])
                tt(hbf[:, c, :], tmpf[ti][:, :], ps[b3][:, :], ALU.mult, [f"tmpf{ti}", f"ps{b3}"], [f"hbf{c}"])
        for g in range(4):
            i2 = load_wB(k2, g * 256, (g + 1) * 256)
            for oc in range(2):
                o = g * 2 + oc
                b = bank()
                for k in range(HCH):
                    mm(ps[b][:, :], wB[i2][:, k, oc * 128:(oc + 1) * 128], hbf[:, k, :], k == 0, k == HCH - 1,
                       [f"wB{i2}", f"hbf{k}"], [f"ps{b}"])
                stt(xt[:, o, :], ps[b][:, :], 0.5 / ALPHA, xt[:, o, :], ALU.mult, ALU.add, [f"ps{b}", f"xin{o}"], [f"xin{o}"])

    XALL = [f"xin{o}" for o in range(NCH)]

    def rsqrt_inplace(r, res):
        act(r, r, AF.Ln, [res], [res])
        act(r, r, AF.Exp, [res], [res], scale=-0.5)

    def layer_norm(li, eps, want_bf):
        xt, xbf, zb, tmpf, mean_s, rstd_s = CM["xin"], CM["xbf"], CM["zb"], CM["tmpf"], CM["mean"], CM["rstd"]
        act(zb[:, :, :], xt[:, :, :], AF.Copy, XALL, ["zb"])
        bm, bq = bank(), bank()
        for k in range(NCH):
            mm(ps[bm][:, :], ones_d[:, :], zb[:, k, :], k == 0, k == NCH - 1, ["ones_d", "zb"], [f"ps{bm}"])
        act(zb[:, :, :], xt[:, :, :], AF.Square, XALL, ["zb"])
        for k in range(NCH):
            mm(ps[bq][:, :], ones_d[:, :], zb[:, k, :], k == 0, k == NCH - 1, ["ones_d", "zb"], [f"ps{bq}"])
        act(mean_s[:, :], ps[bm][:, :], AF.Copy, [f"ps{bm}"], ["mean"])
        ti = tmp()
        tt(tmpf[ti][:, :], mean_s[:, :], mean_s[:, :], ALU.mult, ["mean"], [f"tmpf{ti}"])
        tt(tmpf[ti][:, :], ps[bq][:, :], tmpf[ti][:, :], ALU.subtract, [f"ps{bq}", f"tmpf{ti}"], [f"tmpf{ti}"])
        ts(rstd_s[:, :], tmpf[ti][:, :], eps, None, ALU.add, None, [f"tmpf{ti}"], ["rstd"])
        rsqrt_inplace(rstd_s[:, :], "rstd")
        gcol = li * 16
        for o in range(NCH):
            tt(xt[:, o, :], xt[:, o, :], mean_s[:, :], ALU.subtract, [f"xin{o}", "mean"], [f"xin{o}"])
            tt(xt[:, o, :], xt[:, o, :], rstd_s[:, :], ALU.mult, [f"xin{o}", "rstd"], [f"xin{o}"], eng=G)
            act(xt[:, o, :], xt[:, o, :], AF.Identity, [f"xin{o}", "lnp"], [f"xin{o}"],
                scale=lnp_s[:, gcol + o:gcol + o + 1], bias=lnp_s[:, gcol + 8 + o:gcol + 8 + o + 1])
        if want_bf:
            cp(xbf[:, :, :], xt[:, :, :], XALL, ["xbf"], eng=G)

    pst = ExitStack()
    cur[0] = pst
    alloc_common()
    xt, xbf, tmpf, rstd_s, wA = CM["xin"], CM["xbf"], CM["tmpf"], CM["rstd"], CM["wA"]
    xv = xT.ap().rearrange("c p n -> p c n")
    x1v = X1.ap().rearrange("c p n -> p c n")
    ql = sb("ql", [128, 3, TT], F32)
    qsq = sb("qsq", [128, 3, TT], BF16)
    cqb = sb("cqb", [128, 3, TT], BF16)
    wuq_s = sb("wuq_s", [128, 3, 768], BF16)
    qn = sb("qn", [128, 4, TT], BF16)
    cos_s = sb("cos_s", [128, TT], F32)
    sin_s = sb("sin_s", [128, TT], F32)
    r1s = sb("r1s", [128, TT], F32)
    r2s = sb("r2s", [128, TT], F32)
    ro1 = sb("ro1", [128, TT], BF16)
    ro2 = sb("ro2", [128, TT], BF16)
    u32s = sb("u32s", [128, 4, TT], F32)
    ubfs = sb("ubfs", [128, 4, TT], BF16)
    gsm = [sb(f"gsm{i}", [128, TT], F32) for i in range(2)]
    dma(wuq_s[:, :, :], wview("wuq"), wres("wuq"), ["wuq_s"])

    def rope(pa, pb, np_, outa, outb, ra, rb_):
        act(r1s[0:np_, :], ps[pa][0:np_, :], AF.Copy, [f"ps{pa}"], ["r1s"])
        act(r2s[0:np_, :], ps[pb][0:np_, :], AF.Copy, [f"ps{pb}"], ["r2s"])
        t0, t1 = tmp(), tmp()
        tt(tmpf[t0][0:np_, :], r1s[0:np_, :], cos_s[0:np_, :], ALU.mult, ["r1s", "cos"], [f"tmpf{t0}"])
        tt(tmpf[t1][0:np_, :], r2s[0:np_, :], sin_s[0:np_, :], ALU.mult, ["r2s", "sin"], [f"tmpf{t1}"])
        tt(outa, tmpf[t0][0:np_, :], tmpf[t1][0:np_, :], ALU.subtract, [f"tmpf{t0}", f"tmpf{t1}"], [ra])
        tt(tmpf[t0][0:np_, :], r2s[0:np_, :], cos_s[0:np_, :], ALU.mult, ["r2s", "cos"], [f"tmpf{t0}"])
        tt(tmpf[t1][0:np_, :], r1s[0:np_, :], sin_s[0:np_, :], ALU.mult, ["r1s", "sin"], [f"tmpf{t1}"])
        tt(outb, tmpf[t0][0:np_, :], tmpf[t1][0:np_, :], ALU.add, [f"tmpf{t0}", f"tmpf{t1}"], [rb_])

    def rmsnorm(src, sres, nch, dim, gcol0, dst, dres):
        act(qsq[:, 0:nch, :], src[:, 0:nch, :], AF.Square, [sres], ["qsq"])
        b = bank()
        for k in range(nch):
            mm(ps[b][:, :], ones_q[:, :], qsq[:, k, :], k == 0, k == nch - 1, ["ones_q", "qsq"], [f"ps{b}"])
        ts(rstd_s[:, :], ps[b][:, :], 1.0 / dim, RMS_EPS, ALU.mult, ALU.add, [f"ps{b}"], ["rstd"])
        rsqrt_inplace(rstd_s[:, :], "rstd")
        for k in range(nch):
            tt(src[:, k, :], src[:, k, :], rstd_s[:, :], ALU.mult, [sres, "rstd"], [sres])
            act(dst[:, k, :], src[:, k, :], AF.Copy, [sres, "qkg"], [dres], scale=qkg_s[:, gcol0 + k:gcol0 + k + 1])

    def proj(iw, cc, M=128, col0=None):
        b = bank()
        c_lo = cc * 128 if col0 is None else col0
        for k in range(NCH):
            mm(ps[b][0:M, :], wA[iw][:, k, c_lo:c_lo + M], xbf[:, k, :], k == 0, k == NCH - 1, [f"wA{iw}", "xbf"], [f"ps{b}"])
        return b

    for t in range(NT):
        c0, c1 = t * TT, (t + 1) * TT
        dma(xt[:, :, :], xv[:, :, c0:c1], [], XALL)
        cp(xbf[:, :, :], xt[:, :, :], XALL, ["xbf"], eng=G)
        ffn("w1a", "w3a", "w2a")
        layer_norm(0, LN_EPS / (ALPHA * ALPHA), True)
        dma(x1v[:, :, c0:c1], xt[:, :, :], XALL, [P.wr("X1")], q=G)
        dma(cos_s[:, :], cosT[:, c0:c1], [], ["cos"])
        dma(sin_s[:, :], sinT[:, c0:c1], [], ["sin"])
        iw = load_wA("win", 0, 384)
        for c in range(3):
            b = proj(iw, c)
            act(ql[:, c, :], ps[b][:, :], AF.Copy, [f"ps{b}"], ["ql"])
        rmsnorm(ql, "ql", 3, 384.0, 0, cqb, "cqb")
        for c in range(4):
            b = bank()
            for k in range(3):
                mm(ps[b][:, :], wuq_s[:, k, c * 128:(c + 1) * 128], cqb[:, k, :], k == 0, k == 2, ["wuq_s", "cqb"], [f"ps{b}"])
            act(qn[:, c, :], ps[b][:, :], AF.Copy, [f"ps{b}"], [f"qn{c}"])
            dma(QT[2 * c, 0:64, c0:c1], qn[0:64, c, :], [f"qn{c}"], [P.wr("QT")], q=G)
            dma(QT[2 * c + 1, 0:64, c0:c1], qn[64:128, c, :], [f"qn{c}"], [P.wr("QT")], q=G)
        b1, b2 = bank(), bank()
        for k in range(3):
            mm(ps[b1][:, :], wuq_s[:, k, 512:640], cqb[:, k, :], k == 0, k == 2, ["wuq_s", "cqb"], [f"ps{b1}"])
        for k in range(3):
            mm(ps[b2][:, :], wuq_s[:, k, 640:768], cqb[:, k, :], k == 0, k == 2, ["wuq_s", "cqb"], [f"ps{b2}"])
        rope(b1, b2, 128, ro1[:, :], ro2[:, :], "ro1", "ro2")
        for h in range(NH):
            dma(QT[h, 64:80, c0:c1], ro1[h * 16:(h + 1) * 16, :], ["ro1"], [P.wr("QT")], q=G)
            dma(QT[h, 80:96, c0:c1], ro2[h * 16:(h + 1) * 16, :], ["ro2"], [P.wr("QT")], q=G)
        iw = load_wA("win", 384, 672)
        for c in range(2):
            b = proj(iw, c)
            act(ql[:, c, :], ps[b][:, :], AF.Copy, [f"ps{b}"], ["ql"])
        b1 = proj(iw, 0, M=16, col0=256)
        b2 = proj(iw, 0, M=16, col0=272)
        rmsnorm(ql, "ql", 2, 256.0, 3, cqb, "cqb")
        CKV = CKVs[c0 // Ls]
        s0, s1 = c0 % Ls, c0 % Ls + TT
        for k in range(2):
            dma(CKV[k * 128:(k + 1) * 128, s0:s1], cqb[:, k, :], ["cqb"], [P.wr(f"CKV{c0 // Ls}")], q=G)
        rope(b1, b2, 16, ro1[0:16, :], ro2[0:16, :], "ro1", "ro2")
        dma(CKV[256:272, s0:s1], ro1[0:16, :], ["ro1"], [P.wr(f"CKV{c0 // Ls}")], q=G)
        dma(CKV[272:288, s0:s1], ro2[0:16, :], ["ro2"], [P.wr(f"CKV{c0 // Ls}")], q=G)
        iw = load_wA("win", 672, 1184)
        for c in range(4):
            b = proj(iw, c)
            act(u32s[:, c, :], ps[b][:, :], AF.Copy, [f"ps{b}"], ["u32s"])
        cp(ubfs[:, :, :], u32s[:, :, :], ["u32s"], ["ubfs"], eng=G)
        dma(U32.ap().rearrange("c p n -> p c n")[:, :, c0:c1], u32s[:, :, :], ["u32s"], [P.wr("U32")], q=G)
        dma(UBF.ap().rearrange("c p n -> p c n")[:, :, c0:c1], ubfs[:, :, :], ["ubfs"], [P.wr("UBF")], q=G)
        gi_ = 0
        for gname, GD, base in (("G0", GA, 1184), ("G1", GB, 2208)):
            for half in range(2):
                iw = load_wA("win", base + half * 512, base + (half + 1) * 512)
                for cc in range(4):
                    c = half * 4 + cc
                    b = proj(iw, cc)
                    gb_ = gi_ % 2
                    gi_ += 1
                    act(gsm[gb_][:, :], ps[b][:, :], AF.Sigmoid, [f"ps{b}"], [f"gsm{gb_}"])
                    dma(GD[c, :, c0:c1], gsm[gb_][:, :], [f"gsm{gb_}"], [P.wr(gname)], q=G)
    P.barrier()
    pst.close()
    if _os.environ.get("STOP") == "A":
        P.emit(P.rd("X1"), es); es.close(); return nc

    for i in range(NSPL):
        P.add(G, lambda i=i: nc.gpsimd.collective_compute("AllGather", ALU.bypass, replica_groups=[[0, 1], [2, 3], [4, 5], [6, 7]],
                                                           ins=[CKVs[i].ap().opt()], outs=[CKVALLs[i].ap().opt()]),
              P.rd(f"CKV{i}"), [f"CKVALL{i}"])

    if _os.environ.get("STOP") == "X":
        P.emit([f"CKVALL{i}" for i in range(NSPL)], es); es.close(); return nc
    pst = ExitStack()
    cur[0] = pst
    bbR = [sb(f"bbR{d}", [128, 2048], BF16) for d in range(2)]
    bbI = [sb(f"bbI{d}", [128, 2048], BF16) for d in range(2)]
    ccR = [sb(f"ccR{d}", [128, 2048], BF16) for d in range(2)]
    ccI = [sb(f"ccI{d}", [128, 2048], BF16) for d in range(2)]
    pwA = [[sb(f"pwA{d}{l}", [128, 16, 17], F32) for l in range(3)] for d in range(2)]
    pwB = [[sb(f"pwB{d}{l}", [128, 16, 17], F32) for l in range(3)] for d in range(2)]
    pwN = [[sb(f"pwN{d}{l}", [128, 16, 17], F32) for l in range(3)] for d in range(2)]
    small = [sb(f"sm{i}", [128, 16], F32) for i in range(8)]
    zero_s = sb("zero_s", [128, 32], F32)
    finR = sb("finR", [128, 32], F32)
    gat = sb("gat", [128, 2, 32], F32)
    iniS = sb("iniS", [128, 32], F32)
    P.add(V, lambda: nc.vector.memset(zero_s[:, :], 0.0), [], ["zero_s"])
    pst2 = ExitStack()
    cur[0] = pst2
    pt = [sb(f"pt{i}", [128, 2048], F32) for i in range(7)]

    I32 = mybir.dt.int32
    isml = sb("isml", [128, 16], I32)
    ibig = sb("ibig", [128, 2048], I32)

    def sincos(zi, zres, so, co, res_s, res_c, scratch, sres, itile, ires):
        for shift, dst, dres in ((0.0, so, res_s), (0.5 * PI, co, res_c)):
            ts(scratch, zi, shift, 1.0 / (2 * PI), ALU.add, ALU.mult, [zres], [sres])
            cp(itile, scratch, [sres], [ires])
            cp(scratch, itile, [ires], [sres])
            ts(scratch, scratch, -2 * PI, None, ALU.mult, None, [sres], [sres])
            ts(dst, zi, shift, None, ALU.add, None, [zres], [dres])
            tt(dst, dst, scratch, ALU.add, [dres, sres], [dres])
            ts(scratch, dst, PI, 2 * PI, ALU.is_gt, ALU.mult, [dres], [sres])
            tt(dst, dst, scratch, ALU.subtract, [dres, sres], [dres])
            ts(scratch, dst, -PI, 2 * PI, ALU.is_lt, ALU.mult, [dres], [sres])
            tt(dst, dst, scratch, ALU.add, [dres, sres], [dres])
            act(dst, dst, AF.Sin, [dres], [dres])

    for d in range(2):
        lre, lim, ldt, zr, zi, mg, sn, cs = [small[i][:, :] for i in range(8)]
        dma(lre, sst[d, 0, :, :], [], ["sm0"])
        dma(lim, sst[d, 1, :, :], [], ["sm1"])
        dma(ldt, sst[d, 2, :, :], [], ["sm2"])
        act(ldt, ldt, AF.Exp, ["sm2"], ["sm2"])
        tt(zr, lre, ldt, ALU.mult, ["sm0", "sm2"], ["sm3"])
        tt(zi, lim, ldt, ALU.mult, ["sm1", "sm2"], ["sm4"])
        act(mg, zr, AF.Exp, ["sm3"], ["sm5"])
        sincos(zi, "sm4", sn, cs, "sm6", "sm7", zr, "sm3", isml[:, :], "isml")
        for l in range(3):
            A, B, N = pwA[d][l], pwB[d][l], pwN[d][l]
            rA, rB, rN = f"pwA{d}{l}", f"pwB{d}{l}", f"pwN{d}{l}"
            P.add(V, lambda A=A: nc.vector.memset(A[:, :, 0:1], 1.0), [], [rA])
            P.add(V, lambda B=B: nc.vector.memset(B[:, :, 0:1], 0.0), [], [rB])
            if l == 0:
                tt(A[:, :, 1], mg, cs, ALU.mult, ["sm5", "sm7"], [rA])
                tt(B[:, :, 1], mg, sn, ALU.mult, ["sm5", "sm6"], [rB])
            else:
                cp(A[:, :, 1], pwA[d][l - 1][:, :, 16], [f"pwA{d}{l-1}"], [rA])
                cp(B[:, :, 1], pwB[d][l - 1][:, :, 16], [f"pwB{d}{l-1}"], [rB])
            for j in range(2, 17):
                tt(zr, A[:, :, j - 1], A[:, :, 1], ALU.mult, [rA], ["sm3"])
                tt(zi, B[:, :, j - 1], B[:, :, 1], ALU.mult, [rB], ["sm4"])
                tt(A[:, :, j], zr, zi, ALU.subtract, ["sm3", "sm4"], [rA])
                tt(zr, A[:, :, j - 1], B[:, :, 1], ALU.mult, [rA, rB], ["sm3"])
                tt(zi, B[:, :, j - 1], A[:, :, 1], ALU.mult, [rA, rB], ["sm4"])
                tt(B[:, :, j], zr, zi, ALU.add, ["sm3", "sm4"], [rB])
            ts(N[:, :, :], B[:, :, :], -1.0, None, ALU.mult, None, [rB], [rN])
        LR, LI, DT, t0, t1, t2, t3 = [pt[i][:, :] for i in range(7)]
        dma(LR, ssr[d, 0, :, :], [], ["pt0"])
        dma(LI, ssr[d, 1, :, :], [], ["pt1"])
        dma(DT, ssr[d, 2, :, :], [], ["pt2"])
        act(DT, DT, AF.Exp, ["pt2"], ["pt2"])
        tt(t0, LR, DT, ALU.mult, ["pt0", "pt2"], ["pt3"])
        tt(t1, LI, DT, ALU.mult, ["pt1", "pt2"], ["pt4"])
        act(t0, t0, AF.Exp, ["pt3"], ["pt3"])
        sincos(t1, "pt4", t2, t3, "pt5", "pt6", DT, "pt2", ibig[:, :], "ibig")
        tt(t2, t2, t0, ALU.mult, ["pt5", "pt3"], ["pt5"])
        tt(t3, t3, t0, ALU.mult, ["pt6", "pt3"], ["pt6"])
        ts(t3, t3, -1.0, None, ALU.add, None, ["pt6"], ["pt6"])
        tt(t0, LR, LR, ALU.mult, ["pt0"], ["pt3"])
        tt(t1, LI, LI, ALU.mult, ["pt1"], ["pt4"])
        tt(t0, t0, t1, ALU.add, ["pt3", "pt4"], ["pt3"])
        P.add(V, lambda t0=t0: nc.vector.reciprocal(out=t0, in_=t0), ["pt3"], ["pt3"])
        tt(t1, t3, LR, ALU.mult, ["pt6", "pt0"], ["pt4"])
        tt(DT, t2, LI, ALU.mult, ["pt5", "pt1"], ["pt2"])
        tt(t1, t1, DT, ALU.add, ["pt4", "pt2"], ["pt4"])
        tt(t1, t1, t0, ALU.mult, ["pt4", "pt3"], ["pt4"])
        tt(DT, t2, LR, ALU.mult, ["pt5", "pt0"], ["pt2"])
        tt(LR, t3, LI, ALU.mult, ["pt6", "pt1"], ["pt0"])
        tt(DT, DT, LR, ALU.subtract, ["pt2", "pt0"], ["pt2"])
        tt(DT, DT, t0, ALU.mult, ["pt2", "pt3"], ["pt2"])
        dma(LR, ssb[d, 0, :, :], [], ["pt0"])
        dma(LI, ssb[d, 1, :, :], [], ["pt1"])
        tt(t0, t1, LR, ALU.mult, ["pt4", "pt0"], ["pt3"])
        tt(t2, DT, LI, ALU.mult, ["pt2", "pt1"], ["pt5"])
        tt(bbR[d][:, :], t0, t2, ALU.subtract, ["pt3", "pt5"], [f"bbR{d}"])
        tt(t0, t1, LI, ALU.mult, ["pt4", "pt1"], ["pt3"])
        tt(t2, DT, LR, ALU.mult, ["pt2", "pt0"], ["pt5"])
        tt(bbI[d][:, :], t0, t2, ALU.add, ["pt3", "pt5"], [f"bbI{d}"])
        dma(t3, ssc[d, 0, :, :], [], ["pt6"])
        cp(ccR[d][:, :], t3, ["pt6"], [f"ccR{d}"])
        dma(t3, ssc[d, 1, :, :], [], ["pt6"])
        ts(ccI[d][:, :], t3, -1.0, None, ALU.mult, None, ["pt6"], [f"ccI{d}"])
    P.barrier()
    pst2.close()
    cur[0] = pst
    Rt = sb("Rst", [128, L], F32)
    It = sb("Ist", [128, L], F32)
    Rbf = sb("Rbf", [128, L], BF16)
    Ibf = sb("Ibf", [128, L], BF16)
    ubig = [sb(f"ubig{i}", [128, L], BF16) for i in range(2)]
    yacc = sb("yacc", [128, L], F32)
    e2R = sb("e2R", [128, 16, T3], F32)
    e2I = sb("e2I", [128, 16, T3], F32)
    e3R = sb("e3R", [128, T3], F32)
    e3I = sb("e3I", [128, T3], F32)
    x2pR = sb("x2pR", [128, T3], F32)
    x2pI = sb("x2pI", [128, T3], F32)
    x1pR = sb("x1pR", [128, 16, T3], F32)
    x1pI = sb("x1pI", [128, 16, T3], F32)
    NKC = NK // 128
    ckt = [sb(f"ckt{i}", [128, 2, 512], BF16) for i in range(3)]
    Kt = sb("Kt", [96, NK], BF16)
    Va = sb("Va", [128, NKC, 128], BF16)
    qts = sb("qts", [96, L], BF16)
    wuk_s = sb("wuk_s", [128, 2, 512], BF16)
    wuv_s = sb("wuv_s", [128, 2, 512], BF16)
    Pb = [sb(f"Pb{i}", [128, 1024], BF16) for i in range(2)]
    rcs = sb("rcs", [64, TT], F32)
    ob = sb("ob", [64, TT], BF16)

    b67 = [0]

    def bank67():
        b67[0] += 1
        return 6 + (b67[0] % 2)

    def cmul_acc(oR, oI, pR, pI, a, b, nb, res_o, res_p, extra_reads=()):
        ex = list(extra_reads)
        roR, roI, rpR, rpI = res_o + "R", res_o + "I", res_p + "R", res_p + "I"
        stt(oR, pR, a, oR, ALU.mult, ALU.add, [roR, rpR] + ex, [roR])
        stt(oI, pR, b, oI, ALU.mult, ALU.add, [roI, rpR] + ex, [roI])
        stt(oR, pI, nb, oR, ALU.mult, ALU.add, [roR, rpI] + ex, [roR])
        stt(oI, pI, a, oI, ALU.mult, ALU.add, [roI, rpI] + ex, [roI])

    def blk(tn, j):
        return tn[:, j * K1:(j + 1) * K1]

    def blkres(t, ri):
        j0_, j1_ = (t * TT) // K1, ((t + 1) * TT - 1) // K1
        return [f"b{j}{ri}" for j in range(j0_, j1_ + 1)]

    def blk3(tn, j):
        return tn[:, j * K1:(j + 1) * K1].rearrange("p (a b) -> p a b", b=T3)

    def s5_main():
        for pas in range(2):
            d = pas
            rev = (pas == 1)

            def ix(i, n, rev=rev):
                return (n - 1 - i) if rev else i

            ini = zero_s if pas == 0 else iniS
            ini_res = "zero_s" if pas == 0 else "iniS"
            if pas == 1:
                dma(SX[:, :], finR[:, :], ["finR"], ["SX"], q=G)
                P.add(G, lambda: nc.gpsimd.collective_compute("AllGather", ALU.bypass, replica_groups=[[0, 1], [2, 3], [4, 5], [6, 7]],
                                                               ins=[SX.ap().opt()], outs=[SXALL.ap().opt()]),
                      ["SX"], ["SXALL"])
                dma(gat[:, :, :], SXALL.ap().rearrange("(r p) n -> p r n", p=128), ["SXALL"], ["gat"])
                ts(iniS[:, :], gat[:, 0, :], sel_s[:, 0:1], None, ALU.mult, None, ["gat", "sel"], ["iniS"])
                stt(iniS[:, :], gat[:, 1, :], sel_s[:, 1:2], iniS[:, :], ALU.mult, ALU.add, ["gat", "sel", "iniS"], ["iniS"])
            for c in range(4):
                ub = ubig[c % 2]
                ubr = f"ubig{c % 2}"
                dma(ub[:, :], UBF[c, :, :], P.rd("UBF"), [ubr])
                if pas == 0:
                    dma(yacc[:, :], U32[c, :, :], P.rd("U32"), ["yacc"])
                    ts(yacc[:, :], yacc[:, :], ssd_s[:, c:c + 1], None, ALU.mult, None, ["yacc", "ssd"], ["yacc"])
                else:
                    dma(yacc[:, :], YS[c, :, :], [f"YS{c}"], ["yacc"])
                for qq in range(4):
                    q = c * 4 + qq
                    A0, B0, N0 = pwA[d][0], pwB[d][0], pwN[d][0]
                    A1, B1, N1 = pwA[d][1], pwB[d][1], pwN[d][1]
                    A2, B2, N2 = pwA[d][2], pwB[d][2], pwN[d][2]
                    pres = [f"pw{x}{d}{l}" for x in "ABN" for l in range(3)]
                    for t in range(NT):
                        bR, bI = bank67(), bank67()
                        mm(ps[bR][:, :], bbR[d][:, q * 128:(q + 1) * 128], ub[:, t * TT:(t + 1) * TT], True, True, [f"bbR{d}", ubr], [f"ps{bR}"])
                        mm(ps[bI][:, :], bbI[d][:, q * 128:(q + 1) * 128], ub[:, t * TT:(t + 1) * TT], True, True, [f"bbI{d}", ubr], [f"ps{bI}"])
                        act(Rt[:, t * TT:(t + 1) * TT], ps[bR][:, :], AF.Copy, [f"ps{bR}"], blkres(t, "R"))
                        act(It[:, t * TT:(t + 1) * TT], ps[bI][:, :], AF.Copy, [f"ps{bI}"], blkres(t, "I"))
                    yield 12
                    for j in range(1, 16):
                        jc, jp = ix(j, 16), ix(j - 1, 16)
                        cmul_acc(blk(Rt, jc), blk(It, jc), blk(Rt, jp), blk(It, jp),
                                 A0[:, q, 1:2], B0[:, q, 1:2], N0[:, q, 1:2], f"b{jc}", f"b{jp}", pres)
                        if j % 5 == 0:
                            yield 8
                    jl = ix(15, 16)
                    cp(e2R[:, :, :], blk3(Rt, jl), [f"b{jl}R"], [f"e2_{j}R" for j in range(16)])
                    cp(e2I[:, :, :], blk3(It, jl), [f"b{jl}I"], [f"e2_{j}I" for j in range(16)])
                    for j in range(1, 16):
                        jc, jp = ix(j, 16), ix(j - 1, 16)
                        cmul_acc(e2R[:, jc, :], e2I[:, jc, :], e2R[:, jp, :], e2I[:, jp, :],
                                 A1[:, q, 1:2], B1[:, q, 1:2], N1[:, q, 1:2], f"e2_{jc}", f"e2_{jp}", pres)
                    yield 8
                    cp(e3R[:, :], e2R[:, jl, :], [f"e2_{jl}R"], [f"e3_{k}R" for k in range(T3)])
                    cp(e3I[:, :], e2I[:, jl, :], [f"e2_{jl}I"], [f"e3_{k}I" for k in range(T3)])
                    for k in range(T3):
                        kc = ix(k, T3)
                        if k == 0:
                            pR, pI = ini[:, q:q + 1], ini[:, 16 + q:16 + q + 1]
                            rp_ = "ini"
                        else:
                            kp = ix(k - 1, T3)
                            pR, pI = e3R[:, kp:kp + 1], e3I[:, kp:kp + 1]
                            rp_ = f"e3_{kp}"
                        cmul_acc(e3R[:, kc:kc + 1], e3I[:, kc:kc + 1], pR, pI, A2[:, q, 1:2], B2[:, q, 1:2], N2[:, q, 1:2],
                                 f"e3_{kc}", rp_, pres + [ini_res])
                    yield 8
                    kl = ix(T3 - 1, T3)
                    if pas == 0:
                        cp(finR[:, q:q + 1], e3R[:, kl:kl + 1], [f"e3_{kl}R"], ["finR"])
                        cp(finR[:, 16 + q:16 + q + 1], e3I[:, kl:kl + 1], [f"e3_{kl}I"], ["finR"])
                    k0 = ix(0, T3)
                    e3allR = [f"e3_{k}R" for k in range(T3)]
                    e3allI = [f"e3_{k}I" for k in range(T3)]
                    cp(x2pR[:, k0:k0 + 1], ini[:, q:q + 1], [ini_res], ["x2pR"])
                    cp(x2pI[:, k0:k0 + 1], ini[:, 16 + q:16 + q + 1], [ini_res], ["x2pI"])
                    if T3 > 1:
                        if not rev:
                            cp(x2pR[:, 1:T3], e3R[:, 0:T3 - 1], e3allR, ["x2pR"])
                            cp(x2pI[:, 1:T3], e3I[:, 0:T3 - 1], e3allI, ["x2pI"])
                        else:
                            cp(x2pR[:, 0:T3 - 1], e3R[:, 1:T3], e3allR, ["x2pR"])
                            cp(x2pI[:, 0:T3 - 1], e3I[:, 1:T3], e3allI, ["x2pI"])
                    for j in range(16):
                        jc = ix(j, 16)
                        cmul_acc(e2R[:, jc, :], e2I[:, jc, :], x2pR[:, :], x2pI[:, :],
                                 A1[:, q, j + 1:j + 2], B1[:, q, j + 1:j + 2], N1[:, q, j + 1:j + 2], f"e2_{jc}", "x2p", pres)
                    yield 8
                    j0 = ix(0, 16)
                    e2allR = [f"e2_{j}R" for j in range(16)]
                    e2allI = [f"e2_{j}I" for j in range(16)]
                    cp(x1pR[:, j0, :], x2pR[:, :], ["x2pR"], ["x1pR"])
                    cp(x1pI[:, j0, :], x2pI[:, :], ["x2pI"], ["x1pI"])
                    if not rev:
                        cp(x1pR[:, 1:16, :], e2R[:, 0:15, :], e2allR, ["x1pR"])
                        cp(x1pI[:, 1:16, :], e2I[:, 0:15, :], e2allI, ["x1pI"])
                    else:
                        cp(x1pR[:, 0:15, :], e2R[:, 1:16, :], e2allR, ["x1pR"])
                        cp(x1pI[:, 0:15, :], e2I[:, 1:16, :], e2allI, ["x1pI"])
                    for j in range(16):
                        jc = ix(j, 16)
                        cmul_acc(blk3(Rt, jc), blk3(It, jc), x1pR[:, :, :], x1pI[:, :, :],
                                 A0[:, q, j + 1:j + 2], B0[:, q, j + 1:j + 2], N0[:, q, j + 1:j + 2], f"b{jc}", "x1p", pres)
                        if j % 5 == 4:
                            yield 8
                    act(Rbf[:, :], Rt[:, :], AF.Copy, [f"b{j}R" for j in range(16)], ["Rbf"])
                    cp(Ibf[:, :], It[:, :], [f"b{j}I" for j in range(16)], ["Ibf"], eng=G)
                    yield 6
                    for t in range(NT):
                        b = bank67()
                        mm(ps[b][:, :], ccR[d][:, q * 128:(q + 1) * 128], Rbf[:, t * TT:(t + 1) * TT], True, False, [f"ccR{d}", "Rbf"], [f"ps{b}"])
                        mm(ps[b][:, :], ccI[d][:, q * 128:(q + 1) * 128], Ibf[:, t * TT:(t + 1) * TT], False, True, [f"ccI{d}", "Ibf"], [f"ps{b}"])
                        tt(yacc[:, t * TT:(t + 1) * TT], yacc[:, t * TT:(t + 1) * TT], ps[b][:, :], ALU.add, ["yacc", f"ps{b}"], ["yacc"])
                    yield 6
                dma(YS[c, :, :], yacc[:, :], ["yacc"], [f"YS{c}"], q=G)

    scale = 96.0 ** -0.5

    def attn_main():
        for i in range(NSPL):
            ckall = CKVALLs[i].ap().rearrange("(r f) n -> r f n", r=2)
            for r in range(2):
                o0 = (i * 2 + r) * Ls
                dma(Kt[64:96, o0:o0 + Ls], ckall[r, 256:288, :], [f"CKVALL{i}"], [P.wr("Ktr")])
        dma(wuk_s[:, :, :], wview("wuk"), wres("wuk"), ["wuk_s"])
        dma(wuv_s[:, :, :], wview("wuv"), wres("wuv"), ["wuv_s"])
        P.add(G, lambda: nc.gpsimd.memset(Va[:, :, :], 1.0), [], ["Va"])
        yield 2
        cki = 0
        for h in range(NH):
            for kt in range(NK // 512):
                blkno = (kt * 512) // Ls
                i, r = blkno // 2, blkno % 2
                coff = kt * 512 - blkno * Ls
                ci = cki % 3
                cki += 1
                src = CKVALLs[i].ap()[r * 288:r * 288 + 256, coff:coff + 512].rearrange("(k p) n -> p k n", p=128)
                dma(ckt[ci][:, :, :], src, [f"CKVALL{i}"], [f"ckt{ci}"])
                b = bank67()
                for k in range(2):
                    mm(ps[b][0:64, :], wuk_s[:, k, h * 64:(h + 1) * 64], ckt[ci][:, k, :], k == 0, k == 1, ["wuk_s", f"ckt{ci}"], [f"ps{b}"])
                cp(Kt[0:64, kt * 512:(kt + 1) * 512], ps[b][0:64, :], [f"ps{b}"], ["Kt"])
                b = bank67()
                for j in range(4):
                    for k in range(2):
                        mm(ps[b][:, j * 64:(j + 1) * 64], ckt[ci][:, k, j * 128:(j + 1) * 128], wuv_s[:, k, h * 64:(h + 1) * 64], k == 0, k == 1, ["wuv_s", f"ckt{ci}"], [f"ps{b}"])
                cp(Va[:, kt * 4:(kt + 1) * 4, 0:64], ps[b][:, 0:256].rearrange("p (a b) -> p a b", b=64), [f"ps{b}"], ["Va"])
                if kt % 4 == 3:
                    yield 3
            dma(qts[:, :], QT[h, :, :], P.rd("QT"), ["qts"])
            for qt_ in range(L // 512):
                bo = 4 + (qt_ % 2)
                q0 = qt_ * 512
                npair = NKC // 2

                def s_mm(i):
                    p = i % 2
                    for j in range(2):
                        kc = 2 * i + j
                        mm(ps[2 * p + j][:, :], Kt[0:96, kc * 128:(kc + 1) * 128], qts[0:96, q0:q0 + 512], True, True, ["Kt", "qts"] + P.rd("Ktr"), [f"ps{2*p+j}"])

                def s_exp(i):
                    p = i % 2
                    for j in range(2):
                        act(Pb[p][:, j * 512:(j + 1) * 512], ps[2 * p + j][:, :], AF.Exp, [f"ps{2*p+j}"], [f"Pb{p}"], scale=scale)

                def pv(i):
                    p = i % 2
                    for j in range(2):
                        kc = 2 * i + j
                        mm(ps[bo][:, :], Va[:, kc, :], Pb[p][:, j * 512:(j + 1) * 512], kc == 0, kc == NKC - 1, ["Va", f"Pb{p}"], [f"ps{bo}"])

                s_mm(0)
                s_exp(0)
                for i in range(npair):
                    if i + 1 < npair:
                        s_mm(i + 1)
                        s_exp(i + 1)
                    pv(i)
                    if i % 8 == 7:
                        yield 10
                P.add(V, lambda bo=bo: nc.vector.reciprocal(out=rcs[0:64, :], in_=ps[bo][64:128, :]), [f"ps{bo}"], ["rcs"])
                tt(ob[:, :], ps[bo][0:64, :], rcs[:, :], ALU.mult, [f"ps{bo}", "rcs"], ["ob"])
                dma(OT[h // 2, (h % 2) * 64:(h % 2) * 64 + 64, q0:q0 + 512], ob[:, :], ["ob"], [P.wr("OT")], q=G)
                yield 1

    gens = [s5_main(), attn_main()]
    tacc = [0.0, 0.0]
    alive = [True, True]
    if _os.environ.get("NOOVERLAP"):
        for g_ in gens:
            for _ in g_:
                pass
    else:
        while any(alive):
            gi = min((i for i in range(2) if alive[i]), key=lambda i: tacc[i])
            try:
                tacc[gi] += next(gens[gi])
            except StopIteration:
                alive[gi] = False
    P.barrier()
    pst.close()

    if _os.environ.get("STOP") == "B":
        P.emit(P.rd("OT"), es); es.close(); return nc
    pst = ExitStack()
    cur[0] = pst
    alloc_common()
    xt, xbf, tmpf, wA = CM["xin"], CM["xbf"], CM["tmpf"], CM["wA"]
    woa_s = sb("woa_s", [128, 4, D], BF16)
    wos_s = sb("wos_s", [128, 4, D], BF16)
    wglu_s = sb("wglu_s", [128, 4, 512], BF16)
    wpp_s = sb("wpp_s", [128, 2, D], BF16)
    dma(woa_s[:, :, :], wview("woa"), wres("woa"), ["woa_s"])
    dma(wos_s[:, :, :], wview("wos"), wres("wos"), ["wos_s"])
    dma(wglu_s[:, :, :], wview("wglu"), wres("wglu"), ["wglu_s"])
    dma(wpp_s[:, :, :], wview("wpp"), wres("wpp"), ["wpp_s"])
    ots = sb("ots", [128, 4, TT], BF16)
    ysf = sb("ysf", [128, 4, TT], F32)
    zf = sb("zf", [128, 4, TT], F32)
    zbf4 = sb("zbf4", [128, 4, TT], BF16)
    zzb = sb("zzb", [128, 4, TT], BF16)
    gaf = [sb(f"gaf{i}", [128, TT], F32) for i in range(2)]
    gbf = [sb(f"gbf{i}", [128, TT], F32) for i in range(2)]
    mrb = sb("mrb", [128, NCH, TT], BF16)
    pf = sb("pf", [128, 2, TT], F32)
    pbf = sb("pbf", [128, 2, TT], BF16)
    outv = out.ap().rearrange("c p n -> p c n")
    pv_ = pT.ap().rearrange("c p n -> p c n")
    GC = 1.5957691216057308
    ysall = [f"YS{c}" for c in range(4)]
    for t in range(NT):
        c0, c1 = t * TT, (t + 1) * TT
        dma(xt[:, :, :], x1v[:, :, c0:c1], P.rd("X1"), XALL)
        dma(ots[:, :, :], OT.ap().rearrange("c p n -> p c n")[:, :, c0:c1], P.rd("OT"), ["ots"])
        dma(ysf[:, :, :], YS.ap().rearrange("c p n -> p c n")[:, :, c0:c1], ysall, ["ysf"])
        dma(pf[:, :, :], pv_[:, :, c0:c1], [], ["pf"])
        cp(pbf[:, :, :], pf[:, :, :], ["pf"], ["pbf"], eng=G)
        tt(zf[:, :, :], ysf[:, :, :], ysf[:, :, :], ALU.mult, ["ysf"], ["zf"])
        ts(zf[:, :, :], zf[:, :, :], 0.044715, 1.0, ALU.mult, ALU.add, ["zf"], ["zf"])
        tt(zf[:, :, :], zf[:, :, :], ysf[:, :, :], ALU.mult, ["zf", "ysf"], ["zf"])
        act(zf[:, :, :], zf[:, :, :], AF.Sigmoid, ["zf"], ["zf"], scale=GC)
        tt(zf[:, :, :], zf[:, :, :], ysf[:, :, :], ALU.mult, ["zf", "ysf"], ["zf"])
        cp(zbf4[:, :, :], zf[:, :, :], ["zf"], ["zbf4"], eng=G)
        for c in range(4):
            b = bank()
            for k in range(4):
                mm(ps[b][:, :], wglu_s[:, k, c * 128:(c + 1) * 128], zbf4[:, k, :], k == 0, k == 3, ["wglu_s", "zbf4"], [f"ps{b}"])
            ti = tmp()
            act(tmpf[ti][:, :], ps[b][:, :], AF.Sigmoid, [f"ps{b}"], [f"tmpf{ti}"])
            tt(zzb[:, c, :], zf[:, c, :], tmpf[ti][:, :], ALU.mult, ["zf", f"tmpf{ti}"], ["zzb"])
        for o in range(NCH):
            gi_ = o % 2
            dma(gaf[gi_][:, :], GA[o, :, c0:c1], P.rd("G0"), [f"gaf{gi_}"])
            dma(gbf[gi_][:, :], GB[o, :, c0:c1], P.rd("G1"), [f"gbf{gi_}"])
            ba, bb = bank(), bank()
            for k in range(4):
                mm(ps[ba][:, :], woa_s[:, k, o * 128:(o + 1) * 128], ots[:, k, :], k == 0, k == 3, ["woa_s", "ots"], [f"ps{ba}"])
            for k in range(4):
                mm(ps[bb][:, :], wos_s[:, k, o * 128:(o + 1) * 128], zzb[:, k, :], k == 0, k == 3, ["wos_s", "zzb"], [f"ps{bb}"])
            tt(gaf[gi_][:, :], gaf[gi_][:, :], ps[ba][:, :], ALU.mult, [f"gaf{gi_}", f"ps{ba}"], [f"gaf{gi_}"])
            tt(gbf[gi_][:, :], gbf[gi_][:, :], ps[bb][:, :], ALU.mult, [f"gbf{gi_}", f"ps{bb}"], [f"gbf{gi_}"])
            tt(mrb[:, o, :], gaf[gi_][:, :], gbf[gi_][:, :], ALU.add, [f"gaf{gi_}", f"gbf{gi_}"], ["mrb"], eng=G)
        for half in range(2):
            iw = load_wA("wout", half * 512, (half + 1) * 512)
            for cc in range(4):
                o = half * 4 + cc
                b = bank()
                for k in range(NCH):
                    mm(ps[b][:, :], wA[iw][:, k, cc * 128:(cc + 1) * 128], mrb[:, k, :], k == 0, k == NCH - 1, [f"wA{iw}", "mrb"], [f"ps{b}"])
                stt(xt[:, o, :], ps[b][:, :], 1.0 / ALPHA, xt[:, o, :], ALU.mult, ALU.add, [f"ps{b}", f"xin{o}"], [f"xin{o}"])
        layer_norm(1, LN_EPS / (ALPHA * ALPHA), True)
        ffn("w1b", "w3b", "w2b")
        layer_norm(2, LN_EPS / (ALPHA * ALPHA), True)
        for half in range(2):
            iw = load_wA("wpg", half * 512, (half + 1) * 512)
            for cc in range(4):
                o = half * 4 + cc
                bg, bp = bank(), bank()
                for k in range(NCH):
                    mm(ps[bg][:, :], wA[iw][:, k, cc * 128:(cc + 1) * 128], xbf[:, k, :], k == 0, k == NCH - 1, [f"wA{iw}", "xbf"], [f"ps{bg}"])
                for k in range(2):
                    mm(ps[bp][:, :], wpp_s[:, k, o * 128:(o + 1) * 128], pbf[:, k, :], k == 0, k == 1, ["wpp_s", "pbf"], [f"ps{bp}"])
                ti = tmp()
                act(tmpf[ti][:, :], ps[bg][:, :], AF.Sigmoid, [f"ps{bg}"], [f"tmpf{ti}"])
                tt(tmpf[ti][:, :], tmpf[ti][:, :], ps[bp][:, :], ALU.mult, [f"tmpf{ti}", f"ps{bp}"], [f"tmpf{ti}"])
                stt(xt[:, o, :], tmpf[ti][:, :], 1.0 / ALPHA, xt[:, o, :], ALU.mult, ALU.add, [f"tmpf{ti}", f"xin{o}"], [f"xin{o}"])
        layer_norm(3, LN_EPS / (ALPHA * ALPHA), False)
        dma(outv[:, :, c0:c1], xt[:, :, :], XALL, [P.wr("OUT")], q=G)

    P.emit(P.rd("OUT"), es)
    pst.close()
    es.close()
    return nc


def _perm(L):
    K1 = L // 16
    T3 = K1 // 16
    n = np.arange(L)
    j = n // K1
    j2 = (n % K1) // T3
    k1 = n % T3
    return (k1 * 16 + j2) * 16 + j


_NC_CACHE = {}


def kernel(**inp):
    B, S, _ = inp["x"].shape
    L = S // 2
    f32 = np.float32
    perm = _perm(L)
    if L not in _NC_CACHE:
        _NC_CACHE[L] = build(L)
    nc = _NC_CACHE[L]

    def chunks(a):
        return np.ascontiguousarray(a.reshape(a.shape[0] // 128, 128, a.shape[1]))

    def col(v):
        return np.ascontiguousarray(v.reshape(-1, 128).T)

    inv_freq = (10000.0 ** (-np.arange(0, 32, 2, dtype=np.float32) / 32)).astype(f32)
    W = {
        "w1a": inp["ffn1_w1"][0], "w3a": inp["ffn1_w3"][0], "w2a": inp["ffn1_w2"][0], "win": inp["w_in"][0],
        "wuk": inp["w_uk"][0].reshape(256, 512), "wuv": inp["w_uv"][0].reshape(256, 512),
        "woa": inp["w_o_attn"][0], "wglu": inp["w_glu"][0], "wos": inp["w_o_ssm"][0], "wout": inp["w_out"][0],
        "w1b": inp["ffn2_w1"][0], "w3b": inp["ffn2_w3"][0], "w2b": inp["ffn2_w2"][0],
        "wpp": inp["ple_w_proj"][0], "wpg": inp["ple_w_gate"][0],
    }
    wq = inp["w_uq"][0]
    W["wuq"] = np.concatenate([wq[:, :, 0:64].reshape(384, 512), wq[:, :, 64:80].reshape(384, 128), wq[:, :, 80:96].reshape(384, 128)], axis=1)
    W = {k: np.ascontiguousarray(v, dtype=f32) for k, v in W.items()}
    lnp = np.concatenate([col(inp[f"ln{i}_{gb}"][0]) for i in (1, 2, 3, 4) for gb in ("g", "b")], axis=1).astype(f32)
    qkg = np.concatenate([col(inp["q_norm_g"][0]), col(inp["kv_norm_g"][0])], axis=1).astype(f32)
    ssd = col(inp["ssm_d"][0].reshape(512)).astype(f32)

    def ssm_pack(sfx):
        lre, lim, ldt = inp["ssm_lam_re_" + sfx][0], inp["ssm_lam_im_" + sfx][0], inp["ssm_log_dt_" + sfx][0]
        bre, bim = inp["ssm_b_re_" + sfx][0], inp["ssm_b_im_" + sfx][0]
        cre, cim = inp["ssm_c_re_" + sfx][0], inp["ssm_c_im_" + sfx][0]
        ldtb = np.broadcast_to(ldt[:, None], (32, 64))
        st = np.zeros((3, 128, 16), f32)
        rp = np.zeros((3, 128, 2048), f32)
        bp = np.zeros((2, 128, 2048), f32)
        cpad = np.zeros((2, 128, 2048), f32)
        for q in range(16):
            for gs_ in range(2):
                g = 2 * q + gs_
                for i, a in enumerate((lre, lim, ldtb)):
                    st[i, gs_ * 64:(gs_ + 1) * 64, q] = a[g]
                    rp[i, :, q * 128 + gs_ * 64:q * 128 + (gs_ + 1) * 64] = a[g][None, :]
                r0 = (g % 8) * 16
                bp[0, r0:r0 + 16, q * 128 + gs_ * 64:q * 128 + (gs_ + 1) * 64] = bre[g].T
                bp[1, r0:r0 + 16, q * 128 + gs_ * 64:q * 128 + (gs_ + 1) * 64] = bim[g].T
                cpad[0, gs_ * 64:(gs_ + 1) * 64, q * 128 + r0:q * 128 + r0 + 16] = cre[g].T
                cpad[1, gs_ * 64:(gs_ + 1) * 64, q * 128 + r0:q * 128 + r0 + 16] = cim[g].T
        return st, rp, bp, cpad

    packs = {"f": ssm_pack("f"), "b": ssm_pack("b")}
    in_maps = []
    for c in range(8):
        b, half = c // 2, c % 2
        tl = perm if half == 0 else (L - 1 - perm)
        tg = half * L + tl
        m = dict(W)
        m["xT"] = chunks(np.ascontiguousarray(inp["x"][b][tg].T))
        m["pT"] = chunks(np.ascontiguousarray(inp["p"][0, b][tg].T))
        ang = tg.astype(f32)[None, :] * inv_freq[:, None]
        m["cosT"] = np.ascontiguousarray(np.tile(np.cos(ang).astype(f32), (8, 1)))
        m["sinT"] = np.ascontiguousarray(np.tile(np.sin(ang).astype(f32), (8, 1)))
        order = ("f", "b") if half == 0 else ("b", "f")
        m["sst"] = np.stack([packs[o][0] for o in order])
        m["ssr"] = np.stack([packs[o][1] for o in order])
        m["ssb"] = np.stack([packs[o][2] for o in order])
        m["ssc"] = np.stack([packs[o][3] for o in order])
        m["lnp"], m["qkg"], m["ssd"] = lnp, qkg, ssd
        sel = np.zeros((128, 2), f32)
        sel[:, 1 - half] = 1.0
        m["selp"] = sel
        in_maps.append(m)
    res = run_bass_kernel_spmd(nc, in_maps, core_ids=list(range(8)))
    outp = np.empty((B, S, D), f32)
    for c in range(8):
        b, half = c // 2, c % 2
        tl = perm if half == 0 else (L - 1 - perm)
        tg = half * L + tl
        o = res.results[c]["outT"].reshape(D, L)
        outp[b, tg, :] = o.T
    return outp
```

```python
import math
from contextlib import ExitStack
import numpy as np
import concourse.bass as bass
import concourse.mybir as mybir
from concourse.bass_utils import run_bass_kernel_spmd

F32 = mybir.dt.float32
BF16 = mybir.dt.bfloat16
ALU = mybir.AluOpType
AF = mybir.ActivationFunctionType

D = 1024
DFF = 2816
NCH = 8
HCH = 22
NH = 8
ALPHA = 2.0 ** 0.25
LN_EPS = 1e-5
RMS_EPS = 1e-6
TT = 512
PI = math.pi


import os as _os0
SAME_ENGINE_NOSYNC = set(_os0.environ.get("NOSYNC", "pe").split(","))


class Prog:
    CH = 1000
    NDS = 24

    def __init__(self, nc):
        self.nc = nc
        self.ops = []
        self.groups = {}
        self.eng = {"pe": nc.tensor, "act": nc.scalar, "dve": nc.vector, "pool": nc.gpsimd, "sp": nc.sync}

    def add(self, eng, fn, reads=(), writes=(), dma=False, inc=None):
        self.ops.append(dict(eng=eng, fn=fn, reads=tuple(reads), writes=tuple(writes), dma=dma, inc=inc, bar=False))

    def barrier(self):
        self.ops.append(dict(eng=None, fn=None, reads=(), writes=(), dma=False, inc=None, bar=True))

    def wr(self, grp):
        g = self.groups.setdefault(grp, [])
        nm = f"{grp}#{len(g)}"
        g.append(nm)
        return nm

    def rd(self, grp):
        return list(self.groups.get(grp, []))

    def emit(self, final_res, stack):
        ops = self.ops
        n = len(ops)
        last_w = {}
        readers = {}
        deps = [None] * n
        dma_ids = []
        last_eng = {}
        pend_bar = {}
        for i, o in enumerate(ops):
            if o["bar"]:
                bd = set(last_eng.values()) | set(dma_ids[-self.NDS:])
                pend_bar = {e: set(bd) for e in self.eng}
                deps[i] = []
                continue
            d = set()
            if o["eng"] in pend_bar:
                d |= pend_bar.pop(o["eng"])
            if not o["dma"]:
                last_eng[o["eng"]] = i
            for r in o["reads"]:
                if r in last_w:
                    d.add(last_w[r])
            for w in o["writes"]:
                if w in last_w:
                    d.add(last_w[w])
                for x in readers.get(w, ()):
                    d.add(x)
            if o["dma"]:
                if len(dma_ids) >= self.NDS:
                    d.add(dma_ids[len(dma_ids) - self.NDS])
                dma_ids.append(i)
            d.discard(i)
            keep = []
            for x in d:
                ox = ops[x]
                if (not ox["dma"]) and ox["eng"] == o["eng"] and o["eng"] in SAME_ENGINE_NOSYNC and not o["dma"]:
                    continue
                keep.append(x)
            deps[i] = sorted(keep)
            for r in o["reads"]:
                readers.setdefault(r, []).append(i)
            for w in o["writes"]:
                last_w[w] = i
                readers[w] = []
        fin = sorted({last_w[r] for r in final_res if r in last_w})
        needed = [False] * n
        for i in range(n):
            for x in deps[i]:
                needed[x] = True
        for x in fin:
            needed[x] = True
        cnt = {e: 0 for e in self.eng}
        dcnt = [0] * self.NDS
        sig = [None] * n
        nd = 0
        for i, o in enumerate(ops):
            if o["bar"]:
                continue
            if o["dma"]:
                k = nd % self.NDS
                nd += 1
                dcnt[k] += 1
                sig[i] = ("d", k, dcnt[k] * (o["inc"] or 16))
                if o["inc"]:
                    raise RuntimeError("custom inc unsupported")
            elif needed[i]:
                cnt[o["eng"]] += 1
                sig[i] = ("e", o["eng"], cnt[o["eng"]])
        sems = {}
        for e in self.eng:
            for c in range((cnt[e] + self.CH - 1) // self.CH + 1):
                sems[("e", e, c)] = stack.enter_context(self.nc.semaphore(f"s_{e}_{c}"))
        for k in range(self.NDS):
            sems[("d", k)] = stack.enter_context(self.nc.semaphore(f"s_d{k}"))
        known = {e: {} for e in self.eng}
        snap = [None] * n

        def key_of(s):
            return ("e", s[1]) if s[0] == "e" else ("d", s[1])

        def wait(e, x):
            s = sig[x]
            kk = key_of(s)
            kn = known[e]
            if kn.get(kk, 0) >= s[2]:
                return
            h = self.eng[e]
            if s[0] == "e":
                c = (s[2] - 1) // self.CH
                h.wait_ge(sems[("e", s[1], c)], (s[2] - 1) % self.CH + 1)
            else:
                h.wait_ge(sems[("d", s[1])], s[2])
            kn[kk] = s[2]
            sn = snap[x]
            if sn:
                for k2, v2 in sn.items():
                    if kn.get(k2, 0) < v2:
                        kn[k2] = v2

        for i, o in enumerate(ops):
            if o["bar"]:
                continue
            e = o["eng"]
            for x in deps[i]:
                wait(e, x)
            ins = o["fn"]()
            s = sig[i]
            if s is not None:
                if s[0] == "e":
                    c = (s[2] - 1) // self.CH
                    ins.then_inc(sems[("e", e, c)], 1)
                else:
                    ins.then_inc(sems[("d", s[1])], 16)
                snap[i] = dict(known[e])
        for x in fin:
            wait("sp", x)


def build(L):
    NT = L // TT
    NK = 2 * L
    K1 = L // 16
    T3 = K1 // 16
    assert L % 512 == 0 and K1 % 16 == 0 and T3 >= 1
    nc = bass.Bass("TRN2", target_bir_lowering=False)
    P = Prog(nc)
    es = ExitStack()
    cur = [es]

    def din(name, shape, dt=F32):
        return nc.dram_tensor(name, list(shape), dt, kind="ExternalInput")

    def dscr(name, shape, dt):
        return nc.dram_tensor(name, list(shape), dt)

    sbn = [0]

    def sb(name, shape, dt):
        sbn[0] += 1
        return cur[0].enter_context(nc.sbuf_tensor(f"{name}_{sbn[0]}", list(shape), dt))

    xT = din("xT", [NCH, 128, L])
    pT = din("pT", [2, 128, L])
    cosT = din("cosT", [128, L])
    sinT = din("sinT", [128, L])
    wnames = {
        "w1a": (D, DFF), "w3a": (D, DFF), "w2a": (DFF, D), "win": (D, 3232), "wuq": (384, 768),
        "wuk": (256, 512), "wuv": (256, 512), "woa": (512, D), "wglu": (512, 512), "wos": (512, D),
        "wout": (D, D), "w1b": (D, DFF), "w3b": (D, DFF), "w2b": (DFF, D), "wpp": (256, D), "wpg": (D, D),
    }
    wf = {k: din(k, v) for k, v in wnames.items()}
    wb = {k: dscr(k + "_bf", v, BF16) for k, v in wnames.items()}
    lnp = din("lnp", [128, 8 * NCH])
    qkg = din("qkg", [128, 5])
    ssd = din("ssd", [128, 4])
    sst = din("sst", [2, 3, 128, 16])
    ssr = din("ssr", [2, 3, 128, 2048])
    ssb = din("ssb", [2, 2, 128, 2048])
    ssc = din("ssc", [2, 2, 128, 2048])
    selp = din("selp", [128, 2])
    out = nc.dram_tensor("outT", [NCH, 128, L], F32, kind="ExternalOutput")

    X1 = dscr("X1", [NCH, 128, L], F32)
    QT = dscr("QT", [NH, 96, L], BF16)
    NSPL = max(1, L // 1024)
    Ls = L // NSPL
    CKVs = [dscr(f"CKV{i}", [288, Ls], BF16) for i in range(NSPL)]
    CKVALLs = [dscr(f"CKVALL{i}", [2 * 288, Ls], BF16) for i in range(NSPL)]
    U32 = dscr("U32", [4, 128, L], F32)
    UBF = dscr("UBF", [4, 128, L], BF16)
    GA = dscr("GA", [NCH, 128, L], F32)
    GB = dscr("GB", [NCH, 128, L], F32)
    OT = dscr("OT", [4, 128, L], BF16)
    YS = dscr("YS", [4, 128, L], F32)
    SX = dscr("SX", [128, 32], F32)
    SXALL = dscr("SXALL", [256, 32], F32)
    import os as _os
    if _os.environ.get("DUMMY_MB"):
        DUM = dscr("DUM", [int(_os.environ["DUMMY_MB"]) * 2, 128, 1024], F32)

    ps = [es.enter_context(nc.psum_tensor(f"ps{i}", [128, 512], F32)) for i in range(8)]
    pctr = [0]

    def bank():
        b = pctr[0] % 8
        pctr[0] += 1
        return b

    V, S, T, G, SP_ = "dve", "act", "pe", "pool", "sp"

    def mm(o, lhsT, rhs, start, stop, reads, writes):
        P.add(T, lambda: nc.tensor.matmul(o, lhsT, rhs, start=start, stop=stop), reads, writes)

    def act(o, i, func, reads, writes, scale=None, bias=None):
        kw = {}
        if scale is not None:
            kw["scale"] = scale
        if bias is not None:
            kw["bias"] = bias
        P.add(S, lambda: nc.scalar.activation(out=o, in_=i, func=func, **kw), reads, writes)

    def tt(o, a, b, op, reads, writes, eng=V):
        h = nc.vector if eng == V else nc.gpsimd
        P.add(eng, lambda: h.tensor_tensor(out=o, in0=a, in1=b, op=op), reads, writes)

    def ts(o, a, s1, s2, op0, op1, reads, writes, eng=V):
        h = nc.vector if eng == V else nc.gpsimd
        if op1 is None:
            P.add(eng, lambda: h.tensor_scalar(out=o, in0=a, scalar1=s1, scalar2=None, op0=op0), reads, writes)
        else:
            P.add(eng, lambda: h.tensor_scalar(out=o, in0=a, scalar1=s1, scalar2=s2, op0=op0, op1=op1), reads, writes)

    def stt(o, a, s, b, op0, op1, reads, writes):
        P.add(V, lambda: nc.vector.scalar_tensor_tensor(out=o, in0=a, scalar=s, in1=b, op0=op0, op1=op1), reads, writes)

    def cp(o, i, reads, writes, eng=V):
        h = nc.vector if eng == V else nc.gpsimd
        P.add(eng, lambda: h.tensor_copy(out=o, in_=i), reads, writes)

    def dma(o, i, reads, writes, q=SP_):
        h = {"sp": nc.sync, "pool": nc.gpsimd, "act": nc.scalar}[q]
        P.add(q, lambda: h.dma_start(out=o, in_=i), reads, writes, dma=True)

    for k in ["w1a", "w3a", "w2a", "win", "wuq", "wuk", "wuv", "wout", "woa", "wglu", "wos", "w1b", "w3b", "w2b", "wpp", "wpg"]:
        r, c = wnames[k]
        if r * c > 1500000:
            hh = r // 2
            dma(wb[k][0:hh, :], wf[k][0:hh, :], [], ["W_" + k + "0"], q=G)
            dma(wb[k][hh:r, :], wf[k][hh:r, :], [], ["W_" + k + "1"], q=G)
        else:
            dma(wb[k][:, :], wf[k][:, :], [], ["W_" + k + "0"], q=G)

    def wres(k):
        r, c = wnames[k]
        return ["W_" + k + "0", "W_" + k + "1"] if r * c > 1500000 else ["W_" + k + "0"]

    def wview(k):
        return wb[k].ap().rearrange("(kc p) n -> p kc n", p=128)

    if _os.environ.get("DUMMY_MB"):
        dma(DUM[0, :, :], wf["wpg"][0:128, :], [], ["DUM"], q=G)
    ones_d = sb("ones_d", [128, 128], BF16)
    ones_q = sb("ones_q", [128, 128], BF16)
    lnp_s = sb("lnp_s", [128, 8 * NCH], F32)
    qkg_s = sb("qkg_s", [128, 5], F32)
    ssd_s = sb("ssd_s", [128, 4], F32)
    sel_s = sb("sel_s", [128, 2], F32)
    P.add(V, lambda: nc.vector.memset(ones_d[:, :], 1.0 / 1024.0), [], ["ones_d"])
    P.add(V, lambda: nc.vector.memset(ones_q[:, :], 1.0), [], ["ones_q"])
    dma(lnp_s[:, :], lnp[:, :], [], ["lnp"])
    dma(qkg_s[:, :], qkg[:, :], [], ["qkg"])
    dma(ssd_s[:, :], ssd[:, :], [], ["ssd"])
    dma(sel_s[:, :], selp[:, :], [], ["sel"])

    CM = {}

    def alloc_common(nx):
        CM["xinb"] = [sb(f"xin{i}", [128, NCH, TT], F32) for i in range(nx)]
        CM["xin"] = CM["xinb"][0]
        CM["xp"] = "xinA"
        CM["xbf"] = sb("xbf", [128, NCH, TT], BF16)
        CM["hbf"] = sb("hbf", [128, HCH, TT], BF16)
        CM["zb"] = sb("zb", [128, NCH, TT], BF16)
        CM["zq"] = sb("zq", [128, NCH, TT], BF16)
        CM["tmpf"] = [sb(f"tmpf{i}", [128, TT], F32) for i in range(3)]
        CM["mean"] = sb("mean_s", [128, TT], F32)
        CM["rstd"] = sb("rstd_s", [128, TT], F32)
        CM["wA"] = [sb(f"wA{i}", [128, 8, 512], BF16) for i in range(3)]
        CM["wB"] = [sb(f"wB{i}", [128, HCH, 256], BF16) for i in range(2)]

    def set_x(i):
        CM["xin"] = CM["xinb"][i]
        CM["xp"] = "xin" + "AB"[i]

    def XR(o):
        return f"{CM['xp']}{o}"

    def XA():
        return [XR(o) for o in range(NCH)]

    XBALL = [f"xbf{k}" for k in range(NCH)]
    lbc = [0]

    def lbank():
        b = lbc[0] % 6
        lbc[0] += 1
        return b

    def ln_prep_chunk(o):
        xt = CM["xin"]
        act(CM["zb"][:, o, :], xt[:, o, :], AF.Copy, [XR(o)], [f"zb{o}"])
        act(CM["zq"][:, o, :], xt[:, o, :], AF.Square, [XR(o)], [f"zq{o}"])

    def ln_stats_mm(o):
        mm(ps[6][:, :], ones_d[:, :], CM["zb"][:, o, :], o == 0, o == NCH - 1, ["ones_d", f"zb{o}"], ["ps6"])
        mm(ps[7][:, :], ones_d[:, :], CM["zq"][:, o, :], o == 0, o == NCH - 1, ["ones_d", f"zq{o}"], ["ps7"])

    wactr = [0]
    wbctr = [0]

    def load_wA(k, c0, c1):
        i = wactr[0] % 3
        wactr[0] += 1
        dma(CM["wA"][i][:, :, 0:c1 - c0], wview(k)[:, :, c0:c1], wres(k), [f"wA{i}"])
        return i

    def load_wB(k, c0, c1):
        i = wbctr[0] % 2
        wbctr[0] += 1
        dma(CM["wB"][i][:, :, 0:c1 - c0], wview(k)[:, :, c0:c1], wres(k), [f"wB{i}"])
        return i

    tctr = [0]

    def tmp():
        i = tctr[0] % 3
        tctr[0] += 1
        return i

    def ffn(k1, k3, k2):
        xt, xbf, hbf, tmpf, wA, wB = CM["xin"], CM["xbf"], CM["hbf"], CM["tmpf"], CM["wA"], CM["wB"]
        ngrp = (DFF + 511) // 512
        for g in range(ngrp):
            c0, c1 = g * 512, min(DFF, (g + 1) * 512)
            i1 = load_wA(k1, c0, c1)
            i3 = load_wA(k3, c0, c1)
            for cc in range((c1 - c0) // 128):
                c = g * 4 + cc
                b1, b3 = bank(), bank()
                for k in range(NCH):
                    mm(ps[b1][:, :], wA[i1][:, k, cc * 128:(cc + 1) * 128], xbf[:, k, :], k == 0, k == NCH - 1,
                       [f"wA{i1}", f"xbf{k}"], [f"ps{b1}"])
                for k in range(NCH):
                    mm(ps[b3][:, :], wA[i3][:, k, cc * 128:(cc + 1) * 128], xbf[:, k, :], k == 0, k == NCH - 1,
                       [f"wA{i3}", f"xbf{k}"], [f"ps{b3}"])
                ti = tmp()
                act(tmpf[ti][:, :], ps[b1][:, :], AF.Silu, [f"ps{b1}"], [f"tmpf{ti}"])
                tt(hbf[:, c, :], tmpf[ti][:, :], ps[b3][:, :], ALU.mult, [f"tmpf{ti}", f"ps{b3}"], [f"hbf{c}"])
        for g in range(4):
            i2 = load_wB(k2, g * 256, (g + 1) * 256)
            for oc in range(2):
                o = g * 2 + oc
                b = lbank()
                for k in range(HCH):
                    mm(ps[b][:, :], wB[i2][:, k, oc * 128:(oc + 1) * 128], hbf[:, k, :], k == 0, k == HCH - 1,
                       [f"wB{i2}", f"hbf{k}"], [f"ps{b}"])
                stt(xt[:, o, :], ps[b][:, :], 0.5 / ALPHA, xt[:, o, :], ALU.mult, ALU.add, [f"ps{b}", XR(o)], [XR(o)])
                ln_prep_chunk(o)
                if o >= 1:
                    ln_stats_mm(o - 1)
        ln_stats_mm(NCH - 1)

    def rsqrt_inplace(r, res):
        act(r, r, AF.Ln, [res], [res])
        act(r, r, AF.Exp, [res], [res], scale=-0.5)

    def layer_norm(li, eps, obf, obres):
        xt, tmpf, mean_s, rstd_s = CM["xin"], CM["tmpf"], CM["mean"], CM["rstd"]
        act(mean_s[:, :], ps[6][:, :], AF.Copy, ["ps6"], ["mean"])
        ti = tmp()
        tt(tmpf[ti][:, :], mean_s[:, :], mean_s[:, :], ALU.mult, ["mean"], [f"tmpf{ti}"])
        tt(tmpf[ti][:, :], ps[7][:, :], tmpf[ti][:, :], ALU.subtract, ["ps7", f"tmpf{ti}"], [f"tmpf{ti}"])
        ts(rstd_s[:, :], tmpf[ti][:, :], eps, None, ALU.add, None, [f"tmpf{ti}"], ["rstd"])
        rsqrt_inplace(rstd_s[:, :], "rstd")
        gcol = li * 16
        for o in range(NCH):
            eng = G if o % 4 == 3 else V
            tt(xt[:, o, :], xt[:, o, :], mean_s[:, :], ALU.subtract, [XR(o), "mean"], [XR(o)], eng=eng)
            tt(xt[:, o, :], xt[:, o, :], rstd_s[:, :], ALU.mult, [XR(o), "rstd"], [XR(o)], eng=eng)
            act(xt[:, o, :], xt[:, o, :], AF.Identity, [XR(o), "lnp"], [XR(o)],
                scale=lnp_s[:, gcol + o:gcol + o + 1], bias=lnp_s[:, gcol + 8 + o:gcol + 8 + o + 1])
            if obf is not None:
                act(obf[:, o, :], xt[:, o, :], AF.Copy, [XR(o)], [f"{obres}{o}"])

    pst = ExitStack()
    cur[0] = pst
    alloc_common(2)
    xbf, tmpf, rstd_s, wA = CM["xbf"], CM["tmpf"], CM["rstd"], CM["wA"]
    x1bf = [sb(f"x1bf{i}", [128, NCH, TT], BF16) for i in range(2)]
    xv = xT.ap().rearrange("c p n -> p c n")
    x1v = X1.ap().rearrange("c p n -> p c n")
    ql = sb("ql", [128, 3, TT], F32)
    qsq = sb("qsq", [128, 3, TT], BF16)
    cqb = sb("cqb", [128, 3, TT], BF16)
    wuq_s = sb("wuq_s", [128, 3, 768], BF16)
    qn = sb("qn", [128, 4, TT], BF16)
    cos_s = sb("cos_s", [128, TT], F32)
    sin_s = sb("sin_s", [128, TT], F32)
    r1s = sb("r1s", [128, TT], F32)
    r2s = sb("r2s", [128, TT], F32)
    ro1 = sb("ro1", [128, TT], BF16)
    ro2 = sb("ro2", [128, TT], BF16)
    u32s = sb("u32s", [128, 4, TT], F32)
    ubfs = sb("ubfs", [128, 4, TT], BF16)
    gsm = [sb(f"gsm{i}", [128, TT], F32) for i in range(2)]
    dma(wuq_s[:, :, :], wview("wuq"), wres("wuq"), ["wuq_s"])

    def rope(pa, pb, np_, outa, outb, ra, rb_):
        act(r1s[0:np_, :], ps[pa][0:np_, :], AF.Copy, [f"ps{pa}"], ["r1s"])
        act(r2s[0:np_, :], ps[pb][0:np_, :], AF.Copy, [f"ps{pb}"], ["r2s"])
        t0, t1 = tmp(), tmp()
        tt(tmpf[t0][0:np_, :], r1s[0:np_, :], cos_s[0:np_, :], ALU.mult, ["r1s", "cos"], [f"tmpf{t0}"])
        tt(tmpf[t1][0:np_, :], r2s[0:np_, :], sin_s[0:np_, :], ALU.mult, ["r2s", "sin"], [f"tmpf{t1}"])
        tt(outa, tmpf[t0][0:np_, :], tmpf[t1][0:np_, :], ALU.subtract, [f"tmpf{t0}", f"tmpf{t1}"], [ra])
        tt(tmpf[t0][0:np_, :], r2s[0:np_, :], cos_s[0:np_, :], ALU.mult, ["r2s", "cos"], [f"tmpf{t0}"])
        tt(tmpf[t1][0:np_, :], r1s[0:np_, :], sin_s[0:np_, :], ALU.mult, ["r1s", "sin"], [f"tmpf{t1}"])
        tt(outb, tmpf[t0][0:np_, :], tmpf[t1][0:np_, :], ALU.add, [f"tmpf{t0}", f"tmpf{t1}"], [rb_])

    def rmsnorm(src, sres, nch, dim, gcol0, dst, dres):
        act(qsq[:, 0:nch, :], src[:, 0:nch, :], AF.Square, [sres], ["qsq"])
        b = bank()
        for k in range(nch):
            mm(ps[b][:, :], ones_q[:, :], qsq[:, k, :], k == 0, k == nch - 1, ["ones_q", "qsq"], [f"ps{b}"])
        ts(rstd_s[:, :], ps[b][:, :], 1.0 / dim, RMS_EPS, ALU.mult, ALU.add, [f"ps{b}"], ["rstd"])
        rsqrt_inplace(rstd_s[:, :], "rstd")
        for k in range(nch):
            tt(src[:, k, :], src[:, k, :], rstd_s[:, :], ALU.mult, [sres, "rstd"], [sres])
            act(dst[:, k, :], src[:, k, :], AF.Copy, [sres, "qkg"], [dres], scale=qkg_s[:, gcol0 + k:gcol0 + k + 1])

    PJ = {}

    def proj(iw, cc, M=128, col0=None):
        b = bank()
        c_lo = cc * 128 if col0 is None else col0
        xb_, xr_ = PJ["buf"], PJ["res"]
        for k in range(NCH):
            mm(ps[b][0:M, :], wA[iw][:, k, c_lo:c_lo + M], xb_[:, k, :], k == 0, k == NCH - 1, [f"wA{iw}", f"{xr_}{k}"], [f"ps{b}"])
        return b

    def proj_stage(t):
        c0, c1 = t * TT, (t + 1) * TT
        PJ["buf"], PJ["res"] = x1bf[t % 2], f"x1b{t % 2}_"
        dma(cos_s[:, :], cosT[:, c0:c1], [], ["cos"])
        dma(sin_s[:, :], sinT[:, c0:c1], [], ["sin"])
        iw = load_wA("win", 0, 384)
        for c in range(3):
            b = proj(iw, c)
            act(ql[:, c, :], ps[b][:, :], AF.Copy, [f"ps{b}"], ["ql"])
        rmsnorm(ql, "ql", 3, 384.0, 0, cqb, "cqb")
        for c in range(4):
            b = bank()
            for k in range(3):
                mm(ps[b][:, :], wuq_s[:, k, c * 128:(c + 1) * 128], cqb[:, k, :], k == 0, k == 2, ["wuq_s", "cqb"], [f"ps{b}"])
            act(qn[:, c, :], ps[b][:, :], AF.Copy, [f"ps{b}"], [f"qn{c}"])
            dma(QT[2 * c, 0:64, c0:c1], qn[0:64, c, :], [f"qn{c}"], [P.wr("QT")], q=G)
            dma(QT[2 * c + 1, 0:64, c0:c1], qn[64:128, c, :], [f"qn{c}"], [P.wr("QT")], q=G)
        b1, b2 = bank(), bank()
        for k in range(3):
            mm(ps[b1][:, :], wuq_s[:, k, 512:640], cqb[:, k, :], k == 0, k == 2, ["wuq_s", "cqb"], [f"ps{b1}"])
        for k in range(3):
            mm(ps[b2][:, :], wuq_s[:, k, 640:768], cqb[:, k, :], k == 0, k == 2, ["wuq_s", "cqb"], [f"ps{b2}"])
        rope(b1, b2, 128, ro1[:, :], ro2[:, :], "ro1", "ro2")
        for h in range(NH):
            dma(QT[h, 64:80, c0:c1], ro1[h * 16:(h + 1) * 16, :], ["ro1"], [P.wr("QT")], q=G)
            dma(QT[h, 80:96, c0:c1], ro2[h * 16:(h + 1) * 16, :], ["ro2"], [P.wr("QT")], q=G)
        iw = load_wA("win", 384, 672)
        for c in range(2):
            b = proj(iw, c)
            act(ql[:, c, :], ps[b][:, :], AF.Copy, [f"ps{b}"], ["ql"])
        b1 = proj(iw, 0, M=16, col0=256)
        b2 = proj(iw, 0, M=16, col0=272)
        rmsnorm(ql, "ql", 2, 256.0, 3, cqb, "cqb")
        CKV = CKVs[c0 // Ls]
        s0, s1 = c0 % Ls, c0 % Ls + TT
        for k in range(2):
            dma(CKV[k * 128:(k + 1) * 128, s0:s1], cqb[:, k, :], ["cqb"], [P.wr(f"CKV{c0 // Ls}")], q=G)
        rope(b1, b2, 16, ro1[0:16, :], ro2[0:16, :], "ro1", "ro2")
        dma(CKV[256:272, s0:s1], ro1[0:16, :], ["ro1"], [P.wr(f"CKV{c0 // Ls}")], q=G)
        dma(CKV[272:288, s0:s1], ro2[0:16, :], ["ro2"], [P.wr(f"CKV{c0 // Ls}")], q=G)
        iw = load_wA("win", 672, 1184)
        for c in range(4):
            b = proj(iw, c)
            act(u32s[:, c, :], ps[b][:, :], AF.Copy, [f"ps{b}"], ["u32s"])
        cp(ubfs[:, :, :], u32s[:, :, :], ["u32s"], ["ubfs"], eng=G)
        dma(U32.ap().rearrange("c p n -> p c n")[:, :, c0:c1], u32s[:, :, :], ["u32s"], [P.wr("U32")], q=G)
        dma(UBF.ap().rearrange("c p n -> p c n")[:, :, c0:c1], ubfs[:, :, :], ["ubfs"], [P.wr("UBF")], q=G)
        gi_ = 0
        for gname, GD, base in (("G0", GA, 1184), ("G1", GB, 2208)):
            for half in range(2):
                iw = load_wA("win", base + half * 512, base + (half + 1) * 512)
                for cc in range(4):
                    c = half * 4 + cc
                    b = proj(iw, cc)
                    gb_ = gi_ % 2
                    gi_ += 1
                    act(gsm[gb_][:, :], ps[b][:, :], AF.Sigmoid, [f"ps{b}"], [f"gsm{gb_}"])
                    dma(GD[c, :, c0:c1], gsm[gb_][:, :], [f"gsm{gb_}"], [P.wr(gname)], q=G)

    set_x(0)
    dma(CM["xin"][:, :, :], xv[:, :, 0:TT], [], XA())
    act(xbf[:, :, :], CM["xin"][:, :, :], AF.Copy, XA(), XBALL)
    for t in range(NT):
        c0, c1 = t * TT, (t + 1) * TT
        set_x(t % 2)
        if t + 1 < NT:
            nb = (t + 1) % 2
            nres = [f"xin{'AB'[nb]}{o}" for o in range(NCH)]
            dma(CM["xinb"][nb][:, :, :], xv[:, :, c1:c1 + TT], [], nres)
        ffn("w1a", "w3a", "w2a")
        if t + 1 < NT:
            act(xbf[:, :, :], CM["xinb"][nb][:, :, :], AF.Copy, nres, XBALL)
        layer_norm(0, LN_EPS / (ALPHA * ALPHA), x1bf[t % 2], f"x1b{t % 2}_")
        dma(x1v[:, :, c0:c1], CM["xin"][:, :, :], XA(), [P.wr("X1")], q=G)
        if t >= 1:
            proj_stage(t - 1)
    proj_stage(NT - 1)
    P.barrier()
    pst.close()
    if _os.environ.get("STOP") == "A":
        P.emit(P.rd("X1"), es); es.close(); return nc

    for i in range(NSPL):
        P.add(G, lambda i=i: nc.gpsimd.collective_compute("AllGather", ALU.bypass, replica_groups=[[0, 1], [2, 3], [4, 5], [6, 7]],
                                                           ins=[CKVs[i].ap().opt()], outs=[CKVALLs[i].ap().opt()]),
              P.rd(f"CKV{i}"), [f"CKVALL{i}"])

    if _os.environ.get("STOP") == "X":
        P.emit([f"CKVALL{i}" for i in range(NSPL)], es); es.close(); return nc
    pst = ExitStack()
    cur[0] = pst
    bbR = [sb(f"bbR{d}", [128, 2048], BF16) for d in range(2)]
    bbI = [sb(f"bbI{d}", [128, 2048], BF16) for d in range(2)]
    ccR = [sb(f"ccR{d}", [128, 2048], BF16) for d in range(2)]
    ccI = [sb(f"ccI{d}", [128, 2048], BF16) for d in range(2)]
    pwA = [[sb(f"pwA{d}{l}", [128, 16, 17], F32) for l in range(3)] for d in range(2)]
    pwB = [[sb(f"pwB{d}{l}", [128, 16, 17], F32) for l in range(3)] for d in range(2)]
    pwN = [[sb(f"pwN{d}{l}", [128, 16, 17], F32) for l in range(3)] for d in range(2)]
    small = [sb(f"sm{i}", [128, 16], F32) for i in range(8)]
    zero_s = sb("zero_s", [128, 32], F32)
    finR = sb("finR", [128, 32], F32)
    gat = sb("gat", [128, 2, 32], F32)
    iniS = sb("iniS", [128, 32], F32)
    P.add(V, lambda: nc.vector.memset(zero_s[:, :], 0.0), [], ["zero_s"])
    pst2 = ExitStack()
    cur[0] = pst2
    pt = [sb(f"pt{i}", [128, 2048], F32) for i in range(7)]

    I32 = mybir.dt.int32
    isml = sb("isml", [128, 16], I32)
    ibig = sb("ibig", [128, 2048], I32)

    def sincos(zi, zres, so, co, res_s, res_c, scratch, sres, itile, ires):
        for shift, dst, dres in ((0.0, so, res_s), (0.5 * PI, co, res_c)):
            ts(scratch, zi, shift, 1.0 / (2 * PI), ALU.add, ALU.mult, [zres], [sres])
            cp(itile, scratch, [sres], [ires])
            cp(scratch, itile, [ires], [sres])
            ts(scratch, scratch, -2 * PI, None, ALU.mult, None, [sres], [sres])
            ts(dst, zi, shift, None, ALU.add, None, [zres], [dres])
            tt(dst, dst, scratch, ALU.add, [dres, sres], [dres])
            ts(scratch, dst, PI, 2 * PI, ALU.is_gt, ALU.mult, [dres], [sres])
            tt(dst, dst, scratch, ALU.subtract, [dres, sres], [dres])
            ts(scratch, dst, -PI, 2 * PI, ALU.is_lt, ALU.mult, [dres], [sres])
            tt(dst, dst, scratch, ALU.add, [dres, sres], [dres])
            act(dst, dst, AF.Sin, [dres], [dres])

    for d in range(2):
        lre, lim, ldt, zr, zi, mg, sn, cs = [small[i][:, :] for i in range(8)]
        dma(lre, sst[d, 0, :, :], [], ["sm0"])
        dma(lim, sst[d, 1, :, :], [], ["sm1"])
        dma(ldt, sst[d, 2, :, :], [], ["sm2"])
        act(ldt, ldt, AF.Exp, ["sm2"], ["sm2"])
        tt(zr, lre, ldt, ALU.mult, ["sm0", "sm2"], ["sm3"])
        tt(zi, lim, ldt, ALU.mult, ["sm1", "sm2"], ["sm4"])
        act(mg, zr, AF.Exp, ["sm3"], ["sm5"])
        sincos(zi, "sm4", sn, cs, "sm6", "sm7", zr, "sm3", isml[:, :], "isml")
        for l in range(3):
            A, B, N = pwA[d][l], pwB[d][l], pwN[d][l]
            rA, rB, rN = f"pwA{d}{l}", f"pwB{d}{l}", f"pwN{d}{l}"
            P.add(V, lambda A=A: nc.vector.memset(A[:, :, 0:1], 1.0), [], [rA])
            P.add(V, lambda B=B: nc.vector.memset(B[:, :, 0:1], 0.0), [], [rB])
            if l == 0:
                tt(A[:, :, 1], mg, cs, ALU.mult, ["sm5", "sm7"], [rA])
                tt(B[:, :, 1], mg, sn, ALU.mult, ["sm5", "sm6"], [rB])
            else:
                cp(A[:, :, 1], pwA[d][l - 1][:, :, 16], [f"pwA{d}{l-1}"], [rA])
                cp(B[:, :, 1], pwB[d][l - 1][:, :, 16], [f"pwB{d}{l-1}"], [rB])
            for j in range(2, 17):
                tt(zr, A[:, :, j - 1], A[:, :, 1], ALU.mult, [rA], ["sm3"])
                tt(zi, B[:, :, j - 1], B[:, :, 1], ALU.mult, [rB], ["sm4"])
                tt(A[:, :, j], zr, zi, ALU.subtract, ["sm3", "sm4"], [rA])
                tt(zr, A[:, :, j - 1], B[:, :, 1], ALU.mult, [rA, rB], ["sm3"])
                tt(zi, B[:, :, j - 1], A[:, :, 1], ALU.mult, [rA, rB], ["sm4"])
                tt(B[:, :, j], zr, zi, ALU.add, ["sm3", "sm4"], [rB])
            ts(N[:, :, :], B[:, :, :], -1.0, None, ALU.mult, None, [rB], [rN])
        LR, LI, DT, t0, t1, t2, t3 = [pt[i][:, :] for i in range(7)]
        dma(LR, ssr[d, 0, :, :], [], ["pt0"])
        dma(LI, ssr[d, 1, :, :], [], ["pt1"])
        dma(DT, ssr[d, 2, :, :], [], ["pt2"])
        act(DT, DT, AF.Exp, ["pt2"], ["pt2"])
        tt(t0, LR, DT, ALU.mult, ["pt0", "pt2"], ["pt3"])
        tt(t1, LI, DT, ALU.mult, ["pt1", "pt2"], ["pt4"])
        act(t0, t0, AF.Exp, ["pt3"], ["pt3"])
        sincos(t1, "pt4", t2, t3, "pt5", "pt6", DT, "pt2", ibig[:, :], "ibig")
        tt(t2, t2, t0, ALU.mult, ["pt5", "pt3"], ["pt5"])
        tt(t3, t3, t0, ALU.mult, ["pt6", "pt3"], ["pt6"])
        ts(t3, t3, -1.0, None, ALU.add, None, ["pt6"], ["pt6"])
        tt(t0, LR, LR, ALU.mult, ["pt0"], ["pt3"])
        tt(t1, LI, LI, ALU.mult, ["pt1"], ["pt4"])
        tt(t0, t0, t1, ALU.add, ["pt3", "pt4"], ["pt3"])
        P.add(V, lambda t0=t0: nc.vector.reciprocal(out=t0, in_=t0), ["pt3"], ["pt3"])
        tt(t1, t3, LR, ALU.mult, ["pt6", "pt0"], ["pt4"])
        tt(DT, t2, LI, ALU.mult, ["pt5", "pt1"], ["pt2"])
        tt(t1, t1, DT, ALU.add, ["pt4", "pt2"], ["pt4"])
        tt(t1, t1, t0, ALU.mult, ["pt4", "pt3"], ["pt4"])
        tt(DT, t2, LR, ALU.mult, ["pt5", "pt0"], ["pt2"])
        tt(LR, t3, LI, ALU.mult, ["pt6", "pt1"], ["pt0"])
        tt(DT, DT, LR, ALU.subtract, ["pt2", "pt0"], ["pt2"])
        tt(DT, DT, t0, ALU.mult, ["pt2", "pt3"], ["pt2"])
        dma(LR, ssb[d, 0, :, :], [], ["pt0"])
        dma(LI, ssb[d, 1, :, :], [], ["pt1"])
        tt(t0, t1, LR, ALU.mult, ["pt4", "pt0"], ["pt3"])
        tt(t2, DT, LI, ALU.mult, ["pt2", "pt1"], ["pt5"])
        tt(bbR[d][:, :], t0, t2, ALU.subtract, ["pt3", "pt5"], [f"bbR{d}"])
        tt(t0, t1, LI, ALU.mult, ["pt4", "pt1"], ["pt3"])
        tt(t2, DT, LR, ALU.mult, ["pt2", "pt0"], ["pt5"])
        tt(bbI[d][:, :], t0, t2, ALU.add, ["pt3", "pt5"], [f"bbI{d}"])
        dma(t3, ssc[d, 0, :, :], [], ["pt6"])
        cp(ccR[d][:, :], t3, ["pt6"], [f"ccR{d}"])
        dma(t3, ssc[d, 1, :, :], [], ["pt6"])
        ts(ccI[d][:, :], t3, -1.0, None, ALU.mult, None, ["pt6"], [f"ccI{d}"])
    P.barrier()
    pst2.close()
    cur[0] = pst
    Rt = sb("Rst", [128, L], F32)
    It = sb("Ist", [128, L], F32)
    Rbf = sb("Rbf", [128, L], BF16)
    Ibf = sb("Ibf", [128, L], BF16)
    ubig = [sb(f"ubig{i}", [128, L], BF16) for i in range(2)]
    yacc = sb("yacc", [128, L], F32)
    e2R = sb("e2R", [128, 16, T3], F32)
    e2I = sb("e2I", [128, 16, T3], F32)
    e3R = sb("e3R", [128, T3], F32)
    e3I = sb("e3I", [128, T3], F32)
    x2pR = sb("x2pR", [128, T3], F32)
    x2pI = sb("x2pI", [128, T3], F32)
    x1pR = sb("x1pR", [128, 16, T3], F32)
    x1pI = sb("x1pI", [128, 16, T3], F32)
    NKC = NK // 128
    ckt = [sb(f"ckt{i}", [128, 2, 512], BF16) for i in range(3)]
    Kt = sb("Kt", [96, NK], BF16)
    Va = sb("Va", [128, NKC, 128], BF16)
    qts = sb("qts", [96, L], BF16)
    wuk_s = sb("wuk_s", [128, 2, 512], BF16)
    wuv_s = sb("wuv_s", [128, 2, 512], BF16)
    Pb = [sb(f"Pb{i}", [128, 1024], BF16) for i in range(2)]
    rcs = sb("rcs", [64, TT], F32)
    ob = sb("ob", [64, TT], BF16)

    b67 = [0]

    def bank67():
        b67[0] += 1
        return 6 + (b67[0] % 2)

    def cmul_acc(oR, oI, pR, pI, a, b, nb, res_o, res_p, extra_reads=()):
        ex = list(extra_reads)
        roR, roI, rpR, rpI = res_o + "R", res_o + "I", res_p + "R", res_p + "I"
        stt(oR, pR, a, oR, ALU.mult, ALU.add, [roR, rpR] + ex, [roR])
        stt(oI, pR, b, oI, ALU.mult, ALU.add, [roI, rpR] + ex, [roI])
        stt(oR, pI, nb, oR, ALU.mult, ALU.add, [roR, rpI] + ex, [roR])
        stt(oI, pI, a, oI, ALU.mult, ALU.add, [roI, rpI] + ex, [roI])

    def blk(tn, j):
        return tn[:, j * K1:(j + 1) * K1]

    def blkres(t, ri):
        j0_, j1_ = (t * TT) // K1, ((t + 1) * TT - 1) // K1
        return [f"b{j}{ri}" for j in range(j0_, j1_ + 1)]

    def blk3(tn, j):
        return tn[:, j * K1:(j + 1) * K1].rearrange("p (a b) -> p a b", b=T3)

    def s5_main():
        for pas in range(2):
            d = pas
            rev = (pas == 1)

            def ix(i, n, rev=rev):
                return (n - 1 - i) if rev else i

            ini = zero_s if pas == 0 else iniS
            ini_res = "zero_s" if pas == 0 else "iniS"
            if pas == 1:
                dma(SX[:, :], finR[:, :], ["finR"], ["SX"], q=G)
                P.add(G, lambda: nc.gpsimd.collective_compute("AllGather", ALU.bypass, replica_groups=[[0, 1], [2, 3], [4, 5], [6, 7]],
                                                               ins=[SX.ap().opt()], outs=[SXALL.ap().opt()]),
                      ["SX"], ["SXALL"])
                dma(gat[:, :, :], SXALL.ap().rearrange("(r p) n -> p r n", p=128), ["SXALL"], ["gat"])
                ts(iniS[:, :], gat[:, 0, :], sel_s[:, 0:1], None, ALU.mult, None, ["gat", "sel"], ["iniS"])
                stt(iniS[:, :], gat[:, 1, :], sel_s[:, 1:2], iniS[:, :], ALU.mult, ALU.add, ["gat", "sel", "iniS"], ["iniS"])
            for c in range(4):
                ub = ubig[c % 2]
                ubr = f"ubig{c % 2}"
                dma(ub[:, :], UBF[c, :, :], P.rd("UBF"), [ubr])
                if pas == 0:
                    dma(yacc[:, :], U32[c, :, :], P.rd("U32"), ["yacc"])
                    ts(yacc[:, :], yacc[:, :], ssd_s[:, c:c + 1], None, ALU.mult, None, ["yacc", "ssd"], ["yacc"])
                else:
                    dma(yacc[:, :], YS[c, :, :], [f"YS{c}"], ["yacc"])
                for qq in range(4):
                    q = c * 4 + qq
                    A0, B0, N0 = pwA[d][0], pwB[d][0], pwN[d][0]
                    A1, B1, N1 = pwA[d][1], pwB[d][1], pwN[d][1]
                    A2, B2, N2 = pwA[d][2], pwB[d][2], pwN[d][2]
                    pres = [f"pw{x}{d}{l}" for x in "ABN" for l in range(3)]
                    for t in range(NT):
                        bR, bI = bank67(), bank67()
                        mm(ps[bR][:, :], bbR[d][:, q * 128:(q + 1) * 128], ub[:, t * TT:(t + 1) * TT], True, True, [f"bbR{d}", ubr], [f"ps{bR}"])
                        mm(ps[bI][:, :], bbI[d][:, q * 128:(q + 1) * 128], ub[:, t * TT:(t + 1) * TT], True, True, [f"bbI{d}", ubr], [f"ps{bI}"])
                        act(Rt[:, t * TT:(t + 1) * TT], ps[bR][:, :], AF.Copy, [f"ps{bR}"], blkres(t, "R"))
                        act(It[:, t * TT:(t + 1) * TT], ps[bI][:, :], AF.Copy, [f"ps{bI}"], blkres(t, "I"))
                    yield 12
                    for j in range(1, 16):
                        jc, jp = ix(j, 16), ix(j - 1, 16)
                        cmul_acc(blk(Rt, jc), blk(It, jc), blk(Rt, jp), blk(It, jp),
                                 A0[:, q, 1:2], B0[:, q, 1:2], N0[:, q, 1:2], f"b{jc}", f"b{jp}", pres)
                        if j % 5 == 0:
                            yield 8
                    jl = ix(15, 16)
                    cp(e2R[:, :, :], blk3(Rt, jl), [f"b{jl}R"], [f"e2_{j}R" for j in range(16)])
                    cp(e2I[:, :, :], blk3(It, jl), [f"b{jl}I"], [f"e2_{j}I" for j in range(16)])
                    for j in range(1, 16):
                        jc, jp = ix(j, 16), ix(j - 1, 16)
                        cmul_acc(e2R[:, jc, :], e2I[:, jc, :], e2R[:, jp, :], e2I[:, jp, :],
                                 A1[:, q, 1:2], B1[:, q, 1:2], N1[:, q, 1:2], f"e2_{jc}", f"e2_{jp}", pres)
                    yield 8
                    cp(e3R[:, :], e2R[:, jl, :], [f"e2_{jl}R"], [f"e3_{k}R" for k in range(T3)])
                    cp(e3I[:, :], e2I[:, jl, :], [f"e2_{jl}I"], [f"e3_{k}I" for k in range(T3)])
                    for k in range(T3):
                        kc = ix(k, T3)
                        if k == 0:
                            pR, pI = ini[:, q:q + 1], ini[:, 16 + q:16 + q + 1]
                            rp_ = "ini"
                        else:
                            kp = ix(k - 1, T3)
                            pR, pI = e3R[:, kp:kp + 1], e3I[:, kp:kp + 1]
                            rp_ = f"e3_{kp}"
                        cmul_acc(e3R[:, kc:kc + 1], e3I[:, kc:kc + 1], pR, pI, A2[:, q, 1:2], B2[:, q, 1:2], N2[:, q, 1:2],
                                 f"e3_{kc}", rp_, pres + [ini_res])
                    yield 8
                    kl = ix(T3 - 1, T3)
                    if pas == 0:
                        cp(finR[:, q:q + 1], e3R[:, kl:kl + 1], [f"e3_{kl}R"], ["finR"])
                        cp(finR[:, 16 + q:16 + q + 1], e3I[:, kl:kl + 1], [f"e3_{kl}I"], ["finR"])
                    k0 = ix(0, T3)
                    e3allR = [f"e3_{k}R" for k in range(T3)]
                    e3allI = [f"e3_{k}I" for k in range(T3)]
                    cp(x2pR[:, k0:k0 + 1], ini[:, q:q + 1], [ini_res], ["x2pR"])
                    cp(x2pI[:, k0:k0 + 1], ini[:, 16 + q:16 + q + 1], [ini_res], ["x2pI"])
                    if T3 > 1:
                        if not rev:
                            cp(x2pR[:, 1:T3], e3R[:, 0:T3 - 1], e3allR, ["x2pR"])
                            cp(x2pI[:, 1:T3], e3I[:, 0:T3 - 1], e3allI, ["x2pI"])
                        else:
                            cp(x2pR[:, 0:T3 - 1], e3R[:, 1:T3], e3allR, ["x2pR"])
                            cp(x2pI[:, 0:T3 - 1], e3I[:, 1:T3], e3allI, ["x2pI"])
                    for j in range(16):
                        jc = ix(j, 16)
                        cmul_acc(e2R[:, jc, :], e2I[:, jc, :], x2pR[:, :], x2pI[:, :],
                                 A1[:, q, j + 1:j + 2], B1[:, q, j + 1:j + 2], N1[:, q, j + 1:j + 2], f"e2_{jc}", "x2p", pres)
                    yield 8
                    j0 = ix(0, 16)
                    e2allR = [f"e2_{j}R" for j in range(16)]
                    e2allI = [f"e2_{j}I" for j in range(16)]
                    cp(x1pR[:, j0, :], x2pR[:, :], ["x2pR"], ["x1pR"])
                    cp(x1pI[:, j0, :], x2pI[:, :], ["x2pI"], ["x1pI"])
                    if not rev:
                        cp(x1pR[:, 1:16, :], e2R[:, 0:15, :], e2allR, ["x1pR"])
                        cp(x1pI[:, 1:16, :], e2I[:, 0:15, :], e2allI, ["x1pI"])
                    else:
                        cp(x1pR[:, 0:15, :], e2R[:, 1:16, :], e2allR, ["x1pR"])
                        cp(x1pI[:, 0:15, :], e2I[:, 1:16, :], e2allI, ["x1pI"])
                    for j in range(16):
                        jc = ix(j, 16)
                        cmul_acc(blk3(Rt, jc), blk3(It, jc), x1pR[:, :, :], x1pI[:, :, :],
                                 A0[:, q, j + 1:j + 2], B0[:, q, j + 1:j + 2], N0[:, q, j + 1:j + 2], f"b{jc}", "x1p", pres)
                        if j % 5 == 4:
                            yield 8
                    act(Rbf[:, :], Rt[:, :], AF.Copy, [f"b{j}R" for j in range(16)], ["Rbf"])
                    cp(Ibf[:, :], It[:, :], [f"b{j}I" for j in range(16)], ["Ibf"], eng=G)
                    yield 6
                    for t in range(NT):
                        b = bank67()
                        mm(ps[b][:, :], ccR[d][:, q * 128:(q + 1) * 128], Rbf[:, t * TT:(t + 1) * TT], True, False, [f"ccR{d}", "Rbf"], [f"ps{b}"])
                        mm(ps[b][:, :], ccI[d][:, q * 128:(q + 1) * 128], Ibf[:, t * TT:(t + 1) * TT], False, True, [f"ccI{d}", "Ibf"], [f"ps{b}"])
                        tt(yacc[:, t * TT:(t + 1) * TT], yacc[:, t * TT:(t + 1) * TT], ps[b][:, :], ALU.add, ["yacc", f"ps{b}"], ["yacc"])
                    yield 6
                dma(YS[c, :, :], yacc[:, :], ["yacc"], [f"YS{c}"], q=G)

    scale = 96.0 ** -0.5

    def attn_main():
        for i in range(NSPL):
            ckall = CKVALLs[i].ap().rearrange("(r f) n -> r f n", r=2)
            for r in range(2):
                o0 = (i * 2 + r) * Ls
                dma(Kt[64:96, o0:o0 + Ls], ckall[r, 256:288, :], [f"CKVALL{i}"], [P.wr("Ktr")])
        dma(wuk_s[:, :, :], wview("wuk"), wres("wuk"), ["wuk_s"])
        dma(wuv_s[:, :, :], wview("wuv"), wres("wuv"), ["wuv_s"])
        P.add(G, lambda: nc.gpsimd.memset(Va[:, :, :], 1.0), [], ["Va"])
        yield 2
        cki = 0
        for h in range(NH):
            for kt in range(NK // 512):
                blkno = (kt * 512) // Ls
                i, r = blkno // 2, blkno % 2
                coff = kt * 512 - blkno * Ls
                ci = cki % 3
                cki += 1
                src = CKVALLs[i].ap()[r * 288:r * 288 + 256, coff:coff + 512].rearrange("(k p) n -> p k n", p=128)
                dma(ckt[ci][:, :, :], src, [f"CKVALL{i}"], [f"ckt{ci}"])
                b = bank67()
                for k in range(2):
                    mm(ps[b][0:64, :], wuk_s[:, k, h * 64:(h + 1) * 64], ckt[ci][:, k, :], k == 0, k == 1, ["wuk_s", f"ckt{ci}"], [f"ps{b}"])
                cp(Kt[0:64, kt * 512:(kt + 1) * 512], ps[b][0:64, :], [f"ps{b}"], ["Kt"])
                b = bank67()
                for j in range(4):
                    for k in range(2):
                        mm(ps[b][:, j * 64:(j + 1) * 64], ckt[ci][:, k, j * 128:(j + 1) * 128], wuv_s[:, k, h * 64:(h + 1) * 64], k == 0, k == 1, ["wuv_s", f"ckt{ci}"], [f"ps{b}"])
                cp(Va[:, kt * 4:(kt + 1) * 4, 0:64], ps[b][:, 0:256].rearrange("p (a b) -> p a b", b=64), [f"ps{b}"], ["Va"])
                if kt % 4 == 3:
                    yield 3
            dma(qts[:, :], QT[h, :, :], P.rd("QT"), ["qts"])
            for qt_ in range(L // 512):
                bo = 4 + (qt_ % 2)
                q0 = qt_ * 512
                npair = NKC // 2

                def s_mm(i):
                    p = i % 2
                    for j in range(2):
                        kc = 2 * i + j
                        mm(ps[2 * p + j][:, :], Kt[0:96, kc * 128:(kc + 1) * 128], qts[0:96, q0:q0 + 512], True, True, ["Kt", "qts"] + P.rd("Ktr"), [f"ps{2*p+j}"])

                def s_exp(i):
                    p = i % 2
                    for j in range(2):
                        act(Pb[p][:, j * 512:(j + 1) * 512], ps[2 * p + j][:, :], AF.Exp, [f"ps{2*p+j}"], [f"Pb{p}"], scale=scale)

                def pv(i):
                    p = i % 2
                    for j in range(2):
                        kc = 2 * i + j
                        mm(ps[bo][:, :], Va[:, kc, :], Pb[p][:, j * 512:(j + 1) * 512], kc == 0, kc == NKC - 1, ["Va", f"Pb{p}"], [f"ps{bo}"])

                s_mm(0)
                s_exp(0)
                for i in range(npair):
                    if i + 1 < npair:
                        s_mm(i + 1)
                        s_exp(i + 1)
                    pv(i)
                    if i % 8 == 7:
                        yield 10
                P.add(V, lambda bo=bo: nc.vector.reciprocal(out=rcs[0:64, :], in_=ps[bo][64:128, :]), [f"ps{bo}"], ["rcs"])
                tt(ob[:, :], ps[bo][0:64, :], rcs[:, :], ALU.mult, [f"ps{bo}", "rcs"], ["ob"])
                dma(OT[h // 2, (h % 2) * 64:(h % 2) * 64 + 64, q0:q0 + 512], ob[:, :], ["ob"], [P.wr("OT")], q=G)
                yield 1

    gens = [s5_main(), attn_main()]
    tacc = [0.0, 0.0]
    alive = [True, True]
    if _os.environ.get("NOOVERLAP"):
        for g_ in gens:
            for _ in g_:
                pass
    else:
        while any(alive):
            gi = min((i for i in range(2) if alive[i]), key=lambda i: tacc[i])
            try:
                tacc[gi] += next(gens[gi])
            except StopIteration:
                alive[gi] = False
    P.barrier()
    pst.close()

    if _os.environ.get("STOP") == "B":
        P.emit(P.rd("OT"), es); es.close(); return nc
    pst = ExitStack()
    cur[0] = pst
    alloc_common(1)
    set_x(0)
    xt, xbf, tmpf, wA = CM["xin"], CM["xbf"], CM["tmpf"], CM["wA"]
    woa_s = sb("woa_s", [128, 4, D], BF16)
    wos_s = sb("wos_s", [128, 4, D], BF16)
    wglu_s = sb("wglu_s", [128, 4, 512], BF16)
    wpp_s = sb("wpp_s", [128, 2, D], BF16)
    dma(woa_s[:, :, :], wview("woa"), wres("woa"), ["woa_s"])
    dma(wos_s[:, :, :], wview("wos"), wres("wos"), ["wos_s"])
    dma(wglu_s[:, :, :], wview("wglu"), wres("wglu"), ["wglu_s"])
    dma(wpp_s[:, :, :], wview("wpp"), wres("wpp"), ["wpp_s"])
    ots = sb("ots", [128, 4, TT], BF16)
    ysf = sb("ysf", [128, 4, TT], F32)
    zf = sb("zf", [128, 4, TT], F32)
    zbf4 = sb("zbf4", [128, 4, TT], BF16)
    zzb = sb("zzb", [128, 4, TT], BF16)
    gaf = [sb(f"gaf{i}", [128, TT], F32) for i in range(2)]
    gbf = [sb(f"gbf{i}", [128, TT], F32) for i in range(2)]
    mrb = sb("mrb", [128, NCH, TT], BF16)
    pf = sb("pf", [128, 2, TT], F32)
    pbf = sb("pbf", [128, 2, TT], BF16)
    outv = out.ap().rearrange("c p n -> p c n")
    pv_ = pT.ap().rearrange("c p n -> p c n")
    GC = 1.5957691216057308
    ysall = [f"YS{c}" for c in range(4)]
    for t in range(NT):
        c0, c1 = t * TT, (t + 1) * TT
        dma(xt[:, :, :], x1v[:, :, c0:c1], P.rd("X1"), XA())
        dma(ots[:, :, :], OT.ap().rearrange("c p n -> p c n")[:, :, c0:c1], P.rd("OT"), ["ots"])
        dma(ysf[:, :, :], YS.ap().rearrange("c p n -> p c n")[:, :, c0:c1], ysall, ["ysf"])
        dma(pf[:, :, :], pv_[:, :, c0:c1], [], ["pf"])
        cp(pbf[:, :, :], pf[:, :, :], ["pf"], ["pbf"], eng=G)
        tt(zf[:, :, :], ysf[:, :, :], ysf[:, :, :], ALU.mult, ["ysf"], ["zf"])
        ts(zf[:, :, :], zf[:, :, :], 0.044715, 1.0, ALU.mult, ALU.add, ["zf"], ["zf"])
        tt(zf[:, :, :], zf[:, :, :], ysf[:, :, :], ALU.mult, ["zf", "ysf"], ["zf"])
        act(zf[:, :, :], zf[:, :, :], AF.Sigmoid, ["zf"], ["zf"], scale=GC)
        tt(zf[:, :, :], zf[:, :, :], ysf[:, :, :], ALU.mult, ["zf", "ysf"], ["zf"])
        cp(zbf4[:, :, :], zf[:, :, :], ["zf"], ["zbf4"], eng=G)
        for c in range(4):
            b = bank()
            for k in range(4):
                mm(ps[b][:, :], wglu_s[:, k, c * 128:(c + 1) * 128], zbf4[:, k, :], k == 0, k == 3, ["wglu_s", "zbf4"], [f"ps{b}"])
            ti = tmp()
            act(tmpf[ti][:, :], ps[b][:, :], AF.Sigmoid, [f"ps{b}"], [f"tmpf{ti}"])
            tt(zzb[:, c, :], zf[:, c, :], tmpf[ti][:, :], ALU.mult, ["zf", f"tmpf{ti}"], ["zzb"])
        for o in range(NCH):
            gi_ = o % 2
            dma(gaf[gi_][:, :], GA[o, :, c0:c1], P.rd("G0"), [f"gaf{gi_}"])
            dma(gbf[gi_][:, :], GB[o, :, c0:c1], P.rd("G1"), [f"gbf{gi_}"])
            ba, bb = bank(), bank()
            for k in range(4):
                mm(ps[ba][:, :], woa_s[:, k, o * 128:(o + 1) * 128], ots[:, k, :], k == 0, k == 3, ["woa_s", "ots"], [f"ps{ba}"])
            for k in range(4):
                mm(ps[bb][:, :], wos_s[:, k, o * 128:(o + 1) * 128], zzb[:, k, :], k == 0, k == 3, ["wos_s", "zzb"], [f"ps{bb}"])
            tt(gaf[gi_][:, :], gaf[gi_][:, :], ps[ba][:, :], ALU.mult, [f"gaf{gi_}", f"ps{ba}"], [f"gaf{gi_}"])
            tt(gbf[gi_][:, :], gbf[gi_][:, :], ps[bb][:, :], ALU.mult, [f"gbf{gi_}", f"ps{bb}"], [f"gbf{gi_}"])
            tt(mrb[:, o, :], gaf[gi_][:, :], gbf[gi_][:, :], ALU.add, [f"gaf{gi_}", f"gbf{gi_}"], ["mrb"], eng=G)
        for half in range(2):
            iw = load_wA("wout", half * 512, (half + 1) * 512)
            for cc in range(4):
                o = half * 4 + cc
                b = lbank()
                for k in range(NCH):
                    mm(ps[b][:, :], wA[iw][:, k, cc * 128:(cc + 1) * 128], mrb[:, k, :], k == 0, k == NCH - 1, [f"wA{iw}", "mrb"], [f"ps{b}"])
                stt(xt[:, o, :], ps[b][:, :], 1.0 / ALPHA, xt[:, o, :], ALU.mult, ALU.add, [f"ps{b}", XR(o)], [XR(o)])
                ln_prep_chunk(o)
                if o >= 1:
                    ln_stats_mm(o - 1)
        ln_stats_mm(NCH - 1)
        layer_norm(1, LN_EPS / (ALPHA * ALPHA), xbf, "xbf")
        ffn("w1b", "w3b", "w2b")
        layer_norm(2, LN_EPS / (ALPHA * ALPHA), xbf, "xbf")
        for half in range(2):
            iw = load_wA("wpg", half * 512, (half + 1) * 512)
            for cc in range(4):
                o = half * 4 + cc
                bg, bp = lbank(), lbank()
                for k in range(NCH):
                    mm(ps[bg][:, :], wA[iw][:, k, cc * 128:(cc + 1) * 128], xbf[:, k, :], k == 0, k == NCH - 1, [f"wA{iw}", f"xbf{k}"], [f"ps{bg}"])
                for k in range(2):
                    mm(ps[bp][:, :], wpp_s[:, k, o * 128:(o + 1) * 128], pbf[:, k, :], k == 0, k == 1, ["wpp_s", "pbf"], [f"ps{bp}"])
                ti = tmp()
                act(tmpf[ti][:, :], ps[bg][:, :], AF.Sigmoid, [f"ps{bg}"], [f"tmpf{ti}"])
                tt(tmpf[ti][:, :], tmpf[ti][:, :], ps[bp][:, :], ALU.mult, [f"tmpf{ti}", f"ps{bp}"], [f"tmpf{ti}"])
                stt(xt[:, o, :], tmpf[ti][:, :], 1.0 / ALPHA, xt[:, o, :], ALU.mult, ALU.add, [f"tmpf{ti}", XR(o)], [XR(o)])
                ln_prep_chunk(o)
                if o >= 1:
                    ln_stats_mm(o - 1)
        ln_stats_mm(NCH - 1)
        layer_norm(3, LN_EPS / (ALPHA * ALPHA), None, None)
        dma(outv[:, :, c0:c1], xt[:, :, :], XA(), [P.wr("OUT")], q=G)

    P.emit(P.rd("OUT"), es)
    pst.close()
    es.close()
    return nc


def _perm(L):
    K1 = L // 16
    T3 = K1 // 16
    n = np.arange(L)
    j = n // K1
    j2 = (n % K1) // T3
    k1 = n % T3
    return (k1 * 16 + j2) * 16 + j


_NC_CACHE = {}


def kernel(**inp):
    B, S, _ = inp["x"].shape
    L = S // 2
    f32 = np.float32
    perm = _perm(L)
    if L not in _NC_CACHE:
        _NC_CACHE[L] = build(L)
    nc = _NC_CACHE[L]

    def chunks(a):
        return np.ascontiguousarray(a.reshape(a.shape[0] // 128, 128, a.shape[1]))

    def col(v):
        return np.ascontiguousarray(v.reshape(-1, 128).T)

    inv_freq = (10000.0 ** (-np.arange(0, 32, 2, dtype=np.float32) / 32)).astype(f32)
    W = {
        "w1a": inp["ffn1_w1"][0], "w3a": inp["ffn1_w3"][0], "w2a": inp["ffn1_w2"][0], "win": inp["w_in"][0],
        "wuk": inp["w_uk"][0].reshape(256, 512), "wuv": inp["w_uv"][0].reshape(256, 512),
        "woa": inp["w_o_attn"][0], "wglu": inp["w_glu"][0], "wos": inp["w_o_ssm"][0], "wout": inp["w_out"][0],
        "w1b": inp["ffn2_w1"][0], "w3b": inp["ffn2_w3"][0], "w2b": inp["ffn2_w2"][0],
        "wpp": inp["ple_w_proj"][0], "wpg": inp["ple_w_gate"][0],
    }
    wq = inp["w_uq"][0]
    W["wuq"] = np.concatenate([wq[:, :, 0:64].reshape(384, 512), wq[:, :, 64:80].reshape(384, 128), wq[:, :, 80:96].reshape(384, 128)], axis=1)
    W = {k: np.ascontiguousarray(v, dtype=f32) for k, v in W.items()}
    lnp = np.concatenate([col(inp[f"ln{i}_{gb}"][0]) for i in (1, 2, 3, 4) for gb in ("g", "b")], axis=1).astype(f32)
    qkg = np.concatenate([col(inp["q_norm_g"][0]), col(inp["kv_norm_g"][0])], axis=1).astype(f32)
    ssd = col(inp["ssm_d"][0].reshape(512)).astype(f32)

    def ssm_pack(sfx):
        lre, lim, ldt = inp["ssm_lam_re_" + sfx][0], inp["ssm_lam_im_" + sfx][0], inp["ssm_log_dt_" + sfx][0]
        bre, bim = inp["ssm_b_re_" + sfx][0], inp["ssm_b_im_" + sfx][0]
        cre, cim = inp["ssm_c_re_" + sfx][0], inp["ssm_c_im_" + sfx][0]
        ldtb = np.broadcast_to(ldt[:, None], (32, 64))
        st = np.zeros((3, 128, 16), f32)
        rp = np.zeros((3, 128, 2048), f32)
        bp = np.zeros((2, 128, 2048), f32)
        cpad = np.zeros((2, 128, 2048), f32)
        for q in range(16):
            for gs_ in range(2):
                g = 2 * q + gs_
                for i, a in enumerate((lre, lim, ldtb)):
                    st[i, gs_ * 64:(gs_ + 1) * 64, q] = a[g]
                    rp[i, :, q * 128 + gs_ * 64:q * 128 + (gs_ + 1) * 64] = a[g][None, :]
                r0 = (g % 8) * 16
                bp[0, r0:r0 + 16, q * 128 + gs_ * 64:q * 128 + (gs_ + 1) * 64] = bre[g].T
                bp[1, r0:r0 + 16, q * 128 + gs_ * 64:q * 128 + (gs_ + 1) * 64] = bim[g].T
                cpad[0, gs_ * 64:(gs_ + 1) * 64, q * 128 + r0:q * 128 + r0 + 16] = cre[g].T
                cpad[1, gs_ * 64:(gs_ + 1) * 64, q * 128 + r0:q * 128 + r0 + 16] = cim[g].T
        return st, rp, bp, cpad

    packs = {"f": ssm_pack("f"), "b": ssm_pack("b")}
    in_maps = []
    for c in range(8):
        b, half = c // 2, c % 2
        tl = perm if half == 0 else (L - 1 - perm)
        tg = half * L + tl
        m = dict(W)
        m["xT"] = chunks(np.ascontiguousarray(inp["x"][b][tg].T))
        m["pT"] = chunks(np.ascontiguousarray(inp["p"][0, b][tg].T))
        ang = tg.astype(f32)[None, :] * inv_freq[:, None]
        m["cosT"] = np.ascontiguousarray(np.tile(np.cos(ang).astype(f32), (8, 1)))
        m["sinT"] = np.ascontiguousarray(np.tile(np.sin(ang).astype(f32), (8, 1)))
        order = ("f", "b") if half == 0 else ("b", "f")
        m["sst"] = np.stack([packs[o][0] for o in order])
        m["ssr"] = np.stack([packs[o][1] for o in order])
        m["ssb"] = np.stack([packs[o][2] for o in order])
        m["ssc"] = np.stack([packs[o][3] for o in order])
        m["lnp"], m["qkg"], m["ssd"] = lnp, qkg, ssd
        sel = np.zeros((128, 2), f32)
        sel[:, 1 - half] = 1.0
        m["selp"] = sel
        in_maps.append(m)
    res = run_bass_kernel_spmd(nc, in_maps, core_ids=list(range(8)))
    outp = np.empty((B, S, D), f32)
    for c in range(8):
        b, half = c // 2, c % 2
        tl = perm if half == 0 else (L - 1 - perm)
        tg = half * L + tl
        o = res.results[c]["outT"].reshape(D, L)
        outp[b, tg, :] = o.T
    return outp
```

```python
import math
from contextlib import ExitStack
import numpy as np
import concourse.bass as bass
import concourse.mybir as mybir
from concourse.bass_utils import run_bass_kernel_spmd

F32 = mybir.dt.float32
BF16 = mybir.dt.bfloat16
ALU = mybir.AluOpType
AF = mybir.ActivationFunctionType

D = 1024
DFF = 2816
NCH = 8
HCH = 22
NH = 8
ALPHA = 2.0 ** 0.25
LN_EPS = 1e-5
RMS_EPS = 1e-6
TT = 512
PI = math.pi


import os as _os0
SAME_ENGINE_NOSYNC = set(_os0.environ.get("NOSYNC", "pe").split(","))


class Prog:
    CH = 8000
    NDS = 16

    def __init__(self, nc):
        self.nc = nc
        self.ops = []
        self.groups = {}
        self.eng = {"pe": nc.tensor, "act": nc.scalar, "dve": nc.vector, "pool": nc.gpsimd, "sp": nc.sync}

    def add(self, eng, fn, reads=(), writes=(), dma=False, inc=None, uniq=False):
        self.ops.append(dict(eng=eng, fn=fn, reads=tuple(reads), writes=tuple(writes), dma=dma, inc=inc, bar=False, uniq=uniq))

    def barrier(self):
        self.ops.append(dict(eng=None, fn=None, reads=(), writes=(), dma=False, inc=None, bar=True, uniq=False))

    def wr(self, grp):
        g = self.groups.setdefault(grp, [])
        nm = f"{grp}#{len(g)}"
        g.append(nm)
        return nm

    def rd(self, grp):
        return list(self.groups.get(grp, []))

    def emit(self, final_res, stack):
        ops = self.ops
        n = len(ops)
        last_w = {}
        readers = {}
        deps = [None] * n
        dma_ids = []
        dma_q = {}
        last_eng = {}
        pend_bar = {}
        for i, o in enumerate(ops):
            if o["bar"]:
                bd = set(last_eng.values()) | set(dma_ids[-3 * self.NDS:])
                pend_bar = {e: set(bd) for e in self.eng}
                deps[i] = []
                continue
            d = set()
            if o["eng"] in pend_bar:
                d |= pend_bar.pop(o["eng"])
            if not o["dma"]:
                last_eng[o["eng"]] = i
            for r in o["reads"]:
                if r in last_w:
                    d.add(last_w[r])
            for w in o["writes"]:
                if w in last_w:
                    d.add(last_w[w])
                for x in readers.get(w, ()):
                    d.add(x)
            if o["dma"]:
                if not o["uniq"]:
                    ql = dma_q.setdefault(o["eng"], [])
                    if len(ql) >= self.NDS:
                        d.add(ql[len(ql) - self.NDS])
                    ql.append(i)
                dma_ids.append(i)
            d.discard(i)
            keep = []
            for x in d:
                ox = ops[x]
                if (not ox["dma"]) and ox["eng"] == o["eng"] and o["eng"] in SAME_ENGINE_NOSYNC and not o["dma"]:
                    continue
                keep.append(x)
            deps[i] = sorted(keep)
            for r in o["reads"]:
                readers.setdefault(r, []).append(i)
            for w in o["writes"]:
                last_w[w] = i
                readers[w] = []
        fin = sorted({last_w[r] for r in final_res if r in last_w})
        needed = [False] * n
        for i in range(n):
            for x in deps[i]:
                needed[x] = True
        for x in fin:
            needed[x] = True
        cnt = {e: 0 for e in self.eng}
        dcnt = {}
        sig = [None] * n
        ndq = {}
        nuniq = 0
        for i, o in enumerate(ops):
            if o["bar"]:
                continue
            if o["dma"]:
                if o["uniq"]:
                    k = ("u", nuniq)
                    nuniq += 1
                else:
                    q_ = o["eng"]
                    k = (q_, ndq.get(q_, 0) % self.NDS)
                    ndq[q_] = ndq.get(q_, 0) + 1
                dcnt[k] = dcnt.get(k, 0) + 1
                sig[i] = ("d", k, dcnt[k] * 16)
            elif needed[i]:
                cnt[o["eng"]] += 1
                sig[i] = ("e", o["eng"], cnt[o["eng"]])
        sems = {}
        for e in self.eng:
            for c in range((cnt[e] + self.CH - 1) // self.CH + 1):
                sems[("e", e, c)] = stack.enter_context(self.nc.semaphore(f"s_{e}_{c}"))
        for k in dcnt:
            sems[("d", k)] = stack.enter_context(self.nc.semaphore(f"s_d_{k[0]}_{k[1]}"))
        known = {e: {} for e in self.eng}
        snap = [None] * n

        def key_of(s):
            return ("e", s[1]) if s[0] == "e" else ("d", s[1])

        def wait(e, x):
            s = sig[x]
            kk = key_of(s)
            kn = known[e]
            if kn.get(kk, 0) >= s[2]:
                return
            h = self.eng[e]
            if s[0] == "e":
                c = (s[2] - 1) // self.CH
                h.wait_ge(sems[("e", s[1], c)], (s[2] - 1) % self.CH + 1)
            else:
                h.wait_ge(sems[("d", s[1])], s[2])
            kn[kk] = s[2]
            sn = snap[x]
            if sn:
                for k2, v2 in sn.items():
                    if kn.get(k2, 0) < v2:
                        kn[k2] = v2

        for i, o in enumerate(ops):
            if o["bar"]:
                continue
            e = o["eng"]
            for x in deps[i]:
                wait(e, x)
            ins = o["fn"]()
            s = sig[i]
            if s is not None:
                if s[0] == "e":
                    c = (s[2] - 1) // self.CH
                    ins.then_inc(sems[("e", e, c)], 1)
                else:
                    ins.then_inc(sems[("d", s[1])], 16)
                snap[i] = dict(known[e])
        for x in fin:
            wait("sp", x)


def build(L):
    NT = L // TT
    NK = 2 * L
    K1 = L // 16
    T3 = K1 // 16
    assert L % 512 == 0 and K1 % 16 == 0 and T3 >= 1
    nc = bass.Bass("TRN2", target_bir_lowering=False)
    P = Prog(nc)
    es = ExitStack()
    cur = [es]

    def din(name, shape, dt=F32):
        return nc.dram_tensor(name, list(shape), dt, kind="ExternalInput")

    def dscr(name, shape, dt):
        return nc.dram_tensor(name, list(shape), dt)

    sbn = [0]

    def sb(name, shape, dt):
        sbn[0] += 1
        return cur[0].enter_context(nc.sbuf_tensor(f"{name}_{sbn[0]}", list(shape), dt))

    xT = din("xT", [NCH, 128, L])
    pT = din("pT", [2, 128, L])
    cosT = din("cosT", [128, L])
    sinT = din("sinT", [128, L])
    wnames = {
        "w1a": (D, DFF), "w3a": (D, DFF), "w2a": (DFF, D), "win": (D, 3232), "wuq": (384, 768),
        "wuk": (256, 512), "wuv": (256, 512), "woa": (512, D), "wglu": (512, 512), "wos": (512, D),
        "wout": (D, D), "w1b": (D, DFF), "w3b": (D, DFF), "w2b": (DFF, D), "wpp": (256, D), "wpg": (D, D),
    }
    wf = {k: din(k, v) for k, v in wnames.items()}
    wb = {k: dscr(k + "_bf", v, BF16) for k, v in wnames.items()}
    lnp = din("lnp", [128, 8 * NCH])
    qkg = din("qkg", [128, 5])
    ssd = din("ssd", [128, 4])
    sst = din("sst", [2, 3, 128, 16])
    ssr = din("ssr", [2, 3, 128, 2048])
    ssb = din("ssb", [2, 2, 128, 2048])
    ssc = din("ssc", [2, 2, 128, 2048])
    selp = din("selp", [128, 2])
    out = nc.dram_tensor("outT", [NCH, 128, L], F32, kind="ExternalOutput")

    X1 = dscr("X1", [NCH, 128, L], F32)
    QT = dscr("QT", [NH, 96, L], BF16)
    NSPL = max(1, L // 1024)
    Ls = L // NSPL
    CKVs = [dscr(f"CKV{i}", [288, Ls], BF16) for i in range(NSPL)]
    CKVALLs = [dscr(f"CKVALL{i}", [2 * 288, Ls], BF16) for i in range(NSPL)]
    U32 = dscr("U32", [4, 128, L], F32)
    UBF = dscr("UBF", [4, 128, L], BF16)
    GA = dscr("GA", [NCH, 128, L], F32)
    GB = dscr("GB", [NCH, 128, L], F32)
    OT = dscr("OT", [4, 128, L], BF16)
    YS = dscr("YS", [4, 128, L], F32)
    SX = dscr("SX", [128, 32], F32)
    SXALL = dscr("SXALL", [256, 32], F32)
    import os as _os
    if _os.environ.get("DUMMY_MB"):
        DUM = dscr("DUM", [int(_os.environ["DUMMY_MB"]) * 2, 128, 1024], F32)

    ps = [es.enter_context(nc.psum_tensor(f"ps{i}", [128, 512], F32)) for i in range(8)]
    pctr = [0]

    def bank():
        b = pctr[0] % 8
        pctr[0] += 1
        return b

    V, S, T, G, SP_ = "dve", "act", "pe", "pool", "sp"

    def mm(o, lhsT, rhs, start, stop, reads, writes):
        P.add(T, lambda: nc.tensor.matmul(o, lhsT, rhs, start=start, stop=stop), reads, writes)

    def act(o, i, func, reads, writes, scale=None, bias=None):
        kw = {}
        if scale is not None:
            kw["scale"] = scale
        if bias is not None:
            kw["bias"] = bias
        P.add(S, lambda: nc.scalar.activation(out=o, in_=i, func=func, **kw), reads, writes)

    def tt(o, a, b, op, reads, writes, eng=V):
        h = nc.vector if eng == V else nc.gpsimd
        P.add(eng, lambda: h.tensor_tensor(out=o, in0=a, in1=b, op=op), reads, writes)

    def ts(o, a, s1, s2, op0, op1, reads, writes, eng=V):
        h = nc.vector if eng == V else nc.gpsimd
        if op1 is None:
            P.add(eng, lambda: h.tensor_scalar(out=o, in0=a, scalar1=s1, scalar2=None, op0=op0), reads, writes)
        else:
            P.add(eng, lambda: h.tensor_scalar(out=o, in0=a, scalar1=s1, scalar2=s2, op0=op0, op1=op1), reads, writes)

    def stt(o, a, s, b, op0, op1, reads, writes):
        P.add(V, lambda: nc.vector.scalar_tensor_tensor(out=o, in0=a, scalar=s, in1=b, op0=op0, op1=op1), reads, writes)

    def cp(o, i, reads, writes, eng=V):
        h = nc.vector if eng == V else nc.gpsimd
        P.add(eng, lambda: h.tensor_copy(out=o, in_=i), reads, writes)

    def dma(o, i, reads, writes, q=SP_, uniq=False):
        h = {"sp": nc.sync, "pool": nc.gpsimd, "act": nc.scalar}[q]
        P.add(q, lambda: h.dma_start(out=o, in_=i), reads, writes, dma=True, uniq=uniq)

    def bounds(k):
        r, c = wnames[k]
        if k in ("w1a", "w3a", "w1b", "w3b"):
            return list(range(0, c, 512)) + [c]
        if k in ("w2a", "w2b"):
            return list(range(0, c, 256)) + [c]
        if k == "win":
            return [0, 384, 672, 1184, 1696, 2208, 2720, 3232]
        if k in ("wout", "wpg"):
            return [0, 512, 1024]
        return [0, c]

    conv = []
    for g in range(6):
        conv += [("w1a", g), ("w3a", g)]
    for k in ["w2a", "win", "wuq", "wuk", "wuv", "wout", "woa", "wglu", "wos"]:
        conv += [(k, g) for g in range(len(bounds(k)) - 1)]
    for g in range(6):
        conv += [("w1b", g), ("w3b", g)]
    for k in ["w2b", "wpp", "wpg"]:
        conv += [(k, g) for g in range(len(bounds(k)) - 1)]
    for ci, (k, g) in enumerate(conv):
        bd = bounds(k)
        dma(wb[k][:, bd[g]:bd[g + 1]], wf[k][:, bd[g]:bd[g + 1]], [], [f"W_{k}_{g}"], q=G, uniq=True)

    def wres(k, c0=None, c1=None):
        bd = bounds(k)
        if c0 is None:
            return [f"W_{k}_{g}" for g in range(len(bd) - 1)]
        return [f"W_{k}_{g}" for g in range(len(bd) - 1) if bd[g] < c1 and bd[g + 1] > c0]

    def wview(k):
        return wb[k].ap().rearrange("(kc p) n -> p kc n", p=128)

    if _os.environ.get("DUMMY_MB"):
        dma(DUM[0, :, :], wf["wpg"][0:128, :], [], ["DUM"], q=G)
    ones_d = sb("ones_d", [128, 128], BF16)
    ones_q = sb("ones_q", [128, 128], BF16)
    lnp_s = sb("lnp_s", [128, 8 * NCH], F32)
    qkg_s = sb("qkg_s", [128, 5], F32)
    ssd_s = sb("ssd_s", [128, 4], F32)
    sel_s = sb("sel_s", [128, 2], F32)
    P.add(V, lambda: nc.vector.memset(ones_d[:, :], 1.0 / 1024.0), [], ["ones_d"])
    P.add(V, lambda: nc.vector.memset(ones_q[:, :], 1.0), [], ["ones_q"])
    dma(lnp_s[:, :], lnp[:, :], [], ["lnp"])
    dma(qkg_s[:, :], qkg[:, :], [], ["qkg"])
    dma(ssd_s[:, :], ssd[:, :], [], ["ssd"])
    dma(sel_s[:, :], selp[:, :], [], ["sel"])

    CM = {}

    def alloc_common(nx):
        CM["xinb"] = [sb(f"xin{i}", [128, NCH, TT], F32) for i in range(nx)]
        CM["xin"] = CM["xinb"][0]
        CM["xp"] = "xinA"
        CM["xbf"] = sb("xbf", [128, NCH, TT], BF16)
        CM["hbf"] = sb("hbf", [128, HCH, TT], BF16)
        CM["zb"] = sb("zb", [128, NCH, TT], BF16)
        CM["zq"] = sb("zq", [128, NCH, TT], BF16)
        CM["tmpf"] = [sb(f"tmpf{i}", [128, TT], F32) for i in range(3)]
        CM["mean"] = sb("mean_s", [128, TT], F32)
        CM["rstd"] = sb("rstd_s", [128, TT], F32)
        CM["wA"] = [sb(f"wA{i}", [128, 8, 512], BF16) for i in range(3)]
        CM["wB"] = [sb(f"wB{i}", [128, HCH, 256], BF16) for i in range(2)]

    def set_x(i):
        CM["xin"] = CM["xinb"][i]
        CM["xp"] = "xin" + "AB"[i]

    def XR(o):
        return f"{CM['xp']}{o}"

    def XA():
        return [XR(o) for o in range(NCH)]

    XBALL = [f"xbf{k}" for k in range(NCH)]
    lbc = [0]

    def lbank():
        b = lbc[0] % 6
        lbc[0] += 1
        return b

    def ln_prep_chunk(o):
        xt = CM["xin"]
        act(CM["zb"][:, o, :], xt[:, o, :], AF.Copy, [XR(o)], [f"zb{o}"])
        act(CM["zq"][:, o, :], xt[:, o, :], AF.Square, [XR(o)], [f"zq{o}"])

    def ln_stats_mm(o):
        mm(ps[6][:, :], ones_d[:, :], CM["zb"][:, o, :], o == 0, o == NCH - 1, ["ones_d", f"zb{o}"], ["ps6"])
        mm(ps[7][:, :], ones_d[:, :], CM["zq"][:, o, :], o == 0, o == NCH - 1, ["ones_d", f"zq{o}"], ["ps7"])

    wactr = [0]
    wbctr = [0]

    def load_wA(k, c0, c1):
        i = wactr[0] % 3
        wactr[0] += 1
        dma(CM["wA"][i][:, :, 0:c1 - c0], wview(k)[:, :, c0:c1], wres(k, c0, c1), [f"wA{i}"])
        return i

    def load_wB(k, c0, c1):
        i = wbctr[0] % 2
        wbctr[0] += 1
        dma(CM["wB"][i][:, :, 0:c1 - c0], wview(k)[:, :, c0:c1], wres(k, c0, c1), [f"wB{i}"])
        return i

    tctr = [0]

    def tmp():
        i = tctr[0] % 3
        tctr[0] += 1
        return i

    def ffn(k1, k3, k2):
        xt, xbf, hbf, tmpf, wA, wB = CM["xin"], CM["xbf"], CM["hbf"], CM["tmpf"], CM["wA"], CM["wB"]
        ngrp = (DFF + 511) // 512
        for g in range(ngrp):
            c0, c1 = g * 512, min(DFF, (g + 1) * 512)
            i1 = load_wA(k1, c0, c1)
            i3 = load_wA(k3, c0, c1)
            for cc in range((c1 - c0) // 128):
                c = g * 4 + cc
                b1, b3 = bank(), bank()
                for k in range(NCH):
                    mm(ps[b1][:, :], wA[i1][:, k, cc * 128:(cc + 1) * 128], xbf[:, k, :], k == 0, k == NCH - 1,
                       [f"wA{i1}", f"xbf{k}"], [f"ps{b1}"])
                for k in range(NCH):
                    mm(ps[b3][:, :], wA[i3][:, k, cc * 128:(cc + 1) * 128], xbf[:, k, :], k == 0, k == NCH - 1,
                       [f"wA{i3}", f"xbf{k}"], [f"ps{b3}"])
                ti = tmp()
                act(tmpf[ti][:, :], ps[b1][:, :], AF.Silu, [f"ps{b1}"], [f"tmpf{ti}"])
                tt(hbf[:, c, :], tmpf[ti][:, :], ps[b3][:, :], ALU.mult, [f"tmpf{ti}", f"ps{b3}"], [f"hbf{c}"])
        for g in range(4):
            i2 = load_wB(k2, g * 256, (g + 1) * 256)
            for oc in range(2):
                o = g * 2 + oc
                b = lbank()
                for k in range(HCH):
                    mm(ps[b][:, :], wB[i2][:, k, oc * 128:(oc + 1) * 128], hbf[:, k, :], k == 0, k == HCH - 1,
                       [f"wB{i2}", f"hbf{k}"], [f"ps{b}"])
                stt(xt[:, o, :], ps[b][:, :], 0.5 / ALPHA, xt[:, o, :], ALU.mult, ALU.add, [f"ps{b}", XR(o)], [XR(o)])
                ln_prep_chunk(o)
                if o >= 1:
                    ln_stats_mm(o - 1)
        ln_stats_mm(NCH - 1)

    def rsqrt_inplace(r, res):
        act(r, r, AF.Ln, [res], [res])
        act(r, r, AF.Exp, [res], [res], scale=-0.5)

    def layer_norm(li, eps, obf, obres):
        xt, tmpf, mean_s, rstd_s = CM["xin"], CM["tmpf"], CM["mean"], CM["rstd"]
        act(mean_s[:, :], ps[6][:, :], AF.Copy, ["ps6"], ["mean"])
        ti = tmp()
        tt(tmpf[ti][:, :], mean_s[:, :], mean_s[:, :], ALU.mult, ["mean"], [f"tmpf{ti}"])
        tt(tmpf[ti][:, :], ps[7][:, :], tmpf[ti][:, :], ALU.subtract, ["ps7", f"tmpf{ti}"], [f"tmpf{ti}"])
        ts(rstd_s[:, :], tmpf[ti][:, :], eps, None, ALU.add, None, [f"tmpf{ti}"], ["rstd"])
        rsqrt_inplace(rstd_s[:, :], "rstd")
        gcol = li * 16
        for o in range(NCH):
            eng = G if o % 4 == 3 else V
            tt(xt[:, o, :], xt[:, o, :], mean_s[:, :], ALU.subtract, [XR(o), "mean"], [XR(o)], eng=eng)
            tt(xt[:, o, :], xt[:, o, :], rstd_s[:, :], ALU.mult, [XR(o), "rstd"], [XR(o)], eng=eng)
            act(xt[:, o, :], xt[:, o, :], AF.Identity, [XR(o), "lnp"], [XR(o)],
                scale=lnp_s[:, gcol + o:gcol + o + 1], bias=lnp_s[:, gcol + 8 + o:gcol + 8 + o + 1])
            if obf is not None:
                act(obf[:, o, :], xt[:, o, :], AF.Copy, [XR(o)], [f"{obres}{o}"])

    pst = ExitStack()
    cur[0] = pst
    alloc_common(2)
    xbf, tmpf, rstd_s, wA = CM["xbf"], CM["tmpf"], CM["rstd"], CM["wA"]
    x1bf = [sb(f"x1bf{i}", [128, NCH, TT], BF16) for i in range(2)]
    xv = xT.ap().rearrange("c p n -> p c n")
    x1v = X1.ap().rearrange("c p n -> p c n")
    ql = sb("ql", [128, 3, TT], F32)
    qsq = sb("qsq", [128, 3, TT], BF16)
    cqb = sb("cqb", [128, 3, TT], BF16)
    wuq_s = sb("wuq_s", [128, 3, 768], BF16)
    qn = sb("qn", [128, 4, TT], BF16)
    cos_s = sb("cos_s", [128, TT], F32)
    sin_s = sb("sin_s", [128, TT], F32)
    r1s = sb("r1s", [128, TT], F32)
    r2s = sb("r2s", [128, TT], F32)
    ro1 = sb("ro1", [128, TT], BF16)
    ro2 = sb("ro2", [128, TT], BF16)
    u32s = sb("u32s", [128, 4, TT], F32)
    ubfs = sb("ubfs", [128, 4, TT], BF16)
    gsm = [sb(f"gsm{i}", [128, TT], F32) for i in range(2)]

    def rope(pa, pb, np_, outa, outb, ra, rb_):
        act(r1s[0:np_, :], ps[pa][0:np_, :], AF.Copy, [f"ps{pa}"], ["r1s"])
        act(r2s[0:np_, :], ps[pb][0:np_, :], AF.Copy, [f"ps{pb}"], ["r2s"])
        t0, t1 = tmp(), tmp()
        tt(tmpf[t0][0:np_, :], r1s[0:np_, :], cos_s[0:np_, :], ALU.mult, ["r1s", "cos"], [f"tmpf{t0}"])
        tt(tmpf[t1][0:np_, :], r2s[0:np_, :], sin_s[0:np_, :], ALU.mult, ["r2s", "sin"], [f"tmpf{t1}"])
        tt(outa, tmpf[t0][0:np_, :], tmpf[t1][0:np_, :], ALU.subtract, [f"tmpf{t0}", f"tmpf{t1}"], [ra])
        tt(tmpf[t0][0:np_, :], r2s[0:np_, :], cos_s[0:np_, :], ALU.mult, ["r2s", "cos"], [f"tmpf{t0}"])
        tt(tmpf[t1][0:np_, :], r1s[0:np_, :], sin_s[0:np_, :], ALU.mult, ["r1s", "sin"], [f"tmpf{t1}"])
        tt(outb, tmpf[t0][0:np_, :], tmpf[t1][0:np_, :], ALU.add, [f"tmpf{t0}", f"tmpf{t1}"], [rb_])

    def rmsnorm(src, sres, nch, dim, gcol0, dst, dres):
        act(qsq[:, 0:nch, :], src[:, 0:nch, :], AF.Square, [sres], ["qsq"])
        b = bank()
        for k in range(nch):
            mm(ps[b][:, :], ones_q[:, :], qsq[:, k, :], k == 0, k == nch - 1, ["ones_q", "qsq"], [f"ps{b}"])
        ts(rstd_s[:, :], ps[b][:, :], 1.0 / dim, RMS_EPS, ALU.mult, ALU.add, [f"ps{b}"], ["rstd"])
        rsqrt_inplace(rstd_s[:, :], "rstd")
        for k in range(nch):
            tt(src[:, k, :], src[:, k, :], rstd_s[:, :], ALU.mult, [sres, "rstd"], [sres])
            act(dst[:, k, :], src[:, k, :], AF.Copy, [sres, "qkg"], [dres], scale=qkg_s[:, gcol0 + k:gcol0 + k + 1])

    PJ = {}

    def proj(iw, cc, M=128, col0=None):
        b = bank()
        c_lo = cc * 128 if col0 is None else col0
        xb_, xr_ = PJ["buf"], PJ["res"]
        for k in range(NCH):
            mm(ps[b][0:M, :], wA[iw][:, k, c_lo:c_lo + M], xb_[:, k, :], k == 0, k == NCH - 1, [f"wA{iw}", f"{xr_}{k}"], [f"ps{b}"])
        return b

    def proj_stage(t):
        c0, c1 = t * TT, (t + 1) * TT
        PJ["buf"], PJ["res"] = x1bf[t % 2], f"x1b{t % 2}_"
        if t == 0:
            dma(wuq_s[:, :, :], wview("wuq"), wres("wuq"), ["wuq_s"])
        dma(cos_s[:, :], cosT[:, c0:c1], [], ["cos"])
        dma(sin_s[:, :], sinT[:, c0:c1], [], ["sin"])
        iw = load_wA("win", 0, 384)
        for c in range(3):
            b = proj(iw, c)
            act(ql[:, c, :], ps[b][:, :], AF.Copy, [f"ps{b}"], ["ql"])
        rmsnorm(ql, "ql", 3, 384.0, 0, cqb, "cqb")
        for c in range(4):
            b = bank()
            for k in range(3):
                mm(ps[b][:, :], wuq_s[:, k, c * 128:(c + 1) * 128], cqb[:, k, :], k == 0, k == 2, ["wuq_s", "cqb"], [f"ps{b}"])
            act(qn[:, c, :], ps[b][:, :], AF.Copy, [f"ps{b}"], [f"qn{c}"])
            dma(QT[2 * c, 0:64, c0:c1], qn[0:64, c, :], [f"qn{c}"], [P.wr("QT")], q=G)
            dma(QT[2 * c + 1, 0:64, c0:c1], qn[64:128, c, :], [f"qn{c}"], [P.wr("QT")], q=G)
        b1, b2 = bank(), bank()
        for k in range(3):
            mm(ps[b1][:, :], wuq_s[:, k, 512:640], cqb[:, k, :], k == 0, k == 2, ["wuq_s", "cqb"], [f"ps{b1}"])
        for k in range(3):
            mm(ps[b2][:, :], wuq_s[:, k, 640:768], cqb[:, k, :], k == 0, k == 2, ["wuq_s", "cqb"], [f"ps{b2}"])
        rope(b1, b2, 128, ro1[:, :], ro2[:, :], "ro1", "ro2")
        for h in range(NH):
            dma(QT[h, 64:80, c0:c1], ro1[h * 16:(h + 1) * 16, :], ["ro1"], [P.wr("QT")], q=G)
            dma(QT[h, 80:96, c0:c1], ro2[h * 16:(h + 1) * 16, :], ["ro2"], [P.wr("QT")], q=G)
        iw = load_wA("win", 384, 672)
        for c in range(2):
            b = proj(iw, c)
            act(ql[:, c, :], ps[b][:, :], AF.Copy, [f"ps{b}"], ["ql"])
        b1 = proj(iw, 0, M=16, col0=256)
        b2 = proj(iw, 0, M=16, col0=272)
        rmsnorm(ql, "ql", 2, 256.0, 3, cqb, "cqb")
        CKV = CKVs[c0 // Ls]
        s0, s1 = c0 % Ls, c0 % Ls + TT
        for k in range(2):
            dma(CKV[k * 128:(k + 1) * 128, s0:s1], cqb[:, k, :], ["cqb"], [P.wr(f"CKV{c0 // Ls}")], q=G)
        rope(b1, b2, 16, ro1[0:16, :], ro2[0:16, :], "ro1", "ro2")
        dma(CKV[256:272, s0:s1], ro1[0:16, :], ["ro1"], [P.wr(f"CKV{c0 // Ls}")], q=G)
        dma(CKV[272:288, s0:s1], ro2[0:16, :], ["ro2"], [P.wr(f"CKV{c0 // Ls}")], q=G)
        iw = load_wA("win", 672, 1184)
        for c in range(4):
            b = proj(iw, c)
            act(u32s[:, c, :], ps[b][:, :], AF.Copy, [f"ps{b}"], ["u32s"])
        cp(ubfs[:, :, :], u32s[:, :, :], ["u32s"], ["ubfs"], eng=G)
        dma(U32.ap().rearrange("c p n -> p c n")[:, :, c0:c1], u32s[:, :, :], ["u32s"], [P.wr("U32")], q=G)
        dma(UBF.ap().rearrange("c p n -> p c n")[:, :, c0:c1], ubfs[:, :, :], ["ubfs"], [P.wr("UBF")], q=G)
        gi_ = 0
        for gname, GD, base in (("G0", GA, 1184), ("G1", GB, 2208)):
            for half in range(2):
                iw = load_wA("win", base + half * 512, base + (half + 1) * 512)
                for cc in range(4):
                    c = half * 4 + cc
                    b = proj(iw, cc)
                    gb_ = gi_ % 2
                    gi_ += 1
                    act(gsm[gb_][:, :], ps[b][:, :], AF.Sigmoid, [f"ps{b}"], [f"gsm{gb_}"])
                    dma(GD[c, :, c0:c1], gsm[gb_][:, :], [f"gsm{gb_}"], [P.wr(gname)], q=G)

    set_x(0)
    dma(CM["xin"][:, :, :], xv[:, :, 0:TT], [], XA())
    act(xbf[:, :, :], CM["xin"][:, :, :], AF.Copy, XA(), XBALL)
    for t in range(NT):
        c0, c1 = t * TT, (t + 1) * TT
        set_x(t % 2)
        if t + 1 < NT:
            nb = (t + 1) % 2
            nres = [f"xin{'AB'[nb]}{o}" for o in range(NCH)]
            dma(CM["xinb"][nb][:, :, :], xv[:, :, c1:c1 + TT], [], nres)
        ffn("w1a", "w3a", "w2a")
        if t + 1 < NT:
            act(xbf[:, :, :], CM["xinb"][nb][:, :, :], AF.Copy, nres, XBALL)
        layer_norm(0, LN_EPS / (ALPHA * ALPHA), x1bf[t % 2], f"x1b{t % 2}_")
        dma(x1v[:, :, c0:c1], CM["xin"][:, :, :], XA(), [P.wr("X1")], q=G)
        if t >= 1:
            proj_stage(t - 1)
    proj_stage(NT - 1)
    P.barrier()
    pst.close()
    if _os.environ.get("STOP") == "A":
        P.emit(P.rd("X1"), es); es.close(); return nc

    for i in range(NSPL):
        P.add(G, lambda i=i: nc.gpsimd.collective_compute("AllGather", ALU.bypass, replica_groups=[[0, 1], [2, 3], [4, 5], [6, 7]],
                                                           ins=[CKVs[i].ap().opt()], outs=[CKVALLs[i].ap().opt()]),
              P.rd(f"CKV{i}"), [f"CKVALL{i}"])

    if _os.environ.get("STOP") == "X":
        P.emit([f"CKVALL{i}" for i in range(NSPL)], es); es.close(); return nc
    pst = ExitStack()
    cur[0] = pst
    bbR = [sb(f"bbR{d}", [128, 2048], BF16) for d in range(2)]
    bbI = [sb(f"bbI{d}", [128, 2048], BF16) for d in range(2)]
    ccR = [sb(f"ccR{d}", [128, 2048], BF16) for d in range(2)]
    ccI = [sb(f"ccI{d}", [128, 2048], BF16) for d in range(2)]
    pwA = [[sb(f"pwA{d}{l}", [128, 16, 17], F32) for l in range(3)] for d in range(2)]
    pwB = [[sb(f"pwB{d}{l}", [128, 16, 17], F32) for l in range(3)] for d in range(2)]
    pwN = [[sb(f"pwN{d}{l}", [128, 16, 17], F32) for l in range(3)] for d in range(2)]
    small = [sb(f"sm{i}", [128, 16], F32) for i in range(8)]
    zero_s = sb("zero_s", [128, 32], F32)
    finR = sb("finR", [128, 32], F32)
    gat = sb("gat", [128, 2, 32], F32)
    iniS = sb("iniS", [128, 32], F32)
    P.add(V, lambda: nc.vector.memset(zero_s[:, :], 0.0), [], ["zero_s"])
    pst2 = ExitStack()
    cur[0] = pst2
    pt = [sb(f"pt{i}", [128, 2048], F32) for i in range(7)]

    I32 = mybir.dt.int32
    isml = sb("isml", [128, 16], I32)
    ibig = sb("ibig", [128, 2048], I32)

    def sincos(zi, zres, so, co, res_s, res_c, scratch, sres, itile, ires):
        for shift, dst, dres in ((0.0, so, res_s), (0.5 * PI, co, res_c)):
            ts(scratch, zi, shift, 1.0 / (2 * PI), ALU.add, ALU.mult, [zres], [sres])
            cp(itile, scratch, [sres], [ires])
            cp(scratch, itile, [ires], [sres])
            ts(scratch, scratch, -2 * PI, None, ALU.mult, None, [sres], [sres])
            ts(dst, zi, shift, None, ALU.add, None, [zres], [dres])
            tt(dst, dst, scratch, ALU.add, [dres, sres], [dres])
            ts(scratch, dst, PI, 2 * PI, ALU.is_gt, ALU.mult, [dres], [sres])
            tt(dst, dst, scratch, ALU.subtract, [dres, sres], [dres])
            ts(scratch, dst, -PI, 2 * PI, ALU.is_lt, ALU.mult, [dres], [sres])
            tt(dst, dst, scratch, ALU.add, [dres, sres], [dres])
            act(dst, dst, AF.Sin, [dres], [dres])

    for d in range(2):
        lre, lim, ldt, zr, zi, mg, sn, cs = [small[i][:, :] for i in range(8)]
        dma(lre, sst[d, 0, :, :], [], ["sm0"])
        dma(lim, sst[d, 1, :, :], [], ["sm1"])
        dma(ldt, sst[d, 2, :, :], [], ["sm2"])
        act(ldt, ldt, AF.Exp, ["sm2"], ["sm2"])
        tt(zr, lre, ldt, ALU.mult, ["sm0", "sm2"], ["sm3"])
        tt(zi, lim, ldt, ALU.mult, ["sm1", "sm2"], ["sm4"])
        act(mg, zr, AF.Exp, ["sm3"], ["sm5"])
        sincos(zi, "sm4", sn, cs, "sm6", "sm7", zr, "sm3", isml[:, :], "isml")
        for l in range(3):
            A, B, N = pwA[d][l], pwB[d][l], pwN[d][l]
            rA, rB, rN = f"pwA{d}{l}", f"pwB{d}{l}", f"pwN{d}{l}"
            P.add(V, lambda A=A: nc.vector.memset(A[:, :, 0:1], 1.0), [], [rA])
            P.add(V, lambda B=B: nc.vector.memset(B[:, :, 0:1], 0.0), [], [rB])
            if l == 0:
                tt(A[:, :, 1], mg, cs, ALU.mult, ["sm5", "sm7"], [rA])
                tt(B[:, :, 1], mg, sn, ALU.mult, ["sm5", "sm6"], [rB])
            else:
                cp(A[:, :, 1], pwA[d][l - 1][:, :, 16], [f"pwA{d}{l-1}"], [rA])
                cp(B[:, :, 1], pwB[d][l - 1][:, :, 16], [f"pwB{d}{l-1}"], [rB])
            for j in range(2, 17):
                tt(zr, A[:, :, j - 1], A[:, :, 1], ALU.mult, [rA], ["sm3"])
                tt(zi, B[:, :, j - 1], B[:, :, 1], ALU.mult, [rB], ["sm4"])
                tt(A[:, :, j], zr, zi, ALU.subtract, ["sm3", "sm4"], [rA])
                tt(zr, A[:, :, j - 1], B[:, :, 1], ALU.mult, [rA, rB], ["sm3"])
                tt(zi, B[:, :, j - 1], A[:, :, 1], ALU.mult, [rA, rB], ["sm4"])
                tt(B[:, :, j], zr, zi, ALU.add, ["sm3", "sm4"], [rB])
            ts(N[:, :, :], B[:, :, :], -1.0, None, ALU.mult, None, [rB], [rN])
        LR, LI, DT, t0, t1, t2, t3 = [pt[i][:, :] for i in range(7)]
        dma(LR, ssr[d, 0, :, :], [], ["pt0"])
        dma(LI, ssr[d, 1, :, :], [], ["pt1"])
        dma(DT, ssr[d, 2, :, :], [], ["pt2"])
        act(DT, DT, AF.Exp, ["pt2"], ["pt2"])
        tt(t0, LR, DT, ALU.mult, ["pt0", "pt2"], ["pt3"])
        tt(t1, LI, DT, ALU.mult, ["pt1", "pt2"], ["pt4"])
        act(t0, t0, AF.Exp, ["pt3"], ["pt3"])
        sincos(t1, "pt4", t2, t3, "pt5", "pt6", DT, "pt2", ibig[:, :], "ibig")
        tt(t2, t2, t0, ALU.mult, ["pt5", "pt3"], ["pt5"])
        tt(t3, t3, t0, ALU.mult, ["pt6", "pt3"], ["pt6"])
        ts(t3, t3, -1.0, None, ALU.add, None, ["pt6"], ["pt6"])
        tt(t0, LR, LR, ALU.mult, ["pt0"], ["pt3"])
        tt(t1, LI, LI, ALU.mult, ["pt1"], ["pt4"])
        tt(t0, t0, t1, ALU.add, ["pt3", "pt4"], ["pt3"])
        P.add(V, lambda t0=t0: nc.vector.reciprocal(out=t0, in_=t0), ["pt3"], ["pt3"])
        tt(t1, t3, LR, ALU.mult, ["pt6", "pt0"], ["pt4"])
        tt(DT, t2, LI, ALU.mult, ["pt5", "pt1"], ["pt2"])
        tt(t1, t1, DT, ALU.add, ["pt4", "pt2"], ["pt4"])
        tt(t1, t1, t0, ALU.mult, ["pt4", "pt3"], ["pt4"])
        tt(DT, t2, LR, ALU.mult, ["pt5", "pt0"], ["pt2"])
        tt(LR, t3, LI, ALU.mult, ["pt6", "pt1"], ["pt0"])
        tt(DT, DT, LR, ALU.subtract, ["pt2", "pt0"], ["pt2"])
        tt(DT, DT, t0, ALU.mult, ["pt2", "pt3"], ["pt2"])
        dma(LR, ssb[d, 0, :, :], [], ["pt0"])
        dma(LI, ssb[d, 1, :, :], [], ["pt1"])
        tt(t0, t1, LR, ALU.mult, ["pt4", "pt0"], ["pt3"])
        tt(t2, DT, LI, ALU.mult, ["pt2", "pt1"], ["pt5"])
        tt(bbR[d][:, :], t0, t2, ALU.subtract, ["pt3", "pt5"], [f"bbR{d}"])
        tt(t0, t1, LI, ALU.mult, ["pt4", "pt1"], ["pt3"])
        tt(t2, DT, LR, ALU.mult, ["pt2", "pt0"], ["pt5"])
        tt(bbI[d][:, :], t0, t2, ALU.add, ["pt3", "pt5"], [f"bbI{d}"])
        dma(t3, ssc[d, 0, :, :], [], ["pt6"])
        cp(ccR[d][:, :], t3, ["pt6"], [f"ccR{d}"])
        dma(t3, ssc[d, 1, :, :], [], ["pt6"])
        ts(ccI[d][:, :], t3, -1.0, None, ALU.mult, None, ["pt6"], [f"ccI{d}"])
    P.barrier()
    pst2.close()
    cur[0] = pst
    Rt = sb("Rst", [128, L], F32)
    It = sb("Ist", [128, L], F32)
    Rbf = sb("Rbf", [128, L], BF16)
    Ibf = sb("Ibf", [128, L], BF16)
    ubig = [sb(f"ubig{i}", [128, L], BF16) for i in range(2)]
    yacc = sb("yacc", [128, L], F32)
    e2R = sb("e2R", [128, 16, T3], F32)
    e2I = sb("e2I", [128, 16, T3], F32)
    e3R = sb("e3R", [128, T3], F32)
    e3I = sb("e3I", [128, T3], F32)
    x2pR = sb("x2pR", [128, T3], F32)
    x2pI = sb("x2pI", [128, T3], F32)
    x1pR = sb("x1pR", [128, 16, T3], F32)
    x1pI = sb("x1pI", [128, 16, T3], F32)
    NKC = NK // 128
    ckt = [sb(f"ckt{i}", [128, 2, 512], BF16) for i in range(3)]
    Kt = sb("Kt", [96, NK], BF16)
    Va = sb("Va", [128, NKC, 128], BF16)
    qts = sb("qts", [96, L], BF16)
    wuk_s = sb("wuk_s", [128, 2, 512], BF16)
    wuv_s = sb("wuv_s", [128, 2, 512], BF16)
    Pb = [sb(f"Pb{i}", [128, 1024], BF16) for i in range(2)]
    rcs = sb("rcs", [64, TT], F32)
    ob = sb("ob", [64, TT], BF16)

    b67 = [0]

    def bank67():
        b67[0] += 1
        return 6 + (b67[0] % 2)

    def cmul_acc(oR, oI, pR, pI, a, b, nb, res_o, res_p, extra_reads=()):
        ex = list(extra_reads)
        roR, roI, rpR, rpI = res_o + "R", res_o + "I", res_p + "R", res_p + "I"
        stt(oR, pR, a, oR, ALU.mult, ALU.add, [roR, rpR] + ex, [roR])
        stt(oI, pR, b, oI, ALU.mult, ALU.add, [roI, rpR] + ex, [roI])
        stt(oR, pI, nb, oR, ALU.mult, ALU.add, [roR, rpI] + ex, [roR])
        stt(oI, pI, a, oI, ALU.mult, ALU.add, [roI, rpI] + ex, [roI])

    def blk(tn, j):
        return tn[:, j * K1:(j + 1) * K1]

    def blkres(t, ri):
        j0_, j1_ = (t * TT) // K1, ((t + 1) * TT - 1) // K1
        return [f"b{j}{ri}" for j in range(j0_, j1_ + 1)]

    def blk3(tn, j):
        return tn[:, j * K1:(j + 1) * K1].rearrange("p (a b) -> p a b", b=T3)

    def s5_main():
        for pas in range(2):
            d = pas
            rev = (pas == 1)

            def ix(i, n, rev=rev):
                return (n - 1 - i) if rev else i

            ini = zero_s if pas == 0 else iniS
            ini_res = "zero_s" if pas == 0 else "iniS"
            if pas == 1:
                dma(SX[:, :], finR[:, :], ["finR"], ["SX"], q=G)
                P.add(G, lambda: nc.gpsimd.collective_compute("AllGather", ALU.bypass, replica_groups=[[0, 1], [2, 3], [4, 5], [6, 7]],
                                                               ins=[SX.ap().opt()], outs=[SXALL.ap().opt()]),
                      ["SX"], ["SXALL"])
                dma(gat[:, :, :], SXALL.ap().rearrange("(r p) n -> p r n", p=128), ["SXALL"], ["gat"])
                ts(iniS[:, :], gat[:, 0, :], sel_s[:, 0:1], None, ALU.mult, None, ["gat", "sel"], ["iniS"])
                stt(iniS[:, :], gat[:, 1, :], sel_s[:, 1:2], iniS[:, :], ALU.mult, ALU.add, ["gat", "sel", "iniS"], ["iniS"])
            for c in range(4):
                ub = ubig[c % 2]
                ubr = f"ubig{c % 2}"
                dma(ub[:, :], UBF[c, :, :], P.rd("UBF"), [ubr])
                if pas == 0:
                    dma(yacc[:, :], U32[c, :, :], P.rd("U32"), ["yacc"])
                    ts(yacc[:, :], yacc[:, :], ssd_s[:, c:c + 1], None, ALU.mult, None, ["yacc", "ssd"], ["yacc"])
                else:
                    dma(yacc[:, :], YS[c, :, :], [f"YS{c}"], ["yacc"])
                for qq in range(4):
                    q = c * 4 + qq
                    A0, B0, N0 = pwA[d][0], pwB[d][0], pwN[d][0]
                    A1, B1, N1 = pwA[d][1], pwB[d][1], pwN[d][1]
                    A2, B2, N2 = pwA[d][2], pwB[d][2], pwN[d][2]
                    pres = [f"pw{x}{d}{l}" for x in "ABN" for l in range(3)]
                    for t in range(NT):
                        bR, bI = bank67(), bank67()
                        mm(ps[bR][:, :], bbR[d][:, q * 128:(q + 1) * 128], ub[:, t * TT:(t + 1) * TT], True, True, [f"bbR{d}", ubr], [f"ps{bR}"])
                        mm(ps[bI][:, :], bbI[d][:, q * 128:(q + 1) * 128], ub[:, t * TT:(t + 1) * TT], True, True, [f"bbI{d}", ubr], [f"ps{bI}"])
                        act(Rt[:, t * TT:(t + 1) * TT], ps[bR][:, :], AF.Copy, [f"ps{bR}"], blkres(t, "R"))
                        act(It[:, t * TT:(t + 1) * TT], ps[bI][:, :], AF.Copy, [f"ps{bI}"], blkres(t, "I"))
                    yield 12
                    for j in range(1, 16):
                        jc, jp = ix(j, 16), ix(j - 1, 16)
                        cmul_acc(blk(Rt, jc), blk(It, jc), blk(Rt, jp), blk(It, jp),
                                 A0[:, q, 1:2], B0[:, q, 1:2], N0[:, q, 1:2], f"b{jc}", f"b{jp}", pres)
                        if j % 5 == 0:
                            yield 8
                    jl = ix(15, 16)
                    cp(e2R[:, :, :], blk3(Rt, jl), [f"b{jl}R"], [f"e2_{j}R" for j in range(16)])
                    cp(e2I[:, :, :], blk3(It, jl), [f"b{jl}I"], [f"e2_{j}I" for j in range(16)])
                    for j in range(1, 16):
                        jc, jp = ix(j, 16), ix(j - 1, 16)
                        cmul_acc(e2R[:, jc, :], e2I[:, jc, :], e2R[:, jp, :], e2I[:, jp, :],
                                 A1[:, q, 1:2], B1[:, q, 1:2], N1[:, q, 1:2], f"e2_{jc}", f"e2_{jp}", pres)
                    yield 8
                    cp(e3R[:, :], e2R[:, jl, :], [f"e2_{jl}R"], [f"e3_{k}R" for k in range(T3)])
                    cp(e3I[:, :], e2I[:, jl, :], [f"e2_{jl}I"], [f"e3_{k}I" for k in range(T3)])
                    for k in range(T3):
                        kc = ix(k, T3)
                        if k == 0:
                            pR, pI = ini[:, q:q + 1], ini[:, 16 + q:16 + q + 1]
                            rp_ = "ini"
                        else:
                            kp = ix(k - 1, T3)
                            pR, pI = e3R[:, kp:kp + 1], e3I[:, kp:kp + 1]
                            rp_ = f"e3_{kp}"
                        cmul_acc(e3R[:, kc:kc + 1], e3I[:, kc:kc + 1], pR, pI, A2[:, q, 1:2], B2[:, q, 1:2], N2[:, q, 1:2],
                                 f"e3_{kc}", rp_, pres + [ini_res])
                    yield 8
                    kl = ix(T3 - 1, T3)
                    if pas == 0:
                        cp(finR[:, q:q + 1], e3R[:, kl:kl + 1], [f"e3_{kl}R"], ["finR"])
                        cp(finR[:, 16 + q:16 + q + 1], e3I[:, kl:kl + 1], [f"e3_{kl}I"], ["finR"])
                    k0 = ix(0, T3)
                    e3allR = [f"e3_{k}R" for k in range(T3)]
                    e3allI = [f"e3_{k}I" for k in range(T3)]
                    cp(x2pR[:, k0:k0 + 1], ini[:, q:q + 1], [ini_res], ["x2pR"])
                    cp(x2pI[:, k0:k0 + 1], ini[:, 16 + q:16 + q + 1], [ini_res], ["x2pI"])
                    if T3 > 1:
                        if not rev:
                            cp(x2pR[:, 1:T3], e3R[:, 0:T3 - 1], e3allR, ["x2pR"])
                            cp(x2pI[:, 1:T3], e3I[:, 0:T3 - 1], e3allI, ["x2pI"])
                        else:
                            cp(x2pR[:, 0:T3 - 1], e3R[:, 1:T3], e3allR, ["x2pR"])
                            cp(x2pI[:, 0:T3 - 1], e3I[:, 1:T3], e3allI, ["x2pI"])
                    for j in range(16):
                        jc = ix(j, 16)
                        cmul_acc(e2R[:, jc, :], e2I[:, jc, :], x2pR[:, :], x2pI[:, :],
                                 A1[:, q, j + 1:j + 2], B1[:, q, j + 1:j + 2], N1[:, q, j + 1:j + 2], f"e2_{jc}", "x2p", pres)
                    yield 8
                    j0 = ix(0, 16)
                    e2allR = [f"e2_{j}R" for j in range(16)]
                    e2allI = [f"e2_{j}I" for j in range(16)]
                    cp(x1pR[:, j0, :], x2pR[:, :], ["x2pR"], ["x1pR"])
                    cp(x1pI[:, j0, :], x2pI[:, :], ["x2pI"], ["x1pI"])
                    if not rev:
                        cp(x1pR[:, 1:16, :], e2R[:, 0:15, :], e2allR, ["x1pR"])
                        cp(x1pI[:, 1:16, :], e2I[:, 0:15, :], e2allI, ["x1pI"])
                    else:
                        cp(x1pR[:, 0:15, :], e2R[:, 1:16, :], e2allR, ["x1pR"])
                        cp(x1pI[:, 0:15, :], e2I[:, 1:16, :], e2allI, ["x1pI"])
                    for j in range(16):
                        jc = ix(j, 16)
                        cmul_acc(blk3(Rt, jc), blk3(It, jc), x1pR[:, :, :], x1pI[:, :, :],
                                 A0[:, q, j + 1:j + 2], B0[:, q, j + 1:j + 2], N0[:, q, j + 1:j + 2], f"b{jc}", "x1p", pres)
                        if j % 5 == 4:
                            yield 8
                    act(Rbf[:, :], Rt[:, :], AF.Copy, [f"b{j}R" for j in range(16)], ["Rbf"])
                    cp(Ibf[:, :], It[:, :], [f"b{j}I" for j in range(16)], ["Ibf"], eng=G)
                    yield 6
                    for t in range(NT):
                        b = bank67()
                        mm(ps[b][:, :], ccR[d][:, q * 128:(q + 1) * 128], Rbf[:, t * TT:(t + 1) * TT], True, False, [f"ccR{d}", "Rbf"], [f"ps{b}"])
                        mm(ps[b][:, :], ccI[d][:, q * 128:(q + 1) * 128], Ibf[:, t * TT:(t + 1) * TT], False, True, [f"ccI{d}", "Ibf"], [f"ps{b}"])
                        tt(yacc[:, t * TT:(t + 1) * TT], yacc[:, t * TT:(t + 1) * TT], ps[b][:, :], ALU.add, ["yacc", f"ps{b}"], ["yacc"])
                    yield 6
                dma(YS[c, :, :], yacc[:, :], ["yacc"], [f"YS{c}"], q=G)

    scale = 96.0 ** -0.5

    def attn_main():
        for i in range(NSPL):
            ckall = CKVALLs[i].ap().rearrange("(r f) n -> r f n", r=2)
            for r in range(2):
                o0 = (i * 2 + r) * Ls
                dma(Kt[64:96, o0:o0 + Ls], ckall[r, 256:288, :], [f"CKVALL{i}"], [P.wr("Ktr")])
        dma(wuk_s[:, :, :], wview("wuk"), wres("wuk"), ["wuk_s"])
        dma(wuv_s[:, :, :], wview("wuv"), wres("wuv"), ["wuv_s"])
        P.add(G, lambda: nc.gpsimd.memset(Va[:, :, :], 1.0), [], ["Va"])
        yield 2
        cki = 0
        for h in range(NH):
            for kt in range(NK // 512):
                blkno = (kt * 512) // Ls
                i, r = blkno // 2, blkno % 2
                coff = kt * 512 - blkno * Ls
                ci = cki % 3
                cki += 1
                src = CKVALLs[i].ap()[r * 288:r * 288 + 256, coff:coff + 512].rearrange("(k p) n -> p k n", p=128)
                dma(ckt[ci][:, :, :], src, [f"CKVALL{i}"], [f"ckt{ci}"])
                b = bank67()
                for k in range(2):
                    mm(ps[b][0:64, :], wuk_s[:, k, h * 64:(h + 1) * 64], ckt[ci][:, k, :], k == 0, k == 1, ["wuk_s", f"ckt{ci}"], [f"ps{b}"])
                cp(Kt[0:64, kt * 512:(kt + 1) * 512], ps[b][0:64, :], [f"ps{b}"], ["Kt"])
                b = bank67()
                for j in range(4):
                    for k in range(2):
                        mm(ps[b][:, j * 64:(j + 1) * 64], ckt[ci][:, k, j * 128:(j + 1) * 128], wuv_s[:, k, h * 64:(h + 1) * 64], k == 0, k == 1, ["wuv_s", f"ckt{ci}"], [f"ps{b}"])
                cp(Va[:, kt * 4:(kt + 1) * 4, 0:64], ps[b][:, 0:256].rearrange("p (a b) -> p a b", b=64), [f"ps{b}"], ["Va"])
                if kt % 4 == 3:
                    yield 3
            dma(qts[:, :], QT[h, :, :], P.rd("QT"), ["qts"])
            for qt_ in range(L // 512):
                bo = 4 + (qt_ % 2)
                q0 = qt_ * 512
                npair = NKC // 2

                def s_mm(i):
                    p = i % 2
                    for j in range(2):
                        kc = 2 * i + j
                        mm(ps[2 * p + j][:, :], Kt[0:96, kc * 128:(kc + 1) * 128], qts[0:96, q0:q0 + 512], True, True, ["Kt", "qts"] + P.rd("Ktr"), [f"ps{2*p+j}"])

                def s_exp(i):
                    p = i % 2
                    for j in range(2):
                        act(Pb[p][:, j * 512:(j + 1) * 512], ps[2 * p + j][:, :], AF.Exp, [f"ps{2*p+j}"], [f"Pb{p}"], scale=scale)

                def pv(i):
                    p = i % 2
                    for j in range(2):
                        kc = 2 * i + j
                        mm(ps[bo][:, :], Va[:, kc, :], Pb[p][:, j * 512:(j + 1) * 512], kc == 0, kc == NKC - 1, ["Va", f"Pb{p}"], [f"ps{bo}"])

                s_mm(0)
                s_exp(0)
                for i in range(npair):
                    if i + 1 < npair:
                        s_mm(i + 1)
                        s_exp(i + 1)
                    pv(i)
                    if i % 8 == 7:
                        yield 10
                act(rcs[0:64, :], ps[bo][64:128, :], AF.Ln, [f"ps{bo}"], ["rcs"])
                act(rcs[0:64, :], rcs[0:64, :], AF.Exp, ["rcs"], ["rcs"], scale=-1.0)
                tt(ob[:, :], ps[bo][0:64, :], rcs[:, :], ALU.mult, [f"ps{bo}", "rcs"], ["ob"])
                dma(OT[h // 2, (h % 2) * 64:(h % 2) * 64 + 64, q0:q0 + 512], ob[:, :], ["ob"], [P.wr("OT")], q=G)
                yield 1

    gens = [s5_main(), attn_main()]
    tacc = [0.0, 0.0]
    alive = [True, True]
    if _os.environ.get("NOOVERLAP"):
        for g_ in gens:
            for _ in g_:
                pass
    else:
        while any(alive):
            gi = min((i for i in range(2) if alive[i]), key=lambda i: tacc[i])
            try:
                tacc[gi] += next(gens[gi])
            except StopIteration:
                alive[gi] = False
    P.barrier()
    pst.close()

    if _os.environ.get("STOP") == "B":
        P.emit(P.rd("OT"), es); es.close(); return nc
    pst = ExitStack()
    cur[0] = pst
    alloc_common(1)
    set_x(0)
    xt, xbf, tmpf, wA = CM["xin"], CM["xbf"], CM["tmpf"], CM["wA"]
    woa_s = sb("woa_s", [128, 4, D], BF16)
    wos_s = sb("wos_s", [128, 4, D], BF16)
    wglu_s = sb("wglu_s", [128, 4, 512], BF16)
    wpp_s = sb("wpp_s", [128, 2, D], BF16)
    dma(woa_s[:, :, :], wview("woa"), wres("woa"), ["woa_s"])
    dma(wos_s[:, :, :], wview("wos"), wres("wos"), ["wos_s"])
    dma(wglu_s[:, :, :], wview("wglu"), wres("wglu"), ["wglu_s"])
    dma(wpp_s[:, :, :], wview("wpp"), wres("wpp"), ["wpp_s"])
    ots = sb("ots", [128, 4, TT], BF16)
    ysf = sb("ysf", [128, 4, TT], F32)
    zf = sb("zf", [128, 4, TT], F32)
    zbf4 = sb("zbf4", [128, 4, TT], BF16)
    zzb = sb("zzb", [128, 4, TT], BF16)
    gaf = [sb(f"gaf{i}", [128, TT], F32) for i in range(2)]
    gbf = [sb(f"gbf{i}", [128, TT], F32) for i in range(2)]
    mrb = sb("mrb", [128, NCH, TT], BF16)
    pf = sb("pf", [128, 2, TT], F32)
    pbf = sb("pbf", [128, 2, TT], BF16)
    outv = out.ap().rearrange("c p n -> p c n")
    pv_ = pT.ap().rearrange("c p n -> p c n")
    GC = 1.5957691216057308
    ysall = [f"YS{c}" for c in range(4)]
    for t in range(NT):
        c0, c1 = t * TT, (t + 1) * TT
        dma(xt[:, :, :], x1v[:, :, c0:c1], P.rd("X1"), XA())
        dma(ots[:, :, :], OT.ap().rearrange("c p n -> p c n")[:, :, c0:c1], P.rd("OT"), ["ots"])
        dma(ysf[:, :, :], YS.ap().rearrange("c p n -> p c n")[:, :, c0:c1], ysall, ["ysf"])
        dma(pf[:, :, :], pv_[:, :, c0:c1], [], ["pf"])
        cp(pbf[:, :, :], pf[:, :, :], ["pf"], ["pbf"], eng=G)
        tt(zf[:, :, :], ysf[:, :, :], ysf[:, :, :], ALU.mult, ["ysf"], ["zf"])
        ts(zf[:, :, :], zf[:, :, :], 0.044715, 1.0, ALU.mult, ALU.add, ["zf"], ["zf"])
        tt(zf[:, :, :], zf[:, :, :], ysf[:, :, :], ALU.mult, ["zf", "ysf"], ["zf"])
        act(zf[:, :, :], zf[:, :, :], AF.Sigmoid, ["zf"], ["zf"], scale=GC)
        tt(zf[:, :, :], zf[:, :, :], ysf[:, :, :], ALU.mult, ["zf", "ysf"], ["zf"])
        cp(zbf4[:, :, :], zf[:, :, :], ["zf"], ["zbf4"], eng=G)
        for c in range(4):
            b = bank()
            for k in range(4):
                mm(ps[b][:, :], wglu_s[:, k, c * 128:(c + 1) * 128], zbf4[:, k, :], k == 0, k == 3, ["wglu_s", "zbf4"], [f"ps{b}"])
            ti = tmp()
            act(tmpf[ti][:, :], ps[b][:, :], AF.Sigmoid, [f"ps{b}"], [f"tmpf{ti}"])
            tt(zzb[:, c, :], zf[:, c, :], tmpf[ti][:, :], ALU.mult, ["zf", f"tmpf{ti}"], ["zzb"])
        for o in range(NCH):
            gi_ = o % 2
            dma(gaf[gi_][:, :], GA[o, :, c0:c1], P.rd("G0"), [f"gaf{gi_}"])
            dma(gbf[gi_][:, :], GB[o, :, c0:c1], P.rd("G1"), [f"gbf{gi_}"])
            ba, bb = bank(), bank()
            for k in range(4):
                mm(ps[ba][:, :], woa_s[:, k, o * 128:(o + 1) * 128], ots[:, k, :], k == 0, k == 3, ["woa_s", "ots"], [f"ps{ba}"])
            for k in range(4):
                mm(ps[bb][:, :], wos_s[:, k, o * 128:(o + 1) * 128], zzb[:, k, :], k == 0, k == 3, ["wos_s", "zzb"], [f"ps{bb}"])
            tt(gaf[gi_][:, :], gaf[gi_][:, :], ps[ba][:, :], ALU.mult, [f"gaf{gi_}", f"ps{ba}"], [f"gaf{gi_}"])
            tt(gbf[gi_][:, :], gbf[gi_][:, :], ps[bb][:, :], ALU.mult, [f"gbf{gi_}", f"ps{bb}"], [f"gbf{gi_}"])
            tt(mrb[:, o, :], gaf[gi_][:, :], gbf[gi_][:, :], ALU.add, [f"gaf{gi_}", f"gbf{gi_}"], ["mrb"], eng=G)
        for half in range(2):
            iw = load_wA("wout", half * 512, (half + 1) * 512)
            for cc in range(4):
                o = half * 4 + cc
                b = lbank()
                for k in range(NCH):
                    mm(ps[b][:, :], wA[iw][:, k, cc * 128:(cc + 1) * 128], mrb[:, k, :], k == 0, k == NCH - 1, [f"wA{iw}", "mrb"], [f"ps{b}"])
                stt(xt[:, o, :], ps[b][:, :], 1.0 / ALPHA, xt[:, o, :], ALU.mult, ALU.add, [f"ps{b}", XR(o)], [XR(o)])
                ln_prep_chunk(o)
                if o >= 1:
                    ln_stats_mm(o - 1)
        ln_stats_mm(NCH - 1)
        layer_norm(1, LN_EPS / (ALPHA * ALPHA), xbf, "xbf")
        ffn("w1b", "w3b", "w2b")
        layer_norm(2, LN_EPS / (ALPHA * ALPHA), xbf, "xbf")
        for half in range(2):
            iw = load_wA("wpg", half * 512, (half + 1) * 512)
            for cc in range(4):
                o = half * 4 + cc
                bg, bp = lbank(), lbank()
                for k in range(NCH):
                    mm(ps[bg][:, :], wA[iw][:, k, cc * 128:(cc + 1) * 128], xbf[:, k, :], k == 0, k == NCH - 1, [f"wA{iw}", f"xbf{k}"], [f"ps{bg}"])
                for k in range(2):
                    mm(ps[bp][:, :], wpp_s[:, k, o * 128:(o + 1) * 128], pbf[:, k, :], k == 0, k == 1, ["wpp_s", "pbf"], [f"ps{bp}"])
                ti = tmp()
                act(tmpf[ti][:, :], ps[bg][:, :], AF.Sigmoid, [f"ps{bg}"], [f"tmpf{ti}"])
                tt(tmpf[ti][:, :], tmpf[ti][:, :], ps[bp][:, :], ALU.mult, [f"tmpf{ti}", f"ps{bp}"], [f"tmpf{ti}"])
                stt(xt[:, o, :], tmpf[ti][:, :], 1.0 / ALPHA, xt[:, o, :], ALU.mult, ALU.add, [f"tmpf{ti}", XR(o)], [XR(o)])
                ln_prep_chunk(o)
                if o >= 1:
                    ln_stats_mm(o - 1)
        ln_stats_mm(NCH - 1)
        layer_norm(3, LN_EPS / (ALPHA * ALPHA), None, None)
        dma(outv[:, :, c0:c1], xt[:, :, :], XA(), [P.wr("OUT")], q=G)

    P.emit(P.rd("OUT"), es)
    pst.close()
    es.close()
    return nc


def _perm(L):
    K1 = L // 16
    T3 = K1 // 16
    n = np.arange(L)
    j = n // K1
    j2 = (n % K1) // T3
    k1 = n % T3
    return (k1 * 16 + j2) * 16 + j


_NC_CACHE = {}


def kernel(**inp):
    B, S, _ = inp["x"].shape
    L = S // 2
    f32 = np.float32
    perm = _perm(L)
    if L not in _NC_CACHE:
        _NC_CACHE[L] = build(L)
    nc = _NC_CACHE[L]

    def chunks(a):
        return np.ascontiguousarray(a.reshape(a.shape[0] // 128, 128, a.shape[1]))

    def col(v):
        return np.ascontiguousarray(v.reshape(-1, 128).T)

    inv_freq = (10000.0 ** (-np.arange(0, 32, 2, dtype=np.float32) / 32)).astype(f32)
    W = {
        "w1a": inp["ffn1_w1"][0], "w3a": inp["ffn1_w3"][0], "w2a": inp["ffn1_w2"][0], "win": inp["w_in"][0],
        "wuk": inp["w_uk"][0].reshape(256, 512), "wuv": inp["w_uv"][0].reshape(256, 512),
        "woa": inp["w_o_attn"][0], "wglu": inp["w_glu"][0], "wos": inp["w_o_ssm"][0], "wout": inp["w_out"][0],
        "w1b": inp["ffn2_w1"][0], "w3b": inp["ffn2_w3"][0], "w2b": inp["ffn2_w2"][0],
        "wpp": inp["ple_w_proj"][0], "wpg": inp["ple_w_gate"][0],
    }
    wq = inp["w_uq"][0]
    W["wuq"] = np.concatenate([wq[:, :, 0:64].reshape(384, 512), wq[:, :, 64:80].reshape(384, 128), wq[:, :, 80:96].reshape(384, 128)], axis=1)
    W = {k: np.ascontiguousarray(v, dtype=f32) for k, v in W.items()}
    lnp = np.concatenate([col(inp[f"ln{i}_{gb}"][0]) for i in (1, 2, 3, 4) for gb in ("g", "b")], axis=1).astype(f32)
    qkg = np.concatenate([col(inp["q_norm_g"][0]), col(inp["kv_norm_g"][0])], axis=1).astype(f32)
    ssd = col(inp["ssm_d"][0].reshape(512)).astype(f32)

    def ssm_pack(sfx):
        lre, lim, ldt = inp["ssm_lam_re_" + sfx][0], inp["ssm_lam_im_" + sfx][0], inp["ssm_log_dt_" + sfx][0]
        bre, bim = inp["ssm_b_re_" + sfx][0], inp["ssm_b_im_" + sfx][0]
        cre, cim = inp["ssm_c_re_" + sfx][0], inp["ssm_c_im_" + sfx][0]
        ldtb = np.broadcast_to(ldt[:, None], (32, 64))
        st = np.zeros((3, 128, 16), f32)
        rp = np.zeros((3, 128, 2048), f32)
        bp = np.zeros((2, 128, 2048), f32)
        cpad = np.zeros((2, 128, 2048), f32)
        for q in range(16):
            for gs_ in range(2):
                g = 2 * q + gs_
                for i, a in enumerate((lre, lim, ldtb)):
                    st[i, gs_ * 64:(gs_ + 1) * 64, q] = a[g]
                    rp[i, :, q * 128 + gs_ * 64:q * 128 + (gs_ + 1) * 64] = a[g][None, :]
                r0 = (g % 8) * 16
                bp[0, r0:r0 + 16, q * 128 + gs_ * 64:q * 128 + (gs_ + 1) * 64] = bre[g].T
                bp[1, r0:r0 + 16, q * 128 + gs_ * 64:q * 128 + (gs_ + 1) * 64] = bim[g].T
                cpad[0, gs_ * 64:(gs_ + 1) * 64, q * 128 + r0:q * 128 + r0 + 16] = cre[g].T
                cpad[1, gs_ * 64:(gs_ + 1) * 64, q * 128 + r0:q * 128 + r0 + 16] = cim[g].T
        return st, rp, bp, cpad

    packs = {"f": ssm_pack("f"), "b": ssm_pack("b")}
    in_maps = []
    for c in range(8):
        b, half = c // 2, c % 2
        tl = perm if half == 0 else (L - 1 - perm)
        tg = half * L + tl
        m = dict(W)
        m["xT"] = chunks(np.ascontiguousarray(inp["x"][b][tg].T))
        m["pT"] = chunks(np.ascontiguousarray(inp["p"][0, b][tg].T))
        ang = tg.astype(f32)[None, :] * inv_freq[:, None]
        m["cosT"] = np.ascontiguousarray(np.tile(np.cos(ang).astype(f32), (8, 1)))
        m["sinT"] = np.ascontiguousarray(np.tile(np.sin(ang).astype(f32), (8, 1)))
        order = ("f", "b") if half == 0 else ("b", "f")
        m["sst"] = np.stack([packs[o][0] for o in order])
        m["ssr"] = np.stack([packs[o][1] for o in order])
        m["ssb"] = np.stack([packs[o][2] for o in order])
        m["ssc"] = np.stack([packs[o][3] for o in order])
        m["lnp"], m["qkg"], m["ssd"] = lnp, qkg, ssd
        sel = np.zeros((128, 2), f32)
        sel[:, 1 - half] = 1.0
        m["selp"] = sel
        in_maps.append(m)
    res = run_bass_kernel_spmd(nc, in_maps, core_ids=list(range(8)))
    outp = np.empty((B, S, D), f32)
    for c in range(8):
        b, half = c // 2, c % 2
        tl = perm if half == 0 else (L - 1 - perm)
        tg = half * L + tl
        o = res.results[c]["outT"].reshape(D, L)
        outp[b, tg, :] = o.T
    return outp
```

```python
import math
from contextlib import ExitStack
import numpy as np
import concourse.bass as bass
import concourse.mybir as mybir
from concourse.bass_utils import run_bass_kernel_spmd

F32 = mybir.dt.float32
BF16 = mybir.dt.bfloat16
ALU = mybir.AluOpType
AF = mybir.ActivationFunctionType

D = 1024
DFF = 2816
NCH = 8
HCH = 22
NH = 8
ALPHA = 2.0 ** 0.25
LN_EPS = 1e-5
RMS_EPS = 1e-6
TT = 512
PI = math.pi


import os as _os0
SAME_ENGINE_NOSYNC = set(_os0.environ.get("NOSYNC", "pe").split(","))


class Prog:
    CH = 8000
    NDS = 16

    def __init__(self, nc):
        self.nc = nc
        self.ops = []
        self.groups = {}
        self.eng = {"pe": nc.tensor, "act": nc.scalar, "dve": nc.vector, "pool": nc.gpsimd, "sp": nc.sync}

    def add(self, eng, fn, reads=(), writes=(), dma=False, inc=None, uniq=False):
        self.ops.append(dict(eng=eng, fn=fn, reads=tuple(reads), writes=tuple(writes), dma=dma, inc=inc, bar=False, uniq=uniq))

    def barrier(self):
        self.ops.append(dict(eng=None, fn=None, reads=(), writes=(), dma=False, inc=None, bar=True, uniq=False))

    def wr(self, grp):
        g = self.groups.setdefault(grp, [])
        nm = f"{grp}#{len(g)}"
        g.append(nm)
        return nm

    def rd(self, grp):
        return list(self.groups.get(grp, []))

    def emit(self, final_res, stack):
        ops = self.ops
        n = len(ops)
        last_w = {}
        readers = {}
        deps = [None] * n
        dma_ids = []
        dma_q = {}
        last_eng = {}
        pend_bar = {}
        for i, o in enumerate(ops):
            if o["bar"]:
                bd = set(last_eng.values()) | set(dma_ids[-3 * self.NDS:])
                pend_bar = {e: set(bd) for e in self.eng}
                deps[i] = []
                continue
            d = set()
            if o["eng"] in pend_bar:
                d |= pend_bar.pop(o["eng"])
            if not o["dma"]:
                last_eng[o["eng"]] = i
            for r in o["reads"]:
                if r in last_w:
                    d.add(last_w[r])
            for w in o["writes"]:
                if w in last_w:
                    d.add(last_w[w])
                for x in readers.get(w, ()):
                    d.add(x)
            if o["dma"]:
                if not o["uniq"]:
                    ql = dma_q.setdefault(o["eng"], [])
                    if len(ql) >= self.NDS:
                        d.add(ql[len(ql) - self.NDS])
                    ql.append(i)
                dma_ids.append(i)
            d.discard(i)
            keep = []
            for x in d:
                ox = ops[x]
                if (not ox["dma"]) and ox["eng"] == o["eng"] and o["eng"] in SAME_ENGINE_NOSYNC and not o["dma"]:
                    continue
                keep.append(x)
            deps[i] = sorted(keep)
            for r in o["reads"]:
                readers.setdefault(r, []).append(i)
            for w in o["writes"]:
                last_w[w] = i
                readers[w] = []
        fin = sorted({last_w[r] for r in final_res if r in last_w})
        needed = [False] * n
        for i in range(n):
            for x in deps[i]:
                needed[x] = True
        for x in fin:
            needed[x] = True
        cnt = {e: 0 for e in self.eng}
        dcnt = {}
        sig = [None] * n
        ndq = {}
        nuniq = 0
        for i, o in enumerate(ops):
            if o["bar"]:
                continue
            if o["dma"]:
                if o["uniq"]:
                    k = ("u", nuniq)
                    nuniq += 1
                else:
                    q_ = o["eng"]
                    k = (q_, ndq.get(q_, 0) % self.NDS)
                    ndq[q_] = ndq.get(q_, 0) + 1
                dcnt[k] = dcnt.get(k, 0) + 1
                sig[i] = ("d", k, dcnt[k] * 16)
            elif needed[i]:
                cnt[o["eng"]] += 1
                sig[i] = ("e", o["eng"], cnt[o["eng"]])
        sems = {}
        for e in self.eng:
            for c in range((cnt[e] + self.CH - 1) // self.CH + 1):
                sems[("e", e, c)] = stack.enter_context(self.nc.semaphore(f"s_{e}_{c}"))
        for k in dcnt:
            sems[("d", k)] = stack.enter_context(self.nc.semaphore(f"s_d_{k[0]}_{k[1]}"))
        known = {e: {} for e in self.eng}
        snap = [None] * n

        def key_of(s):
            return ("e", s[1]) if s[0] == "e" else ("d", s[1])

        def wait(e, x):
            s = sig[x]
            kk = key_of(s)
            kn = known[e]
            if kn.get(kk, 0) >= s[2]:
                return
            h = self.eng[e]
            if s[0] == "e":
                c = (s[2] - 1) // self.CH
                h.wait_ge(sems[("e", s[1], c)], (s[2] - 1) % self.CH + 1)
            else:
                h.wait_ge(sems[("d", s[1])], s[2])
            kn[kk] = s[2]
            sn = snap[x]
            if sn:
                for k2, v2 in sn.items():
                    if kn.get(k2, 0) < v2:
                        kn[k2] = v2

        for i, o in enumerate(ops):
            if o["bar"]:
                continue
            e = o["eng"]
            for x in deps[i]:
                wait(e, x)
            ins = o["fn"]()
            s = sig[i]
            if s is not None:
                if s[0] == "e":
                    c = (s[2] - 1) // self.CH
                    ins.then_inc(sems[("e", e, c)], 1)
                else:
                    ins.then_inc(sems[("d", s[1])], 16)
                snap[i] = dict(known[e])
        for x in fin:
            wait("sp", x)


def build(L):
    NT = L // TT
    NK = 2 * L
    K1 = L // 16
    T3 = K1 // 16
    assert L % 512 == 0 and K1 % 16 == 0 and T3 >= 1
    nc = bass.Bass("TRN2", target_bir_lowering=False)
    P = Prog(nc)
    es = ExitStack()
    cur = [es]

    def din(name, shape, dt=F32):
        return nc.dram_tensor(name, list(shape), dt, kind="ExternalInput")

    def dscr(name, shape, dt):
        return nc.dram_tensor(name, list(shape), dt)

    sbn = [0]

    def sb(name, shape, dt):
        sbn[0] += 1
        return cur[0].enter_context(nc.sbuf_tensor(f"{name}_{sbn[0]}", list(shape), dt))

    xT = din("xT", [NCH, 128, L])
    pT = din("pT", [2, 128, L])
    cosT = din("cosT", [128, L])
    sinT = din("sinT", [128, L])
    wnames = {
        "w1a": (D, DFF), "w3a": (D, DFF), "w2a": (DFF, D), "win": (D, 3232), "wuq": (384, 768),
        "wuk": (256, 512), "wuv": (256, 512), "woa": (512, D), "wglu": (512, 512), "wos": (512, D),
        "wout": (D, D), "w1b": (D, DFF), "w3b": (D, DFF), "w2b": (DFF, D), "wpp": (256, D), "wpg": (D, D),
    }
    wf = {k: din(k, v) for k, v in wnames.items()}
    wb = {k: dscr(k + "_bf", v, BF16) for k, v in wnames.items()}
    lnp = din("lnp", [128, 8 * NCH])
    qkg = din("qkg", [128, 5])
    ssd = din("ssd", [128, 4])
    sst = din("sst", [2, 3, 128, 16])
    ssr = din("ssr", [2, 3, 128, 2048])
    ssb = din("ssb", [2, 2, 128, 2048])
    ssc = din("ssc", [2, 2, 128, 2048])
    selp = din("selp", [128, 2])
    out = nc.dram_tensor("outT", [NCH, 128, L], F32, kind="ExternalOutput")

    X1 = dscr("X1", [NCH, 128, L], F32)
    QT = dscr("QT", [NH, 96, L], BF16)
    NSPL = max(1, L // 1024)
    Ls = L // NSPL
    CKVs = [dscr(f"CKV{i}", [288, Ls], BF16) for i in range(NSPL)]
    CKVALLs = [dscr(f"CKVALL{i}", [2 * 288, Ls], BF16) for i in range(NSPL)]
    U32 = dscr("U32", [4, 128, L], F32)
    UBF = dscr("UBF", [4, 128, L], BF16)
    GA = dscr("GA", [NCH, 128, L], F32)
    GB = dscr("GB", [NCH, 128, L], F32)
    OT = dscr("OT", [4, 128, L], BF16)
    YS = dscr("YS", [4, 128, L], F32)
    SX = dscr("SX", [128, 32], F32)
    SXALL = dscr("SXALL", [256, 32], F32)
    import os as _os
    if _os.environ.get("DUMMY_MB"):
        DUM = dscr("DUM", [int(_os.environ["DUMMY_MB"]) * 2, 128, 1024], F32)

    ps = [es.enter_context(nc.psum_tensor(f"ps{i}", [128, 512], F32)) for i in range(8)]
    pctr = [0]

    def bank():
        b = pctr[0] % 8
        pctr[0] += 1
        return b

    V, S, T, G, SP_ = "dve", "act", "pe", "pool", "sp"

    def mm(o, lhsT, rhs, start, stop, reads, writes):
        P.add(T, lambda: nc.tensor.matmul(o, lhsT, rhs, start=start, stop=stop), reads, writes)

    def act(o, i, func, reads, writes, scale=None, bias=None):
        kw = {}
        if scale is not None:
            kw["scale"] = scale
        if bias is not None:
            kw["bias"] = bias
        P.add(S, lambda: nc.scalar.activation(out=o, in_=i, func=func, **kw), reads, writes)

    def tt(o, a, b, op, reads, writes, eng=V):
        h = nc.vector if eng == V else nc.gpsimd
        P.add(eng, lambda: h.tensor_tensor(out=o, in0=a, in1=b, op=op), reads, writes)

    def ts(o, a, s1, s2, op0, op1, reads, writes, eng=V):
        h = nc.vector if eng == V else nc.gpsimd
        if op1 is None:
            P.add(eng, lambda: h.tensor_scalar(out=o, in0=a, scalar1=s1, scalar2=None, op0=op0), reads, writes)
        else:
            P.add(eng, lambda: h.tensor_scalar(out=o, in0=a, scalar1=s1, scalar2=s2, op0=op0, op1=op1), reads, writes)

    def stt(o, a, s, b, op0, op1, reads, writes):
        P.add(V, lambda: nc.vector.scalar_tensor_tensor(out=o, in0=a, scalar=s, in1=b, op0=op0, op1=op1), reads, writes)

    def cp(o, i, reads, writes, eng=V):
        h = nc.vector if eng == V else nc.gpsimd
        P.add(eng, lambda: h.tensor_copy(out=o, in_=i), reads, writes)

    def dma(o, i, reads, writes, q=SP_, uniq=False):
        h = {"sp": nc.sync, "pool": nc.gpsimd, "act": nc.scalar}[q]
        P.add(q, lambda: h.dma_start(out=o, in_=i), reads, writes, dma=True, uniq=uniq)

    def bounds(k):
        r, c = wnames[k]
        if k in ("w1a", "w3a", "w1b", "w3b"):
            return list(range(0, c, 512)) + [c]
        if k in ("w2a", "w2b"):
            return list(range(0, c, 256)) + [c]
        if k == "win":
            return [0, 384, 672, 1184, 1696, 2208, 2720, 3232]
        if k in ("wout", "wpg"):
            return [0, 512, 1024]
        return [0, c]

    conv = []
    for g in range(6):
        conv += [("w1a", g), ("w3a", g)]
    for k in ["w2a", "win", "wuq", "wuk", "wuv", "wout", "woa", "wglu", "wos"]:
        conv += [(k, g) for g in range(len(bounds(k)) - 1)]
    for g in range(6):
        conv += [("w1b", g), ("w3b", g)]
    for k in ["w2b", "wpp", "wpg"]:
        conv += [(k, g) for g in range(len(bounds(k)) - 1)]
    for ci, (k, g) in enumerate(conv):
        bd = bounds(k)
        dma(wb[k][:, bd[g]:bd[g + 1]], wf[k][:, bd[g]:bd[g + 1]], [], [f"W_{k}_{g}"], q=G, uniq=True)

    def wres(k, c0=None, c1=None):
        bd = bounds(k)
        if c0 is None:
            return [f"W_{k}_{g}" for g in range(len(bd) - 1)]
        return [f"W_{k}_{g}" for g in range(len(bd) - 1) if bd[g] < c1 and bd[g + 1] > c0]

    def wview(k):
        return wb[k].ap().rearrange("(kc p) n -> p kc n", p=128)

    if _os.environ.get("DUMMY_MB"):
        dma(DUM[0, :, :], wf["wpg"][0:128, :], [], ["DUM"], q=G)
    ones_d = sb("ones_d", [128, 128], BF16)
    ones_q = sb("ones_q", [128, 128], BF16)
    lnp_s = sb("lnp_s", [128, 8 * NCH], F32)
    qkg_s = sb("qkg_s", [128, 5], F32)
    ssd_s = sb("ssd_s", [128, 4], F32)
    sel_s = sb("sel_s", [128, 2], F32)
    P.add(V, lambda: nc.vector.memset(ones_d[:, :], 1.0 / 1024.0), [], ["ones_d"])
    P.add(V, lambda: nc.vector.memset(ones_q[:, :], 1.0), [], ["ones_q"])
    dma(lnp_s[:, :], lnp[:, :], [], ["lnp"])
    dma(qkg_s[:, :], qkg[:, :], [], ["qkg"])
    dma(ssd_s[:, :], ssd[:, :], [], ["ssd"])
    dma(sel_s[:, :], selp[:, :], [], ["sel"])

    CM = {}

    def alloc_common(nx):
        CM["xinb"] = [sb(f"xin{i}", [128, NCH, TT], F32) for i in range(nx)]
        CM["xin"] = CM["xinb"][0]
        CM["xp"] = "xinA"
        CM["xbf"] = sb("xbf", [128, NCH, TT], BF16)
        CM["hbf"] = sb("hbf", [128, HCH, TT], BF16)
        CM["zb"] = sb("zb", [128, NCH, TT], BF16)
        CM["zq"] = sb("zq", [128, NCH, TT], BF16)
        CM["tmpf"] = [sb(f"tmpf{i}", [128, TT], F32) for i in range(3)]
        CM["mean"] = sb("mean_s", [128, TT], F32)
        CM["rstd"] = sb("rstd_s", [128, TT], F32)
        CM["wA"] = [sb(f"wA{i}", [128, 8, 512], BF16) for i in range(3)]
        CM["wB"] = [sb(f"wB{i}", [128, HCH, 256], BF16) for i in range(2)]

    def set_x(i):
        CM["xin"] = CM["xinb"][i]
        CM["xp"] = "xin" + "AB"[i]

    def XR(o):
        return f"{CM['xp']}{o}"

    def XA():
        return [XR(o) for o in range(NCH)]

    XBALL = [f"xbf{k}" for k in range(NCH)]
    lbc = [0]

    def lbank():
        b = lbc[0] % 6
        lbc[0] += 1
        return b

    def ln_prep_chunk(o):
        xt = CM["xin"]
        act(CM["zb"][:, o, :], xt[:, o, :], AF.Copy, [XR(o)], [f"zb{o}"])
        act(CM["zq"][:, o, :], xt[:, o, :], AF.Square, [XR(o)], [f"zq{o}"])

    def ln_stats_mm(o):
        mm(ps[6][:, :], ones_d[:, :], CM["zb"][:, o, :], o == 0, o == NCH - 1, ["ones_d", f"zb{o}"], ["ps6"])
        mm(ps[7][:, :], ones_d[:, :], CM["zq"][:, o, :], o == 0, o == NCH - 1, ["ones_d", f"zq{o}"], ["ps7"])

    wactr = [0]
    wbctr = [0]

    def load_wA(k, c0, c1):
        i = wactr[0] % 3
        wactr[0] += 1
        dma(CM["wA"][i][:, :, 0:c1 - c0], wview(k)[:, :, c0:c1], wres(k, c0, c1), [f"wA{i}"])
        return i

    def load_wB(k, c0, c1):
        i = wbctr[0] % 2
        wbctr[0] += 1
        dma(CM["wB"][i][:, :, 0:c1 - c0], wview(k)[:, :, c0:c1], wres(k, c0, c1), [f"wB{i}"])
        return i

    tctr = [0]

    def tmp():
        i = tctr[0] % 3
        tctr[0] += 1
        return i

    def ffn(k1, k3, k2):
        xt, xbf, hbf, tmpf, wA, wB = CM["xin"], CM["xbf"], CM["hbf"], CM["tmpf"], CM["wA"], CM["wB"]
        ngrp = (DFF + 511) // 512
        for g in range(ngrp):
            c0, c1 = g * 512, min(DFF, (g + 1) * 512)
            i1 = load_wA(k1, c0, c1)
            i3 = load_wA(k3, c0, c1)
            for cc in range((c1 - c0) // 128):
                c = g * 4 + cc
                b1, b3 = bank(), bank()
                for k in range(NCH):
                    mm(ps[b1][:, :], wA[i1][:, k, cc * 128:(cc + 1) * 128], xbf[:, k, :], k == 0, k == NCH - 1,
                       [f"wA{i1}", f"xbf{k}"], [f"ps{b1}"])
                for k in range(NCH):
                    mm(ps[b3][:, :], wA[i3][:, k, cc * 128:(cc + 1) * 128], xbf[:, k, :], k == 0, k == NCH - 1,
                       [f"wA{i3}", f"xbf{k}"], [f"ps{b3}"])
                ti = tmp()
                act(tmpf[ti][:, :], ps[b1][:, :], AF.Silu, [f"ps{b1}"], [f"tmpf{ti}"])
                tt(hbf[:, c, :], tmpf[ti][:, :], ps[b3][:, :], ALU.mult, [f"tmpf{ti}", f"ps{b3}"], [f"hbf{c}"])
        for g in range(4):
            i2 = load_wB(k2, g * 256, (g + 1) * 256)
            for oc in range(2):
                o = g * 2 + oc
                b = lbank()
                for k in range(HCH):
                    mm(ps[b][:, :], wB[i2][:, k, oc * 128:(oc + 1) * 128], hbf[:, k, :], k == 0, k == HCH - 1,
                       [f"wB{i2}", f"hbf{k}"], [f"ps{b}"])
                stt(xt[:, o, :], ps[b][:, :], 0.5 / ALPHA, xt[:, o, :], ALU.mult, ALU.add, [f"ps{b}", XR(o)], [XR(o)])
                ln_prep_chunk(o)
                if o >= 1:
                    ln_stats_mm(o - 1)
        ln_stats_mm(NCH - 1)

    def rsqrt_inplace(r, res):
        act(r, r, AF.Ln, [res], [res])
        act(r, r, AF.Exp, [res], [res], scale=-0.5)

    def layer_norm(li, eps, obf, obres):
        xt, tmpf, mean_s, rstd_s = CM["xin"], CM["tmpf"], CM["mean"], CM["rstd"]
        act(mean_s[:, :], ps[6][:, :], AF.Copy, ["ps6"], ["mean"])
        ti = tmp()
        tt(tmpf[ti][:, :], mean_s[:, :], mean_s[:, :], ALU.mult, ["mean"], [f"tmpf{ti}"])
        tt(tmpf[ti][:, :], ps[7][:, :], tmpf[ti][:, :], ALU.subtract, ["ps7", f"tmpf{ti}"], [f"tmpf{ti}"])
        ts(rstd_s[:, :], tmpf[ti][:, :], eps, None, ALU.add, None, [f"tmpf{ti}"], ["rstd"])
        rsqrt_inplace(rstd_s[:, :], "rstd")
        gcol = li * 16
        for o in range(NCH):
            eng = G if o % 4 == 3 else V
            tt(xt[:, o, :], xt[:, o, :], mean_s[:, :], ALU.subtract, [XR(o), "mean"], [XR(o)], eng=eng)
            tt(xt[:, o, :], xt[:, o, :], rstd_s[:, :], ALU.mult, [XR(o), "rstd"], [XR(o)], eng=eng)
            act(xt[:, o, :], xt[:, o, :], AF.Identity, [XR(o), "lnp"], [XR(o)],
                scale=lnp_s[:, gcol + o:gcol + o + 1], bias=lnp_s[:, gcol + 8 + o:gcol + 8 + o + 1])
            if obf is not None:
                act(obf[:, o, :], xt[:, o, :], AF.Copy, [XR(o)], [f"{obres}{o}"])

    pst = ExitStack()
    cur[0] = pst
    alloc_common(2)
    xbf, tmpf, rstd_s, wA = CM["xbf"], CM["tmpf"], CM["rstd"], CM["wA"]
    x1bf = [sb(f"x1bf{i}", [128, NCH, TT], BF16) for i in range(2)]
    xv = xT.ap().rearrange("c p n -> p c n")
    x1v = X1.ap().rearrange("c p n -> p c n")
    ql = sb("ql", [128, 3, TT], F32)
    qsq = sb("qsq", [128, 3, TT], BF16)
    cqb = sb("cqb", [128, 3, TT], BF16)
    wuq_s = sb("wuq_s", [128, 3, 768], BF16)
    qn = sb("qn", [128, 4, TT], BF16)
    cos_s = sb("cos_s", [128, TT], F32)
    sin_s = sb("sin_s", [128, TT], F32)
    r1s = sb("r1s", [128, TT], F32)
    r2s = sb("r2s", [128, TT], F32)
    ro1 = sb("ro1", [128, TT], BF16)
    ro2 = sb("ro2", [128, TT], BF16)
    u32s = sb("u32s", [128, 4, TT], F32)
    ubfs = sb("ubfs", [128, 4, TT], BF16)
    gsm = [sb(f"gsm{i}", [128, TT], F32) for i in range(2)]

    def rope(pa, pb, np_, outa, outb, ra, rb_):
        act(r1s[0:np_, :], ps[pa][0:np_, :], AF.Copy, [f"ps{pa}"], ["r1s"])
        act(r2s[0:np_, :], ps[pb][0:np_, :], AF.Copy, [f"ps{pb}"], ["r2s"])
        t0, t1 = tmp(), tmp()
        tt(tmpf[t0][0:np_, :], r1s[0:np_, :], cos_s[0:np_, :], ALU.mult, ["r1s", "cos"], [f"tmpf{t0}"])
        tt(tmpf[t1][0:np_, :], r2s[0:np_, :], sin_s[0:np_, :], ALU.mult, ["r2s", "sin"], [f"tmpf{t1}"])
        tt(outa, tmpf[t0][0:np_, :], tmpf[t1][0:np_, :], ALU.subtract, [f"tmpf{t0}", f"tmpf{t1}"], [ra])
        tt(tmpf[t0][0:np_, :], r2s[0:np_, :], cos_s[0:np_, :], ALU.mult, ["r2s", "cos"], [f"tmpf{t0}"])
        tt(tmpf[t1][0:np_, :], r1s[0:np_, :], sin_s[0:np_, :], ALU.mult, ["r1s", "sin"], [f"tmpf{t1}"])
        tt(outb, tmpf[t0][0:np_, :], tmpf[t1][0:np_, :], ALU.add, [f"tmpf{t0}", f"tmpf{t1}"], [rb_])

    def rmsnorm(src, sres, nch, dim, gcol0, dst, dres):
        act(qsq[:, 0:nch, :], src[:, 0:nch, :], AF.Square, [sres], ["qsq"])
        b = bank()
        for k in range(nch):
            mm(ps[b][:, :], ones_q[:, :], qsq[:, k, :], k == 0, k == nch - 1, ["ones_q", "qsq"], [f"ps{b}"])
        ts(rstd_s[:, :], ps[b][:, :], 1.0 / dim, RMS_EPS, ALU.mult, ALU.add, [f"ps{b}"], ["rstd"])
        rsqrt_inplace(rstd_s[:, :], "rstd")
        for k in range(nch):
            tt(src[:, k, :], src[:, k, :], rstd_s[:, :], ALU.mult, [sres, "rstd"], [sres])
            act(dst[:, k, :], src[:, k, :], AF.Copy, [sres, "qkg"], [dres], scale=qkg_s[:, gcol0 + k:gcol0 + k + 1])

    PJ = {}

    def proj(iw, cc, M=128, col0=None):
        b = bank()
        c_lo = cc * 128 if col0 is None else col0
        xb_, xr_ = PJ["buf"], PJ["res"]
        for k in range(NCH):
            mm(ps[b][0:M, :], wA[iw][:, k, c_lo:c_lo + M], xb_[:, k, :], k == 0, k == NCH - 1, [f"wA{iw}", f"{xr_}{k}"], [f"ps{b}"])
        return b

    def proj_stage(t):
        c0, c1 = t * TT, (t + 1) * TT
        PJ["buf"], PJ["res"] = x1bf[t % 2], f"x1b{t % 2}_"
        if t == 0:
            dma(wuq_s[:, :, :], wview("wuq"), wres("wuq"), ["wuq_s"])
        dma(cos_s[:, :], cosT[:, c0:c1], [], ["cos"])
        dma(sin_s[:, :], sinT[:, c0:c1], [], ["sin"])
        iw = load_wA("win", 0, 384)
        for c in range(3):
            b = proj(iw, c)
            act(ql[:, c, :], ps[b][:, :], AF.Copy, [f"ps{b}"], ["ql"])
        rmsnorm(ql, "ql", 3, 384.0, 0, cqb, "cqb")
        for c in range(4):
            b = bank()
            for k in range(3):
                mm(ps[b][:, :], wuq_s[:, k, c * 128:(c + 1) * 128], cqb[:, k, :], k == 0, k == 2, ["wuq_s", "cqb"], [f"ps{b}"])
            act(qn[:, c, :], ps[b][:, :], AF.Copy, [f"ps{b}"], [f"qn{c}"])
            dma(QT[2 * c, 0:64, c0:c1], qn[0:64, c, :], [f"qn{c}"], [P.wr("QT")], q=G)
            dma(QT[2 * c + 1, 0:64, c0:c1], qn[64:128, c, :], [f"qn{c}"], [P.wr("QT")], q=G)
        b1, b2 = bank(), bank()
        for k in range(3):
            mm(ps[b1][:, :], wuq_s[:, k, 512:640], cqb[:, k, :], k == 0, k == 2, ["wuq_s", "cqb"], [f"ps{b1}"])
        for k in range(3):
            mm(ps[b2][:, :], wuq_s[:, k, 640:768], cqb[:, k, :], k == 0, k == 2, ["wuq_s", "cqb"], [f"ps{b2}"])
        rope(b1, b2, 128, ro1[:, :], ro2[:, :], "ro1", "ro2")
        for h in range(NH):
            dma(QT[h, 64:80, c0:c1], ro1[h * 16:(h + 1) * 16, :], ["ro1"], [P.wr("QT")], q=G)
            dma(QT[h, 80:96, c0:c1], ro2[h * 16:(h + 1) * 16, :], ["ro2"], [P.wr("QT")], q=G)
        iw = load_wA("win", 384, 672)
        for c in range(2):
            b = proj(iw, c)
            act(ql[:, c, :], ps[b][:, :], AF.Copy, [f"ps{b}"], ["ql"])
        b1 = proj(iw, 0, M=16, col0=256)
        b2 = proj(iw, 0, M=16, col0=272)
        rmsnorm(ql, "ql", 2, 256.0, 3, cqb, "cqb")
        CKV = CKVs[c0 // Ls]
        s0, s1 = c0 % Ls, c0 % Ls + TT
        for k in range(2):
            dma(CKV[k * 128:(k + 1) * 128, s0:s1], cqb[:, k, :], ["cqb"], [P.wr(f"CKV{c0 // Ls}")], q=G)
        rope(b1, b2, 16, ro1[0:16, :], ro2[0:16, :], "ro1", "ro2")
        dma(CKV[256:272, s0:s1], ro1[0:16, :], ["ro1"], [P.wr(f"CKV{c0 // Ls}")], q=G)
        dma(CKV[272:288, s0:s1], ro2[0:16, :], ["ro2"], [P.wr(f"CKV{c0 // Ls}")], q=G)
        iw = load_wA("win", 672, 1184)
        for c in range(4):
            b = proj(iw, c)
            act(u32s[:, c, :], ps[b][:, :], AF.Copy, [f"ps{b}"], ["u32s"])
        cp(ubfs[:, :, :], u32s[:, :, :], ["u32s"], ["ubfs"], eng=G)
        dma(U32.ap().rearrange("c p n -> p c n")[:, :, c0:c1], u32s[:, :, :], ["u32s"], [P.wr("U32")], q=G)
        dma(UBF.ap().rearrange("c p n -> p c n")[:, :, c0:c1], ubfs[:, :, :], ["ubfs"], [P.wr("UBF")], q=G)
        gi_ = 0
        for gname, GD, base in (("G0", GA, 1184), ("G1", GB, 2208)):
            for half in range(2):
                iw = load_wA("win", base + half * 512, base + (half + 1) * 512)
                for cc in range(4):
                    c = half * 4 + cc
                    b = proj(iw, cc)
                    gb_ = gi_ % 2
                    gi_ += 1
                    act(gsm[gb_][:, :], ps[b][:, :], AF.Sigmoid, [f"ps{b}"], [f"gsm{gb_}"])
                    dma(GD[c, :, c0:c1], gsm[gb_][:, :], [f"gsm{gb_}"], [P.wr(gname)], q=G)

    set_x(0)
    dma(CM["xin"][:, :, :], xv[:, :, 0:TT], [], XA())
    act(xbf[:, :, :], CM["xin"][:, :, :], AF.Copy, XA(), XBALL)
    for t in range(NT):
        c0, c1 = t * TT, (t + 1) * TT
        set_x(t % 2)
        if t + 1 < NT:
            nb = (t + 1) % 2
            nres = [f"xin{'AB'[nb]}{o}" for o in range(NCH)]
            dma(CM["xinb"][nb][:, :, :], xv[:, :, c1:c1 + TT], [], nres)
        ffn("w1a", "w3a", "w2a")
        if t + 1 < NT:
            act(xbf[:, :, :], CM["xinb"][nb][:, :, :], AF.Copy, nres, XBALL)
        layer_norm(0, LN_EPS / (ALPHA * ALPHA), x1bf[t % 2], f"x1b{t % 2}_")
        dma(x1v[:, :, c0:c1], CM["xin"][:, :, :], XA(), [P.wr("X1")], q=G)
        if t >= 1:
            proj_stage(t - 1)
    proj_stage(NT - 1)
    P.barrier()
    pst.close()
    if _os.environ.get("STOP") == "A":
        P.emit(P.rd("X1"), es); es.close(); return nc

    for i in range(NSPL):
        P.add(G, lambda i=i: nc.gpsimd.collective_compute("AllGather", ALU.bypass, replica_groups=[[0, 1], [2, 3], [4, 5], [6, 7]],
                                                           ins=[CKVs[i].ap().opt()], outs=[CKVALLs[i].ap().opt()]),
              P.rd(f"CKV{i}"), [f"CKVALL{i}"])

    if _os.environ.get("STOP") == "X":
        P.emit([f"CKVALL{i}" for i in range(NSPL)], es); es.close(); return nc
    pst = ExitStack()
    cur[0] = pst
    bbR = [sb(f"bbR{d}", [128, 2048], BF16) for d in range(2)]
    bbI = [sb(f"bbI{d}", [128, 2048], BF16) for d in range(2)]
    ccR = [sb(f"ccR{d}", [128, 2048], BF16) for d in range(2)]
    ccI = [sb(f"ccI{d}", [128, 2048], BF16) for d in range(2)]
    pwA = [[sb(f"pwA{d}{l}", [128, 16, 17], F32) for l in range(3)] for d in range(2)]
    pwB = [[sb(f"pwB{d}{l}", [128, 16, 17], F32) for l in range(3)] for d in range(2)]
    pwN = [[sb(f"pwN{d}{l}", [128, 16, 17], F32) for l in range(3)] for d in range(2)]
    small = [sb(f"sm{i}", [128, 16], F32) for i in range(8)]
    zero_s = sb("zero_s", [128, 32], F32)
    finR = sb("finR", [128, 32], F32)
    gat = sb("gat", [128, 2, 32], F32)
    iniS = sb("iniS", [128, 32], F32)
    P.add(V, lambda: nc.vector.memset(zero_s[:, :], 0.0), [], ["zero_s"])
    pst2 = ExitStack()
    cur[0] = pst2
    pt = [sb(f"pt{i}", [128, 2048], F32) for i in range(7)]

    I32 = mybir.dt.int32
    isml = sb("isml", [128, 16], I32)
    ibig = sb("ibig", [128, 2048], I32)

    def sincos(zi, zres, so, co, res_s, res_c, scratch, sres, itile, ires):
        for shift, dst, dres in ((0.0, so, res_s), (0.5 * PI, co, res_c)):
            ts(scratch, zi, shift, 1.0 / (2 * PI), ALU.add, ALU.mult, [zres], [sres])
            cp(itile, scratch, [sres], [ires])
            cp(scratch, itile, [ires], [sres])
            ts(scratch, scratch, -2 * PI, None, ALU.mult, None, [sres], [sres])
            ts(dst, zi, shift, None, ALU.add, None, [zres], [dres])
            tt(dst, dst, scratch, ALU.add, [dres, sres], [dres])
            ts(scratch, dst, PI, 2 * PI, ALU.is_gt, ALU.mult, [dres], [sres])
            tt(dst, dst, scratch, ALU.subtract, [dres, sres], [dres])
            ts(scratch, dst, -PI, 2 * PI, ALU.is_lt, ALU.mult, [dres], [sres])
            tt(dst, dst, scratch, ALU.add, [dres, sres], [dres])
            act(dst, dst, AF.Sin, [dres], [dres])

    for d in range(2):
        lre, lim, ldt, zr, zi, mg, sn, cs = [small[i][:, :] for i in range(8)]
        dma(lre, sst[d, 0, :, :], [], ["sm0"])
        dma(lim, sst[d, 1, :, :], [], ["sm1"])
        dma(ldt, sst[d, 2, :, :], [], ["sm2"])
        act(ldt, ldt, AF.Exp, ["sm2"], ["sm2"])
        tt(zr, lre, ldt, ALU.mult, ["sm0", "sm2"], ["sm3"])
        tt(zi, lim, ldt, ALU.mult, ["sm1", "sm2"], ["sm4"])
        act(mg, zr, AF.Exp, ["sm3"], ["sm5"])
        sincos(zi, "sm4", sn, cs, "sm6", "sm7", zr, "sm3", isml[:, :], "isml")
        for l in range(3):
            A, B, N = pwA[d][l], pwB[d][l], pwN[d][l]
            rA, rB, rN = f"pwA{d}{l}", f"pwB{d}{l}", f"pwN{d}{l}"
            P.add(V, lambda A=A: nc.vector.memset(A[:, :, 0:1], 1.0), [], [rA])
            P.add(V, lambda B=B: nc.vector.memset(B[:, :, 0:1], 0.0), [], [rB])
            if l == 0:
                tt(A[:, :, 1], mg, cs, ALU.mult, ["sm5", "sm7"], [rA])
                tt(B[:, :, 1], mg, sn, ALU.mult, ["sm5", "sm6"], [rB])
            else:
                cp(A[:, :, 1], pwA[d][l - 1][:, :, 16], [f"pwA{d}{l-1}"], [rA])
                cp(B[:, :, 1], pwB[d][l - 1][:, :, 16], [f"pwB{d}{l-1}"], [rB])
            for j in range(2, 17):
                tt(zr, A[:, :, j - 1], A[:, :, 1], ALU.mult, [rA], ["sm3"])
                tt(zi, B[:, :, j - 1], B[:, :, 1], ALU.mult, [rB], ["sm4"])
                tt(A[:, :, j], zr, zi, ALU.subtract, ["sm3", "sm4"], [rA])
                tt(zr, A[:, :, j - 1], B[:, :, 1], ALU.mult, [rA, rB], ["sm3"])
                tt(zi, B[:, :, j - 1], A[:, :, 1], ALU.mult, [rA, rB], ["sm4"])
                tt(B[:, :, j], zr, zi, ALU.add, ["sm3", "sm4"], [rB])
            ts(N[:, :, :], B[:, :, :], -1.0, None, ALU.mult, None, [rB], [rN])
        LR, LI, DT, t0, t1, t2, t3 = [pt[i][:, :] for i in range(7)]
        dma(LR, ssr[d, 0, :, :], [], ["pt0"])
        dma(LI, ssr[d, 1, :, :], [], ["pt1"])
        dma(DT, ssr[d, 2, :, :], [], ["pt2"])
        act(DT, DT, AF.Exp, ["pt2"], ["pt2"])
        tt(t0, LR, DT, ALU.mult, ["pt0", "pt2"], ["pt3"])
        tt(t1, LI, DT, ALU.mult, ["pt1", "pt2"], ["pt4"])
        act(t0, t0, AF.Exp, ["pt3"], ["pt3"])
        sincos(t1, "pt4", t2, t3, "pt5", "pt6", DT, "pt2", ibig[:, :], "ibig")
        tt(t2, t2, t0, ALU.mult, ["pt5", "pt3"], ["pt5"])
        tt(t3, t3, t0, ALU.mult, ["pt6", "pt3"], ["pt6"])
        ts(t3, t3, -1.0, None, ALU.add, None, ["pt6"], ["pt6"])
        tt(t0, LR, LR, ALU.mult, ["pt0"], ["pt3"])
        tt(t1, LI, LI, ALU.mult, ["pt1"], ["pt4"])
        tt(t0, t0, t1, ALU.add, ["pt3", "pt4"], ["pt3"])
        P.add(V, lambda t0=t0: nc.vector.reciprocal(out=t0, in_=t0), ["pt3"], ["pt3"])
        tt(t1, t3, LR, ALU.mult, ["pt6", "pt0"], ["pt4"])
        tt(DT, t2, LI, ALU.mult, ["pt5", "pt1"], ["pt2"])
        tt(t1, t1, DT, ALU.add, ["pt4", "pt2"], ["pt4"])
        tt(t1, t1, t0, ALU.mult, ["pt4", "pt3"], ["pt4"])
        tt(DT, t2, LR, ALU.mult, ["pt5", "pt0"], ["pt2"])
        tt(LR, t3, LI, ALU.mult, ["pt6", "pt1"], ["pt0"])
        tt(DT, DT, LR, ALU.subtract, ["pt2", "pt0"], ["pt2"])
        tt(DT, DT, t0, ALU.mult, ["pt2", "pt3"], ["pt2"])
        dma(LR, ssb[d, 0, :, :], [], ["pt0"])
        dma(LI, ssb[d, 1, :, :], [], ["pt1"])
        tt(t0, t1, LR, ALU.mult, ["pt4", "pt0"], ["pt3"])
        tt(t2, DT, LI, ALU.mult, ["pt2", "pt1"], ["pt5"])
        tt(bbR[d][:, :], t0, t2, ALU.subtract, ["pt3", "pt5"], [f"bbR{d}"])
        tt(t0, t1, LI, ALU.mult, ["pt4", "pt1"], ["pt3"])
        tt(t2, DT, LR, ALU.mult, ["pt2", "pt0"], ["pt5"])
        tt(bbI[d][:, :], t0, t2, ALU.add, ["pt3", "pt5"], [f"bbI{d}"])
        dma(t3, ssc[d, 0, :, :], [], ["pt6"])
        cp(ccR[d][:, :], t3, ["pt6"], [f"ccR{d}"])
        dma(t3, ssc[d, 1, :, :], [], ["pt6"])
        ts(ccI[d][:, :], t3, -1.0, None, ALU.mult, None, ["pt6"], [f"ccI{d}"])
    P.barrier()
    pst2.close()
    cur[0] = pst
    Rt = sb("Rst", [128, L], F32)
    It = sb("Ist", [128, L], F32)
    Rbf = sb("Rbf", [128, L], BF16)
    Ibf = sb("Ibf", [128, L], BF16)
    ubig = [sb(f"ubig{i}", [128, L], BF16) for i in range(2)]
    yacc = sb("yacc", [128, L], F32)
    e2R = sb("e2R", [128, 16, T3], F32)
    e2I = sb("e2I", [128, 16, T3], F32)
    e3R = sb("e3R", [128, T3], F32)
    e3I = sb("e3I", [128, T3], F32)
    x2pR = sb("x2pR", [128, T3], F32)
    x2pI = sb("x2pI", [128, T3], F32)
    x1pR = sb("x1pR", [128, 16, T3], F32)
    x1pI = sb("x1pI", [128, 16, T3], F32)
    NKC = NK // 128
    ckt = [sb(f"ckt{i}", [128, 2, 512], BF16) for i in range(3)]
    Kt = sb("Kt", [96, NK], BF16)
    Va = sb("Va", [128, NKC, 128], BF16)
    qts = sb("qts", [96, L], BF16)
    wuk_s = sb("wuk_s", [128, 2, 512], BF16)
    wuv_s = sb("wuv_s", [128, 2, 512], BF16)
    Pb = [sb(f"Pb{i}", [128, 1024], BF16) for i in range(2)]
    rcs = sb("rcs", [64, TT], F32)
    ob = sb("ob", [64, TT], BF16)

    b67 = [0]

    def bank67():
        b67[0] += 1
        return 6 + (b67[0] % 2)

    def cmul_acc(oR, oI, pR, pI, a, b, nb, res_o, res_p, extra_reads=()):
        ex = list(extra_reads)
        roR, roI, rpR, rpI = res_o + "R", res_o + "I", res_p + "R", res_p + "I"
        stt(oR, pR, a, oR, ALU.mult, ALU.add, [roR, rpR] + ex, [roR])
        stt(oI, pR, b, oI, ALU.mult, ALU.add, [roI, rpR] + ex, [roI])
        stt(oR, pI, nb, oR, ALU.mult, ALU.add, [roR, rpI] + ex, [roR])
        stt(oI, pI, a, oI, ALU.mult, ALU.add, [roI, rpI] + ex, [roI])

    def blk(tn, j):
        return tn[:, j * K1:(j + 1) * K1]

    def blkres(t, ri):
        j0_, j1_ = (t * TT) // K1, ((t + 1) * TT - 1) // K1
        return [f"b{j}{ri}" for j in range(j0_, j1_ + 1)]

    def blk3(tn, j):
        return tn[:, j * K1:(j + 1) * K1].rearrange("p (a b) -> p a b", b=T3)

    def s5_main():
        for pas in range(2):
            d = pas
            rev = (pas == 1)

            def ix(i, n, rev=rev):
                return (n - 1 - i) if rev else i

            ini = zero_s if pas == 0 else iniS
            ini_res = "zero_s" if pas == 0 else "iniS"
            if pas == 1:
                dma(SX[:, :], finR[:, :], ["finR"], ["SX"], q=G)
                P.add(G, lambda: nc.gpsimd.collective_compute("AllGather", ALU.bypass, replica_groups=[[0, 1], [2, 3], [4, 5], [6, 7]],
                                                               ins=[SX.ap().opt()], outs=[SXALL.ap().opt()]),
                      ["SX"], ["SXALL"])
                dma(gat[:, :, :], SXALL.ap().rearrange("(r p) n -> p r n", p=128), ["SXALL"], ["gat"])
                ts(iniS[:, :], gat[:, 0, :], sel_s[:, 0:1], None, ALU.mult, None, ["gat", "sel"], ["iniS"])
                stt(iniS[:, :], gat[:, 1, :], sel_s[:, 1:2], iniS[:, :], ALU.mult, ALU.add, ["gat", "sel", "iniS"], ["iniS"])
            for c in range(4):
                ub = ubig[c % 2]
                ubr = f"ubig{c % 2}"
                dma(ub[:, :], UBF[c, :, :], P.rd("UBF"), [ubr])
                if pas == 0:
                    dma(yacc[:, :], U32[c, :, :], P.rd("U32"), ["yacc"])
                    ts(yacc[:, :], yacc[:, :], ssd_s[:, c:c + 1], None, ALU.mult, None, ["yacc", "ssd"], ["yacc"])
                else:
                    dma(yacc[:, :], YS[c, :, :], [f"YS{c}"], ["yacc"])
                for qq in range(4):
                    q = c * 4 + qq
                    A0, B0, N0 = pwA[d][0], pwB[d][0], pwN[d][0]
                    A1, B1, N1 = pwA[d][1], pwB[d][1], pwN[d][1]
                    A2, B2, N2 = pwA[d][2], pwB[d][2], pwN[d][2]
                    pres = [f"pw{x}{d}{l}" for x in "ABN" for l in range(3)]
                    for t in range(NT):
                        bR, bI = bank67(), bank67()
                        mm(ps[bR][:, :], bbR[d][:, q * 128:(q + 1) * 128], ub[:, t * TT:(t + 1) * TT], True, True, [f"bbR{d}", ubr], [f"ps{bR}"])
                        mm(ps[bI][:, :], bbI[d][:, q * 128:(q + 1) * 128], ub[:, t * TT:(t + 1) * TT], True, True, [f"bbI{d}", ubr], [f"ps{bI}"])
                        act(Rt[:, t * TT:(t + 1) * TT], ps[bR][:, :], AF.Copy, [f"ps{bR}"], blkres(t, "R"))
                        act(It[:, t * TT:(t + 1) * TT], ps[bI][:, :], AF.Copy, [f"ps{bI}"], blkres(t, "I"))
                    yield 12
                    for j in range(1, 16):
                        jc, jp = ix(j, 16), ix(j - 1, 16)
                        cmul_acc(blk(Rt, jc), blk(It, jc), blk(Rt, jp), blk(It, jp),
                                 A0[:, q, 1:2], B0[:, q, 1:2], N0[:, q, 1:2], f"b{jc}", f"b{jp}", pres)
                        if j % 5 == 0:
                            yield 8
                    jl = ix(15, 16)
                    cp(e2R[:, :, :], blk3(Rt, jl), [f"b{jl}R"], [f"e2_{j}R" for j in range(16)])
                    cp(e2I[:, :, :], blk3(It, jl), [f"b{jl}I"], [f"e2_{j}I" for j in range(16)])
                    for j in range(1, 16):
                        jc, jp = ix(j, 16), ix(j - 1, 16)
                        cmul_acc(e2R[:, jc, :], e2I[:, jc, :], e2R[:, jp, :], e2I[:, jp, :],
                                 A1[:, q, 1:2], B1[:, q, 1:2], N1[:, q, 1:2], f"e2_{jc}", f"e2_{jp}", pres)
                    yield 8
                    cp(e3R[:, :], e2R[:, jl, :], [f"e2_{jl}R"], [f"e3_{k}R" for k in range(T3)])
                    cp(e3I[:, :], e2I[:, jl, :], [f"e2_{jl}I"], [f"e3_{k}I" for k in range(T3)])
                    for k in range(T3):
                        kc = ix(k, T3)
                        if k == 0:
                            pR, pI = ini[:, q:q + 1], ini[:, 16 + q:16 + q + 1]
                            rp_ = "ini"
                        else:
                            kp = ix(k - 1, T3)
                            pR, pI = e3R[:, kp:kp + 1], e3I[:, kp:kp + 1]
                            rp_ = f"e3_{kp}"
                        cmul_acc(e3R[:, kc:kc + 1], e3I[:, kc:kc + 1], pR, pI, A2[:, q, 1:2], B2[:, q, 1:2], N2[:, q, 1:2],
                                 f"e3_{kc}", rp_, pres + [ini_res])
                    yield 8
                    kl = ix(T3 - 1, T3)
                    if pas == 0:
                        cp(finR[:, q:q + 1], e3R[:, kl:kl + 1], [f"e3_{kl}R"], ["finR"])
                        cp(finR[:, 16 + q:16 + q + 1], e3I[:, kl:kl + 1], [f"e3_{kl}I"], ["finR"])
                    k0 = ix(0, T3)
                    e3allR = [f"e3_{k}R" for k in range(T3)]
                    e3allI = [f"e3_{k}I" for k in range(T3)]
                    cp(x2pR[:, k0:k0 + 1], ini[:, q:q + 1], [ini_res], ["x2pR"])
                    cp(x2pI[:, k0:k0 + 1], ini[:, 16 + q:16 + q + 1], [ini_res], ["x2pI"])
                    if T3 > 1:
                        if not rev:
                            cp(x2pR[:, 1:T3], e3R[:, 0:T3 - 1], e3allR, ["x2pR"])
                            cp(x2pI[:, 1:T3], e3I[:, 0:T3 - 1], e3allI, ["x2pI"])
                        else:
                            cp(x2pR[:, 0:T3 - 1], e3R[:, 1:T3], e3allR, ["x2pR"])
                            cp(x2pI[:, 0:T3 - 1], e3I[:, 1:T3], e3allI, ["x2pI"])
                    for j in range(16):
                        jc = ix(j, 16)
                        cmul_acc(e2R[:, jc, :], e2I[:, jc, :], x2pR[:, :], x2pI[:, :],
                                 A1[:, q, j + 1:j + 2], B1[:, q, j + 1:j + 2], N1[:, q, j + 1:j + 2], f"e2_{jc}", "x2p", pres)
                    yield 8
                    j0 = ix(0, 16)
                    e2allR = [f"e2_{j}R" for j in range(16)]
                    e2allI = [f"e2_{j}I" for j in range(16)]
                    cp(x1pR[:, j0, :], x2pR[:, :], ["x2pR"], ["x1pR"])
                    cp(x1pI[:, j0, :], x2pI[:, :], ["x2pI"], ["x1pI"])
                    if not rev:
                        cp(x1pR[:, 1:16, :], e2R[:, 0:15, :], e2allR, ["x1pR"])
                        cp(x1pI[:, 1:16, :], e2I[:, 0:15, :], e2allI, ["x1pI"])
                    else:
                        cp(x1pR[:, 0:15, :], e2R[:, 1:16, :], e2allR, ["x1pR"])
                        cp(x1pI[:, 0:15, :], e2I[:, 1:16, :], e2allI, ["x1pI"])
                    for j in range(16):
                        jc = ix(j, 16)
                        cmul_acc(blk3(Rt, jc), blk3(It, jc), x1pR[:, :, :], x1pI[:, :, :],
                                 A0[:, q, j + 1:j + 2], B0[:, q, j + 1:j + 2], N0[:, q, j + 1:j + 2], f"b{jc}", "x1p", pres)
                        if j % 5 == 4:
                            yield 8
                    act(Rbf[:, :], Rt[:, :], AF.Copy, [f"b{j}R" for j in range(16)], ["Rbf"])
                    act(Ibf[:, :], It[:, :], AF.Copy, [f"b{j}I" for j in range(16)], ["Ibf"])
                    yield 6
                    for t in range(NT):
                        b = bank67()
                        mm(ps[b][:, :], ccR[d][:, q * 128:(q + 1) * 128], Rbf[:, t * TT:(t + 1) * TT], True, False, [f"ccR{d}", "Rbf"], [f"ps{b}"])
                        mm(ps[b][:, :], ccI[d][:, q * 128:(q + 1) * 128], Ibf[:, t * TT:(t + 1) * TT], False, True, [f"ccI{d}", "Ibf"], [f"ps{b}"])
                        tt(yacc[:, t * TT:(t + 1) * TT], yacc[:, t * TT:(t + 1) * TT], ps[b][:, :], ALU.add, ["yacc", f"ps{b}"], ["yacc"])
                    yield 6
                dma(YS[c, :, :], yacc[:, :], ["yacc"], [f"YS{c}"], q=G)

    scale = 96.0 ** -0.5

    def attn_main():
        for i in range(NSPL):
            ckall = CKVALLs[i].ap().rearrange("(r f) n -> r f n", r=2)
            for r in range(2):
                o0 = (i * 2 + r) * Ls
                dma(Kt[64:96, o0:o0 + Ls], ckall[r, 256:288, :], [f"CKVALL{i}"], [P.wr("Ktr")])
        dma(wuk_s[:, :, :], wview("wuk"), wres("wuk"), ["wuk_s"])
        dma(wuv_s[:, :, :], wview("wuv"), wres("wuv"), ["wuv_s"])
        P.add(G, lambda: nc.gpsimd.memset(Va[:, :, :], 1.0), [], ["Va"])
        yield 2
        cki = 0
        for h in range(NH):
            for kt in range(NK // 512):
                blkno = (kt * 512) // Ls
                i, r = blkno // 2, blkno % 2
                coff = kt * 512 - blkno * Ls
                ci = cki % 3
                cki += 1
                src = CKVALLs[i].ap()[r * 288:r * 288 + 256, coff:coff + 512].rearrange("(k p) n -> p k n", p=128)
                dma(ckt[ci][:, :, :], src, [f"CKVALL{i}"], [f"ckt{ci}"])
                b = bank67()
                for k in range(2):
                    mm(ps[b][0:64, :], wuk_s[:, k, h * 64:(h + 1) * 64], ckt[ci][:, k, :], k == 0, k == 1, ["wuk_s", f"ckt{ci}"], [f"ps{b}"])
                cp(Kt[0:64, kt * 512:(kt + 1) * 512], ps[b][0:64, :], [f"ps{b}"], ["Kt"])
                b = bank67()
                for j in range(4):
                    for k in range(2):
                        mm(ps[b][:, j * 64:(j + 1) * 64], ckt[ci][:, k, j * 128:(j + 1) * 128], wuv_s[:, k, h * 64:(h + 1) * 64], k == 0, k == 1, ["wuv_s", f"ckt{ci}"], [f"ps{b}"])
                cp(Va[:, kt * 4:(kt + 1) * 4, 0:64], ps[b][:, 0:256].rearrange("p (a b) -> p a b", b=64), [f"ps{b}"], ["Va"])
                if kt % 4 == 3:
                    yield 3
            dma(qts[:, :], QT[h, :, :], P.rd("QT"), ["qts"])
            for qt_ in range(L // 512):
                bo = 4 + (qt_ % 2)
                q0 = qt_ * 512
                npair = NKC // 2

                def s_mm(i):
                    p = i % 2
                    for j in range(2):
                        kc = 2 * i + j
                        mm(ps[2 * p + j][:, :], Kt[0:96, kc * 128:(kc + 1) * 128], qts[0:96, q0:q0 + 512], True, True, ["Kt", "qts"] + P.rd("Ktr"), [f"ps{2*p+j}"])

                def s_exp(i):
                    p = i % 2
                    for j in range(2):
                        act(Pb[p][:, j * 512:(j + 1) * 512], ps[2 * p + j][:, :], AF.Exp, [f"ps{2*p+j}"], [f"Pb{p}"], scale=scale)

                def pv(i):
                    p = i % 2
                    for j in range(2):
                        kc = 2 * i + j
                        mm(ps[bo][:, :], Va[:, kc, :], Pb[p][:, j * 512:(j + 1) * 512], kc == 0, kc == NKC - 1, ["Va", f"Pb{p}"], [f"ps{bo}"])

                s_mm(0)
                s_exp(0)
                for i in range(npair):
                    if i + 1 < npair:
                        s_mm(i + 1)
                        s_exp(i + 1)
                    pv(i)
                    if i % 8 == 7:
                        yield 10
                act(rcs[0:64, :], ps[bo][64:128, :], AF.Ln, [f"ps{bo}"], ["rcs"])
                act(rcs[0:64, :], rcs[0:64, :], AF.Exp, ["rcs"], ["rcs"], scale=-1.0)
                tt(ob[:, :], ps[bo][0:64, :], rcs[:, :], ALU.mult, [f"ps{bo}", "rcs"], ["ob"])
                dma(OT[h // 2, (h % 2) * 64:(h % 2) * 64 + 64, q0:q0 + 512], ob[:, :], ["ob"], [P.wr("OT")], q=G)
                yield 1

    gens = [s5_main(), attn_main()]
    tacc = [0.0, 0.0]
    alive = [True, True]
    if _os.environ.get("NOOVERLAP"):
        for g_ in gens:
            for _ in g_:
                pass
    else:
        while any(alive):
            gi = min((i for i in range(2) if alive[i]), key=lambda i: tacc[i])
            try:
                tacc[gi] += next(gens[gi])
            except StopIteration:
                alive[gi] = False
    P.barrier()
    pst.close()

    if _os.environ.get("STOP") == "B":
        P.emit(P.rd("OT"), es); es.close(); return nc
    pst = ExitStack()
    cur[0] = pst
    alloc_common(1)
    set_x(0)
    xt, xbf, tmpf, wA = CM["xin"], CM["xbf"], CM["tmpf"], CM["wA"]
    woa_s = sb("woa_s", [128, 4, D], BF16)
    wos_s = sb("wos_s", [128, 4, D], BF16)
    wglu_s = sb("wglu_s", [128, 4, 512], BF16)
    wpp_s = sb("wpp_s", [128, 2, D], BF16)
    dma(woa_s[:, :, :], wview("woa"), wres("woa"), ["woa_s"])
    dma(wos_s[:, :, :], wview("wos"), wres("wos"), ["wos_s"])
    dma(wglu_s[:, :, :], wview("wglu"), wres("wglu"), ["wglu_s"])
    dma(wpp_s[:, :, :], wview("wpp"), wres("wpp"), ["wpp_s"])
    ots = sb("ots", [128, 4, TT], BF16)
    ysf = sb("ysf", [128, 4, TT], F32)
    zf = sb("zf", [128, 4, TT], F32)
    zbf4 = sb("zbf4", [128, 4, TT], BF16)
    zzb = sb("zzb", [128, 4, TT], BF16)
    gaf = [sb(f"gaf{i}", [128, TT], F32) for i in range(2)]
    gbf = [sb(f"gbf{i}", [128, TT], F32) for i in range(2)]
    mrb = sb("mrb", [128, NCH, TT], BF16)
    pf = sb("pf", [128, 2, TT], F32)
    pbf = sb("pbf", [128, 2, TT], BF16)
    outv = out.ap().rearrange("c p n -> p c n")
    pv_ = pT.ap().rearrange("c p n -> p c n")
    GC = 1.5957691216057308
    ysall = [f"YS{c}" for c in range(4)]
    for t in range(NT):
        c0, c1 = t * TT, (t + 1) * TT
        dma(xt[:, :, :], x1v[:, :, c0:c1], P.rd("X1"), XA())
        dma(ots[:, :, :], OT.ap().rearrange("c p n -> p c n")[:, :, c0:c1], P.rd("OT"), ["ots"])
        dma(ysf[:, :, :], YS.ap().rearrange("c p n -> p c n")[:, :, c0:c1], ysall, ["ysf"])
        dma(pf[:, :, :], pv_[:, :, c0:c1], [], ["pf"])
        cp(pbf[:, :, :], pf[:, :, :], ["pf"], ["pbf"], eng=G)
        tt(zf[:, :, :], ysf[:, :, :], ysf[:, :, :], ALU.mult, ["ysf"], ["zf"])
        ts(zf[:, :, :], zf[:, :, :], 0.044715, 1.0, ALU.mult, ALU.add, ["zf"], ["zf"])
        tt(zf[:, :, :], zf[:, :, :], ysf[:, :, :], ALU.mult, ["zf", "ysf"], ["zf"])
        act(zf[:, :, :], zf[:, :, :], AF.Sigmoid, ["zf"], ["zf"], scale=GC)
        tt(zf[:, :, :], zf[:, :, :], ysf[:, :, :], ALU.mult, ["zf", "ysf"], ["zf"])
        cp(zbf4[:, :, :], zf[:, :, :], ["zf"], ["zbf4"], eng=G)
        for c in range(4):
            b = bank()
            for k in range(4):
                mm(ps[b][:, :], wglu_s[:, k, c * 128:(c + 1) * 128], zbf4[:, k, :], k == 0, k == 3, ["wglu_s", "zbf4"], [f"ps{b}"])
            ti = tmp()
            act(tmpf[ti][:, :], ps[b][:, :], AF.Sigmoid, [f"ps{b}"], [f"tmpf{ti}"])
            tt(zzb[:, c, :], zf[:, c, :], tmpf[ti][:, :], ALU.mult, ["zf", f"tmpf{ti}"], ["zzb"])
        for o in range(NCH):
            gi_ = o % 2
            dma(gaf[gi_][:, :], GA[o, :, c0:c1], P.rd("G0"), [f"gaf{gi_}"])
            dma(gbf[gi_][:, :], GB[o, :, c0:c1], P.rd("G1"), [f"gbf{gi_}"])
            ba, bb = bank(), bank()
            for k in range(4):
                mm(ps[ba][:, :], woa_s[:, k, o * 128:(o + 1) * 128], ots[:, k, :], k == 0, k == 3, ["woa_s", "ots"], [f"ps{ba}"])
            for k in range(4):
                mm(ps[bb][:, :], wos_s[:, k, o * 128:(o + 1) * 128], zzb[:, k, :], k == 0, k == 3, ["wos_s", "zzb"], [f"ps{bb}"])
            tt(gaf[gi_][:, :], gaf[gi_][:, :], ps[ba][:, :], ALU.mult, [f"gaf{gi_}", f"ps{ba}"], [f"gaf{gi_}"])
            tt(gbf[gi_][:, :], gbf[gi_][:, :], ps[bb][:, :], ALU.mult, [f"gbf{gi_}", f"ps{bb}"], [f"gbf{gi_}"])
            tt(mrb[:, o, :], gaf[gi_][:, :], gbf[gi_][:, :], ALU.add, [f"gaf{gi_}", f"gbf{gi_}"], ["mrb"], eng=G)
        for half in range(2):
            iw = load_wA("wout", half * 512, (half + 1) * 512)
            for cc in range(4):
                o = half * 4 + cc
                b = lbank()
                for k in range(NCH):
                    mm(ps[b][:, :], wA[iw][:, k, cc * 128:(cc + 1) * 128], mrb[:, k, :], k == 0, k == NCH - 1, [f"wA{iw}", "mrb"], [f"ps{b}"])
                stt(xt[:, o, :], ps[b][:, :], 1.0 / ALPHA, xt[:, o, :], ALU.mult, ALU.add, [f"ps{b}", XR(o)], [XR(o)])
                ln_prep_chunk(o)
                if o >= 1:
                    ln_stats_mm(o - 1)
        ln_stats_mm(NCH - 1)
        layer_norm(1, LN_EPS / (ALPHA * ALPHA), xbf, "xbf")
        ffn("w1b", "w3b", "w2b")
        layer_norm(2, LN_EPS / (ALPHA * ALPHA), xbf, "xbf")
        for half in range(2):
            iw = load_wA("wpg", half * 512, (half + 1) * 512)
            for cc in range(4):
                o = half * 4 + cc
                bg, bp = lbank(), lbank()
                for k in range(NCH):
                    mm(ps[bg][:, :], wA[iw][:, k, cc * 128:(cc + 1) * 128], xbf[:, k, :], k == 0, k == NCH - 1, [f"wA{iw}", f"xbf{k}"], [f"ps{bg}"])
                for k in range(2):
                    mm(ps[bp][:, :], wpp_s[:, k, o * 128:(o + 1) * 128], pbf[:, k, :], k == 0, k == 1, ["wpp_s", "pbf"], [f"ps{bp}"])
                ti = tmp()
                act(tmpf[ti][:, :], ps[bg][:, :], AF.Sigmoid, [f"ps{bg}"], [f"tmpf{ti}"])
                tt(tmpf[ti][:, :], tmpf[ti][:, :], ps[bp][:, :], ALU.mult, [f"tmpf{ti}", f"ps{bp}"], [f"tmpf{ti}"])
                stt(xt[:, o, :], tmpf[ti][:, :], 1.0 / ALPHA, xt[:, o, :], ALU.mult, ALU.add, [f"tmpf{ti}", XR(o)], [XR(o)])
                ln_prep_chunk(o)
                if o >= 1:
                    ln_stats_mm(o - 1)
        ln_stats_mm(NCH - 1)
        layer_norm(3, LN_EPS / (ALPHA * ALPHA), None, None)
        dma(outv[:, :, c0:c1], xt[:, :, :], XA(), [P.wr("OUT")], q=G)

    P.emit(P.rd("OUT"), es)
    pst.close()
    es.close()
    return nc


def _perm(L):
    K1 = L // 16
    T3 = K1 // 16
    n = np.arange(L)
    j = n // K1
    j2 = (n % K1) // T3
    k1 = n % T3
    return (k1 * 16 + j2) * 16 + j


_NC_CACHE = {}


def kernel(**inp):
    B, S, _ = inp["x"].shape
    L = S // 2
    f32 = np.float32
    perm = _perm(L)
    if L not in _NC_CACHE:
        _NC_CACHE[L] = build(L)
    nc = _NC_CACHE[L]

    def chunks(a):
        return np.ascontiguousarray(a.reshape(a.shape[0] // 128, 128, a.shape[1]))

    def col(v):
        return np.ascontiguousarray(v.reshape(-1, 128).T)

    inv_freq = (10000.0 ** (-np.arange(0, 32, 2, dtype=np.float32) / 32)).astype(f32)
    W = {
        "w1a": inp["ffn1_w1"][0], "w3a": inp["ffn1_w3"][0], "w2a": inp["ffn1_w2"][0], "win": inp["w_in"][0],
        "wuk": inp["w_uk"][0].reshape(256, 512), "wuv": inp["w_uv"][0].reshape(256, 512),
        "woa": inp["w_o_attn"][0], "wglu": inp["w_glu"][0], "wos": inp["w_o_ssm"][0], "wout": inp["w_out"][0],
        "w1b": inp["ffn2_w1"][0], "w3b": inp["ffn2_w3"][0], "w2b": inp["ffn2_w2"][0],
        "wpp": inp["ple_w_proj"][0], "wpg": inp["ple_w_gate"][0],
    }
    wq = inp["w_uq"][0]
    W["wuq"] = np.concatenate([wq[:, :, 0:64].reshape(384, 512), wq[:, :, 64:80].reshape(384, 128), wq[:, :, 80:96].reshape(384, 128)], axis=1)
    W = {k: np.ascontiguousarray(v, dtype=f32) for k, v in W.items()}
    lnp = np.concatenate([col(inp[f"ln{i}_{gb}"][0]) for i in (1, 2, 3, 4) for gb in ("g", "b")], axis=1).astype(f32)
    qkg = np.concatenate([col(inp["q_norm_g"][0]), col(inp["kv_norm_g"][0])], axis=1).astype(f32)
    ssd = col(inp["ssm_d"][0].reshape(512)).astype(f32)

    def ssm_pack(sfx):
        lre, lim, ldt = inp["ssm_lam_re_" + sfx][0], inp["ssm_lam_im_" + sfx][0], inp["ssm_log_dt_" + sfx][0]
        bre, bim = inp["ssm_b_re_" + sfx][0], inp["ssm_b_im_" + sfx][0]
        cre, cim = inp["ssm_c_re_" + sfx][0], inp["ssm_c_im_" + sfx][0]
        ldtb = np.broadcast_to(ldt[:, None], (32, 64))
        st = np.zeros((3, 128, 16), f32)
        rp = np.zeros((3, 128, 2048), f32)
        bp = np.zeros((2, 128, 2048), f32)
        cpad = np.zeros((2, 128, 2048), f32)
        for q in range(16):
            for gs_ in range(2):
                g = 2 * q + gs_
                for i, a in enumerate((lre, lim, ldtb)):
                    st[i, gs_ * 64:(gs_ + 1) * 64, q] = a[g]
                    rp[i, :, q * 128 + gs_ * 64:q * 128 + (gs_ + 1) * 64] = a[g][None, :]
                r0 = (g % 8) * 16
                bp[0, r0:r0 + 16, q * 128 + gs_ * 64:q * 128 + (gs_ + 1) * 64] = bre[g].T
                bp[1, r0:r0 + 16, q * 128 + gs_ * 64:q * 128 + (gs_ + 1) * 64] = bim[g].T
                cpad[0, gs_ * 64:(gs_ + 1) * 64, q * 128 + r0:q * 128 + r0 + 16] = cre[g].T
                cpad[1, gs_ * 64:(gs_ + 1) * 64, q * 128 + r0:q * 128 + r0 + 16] = cim[g].T
        return st, rp, bp, cpad

    packs = {"f": ssm_pack("f"), "b": ssm_pack("b")}
    in_maps = []
    for c in range(8):
        b, half = c // 2, c % 2
        tl = perm if half == 0 else (L - 1 - perm)
        tg = half * L + tl
        m = dict(W)
        m["xT"] = chunks(np.ascontiguousarray(inp["x"][b][tg].T))
        m["pT"] = chunks(np.ascontiguousarray(inp["p"][0, b][tg].T))
        ang = tg.astype(f32)[None, :] * inv_freq[:, None]
        m["cosT"] = np.ascontiguousarray(np.tile(np.cos(ang).astype(f32), (8, 1)))
        m["sinT"] = np.ascontiguousarray(np.tile(np.sin(ang).astype(f32), (8, 1)))
        order = ("f", "b") if half == 0 else ("b", "f")
        m["sst"] = np.stack([packs[o][0] for o in order])
        m["ssr"] = np.stack([packs[o][1] for o in order])
        m["ssb"] = np.stack([packs[o][2] for o in order])
        m["ssc"] = np.stack([packs[o][3] for o in order])
        m["lnp"], m["qkg"], m["ssd"] = lnp, qkg, ssd
        sel = np.zeros((128, 2), f32)
        sel[:, 1 - half] = 1.0
        m["selp"] = sel
        in_maps.append(m)
    res = run_bass_kernel_spmd(nc, in_maps, core_ids=list(range(8)))
    outp = np.empty((B, S, D), f32)
    for c in range(8):
        b, half = c // 2, c % 2
        tl = perm if half == 0 else (L - 1 - perm)
        tg = half * L + tl
        o = res.results[c]["outT"].reshape(D, L)
        outp[b, tg, :] = o.T
    return outp
```
